# Optimizing a Trainium2 kernel written in Bass

```python
import math
import jax
import jax.numpy as jnp
from jax import lax
import numpy as np

D_MODEL = 1024
BATCH = 32
SEQ = 256
DEPTH = 4
DEC_BATCH = 8
DEC_SEQ = 4096
PAST_LEN = 512

GRID_W = 64
N_AH = (DEPTH + 1) // 2
N_DN = DEPTH // 2
H_A = 8
KVH_A = 2
G_A = H_A // KVH_A
HD_A = 64
WINDOW = 128
ATTN_BLOCK = 128
ROPE_THETA = 10000.0
ROPE_AXIS_DIM = HD_A // 2
HY_C = 512
HY_ORDER = 2
SHORT_W = 3
HY_EMB = 33
HY_BANDS = (HY_EMB - 1) // 2
HY_FORD = 64
HY_MIN_DECAY = math.log(1e-2) / 1.5
HY_MAX_DECAY = math.log(1e-2) / 0.3
AH_IN = H_A * HD_A + 2 * KVH_A * HD_A + 3 * HY_C
AH_OUT = H_A * HD_A + HY_C
H_C = 8
DK_C = 128
DV_C = 128
DN_CHUNK = 64
DN_IN = 2 * H_C * DK_C + 2 * H_C * DV_C + 4 * H_C
DN_OUT = H_C * DV_C
D_FF = 2816
NORM_EPS = 1e-6
NEG_INF = -1e30

kernel_name = 'hybrid_diffusion_swa_hyena_gdn_step'


def rms_norm(x, g):
    xf = x.astype(jnp.float32)
    y = xf * lax.rsqrt(jnp.mean(xf * xf, axis=-1, keepdims=True) + NORM_EPS)
    return (y * g.astype(jnp.float32)).astype(x.dtype)


def l2_norm(x):
    return x * lax.rsqrt(jnp.sum(x * x, axis=-1, keepdims=True) + NORM_EPS)


def ada_modulation(cond, w, b):
    m = jax.nn.silu(cond) @ w + b
    return m.reshape(cond.shape[0], 1, 9, D_MODEL)


def ada_norm(x, g, mod, j):
    return rms_norm(x, g) * (1 + mod[:, :, 3 * j + 1]) + mod[:, :, 3 * j]


def half_ffn(x, g, mod, j, w13, w2):
    gt, up = jnp.split(ada_norm(x, g, mod, j) @ w13, 2, axis=-1)
    return x + 0.5 * mod[:, :, 3 * j + 2] * ((jax.nn.silu(gt) * up) @ w2)


def depthwise_conv(x, w):
    k = w.shape[0]
    return lax.conv_general_dilated(x, w.astype(x.dtype)[:, None, :], window_strides=(1,),
                                    padding=[(k // 2, k // 2)],
                                    dimension_numbers=('NWC', 'WIO', 'NWC'),
                                    feature_group_count=x.shape[-1])


def axial_rope(length):
    rows = length // GRID_W
    r, col = jnp.meshgrid(jnp.arange(rows), jnp.arange(GRID_W), indexing='ij')
    inv = ROPE_THETA ** (-jnp.arange(0, ROPE_AXIS_DIM, 2, dtype=jnp.float32) / ROPE_AXIS_DIM)
    ang = jnp.concatenate([r.reshape(-1, 1).astype(jnp.float32) * inv,
                           col.reshape(-1, 1).astype(jnp.float32) * inv], axis=-1)
    return jnp.cos(ang), jnp.sin(ang)


def apply_rope(x, cos, sin):
    half = x.shape[-1] // 2
    shape = (1, x.shape[1]) + (1,) * (x.ndim - 3) + (half,)
    c = cos.reshape(shape).astype(x.dtype)
    s = sin.reshape(shape).astype(x.dtype)
    x1, x2 = x[..., :half], x[..., half:]
    return jnp.concatenate([x1 * c - x2 * s, x2 * c + x1 * s], axis=-1)


def sink_softmax(logits, sink):
    s = jnp.broadcast_to(sink.astype(jnp.float32).reshape(1, KVH_A, G_A, 1, 1), logits.shape[:-1] + (1,))
    return jax.nn.softmax(jnp.concatenate([s, logits], axis=-1), axis=-1)[..., 1:]


def context_attention(q, k, v, sink):
    b, length = q.shape[:2]
    qb = jnp.moveaxis(q.reshape(b, length // ATTN_BLOCK, ATTN_BLOCK, KVH_A, G_A, HD_A), 1, 0) * (HD_A ** -0.5)

    def block(qi):
        s = jnp.einsum('bqkgd,bskd->bkgqs', qi, k).astype(jnp.float32)
        p = sink_softmax(s, sink).astype(v.dtype)
        return jnp.einsum('bkgqs,bskd->bqkgd', p, v)

    out = lax.map(block, qb)
    return jnp.moveaxis(out, 0, 1).reshape(b, length, H_A * HD_A)


def window_attention(q, k, v, ck, cv, sink):
    b, length = q.shape[:2]
    nb = length // ATTN_BLOCK
    n_ctx = ck.shape[1]
    pad = ((0, 0), (ATTN_BLOCK, ATTN_BLOCK), (0, 0), (0, 0))
    kp = jnp.pad(k, pad)
    vp = jnp.pad(v, pad)
    qb = jnp.moveaxis(q.reshape(b, nb, ATTN_BLOCK, KVH_A, G_A, HD_A), 1, 0) * (HD_A ** -0.5)
    q_off = jnp.arange(ATTN_BLOCK)
    k_off = jnp.arange(3 * ATTN_BLOCK)

    def block(args):
        i, qi = args
        ks = lax.dynamic_slice_in_dim(kp, i * ATTN_BLOCK, 3 * ATTN_BLOCK, axis=1)
        vs = lax.dynamic_slice_in_dim(vp, i * ATTN_BLOCK, 3 * ATTN_BLOCK, axis=1)
        q_pos = i * ATTN_BLOCK + q_off
        k_pos = (i - 1) * ATTN_BLOCK + k_off
        valid = ((jnp.abs(q_pos[:, None] - k_pos[None, :]) <= WINDOW)
                 & (k_pos >= 0)[None, :] & (k_pos < length)[None, :])
        s_loc = jnp.where(valid, jnp.einsum('bqkgd,bskd->bkgqs', qi, ks).astype(jnp.float32), NEG_INF)
        s_ctx = jnp.einsum('bqkgd,bskd->bkgqs', qi, ck).astype(jnp.float32)
        p = sink_softmax(jnp.concatenate([s_ctx, s_loc], axis=-1), sink).astype(v.dtype)
        return (jnp.einsum('bkgqs,bskd->bqkgd', p[..., :n_ctx], cv)
                + jnp.einsum('bkgqs,bskd->bqkgd', p[..., n_ctx:], vs))

    out = lax.map(block, (jnp.arange(nb), qb))
    return jnp.moveaxis(out, 0, 1).reshape(b, length, H_A * HD_A)


def hyena_filter(length, w1, b1, f1, w2, b2, f2, w3):
    t = jnp.linspace(0.0, 1.0, length, dtype=jnp.float32)[:, None]
    w = 2 * math.pi * jnp.arange(length, dtype=jnp.float32)[:, None] / length
    f = jnp.linspace(1e-4, HY_BANDS - 1, HY_BANDS, dtype=jnp.float32)[None, :]
    z = jnp.concatenate([t, jnp.cos(f * w), -jnp.sin(f * w)], axis=-1)
    h = jnp.sin(f1.astype(jnp.float32) * (z @ w1.astype(jnp.float32) + b1.astype(jnp.float32)))
    h = jnp.sin(f2.astype(jnp.float32) * (h @ w2.astype(jnp.float32) + b2.astype(jnp.float32)))
    h = h @ w3.astype(jnp.float32)
    deltas = jnp.abs(jnp.linspace(HY_MIN_DECAY, HY_MAX_DECAY, h.shape[-1], dtype=jnp.float32))
    h = (h * jnp.exp(-t * deltas)).reshape(length, 2, HY_ORDER, HY_C)
    kern = jnp.concatenate([h[:, 0], jnp.zeros((1, HY_ORDER, HY_C), jnp.float32), h[:0:-1, 1]], axis=0)
    return kern * lax.rsqrt(jnp.sum(kern * kern, axis=0, keepdims=True) + NORM_EPS)


def hyena_mixer(u3, conv_w, conv_b, w1, b1, f1, w2, b2, f2, w3, bias):
    length = u3.shape[1]
    u3 = depthwise_conv(u3, conv_w) + conv_b
    x1, x2, v = jnp.split(u3, 3, axis=-1)
    kf = jnp.fft.rfft(hyena_filter(length, w1, b1, f1, w2, b2, f2, w3), n=2 * length, axis=0)
    z = v.astype(jnp.float32)
    for o, gate in enumerate((x1, x2)):
        y = jnp.fft.irfft(jnp.fft.rfft(z, n=2 * length, axis=1) * kf[None, :, o], n=2 * length, axis=1)[:, :length]
        z = gate.astype(jnp.float32) * (y + z * bias[o].astype(jnp.float32))
    return z.astype(u3.dtype)


def attn_hyena_mixer(h, ctx_kv, w_in, w_out, q_norm, k_norm, sink, conv_w, conv_b,
                     w1, b1, f1, w2, b2, f2, w3, hy_bias):
    b, length, _ = h.shape
    q, k, v, u3 = jnp.split(h @ w_in, [H_A * HD_A, H_A * HD_A + KVH_A * HD_A,
                                       H_A * HD_A + 2 * KVH_A * HD_A], axis=-1)
    q = rms_norm(q.reshape(b, length, KVH_A, G_A, HD_A), q_norm)
    k = rms_norm(k.reshape(b, length, KVH_A, HD_A), k_norm)
    v = v.reshape(b, length, KVH_A, HD_A)
    if ctx_kv is None:
        a = context_attention(q, k, v, sink)
    else:
        cos, sin = axial_rope(length)
        a = window_attention(apply_rope(q, cos, sin), apply_rope(k, cos, sin), v, ctx_kv[0], ctx_kv[1], sink)
    y = hyena_mixer(u3, conv_w, conv_b, w1, b1, f1, w2, b2, f2, w3, hy_bias)
    return jnp.concatenate([a, y], axis=-1) @ w_out, k, v


def chunk_gated_delta_rule(q, k, v, g, beta, s0):
    b, length, h, dk = q.shape
    dv = v.shape[-1]
    n = length // DN_CHUNK

    def chunks(t):
        t = t.reshape((b, n, DN_CHUNK, h) + t.shape[3:])
        return jnp.moveaxis(t, (1, 3), (0, 2))

    qc, kc, vc = chunks(q * (dk ** -0.5)), chunks(k), chunks(v)
    gcum = jnp.cumsum(chunks(g), axis=-1)
    bc = chunks(beta)[..., None]
    idx = jnp.arange(DN_CHUNK)
    lower = idx[:, None] >= idx[None, :]
    decay = jnp.where(lower, jnp.exp(jnp.where(lower, gcum[..., :, None] - gcum[..., None, :], 0.0)), 0.0)
    kb = kc * bc
    lmat = jnp.where(idx[:, None] > idx[None, :], jnp.einsum('nbhik,nbhjk->nbhij', kb, kc) * decay, 0.0)
    rhs = jnp.concatenate([vc * bc, kb * jnp.exp(gcum)[..., None]], axis=-1)
    sol = lax.linalg.triangular_solve(lmat + jnp.eye(DN_CHUNK, dtype=jnp.float32), rhs,
                                      left_side=True, lower=True, unit_diagonal=True)
    u, w = sol[..., :dv], sol[..., dv:]

    def step(s, inp):
        qi, ki, ui, wi, gi, di = inp
        v_new = ui - jnp.einsum('bhck,bhkv->bhcv', wi, s)
        o = (jnp.einsum('bhck,bhkv->bhcv', qi * jnp.exp(gi)[..., None], s)
             + jnp.einsum('bhij,bhjv->bhiv', jnp.einsum('bhik,bhjk->bhij', qi, ki) * di, v_new))
        gl = gi[..., -1]
        s = (s * jnp.exp(gl)[..., None, None]
             + jnp.einsum('bhck,bhcv->bhkv', ki * jnp.exp(gl[..., None] - gi)[..., None], v_new))
        return s, o

    s_final, o = lax.scan(step, s0, (qc, kc, u, w, gcum, decay))
    o = jnp.moveaxis(o, (0, 2), (1, 3)).reshape(b, length, h, dv)
    return o, s_final


def deltanet_mixer(h, s0_f, s0_b, w_in, w_out, conv_w, a_log, dt_bias, norm_g):
    b, length, _ = h.shape
    nq = H_C * DK_C
    qkv, z, bl, al = jnp.split(h @ w_in, [2 * nq + H_C * DV_C, 2 * nq + 2 * H_C * DV_C,
                                          2 * nq + 2 * H_C * DV_C + 2 * H_C], axis=-1)
    qkv = jax.nn.silu(depthwise_conv(qkv, conv_w)).astype(jnp.float32)
    q, k, v = jnp.split(qkv, [nq, 2 * nq], axis=-1)
    q = l2_norm(q.reshape(b, length, H_C, DK_C))
    k = l2_norm(k.reshape(b, length, H_C, DK_C))
    v = v.reshape(b, length, H_C, DV_C)
    beta = jax.nn.sigmoid(bl.astype(jnp.float32)).reshape(b, length, 2, H_C)
    g = -jnp.exp(a_log.astype(jnp.float32)) * jax.nn.softplus(
        al.astype(jnp.float32).reshape(b, length, 2, H_C) + dt_bias.astype(jnp.float32))
    o_f, s_f = chunk_gated_delta_rule(q, k, v, g[:, :, 0], beta[:, :, 0], s0_f.astype(jnp.float32))
    o_b, s_b = chunk_gated_delta_rule(q[:, ::-1], k[:, ::-1], v[:, ::-1], g[:, ::-1, 1], beta[:, ::-1, 1],
                                      s0_b.astype(jnp.float32))
    o = rms_norm(o_f + o_b[:, ::-1], norm_g) * jax.nn.silu(z.astype(jnp.float32).reshape(b, length, H_C, DV_C))
    return o.reshape(b, length, DN_OUT).astype(h.dtype) @ w_out, s_f, s_b


def setup_inputs(seed: int = 0) -> dict:
    key = jax.random.key(seed)
    keys = iter(jax.random.split(key, 48))

    def nrm(shape, scale=1.0):
        return scale * jax.random.normal(next(keys), shape, jnp.float32)

    def gain(shape):
        return 1.0 + nrm(shape, 0.05)

    x_prompt = nrm((BATCH, SEQ, D_MODEL))
    x_sample = nrm((DEC_BATCH, DEC_SEQ, D_MODEL))
    cache_k = nrm((DEC_BATCH, N_AH, PAST_LEN, KVH_A, HD_A))
    cache_v = nrm((DEC_BATCH, N_AH, PAST_LEN, KVH_A, HD_A))
    state_fwd = nrm((DEC_BATCH, N_DN, H_C, DK_C, DV_C), 0.5)
    state_bwd = nrm((DEC_BATCH, N_DN, H_C, DK_C, DV_C), 0.5)
    c = nrm((DEC_BATCH, D_MODEL))
    c_ctx = nrm((D_MODEL,))
    norm_g = gain((DEPTH, 3, D_MODEL))
    ada_w = nrm((DEPTH, D_MODEL, 9 * D_MODEL), 0.5 * D_MODEL ** -0.5)
    ada_b = nrm((DEPTH, 9 * D_MODEL), 0.02)
    ffn_w13 = nrm((DEPTH, 2, D_MODEL, 2 * D_FF), D_MODEL ** -0.5)
    ffn_w2 = nrm((DEPTH, 2, D_FF, D_MODEL), D_FF ** -0.5)
    mx_w_in = nrm((N_AH, D_MODEL, AH_IN), D_MODEL ** -0.5)
    mx_w_out = nrm((N_AH, AH_OUT, D_MODEL), AH_OUT ** -0.5)
    q_norm = gain((N_AH, HD_A))
    k_norm = gain((N_AH, HD_A))
    attn_sink = nrm((N_AH, H_A), 0.5)
    hy_conv_w = nrm((N_AH, SHORT_W, 3 * HY_C), SHORT_W ** -0.5)
    hy_conv_b = nrm((N_AH, 3 * HY_C), 0.02)
    hy_w1 = nrm((N_AH, HY_EMB, HY_FORD), HY_EMB ** -0.5)
    hy_b1 = nrm((N_AH, HY_FORD), 0.1)
    hy_freq1 = gain((N_AH, HY_FORD))
    hy_w2 = nrm((N_AH, HY_FORD, HY_FORD), HY_FORD ** -0.5)
    hy_b2 = nrm((N_AH, HY_FORD), 0.1)
    hy_freq2 = gain((N_AH, HY_FORD))
    hy_w3 = nrm((N_AH, HY_FORD, 2 * HY_ORDER * HY_C), HY_FORD ** -0.5)
    hy_bias = nrm((N_AH, HY_ORDER, HY_C))
    dn_w_in = nrm((N_DN, D_MODEL, DN_IN), D_MODEL ** -0.5)
    dn_w_out = nrm((N_DN, DN_OUT, D_MODEL), DN_OUT ** -0.5)
    dn_conv_w = nrm((N_DN, SHORT_W, 2 * H_C * DK_C + H_C * DV_C), SHORT_W ** -0.5)
    dn_a_log = jnp.log(jax.random.uniform(next(keys), (N_DN, 2, H_C), jnp.float32, 1.0, 16.0))
    dt = jax.random.uniform(next(keys), (N_DN, 2, H_C), jnp.float32, 0.001, 0.1)
    dn_dt_bias = dt + jnp.log(-jnp.expm1(-dt))
    dn_norm_g = gain((N_DN, DV_C))
    return {'x_prompt': x_prompt, 'x_sample': x_sample, 'cache_k': cache_k, 'cache_v': cache_v,
            'state_fwd': state_fwd, 'state_bwd': state_bwd, 'c': c, 'c_ctx': c_ctx,
            'norm_g': norm_g, 'ada_w': ada_w, 'ada_b': ada_b, 'ffn_w13': ffn_w13, 'ffn_w2': ffn_w2,
            'mx_w_in': mx_w_in, 'mx_w_out': mx_w_out, 'q_norm': q_norm, 'k_norm': k_norm,
            'attn_sink': attn_sink, 'hy_conv_w': hy_conv_w, 'hy_conv_b': hy_conv_b,
            'hy_w1': hy_w1, 'hy_b1': hy_b1, 'hy_freq1': hy_freq1, 'hy_w2': hy_w2, 'hy_b2': hy_b2,
            'hy_freq2': hy_freq2, 'hy_w3': hy_w3, 'hy_bias': hy_bias,
            'dn_w_in': dn_w_in, 'dn_w_out': dn_w_out, 'dn_conv_w': dn_conv_w,
            'dn_a_log': dn_a_log, 'dn_dt_bias': dn_dt_bias, 'dn_norm_g': dn_norm_g}


def reference(x_prompt, x_sample, cache_k, cache_v, state_fwd, state_bwd, c, c_ctx,
              norm_g, ada_w, ada_b, ffn_w13, ffn_w2, mx_w_in, mx_w_out, q_norm, k_norm,
              attn_sink, hy_conv_w, hy_conv_b, hy_w1, hy_b1, hy_freq1, hy_w2, hy_b2,
              hy_freq2, hy_w3, hy_bias, dn_w_in, dn_w_out, dn_conv_w, dn_a_log, dn_dt_bias,
              dn_norm_g):
    xp, xs = x_prompt, x_sample
    new_k, new_v, new_sf, new_sb = [], [], [], []
    zero_state = jnp.zeros((xp.shape[0], H_C, DK_C, DV_C), jnp.float32)

    def ah_mixer(h, ctx_kv, i):
        return attn_hyena_mixer(h, ctx_kv, mx_w_in[i], mx_w_out[i], q_norm[i], k_norm[i], attn_sink[i],
                                hy_conv_w[i], hy_conv_b[i], hy_w1[i], hy_b1[i], hy_freq1[i],
                                hy_w2[i], hy_b2[i], hy_freq2[i], hy_w3[i], hy_bias[i])

    def dn_mixer(h, s0_f, s0_b, i):
        return deltanet_mixer(h, s0_f, s0_b, dn_w_in[i], dn_w_out[i], dn_conv_w[i],
                              dn_a_log[i], dn_dt_bias[i], dn_norm_g[i])

    for layer in range(DEPTH):
        mp = ada_modulation(c_ctx[None, :], ada_w[layer], ada_b[layer])
        ms = ada_modulation(c, ada_w[layer], ada_b[layer])
        xp = half_ffn(xp, norm_g[layer, 0], mp, 0, ffn_w13[layer, 0], ffn_w2[layer, 0])
        xs = half_ffn(xs, norm_g[layer, 0], ms, 0, ffn_w13[layer, 0], ffn_w2[layer, 0])
        hp = ada_norm(xp, norm_g[layer, 1], mp, 1)
        hs = ada_norm(xs, norm_g[layer, 1], ms, 1)
        i = layer // 2
        if layer % 2 == 0:
            o_p, k_p, v_p = ah_mixer(hp, None, i)
            o_s, _, _ = ah_mixer(hs, (cache_k[:, i], cache_v[:, i]), i)
            new_k.append(k_p)
            new_v.append(v_p)
        else:
            o_p, s_f, s_b = dn_mixer(hp, zero_state, zero_state, i)
            o_s, _, _ = dn_mixer(hs, state_fwd[:, i], state_bwd[:, i], i)
            new_sf.append(s_f.astype(xp.dtype))
            new_sb.append(s_b.astype(xp.dtype))
        xp = xp + mp[:, :, 5] * o_p
        xs = xs + ms[:, :, 5] * o_s
        xp = half_ffn(xp, norm_g[layer, 2], mp, 2, ffn_w13[layer, 1], ffn_w2[layer, 1])
        xs = half_ffn(xs, norm_g[layer, 2], ms, 2, ffn_w13[layer, 1], ffn_w2[layer, 1])

    new_cache_k = jnp.stack(new_k, axis=1)
    new_cache_v = jnp.stack(new_v, axis=1)
    new_state_fwd = jnp.stack(new_sf, axis=1)
    new_state_bwd = jnp.stack(new_sb, axis=1)
    return (xp, xs, new_cache_k, new_cache_v, new_state_fwd, new_state_bwd)
```

```python
import contextlib
import math
import numpy as np
import ml_dtypes
import concourse.bass as bass
import concourse.mybir as mybir
from concourse.bass_utils import run_bass_kernel_spmd

F32 = mybir.dt.float32
BF16 = mybir.dt.bfloat16
AF = mybir.ActivationFunctionType
ALU = mybir.AluOpType

D = 1024
NTOK = 5120
TM = 1024
NMT = NTOK // TM
DFF = 2816
NFC = DFF // 128
DEPTH = 4
EPS = 1e-6

ENGS = ("pe", "dve", "act", "pool", "sp")
NDMASEM = 8


class Buf:
    __slots__ = ("name", "w", "rs", "excl")

    def __init__(self, name="", excl=False):
        self.name = name
        self.w = None
        self.rs = []
        self.excl = excl


class Op:
    __slots__ = ("eng", "fn", "deps", "dma", "sig", "cnt", "semi", "use", "gid", "cost")


class Sched:
    def __init__(self, nc, st):
        self.nc = nc
        self.csem = {e: st.enter_context(nc.semaphore("c_" + e)) for e in ENGS}
        self.dsem = {e: [st.enter_context(nc.semaphore("d_%s%d" % (e, i))) for i in range(NDMASEM)]
                     for e in ("sp", "pool", "act")}
        self.cnt = {e: 0 for e in ENGS}
        self.ndma = {e: 0 for e in ENGS}
        self.ops = []
        self.bufs = []
        self.nstage = 0
        self.ninstr = 0
        self.xlat = 2.0

    def buf(self, name=""):
        return Buf(name)

    COST = {"pe": 0.12, "dve": 0.6, "act": 0.7, "pool": 0.9, "sp": 2.5}

    def add(self, eng, fn, reads=(), writes=(), dma=False, cost=None):
        op = Op()
        op.cost = cost if cost is not None else (2.5 if dma else self.COST[eng])
        op.eng = eng
        op.fn = fn
        op.dma = dma
        op.sig = False
        op.gid = len(self.ops)
        deps = set()
        ex = [b for b in reads if b.excl]
        if ex:
            reads = [b for b in reads if not b.excl]
            writes = list(writes) + [b for b in ex if b not in writes]
        for b in reads:
            if b.w is not None:
                deps.add(b.w)
        for b in writes:
            if b.w is not None:
                deps.add(b.w)
            deps.update(b.rs)
        deps.discard(op.gid)
        op.deps = deps
        for b in reads:
            b.rs.append(op.gid)
            self.bufs.append(b)
        for b in writes:
            b.w = op.gid
            b.rs = []
            self.bufs.append(b)
        self.ops.append(op)
        return op

    def _resched(self, ops):
        import heapq
        n = len(ops)
        succ = [[] for _ in range(n)]
        indeg = [0] * n
        for op in ops:
            indeg[op.gid] = len(op.deps)
            for d in op.deps:
                succ[d].append(op.gid)
        ready_t = [0.0] * n
        fin = [0.0] * n
        heaps = {e: [] for e in ENGS}
        free = {e: 0.0 for e in ENGS}
        for op in ops:
            if indeg[op.gid] == 0:
                heapq.heappush(heaps[op.eng], (0.0, op.gid))
        order = []
        while len(order) < n:
            best = None
            for e in ENGS:
                h = heaps[e]
                if not h:
                    continue
                rt, gid = h[0]
                st_ = max(free[e], rt)
                if best is None or (st_, gid) < (best[0], best[1]):
                    best = (st_, gid, e)
            st_, gid, e = best
            heapq.heappop(heaps[e])
            op = ops[gid]
            if op.dma:
                free[e] = st_ + 0.15
                fin[gid] = st_ + op.cost
            else:
                free[e] = st_ + op.cost
                fin[gid] = st_ + op.cost
            order.append(gid)
            for s_ in succ[gid]:
                so = ops[s_]
                lat = 0.05 if (so.eng == op.eng and not op.dma) else self.xlat
                ready_t[s_] = max(ready_t[s_], fin[gid] + lat)
                indeg[s_] -= 1
                if indeg[s_] == 0:
                    heapq.heappush(heaps[so.eng], (ready_t[s_], s_))
        return order

    def end_stage(self, resched=False):
        nc = self.nc
        ops = self.ops
        if resched and len(ops) > 2:
            order = self._resched(ops)
            remap = {g: k for k, g in enumerate(order)}
            ops = [ops[g] for g in order]
            for k, op in enumerate(ops):
                op.gid = k
                op.deps = {remap[d] for d in op.deps}
        for op in ops:
            if op.dma:
                k = self.ndma[op.eng]
                self.ndma[op.eng] = k + 1
                op.semi = k % NDMASEM
                op.use = k // NDMASEM + 1
        for op in ops:
            nd = set()
            for d in op.deps:
                p = ops[d]
                if (not p.dma) and (not op.dma) and p.eng == "pe" and op.eng == "pe":
                    continue
                nd.add(d)
                p.sig = True
            op.deps = nd
        for op in ops:
            if (not op.dma) and op.sig:
                self.cnt[op.eng] += 1
                op.cnt = self.cnt[op.eng]
        per = {e: [op for op in ops if op.eng == e] for e in ENGS}
        csem, dsem = self.csem, self.dsem
        self.ninstr += len(ops)
        with nc.Block() as block:
            def gen(ename):
                def body(eng):
                    seen_c = {}
                    seen_d = {}
                    for op in per[ename]:
                        need_c = {}
                        need_d = {}
                        for d in op.deps:
                            p = ops[d]
                            if p.dma:
                                key = (p.eng, p.semi)
                                need_d[key] = max(need_d.get(key, 0), 16 * p.use)
                            else:
                                need_c[p.eng] = max(need_c.get(p.eng, 0), p.cnt)
                        if op.dma and op.use > 1:
                            key = (op.eng, op.semi)
                            need_d[key] = max(need_d.get(key, 0), 16 * (op.use - 1))
                        for e, v in need_c.items():
                            if v > seen_c.get(e, 0):
                                eng.wait_ge(csem[e], v)
                                seen_c[e] = v
                        for key, v in need_d.items():
                            if v > seen_d.get(key, 0):
                                eng.wait_ge(dsem[key[0]][key[1]], v)
                                seen_d[key] = v
                        ins = op.fn(eng)
                        if op.dma:
                            ins.then_inc(dsem[op.eng][op.semi], 16)
                        elif op.sig:
                            ins.then_inc(csem[op.eng], 1)
                    last = {}
                    for op in per[ename]:
                        if op.dma:
                            last[op.semi] = op.use
                    for semi, use in last.items():
                        eng.wait_ge(dsem[ename][semi], 16 * use)
                return body

            block.tensor(gen("pe"))
            block.vector(gen("dve"))
            block.scalar(gen("act"))
            block.gpsimd(gen("pool"))
            block.sync(gen("sp"))
        for b in self.bufs:
            b.w = None
            b.rs = []
        self.bufs = []
        self.ops = []
        self.nstage += 1


class T:
    def __init__(self, t, name=""):
        self.t = t
        self.b = Buf(name)


def build_program(cfg):
    nlayers = cfg.get("nlayers", DEPTH)
    do_mix = cfg.get("mixers", True)
    nc = bass.Bass("TRN2", target_bir_lowering=False)

    def din(name, shape, dt=F32):
        return nc.dram_tensor(name, list(shape), dt, kind="ExternalInput").ap()

    def dout(name, shape, dt=F32):
        return nc.dram_tensor(name, list(shape), dt, kind="ExternalOutput").ap()

    dbg = cfg.get("debug", ())

    def dscr(name, shape, dt=F32):
        kind = "ExternalOutput" if name in dbg else "Internal"
        return nc.dram_tensor(name, list(shape), dt, kind=kind).ap()

    xT = din("xT", [128, 8, NTOK])
    condT = din("condT", [128, 8, 2])
    ada_w = din("ada_w", [DEPTH, 9, 128, 8, 1024])
    ada_b = din("ada_b", [128, DEPTH, 72, 2])
    norm_g = din("norm_g", [128, DEPTH, 3, 8, 2])
    w13 = din("w13", [DEPTH * 2, NFC, 128, 8, 256])
    w2 = din("w2", [DEPTH * 2, 8, 128, NFC, 128])
    yT = dout("yT", [128, 8, NTOK])
    xs = dscr("xs", [128, 8, NTOK])
    win = din("win", [2, 128, 8, 2432])
    wout = din("wout", [2, 128, 8, 1024])
    qkn = din("qkn", [128, 2, 2])
    sinkT = din("sinkT", [128, 2, 4])
    cosT_d = din("cosT", [128, 4096])
    sinT_d = din("sinT", [128, 4096])
    rotT_d = din("rotT", [128, 128])
    blk1_d = din("blk1", [128, 128])
    mprev_d = din("mprev", [128, 128])
    mnext_d = din("mnext", [128, 128])
    ckT = din("ckT", [2, 128, 2, 512])
    cvv = din("cvv", [2, 128, 4, 128])
    newk = dout("newk", [2, 2, 64, 1024])
    newv = dout("newv", [2, 1024, 128])
    qT_d = dscr("qT_d", [128, 4, NTOK])
    kT_d = dscr("kT_d", [128, 2, NTOK])
    u3T_d = dscr("u3T_d", [128, 12, NTOK])
    v_d = dscr("v_d", [NTOK, 128])
    ay_d = dscr("ay_d", [128, 8, NTOK], BF16)
    hcw = din("hcw", [128, 2, 12, 3])
    hcb = din("hcb", [128, 2, 12])
    hw1 = din("hw1", [2, 33, 64])
    hb1 = din("hb1", [64, 2, 2])
    hw2 = din("hw2", [2, 64, 64])
    hb2 = din("hb2", [64, 2, 2])
    hw3 = din("hw3", [2, 64, 2048])
    hbias = din("hbias", [128, 2, 2, 4])
    ident_d = din("ident", [128, 128])
    deltas_d = din("deltas", [128, 2048])
    HG = {}
    for L_ in (4096, 256):
        nb_ = L_ // 128
        TT_ = min(512, L_)
        HG[L_] = dict(
            F=din("dftF%d" % L_, [2, nb_, 128, nb_, 128], BF16),
            I=din("dftI%d" % L_, [2, L_ // TT_, 128, nb_, TT_], BF16),
            zf=din("zfeat%d" % L_, [33, L_]),
            tl=din("tlag%d" % L_, [128, nb_]))
    uc_d = dscr("uc_d", [128, 12, NTOK])
    z1T_d = dscr("z1T_d", [128, 4, NTOK])
    hsd_d = dscr("hsd_d", [2, 32, 128, 1024], BF16)
    H_d = dscr("H_d", [2, 2, 32, 128, 512])
    dwin = din("dwin", [2, 128, 8, 4128])
    dwout = din("dwout", [2, 128, 8, 1024])
    dcw = din("dcw", [128, 2, 24, 3])
    dprm = din("dprm", [128, 2, 32])
    dng = din("dng", [128, 2])
    dmask_d = din("dmask", [64, 5, 64])
    lvmask_d = din("lvmask", [64, 12, 64])
    sf0 = din("sf0", [2, 2, 128, 8, 128])
    nst = dout("nst", [2, 2, 4, 128, 8, 128])
    qkvT_d = dscr("qkvT_d", [128, 24, NTOK])
    qkvn_d = dscr("qkvn_d", [128, 24, NTOK])
    zT_d = dscr("zT_d", [128, 8, NTOK])
    ba_d = dscr("ba_d", [NTOK, 32])
    NCH = NTOK // 64
    u_d = dscr("u_d", [2, NCH, 64, 8, 128])
    wT_d = dscr("wT_d", [2, NCH, 128, 8, 64], BF16)
    QgT_d = dscr("QgT_d", [2, NCH, 128, 8, 64], BF16)
    AT_d = dscr("AT_d", [2, NCH, 64, 8, 64], BF16)
    Kd_d = dscr("Kd_d", [2, NCH, 64, 8, 128], BF16)
    egl_d = dscr("egl_d", [2, NCH, 128, 8])
    oT_d = dscr("oT_d", [2, 128, 8, NTOK])

    with contextlib.ExitStack() as top:
        S = Sched(nc, top)
        S.xlat = cfg.get("xlat", 2.0)

        uid = [0]

        def sb(st, name, shape, dt=F32):
            uid[0] += 1
            name = "%s_%d" % (name, uid[0])
            return T(st.enter_context(nc.sbuf_tensor(name, list(shape), dt)), name)

        PS = [T(top.enter_context(nc.psum_tensor("ps%d" % i, [128, 512], F32)), "ps%d" % i) for i in range(8)]
        for p_ in PS:
            p_.b.excl = True
        ones32 = sb(top, "ones32", [128, 128])
        epsT = sb(top, "epsT", [128, 1])
        scond = sb(top, "scond", [128, 8, 2], BF16)
        modT = sb(top, "modT", [128, 72, 2])
        gsT = sb(top, "gsT", [128, 3, 8, 2])
        hgT = sb(top, "hgT", [128, 3, 8, 2])
        adab = sb(top, "adab", [128, DEPTH, 72, 2])
        ng = sb(top, "ng", [128, DEPTH, 3, 8, 2])

        with contextlib.ExitStack() as st:
            cnd = sb(st, "cnd", [128, 8, 2])
            S.add("dve", lambda e: e.memset(ones32.t[:], 1.0), writes=[ones32.b])
            S.add("dve", lambda e: e.memset(epsT.t[:], EPS), writes=[epsT.b])
            S.add("sp", lambda e: e.dma_start(out=cnd.t[:], in_=condT), writes=[cnd.b], dma=True)
            S.add("sp", lambda e: e.dma_start(out=adab.t[:], in_=ada_b), writes=[adab.b], dma=True)
            S.add("sp", lambda e: e.dma_start(out=ng.t[:], in_=norm_g), writes=[ng.b], dma=True)
            S.add("act", lambda e: e.activation(out=scond.t[:], in_=cnd.t[:], func=AF.Silu),
                  reads=[cnd.b], writes=[scond.b])
            S.end_stage()

        def modulation_stage(l):
            with contextlib.ExitStack() as st:
                wa = [sb(st, "wa%d" % i, [128, 8, 1024], BF16) for i in range(2)]
                for blk in range(9):
                    w = wa[blk % 2]
                    S.add("pool", lambda e, w=w, blk=blk: e.dma_start(out=w.t[:], in_=ada_w[l, blk]),
                          writes=[w.b], dma=True)
                    ps = PS[blk % 2]
                    for cc in range(8):
                        for kc in range(8):
                            S.add("pe", lambda e, w=w, ps=ps, cc=cc, kc=kc: e.matmul(
                                ps.t[:, cc * 2:cc * 2 + 2], lhsT=w.t[:, kc, cc * 128:(cc + 1) * 128],
                                rhs=scond.t[:, kc, :], start=(kc == 0), stop=(kc == 7)),
                                reads=[w.b, scond.b], writes=[ps.b])
                    S.add("dve", lambda e, ps=ps, blk=blk: e.tensor_tensor(
                        out=modT.t[:, blk * 8:(blk + 1) * 8, :],
                        in0=ps.t[:, 0:16].rearrange("p (a b) -> p a b", b=2),
                        in1=adab.t[:, l, blk * 8:(blk + 1) * 8, :], op=ALU.add),
                        reads=[ps.b, adab.b], writes=[modT.b])
                for j in range(3):
                    S.add("dve", lambda e, j=j: e.scalar_tensor_tensor(
                        out=gsT.t[:, j], in0=modT.t[:, (3 * j + 1) * 8:(3 * j + 2) * 8, :], scalar=1.0,
                        in1=ng.t[:, l, j], op0=ALU.add, op1=ALU.mult),
                        reads=[modT.b, ng.b], writes=[gsT.b])
                    S.add("dve", lambda e, j=j: e.tensor_scalar(
                        out=hgT.t[:, j], in0=modT.t[:, (3 * j + 2) * 8:(3 * j + 3) * 8, :],
                        scalar1=(1.0 if j == 1 else 0.5), scalar2=None, op0=ALU.mult),
                        reads=[modT.b], writes=[hgT.b])
                S.end_stage()

        def norm_mod(st_tiles, xb, hb, j, which, ps_pair):
            sq, rstd, tmp = st_tiles
            for c in range(8):
                q = sq[c % 2]
                S.add("act", lambda e, q=q, c=c: e.activation(out=q.t[:], in_=xb.t[:, c, :], func=AF.Square),
                      reads=[xb.b], writes=[q.b])
                for s in range(TM // 512):
                    S.add("pe", lambda e, q=q, c=c, s=s: e.matmul(
                        ps_pair[s].t[:], lhsT=ones32.t[:], rhs=q.t[:, s * 512:(s + 1) * 512],
                        start=(c == 0), stop=(c == 7)), reads=[q.b, ones32.b], writes=[ps_pair[s].b])
            for s in range(TM // 512):
                S.add("act", lambda e, s=s: e.activation(
                    out=rstd.t[:, s * 512:(s + 1) * 512], in_=ps_pair[s].t[:], func=AF.Sqrt,
                    scale=1.0 / D, bias=epsT.t[:, 0:1]), reads=[ps_pair[s].b, epsT.b], writes=[rstd.b])
            S.add("dve", lambda e: e.reciprocal(out=rstd.t[:], in_=rstd.t[:]), reads=[rstd.b], writes=[rstd.b])
            for c in range(8):
                tp = tmp[c % 2]
                S.add("dve", lambda e, tp=tp, c=c: e.tensor_tensor(
                    out=tp.t[:], in0=xb.t[:, c, :], in1=rstd.t[:], op=ALU.mult),
                    reads=[xb.b, rstd.b], writes=[tp.b])
                S.add("act", lambda e, tp=tp, c=c: e.activation(
                    out=hb.t[:, c, :], in_=tp.t[:], func=AF.Identity,
                    scale=gsT.t[:, j, c, which:which + 1], bias=modT.t[:, 3 * j * 8 + c, which:which + 1]),
                    reads=[tp.b, gsT.b, modT.b], writes=[hb.b])

        def ffn_stage(l, hf, src, dst):
            j = 0 if hf == 0 else 2
            lh = l * 2 + hf
            with contextlib.ExitStack() as st:
                X = [sb(st, "x%d" % i, [128, 8, TM]) for i in range(2)]
                Hh = [sb(st, "h%d" % i, [128, 8, TM], BF16) for i in range(2)]
                sq = [sb(st, "sq%d" % i, [128, TM]) for i in range(2)]
                tmp = [sb(st, "tmp%d" % i, [128, TM]) for i in range(2)]
                rstd = sb(st, "rstd", [128, TM])
                actT = sb(st, "actT", [128, NFC, TM], BF16)
                sg = [sb(st, "sg%d" % i, [128, 512]) for i in range(2)]
                wp = [sb(st, "wp%d" % i, [128, 8, 256], BF16) for i in range(4)]
                w2t = [sb(st, "w2t%d" % i, [128, NFC, 128], BF16) for i in range(2)]
                dsrc = [Buf() for _ in range(NMT)]
                wctr = [0, 0]

                def load_x(mt):
                    xb = X[mt % 2]
                    S.add("sp", lambda e: e.dma_start(out=xb.t[:], in_=src[:, :, mt * TM:(mt + 1) * TM]),
                          reads=[dsrc[mt]], writes=[xb.b], dma=True)

                def stage_a(mt):
                    which = 0 if mt < 4 else 1
                    norm_mod((sq, rstd, tmp), X[mt % 2], Hh[mt % 2], j, which, (PS[4], PS[5]))

                load_x(0)
                stage_a(0)
                for mt in range(NMT):
                    which = 0 if mt < 4 else 1
                    xb = X[mt % 2]
                    hb = Hh[mt % 2]
                    if mt + 1 < NMT:
                        load_x(mt + 1)
                    for jp in range(NFC):
                        w = wp[wctr[0] % 4]
                        wctr[0] += 1
                        S.add("pool", lambda e, w=w, jp=jp: e.dma_start(out=w.t[:], in_=w13[lh, jp]),
                              writes=[w.b], dma=True)
                        for s in range(TM // 512):
                            k = jp * 2 + s
                            pg, pu = PS[k % 2], PS[2 + k % 2]
                            for half, pb in ((0, pg), (1, pu)):
                                for c in range(8):
                                    S.add("pe", lambda e, w=w, pb=pb, c=c, s=s, half=half, hb=hb: e.matmul(
                                        pb.t[:], lhsT=w.t[:, c, half * 128:(half + 1) * 128],
                                        rhs=hb.t[:, c, s * 512:(s + 1) * 512], start=(c == 0), stop=(c == 7)),
                                        reads=[w.b, hb.b], writes=[pb.b])
                            g = sg[k % 2]
                            S.add("act", lambda e, g=g, pg=pg: e.activation(out=g.t[:], in_=pg.t[:], func=AF.Silu),
                                  reads=[pg.b], writes=[g.b])
                            S.add("dve", lambda e, g=g, pu=pu, jp=jp, s=s: e.tensor_tensor(
                                out=actT.t[:, jp, s * 512:(s + 1) * 512], in0=g.t[:], in1=pu.t[:], op=ALU.mult),
                                reads=[g.b, pu.b], writes=[actT.b])
                        if jp == 11 and mt + 1 < NMT:
                            stage_a(mt + 1)
                    for m in range(8):
                        w = w2t[wctr[1] % 2]
                        wctr[1] += 1
                        S.add("pool", lambda e, w=w, m=m: e.dma_start(out=w.t[:], in_=w2[lh, m], max_dma_last_dim=4096),
                              writes=[w.b], dma=True)
                        for s in range(TM // 512):
                            pb = PS[6 + (m * 2 + s) % 2]
                            for f in range(NFC):
                                S.add("pe", lambda e, w=w, pb=pb, f=f, s=s: e.matmul(
                                    pb.t[:], lhsT=w.t[:, f, :], rhs=actT.t[:, f, s * 512:(s + 1) * 512],
                                    start=(f == 0), stop=(f == NFC - 1)), reads=[w.b, actT.b], writes=[pb.b])
                            S.add("dve", lambda e, pb=pb, m=m, s=s, xb=xb, which=which: e.scalar_tensor_tensor(
                                out=xb.t[:, m, s * 512:(s + 1) * 512], in0=pb.t[:],
                                scalar=hgT.t[:, j, m, which:which + 1], in1=xb.t[:, m, s * 512:(s + 1) * 512],
                                op0=ALU.mult, op1=ALU.add), reads=[pb.b, hgT.b, xb.b], writes=[xb.b])
                    S.add("sp", lambda e, xb=xb, mt=mt: e.dma_start(out=dst[:, :, mt * TM:(mt + 1) * TM], in_=xb.t[:]),
                          reads=[xb.b], writes=[dsrc[mt]], dma=True)
                S.end_stage()

        def ah_inproj(l, i):
            with contextlib.ExitStack() as st:
                X = [sb(st, "x", [128, 8, TM]) for _ in range(2)]
                Hh = [sb(st, "h", [128, 8, TM], BF16) for _ in range(2)]
                sq = [sb(st, "sq", [128, TM]) for _ in range(2)]
                tmp = [sb(st, "tmp", [128, TM]) for _ in range(2)]
                rstd = sb(st, "rstd", [128, TM])
                W = sb(st, "win", [128, 8, 2432], BF16)
                stg = [sb(st, "stg", [128, 512]) for _ in range(4)]
                vst = [sb(st, "vst", [128, 8, 128]) for _ in range(2)]
                for kc2 in range(4):
                    S.add("pool", lambda e, kc2=kc2: e.dma_start(out=W.t[:, 2 * kc2:2 * kc2 + 2, :], in_=win[i, :, 2 * kc2:2 * kc2 + 2, :],
                                                               max_dma_last_dim=4096), writes=[W.b], dma=True)

                def load_x(mt):
                    xb = X[mt % 2]
                    S.add("sp", lambda e: e.dma_start(out=xb.t[:], in_=xs[:, :, mt * TM:(mt + 1) * TM]),
                          writes=[xb.b], dma=True)

                def stage_a(mt):
                    which = 0 if mt < 4 else 1
                    norm_mod((sq, rstd, tmp), X[mt % 2], Hh[mt % 2], 1, which, (PS[4], PS[5]))

                load_x(0)
                stage_a(0)
                ctr = [0]
                for mt in range(NMT):
                    hb = Hh[mt % 2]
                    if mt + 1 < NMT:
                        load_x(mt + 1)
                    for oc in range(18):
                        if oc < 4:
                            dst, ch = qT_d, oc
                        elif oc < 6:
                            dst, ch = kT_d, oc - 4
                        else:
                            dst, ch = u3T_d, oc - 6
                        for s_ in range(TM // 512):
                            k = ctr[0]
                            ctr[0] += 1
                            pb = PS[k % 4]
                            sg_ = stg[k % 4]
                            for c in range(8):
                                S.add("pe", lambda e, pb=pb, c=c, s_=s_, oc=oc, hb=hb: e.matmul(
                                    pb.t[:], lhsT=W.t[:, c, oc * 128:(oc + 1) * 128],
                                    rhs=hb.t[:, c, s_ * 512:(s_ + 1) * 512], start=(c == 0), stop=(c == 7)),
                                    reads=[W.b, hb.b], writes=[pb.b])
                            if k % 2 == 0:
                                S.add("act", lambda e, pb=pb, sg_=sg_: e.activation(out=sg_.t[:], in_=pb.t[:], func=AF.Identity),
                                      reads=[pb.b], writes=[sg_.b])
                            else:
                                S.add("dve", lambda e, pb=pb, sg_=sg_: e.tensor_copy(out=sg_.t[:], in_=pb.t[:]),
                                      reads=[pb.b], writes=[sg_.b])
                            t0 = mt * TM + s_ * 512
                            S.add("sp", lambda e, dst=dst, ch=ch, t0=t0, sg_=sg_: e.dma_start(
                                out=dst[:, ch, t0:t0 + 512], in_=sg_.t[:]), reads=[sg_.b], dma=True)
                        if oc == 9 and mt + 1 < NMT:
                            stage_a(mt + 1)
                    vs_ = vst[mt % 2]
                    for tb in range(TM // 128):
                        pb = PS[6 + tb % 2]
                        for c in range(8):
                            S.add("pe", lambda e, pb=pb, c=c, tb=tb, hb=hb: e.matmul(
                                pb.t[:, 0:128], lhsT=hb.t[:, c, tb * 128:(tb + 1) * 128], rhs=W.t[:, c, 2304:2432],
                                start=(c == 0), stop=(c == 7)), reads=[W.b, hb.b], writes=[pb.b])
                        S.add("dve", lambda e, pb=pb, tb=tb, vs_=vs_: e.tensor_copy(out=vs_.t[:, tb, :], in_=pb.t[:, 0:128]),
                              reads=[pb.b], writes=[vs_.b])
                    S.add("sp", lambda e, mt=mt, vs_=vs_: e.dma_start(
                        out=v_d[mt * TM:(mt + 1) * TM, :].rearrange("(tb p) n -> p tb n", p=128), in_=vs_.t[:]),
                        reads=[vs_.b], dma=True)
                S.end_stage()

        def ah_attention(l, i):
            with contextlib.ExitStack() as st:
                cosT = sb(st, "cosT", [128, 4096])
                sinT = sb(st, "sinT", [128, 4096])
                rotT = sb(st, "rotT", [128, 128])
                blk1 = sb(st, "blk1", [128, 128])
                mprev = sb(st, "mprev", [128, 128], BF16)
                mnext = sb(st, "mnext", [128, 128], BF16)
                onesb = sb(st, "onesb", [128, 64], BF16)
                gn = sb(st, "gn", [128, 2])
                gq8 = sb(st, "gq8", [128, 1])
                esink = sb(st, "esink", [128, 4])
                KT = sb(st, "KT", [128, 2, 4096], BF16)
                VV = sb(st, "VV", [128, 32, 128], BF16)
                CK = sb(st, "CK", [128, 2, 512], BF16)
                CV = sb(st, "CV", [128, 4, 128], BF16)
                kin = [sb(st, "kin", [128, 2, 512]) for _ in range(2)]
                qin = [sb(st, "qin", [128, 4, 512]) for _ in range(2)]
                qp = [sb(st, "qp", [128, 4, 512], BF16) for _ in range(2)]
                sqq = [sb(st, "sqq", [128, 512]) for _ in range(2)]
                rs_ = [sb(st, "rs", [128, 512]) for _ in range(2)]
                kg_ = [sb(st, "kg", [128, 512]) for _ in range(2)]
                t1_ = [sb(st, "t1", [128, 512]) for _ in range(2)]
                t2_ = [sb(st, "t2", [128, 512]) for _ in range(2)]
                pt = [sb(st, "pt", [128, 512], BF16) for _ in range(3)]
                den = [sb(st, "den", [128, 512]) for _ in range(2)]
                aout = [sb(st, "aout", [128, 4, 512], BF16) for _ in range(2)]
                kno = [sb(st, "kno", [128, 2, 256]) for _ in range(2)]

                S.add("sp", lambda e: e.dma_start(out=cosT.t[:], in_=cosT_d), writes=[cosT.b], dma=True)
                S.add("sp", lambda e: e.dma_start(out=sinT.t[:], in_=sinT_d), writes=[sinT.b], dma=True)
                S.add("sp", lambda e: e.dma_start(out=rotT.t[:], in_=rotT_d), writes=[rotT.b], dma=True)
                S.add("sp", lambda e: e.dma_start(out=blk1.t[:], in_=blk1_d), writes=[blk1.b], dma=True)
                S.add("pool", lambda e: e.dma_start(out=mprev.t[:], in_=mprev_d), writes=[mprev.b], dma=True)
                S.add("pool", lambda e: e.dma_start(out=mnext.t[:], in_=mnext_d), writes=[mnext.b], dma=True)
                S.add("sp", lambda e: e.dma_start(out=gn.t[:], in_=qkn[:, i, :]), writes=[gn.b], dma=True)
                S.add("sp", lambda e: e.dma_start(out=esink.t[:], in_=sinkT[:, i, :]), writes=[esink.b], dma=True)
                S.add("pool", lambda e: e.dma_start(out=CK.t[:], in_=ckT[i]), writes=[CK.b], dma=True)
                S.add("pool", lambda e: e.dma_start(out=CV.t[:], in_=cvv[i]), writes=[CV.b], dma=True)
                S.add("dve", lambda e: e.memset(onesb.t[:], 1.0), writes=[onesb.b])
                S.add("dve", lambda e: e.tensor_scalar(out=gq8.t[:], in0=gn.t[:, 0:1], scalar1=0.125, scalar2=None, op0=ALU.mult),
                      reads=[gn.b], writes=[gq8.b])
                S.add("act", lambda e: e.activation(out=esink.t[:], in_=esink.t[:], func=AF.Exp), reads=[esink.b], writes=[esink.b])
                pctr = [0]

                def qk_prep(src, srcb, n, gain, gainb, rope_t0, out, outb, nout=None, noutb=None):
                    k = pctr[0]
                    pctr[0] += 1
                    sq_, r_, g_, a_, b_ = sqq[k % 2], rs_[k % 2], kg_[k % 2], t1_[k % 2], t2_[k % 2]
                    pb = PS[3]
                    S.add("act", lambda e: e.activation(out=sq_.t[:, 0:n], in_=src, func=AF.Square), reads=[srcb], writes=[sq_.b])
                    S.add("pe", lambda e: e.matmul(pb.t[:, 0:n], lhsT=blk1.t[:], rhs=sq_.t[:, 0:n], start=True, stop=True),
                          reads=[blk1.b, sq_.b], writes=[pb.b])
                    S.add("act", lambda e: e.activation(out=r_.t[:, 0:n], in_=pb.t[:, 0:n], func=AF.Sqrt, scale=1.0 / 64,
                                                        bias=epsT.t[:, 0:1]), reads=[pb.b, epsT.b], writes=[r_.b])
                    S.add("dve", lambda e: e.reciprocal(out=r_.t[:, 0:n], in_=r_.t[:, 0:n]), reads=[r_.b], writes=[r_.b])
                    if rope_t0 is None:
                        if nout is not None:
                            S.add("dve", lambda e: e.scalar_tensor_tensor(out=nout, in0=src, scalar=gain, in1=r_.t[:, 0:n],
                                                                          op0=ALU.mult, op1=ALU.mult),
                                  reads=[srcb, gainb, r_.b], writes=[noutb])
                        S.add("dve", lambda e: e.scalar_tensor_tensor(out=out, in0=src, scalar=gain, in1=r_.t[:, 0:n],
                                                                      op0=ALU.mult, op1=ALU.mult),
                              reads=[srcb, gainb, r_.b], writes=[outb])
                        return
                    S.add("dve", lambda e: e.scalar_tensor_tensor(out=g_.t[:, 0:n], in0=src, scalar=gain, in1=r_.t[:, 0:n],
                                                                  op0=ALU.mult, op1=ALU.mult),
                          reads=[srcb, gainb, r_.b], writes=[g_.b])
                    S.add("pe", lambda e: e.matmul(pb.t[:, 0:n], lhsT=rotT.t[:], rhs=g_.t[:, 0:n], start=True, stop=True),
                          reads=[rotT.b, g_.b], writes=[pb.b])
                    S.add("dve", lambda e: e.tensor_tensor(out=b_.t[:, 0:n], in0=pb.t[:, 0:n], in1=sinT.t[:, rope_t0:rope_t0 + n], op=ALU.mult),
                          reads=[pb.b, sinT.b], writes=[b_.b])
                    S.add("pool", lambda e: e.tensor_tensor(out=a_.t[:, 0:n], in0=g_.t[:, 0:n], in1=cosT.t[:, rope_t0:rope_t0 + n], op=ALU.mult),
                          reads=[g_.b, cosT.b], writes=[a_.b])
                    S.add("dve", lambda e: e.tensor_tensor(out=out, in0=a_.t[:, 0:n], in1=b_.t[:, 0:n], op=ALU.add),
                          reads=[a_.b, b_.b], writes=[outb])

                stc = [0]
                gctr = [0]

                def attend(qpt, nq, blocks, dst_t0):
                    gi = gctr[0]
                    gctr[0] += 1
                    ao = aout[gi % 2]
                    for c in range(4):
                        par = (gi * 4 + c) % 2
                        PA, PB = PS[4 + 2 * par], PS[5 + 2 * par]
                        for hh in range(2):
                            h = 2 * c + hh
                            kvh = h // 4
                            lo = hh * 64
                            nb = len(blocks)
                            for idx, (Kt, kcol, Vt, vblk, q0, q1, masks) in enumerate(blocks):
                                n = q1 - q0
                                k = stc[0]
                                stc[0] += 1
                                ST = PS[k % 3]
                                P_ = pt[k % 3]
                                S.add("pe", lambda e, ST=ST, Kt=Kt, kcol=kcol, q0=q0, q1=q1, n=n, lo=lo, kvh=kvh, c=c: e.matmul(
                                    ST.t[:, 0:n], lhsT=Kt.t[lo:lo + 64, kvh, kcol:kcol + 128], rhs=qpt.t[lo:lo + 64, c, q0:q1],
                                    start=True, stop=True), reads=[Kt.b, qpt.b], writes=[ST.b])
                                S.add("act", lambda e, ST=ST, P_=P_, n=n: e.activation(out=P_.t[:, 0:n], in_=ST.t[:, 0:n], func=AF.Exp),
                                      reads=[ST.b], writes=[P_.b])
                                for (moff, mt_) in masks:
                                    S.add("dve", lambda e, P_=P_, moff=moff, mt_=mt_: e.tensor_tensor(
                                        out=P_.t[:, moff:moff + 128], in0=P_.t[:, moff:moff + 128], in1=mt_.t[:], op=ALU.mult),
                                        reads=[P_.b, mt_.b], writes=[P_.b])
                                S.add("pe", lambda e, PA=PA, Vt=Vt, vblk=vblk, P_=P_, n=n, q0=q0, q1=q1, lo=lo, kvh=kvh, idx=idx, nb=nb: e.matmul(
                                    PA.t[lo:lo + 64, q0:q1], lhsT=Vt.t[:, vblk, kvh * 64:(kvh + 1) * 64], rhs=P_.t[:, 0:n],
                                    start=(idx == 0), stop=(idx == nb - 1)), reads=[Vt.b, P_.b], writes=[PA.b])
                                S.add("pe", lambda e, PB=PB, P_=P_, n=n, q0=q0, q1=q1, lo=lo, idx=idx, nb=nb: e.matmul(
                                    PB.t[lo:lo + 64, q0:q1], lhsT=onesb.t[:, 0:64], rhs=P_.t[:, 0:n],
                                    start=(idx == 0), stop=(idx == nb - 1)), reads=[onesb.b, P_.b], writes=[PB.b])
                        dn = den[(gi * 4 + c) % 2]
                        S.add("dve", lambda e, dn=dn, PB=PB, c=c: e.tensor_scalar(out=dn.t[:, 0:nq], in0=PB.t[:, 0:nq], scalar1=esink.t[:, c:c + 1],
                                                                             scalar2=None, op0=ALU.add), reads=[PB.b, esink.b], writes=[dn.b])
                        S.add("dve", lambda e, dn=dn: e.reciprocal(out=dn.t[:, 0:nq], in_=dn.t[:, 0:nq]), reads=[dn.b], writes=[dn.b])
                        S.add("dve", lambda e, dn=dn, PA=PA, c=c: e.tensor_tensor(out=ao.t[:, c, 0:nq], in0=PA.t[:, 0:nq], in1=dn.t[:, 0:nq], op=ALU.mult),
                              reads=[PA.b, dn.b], writes=[ao.b])
                    S.add("sp", lambda e: e.dma_start(out=ay_d[:, 0:4, dst_t0:dst_t0 + nq], in_=ao.t[:, :, 0:nq]), reads=[ao.b], dma=True)

                for tt in range(8):
                    ki = kin[tt % 2]
                    S.add("sp", lambda e, ki=ki, tt=tt: e.dma_start(out=ki.t[:], in_=kT_d[:, :, tt * 512:(tt + 1) * 512]), writes=[ki.b], dma=True)
                    S.add("pool", lambda e, tt=tt: e.dma_start(
                        out=VV.t[:, tt * 4:(tt + 1) * 4, :], in_=v_d[tt * 512:(tt + 1) * 512, :].rearrange("(tb p) n -> p tb n", p=128)),
                        writes=[VV.b], dma=True)
                    for ch in range(2):
                        qk_prep(ki.t[:, ch, :], ki.b, 512, gn.t[:, 1:2], gn.b, tt * 512, KT.t[:, ch, tt * 512:(tt + 1) * 512], KT.b)
                for g in range(8):
                    qi = qin[g % 2]
                    qq = qp[g % 2]
                    S.add("sp", lambda e, qi=qi, g=g: e.dma_start(out=qi.t[:], in_=qT_d[:, :, g * 512:(g + 1) * 512]), writes=[qi.b], dma=True)
                    for c in range(4):
                        qk_prep(qi.t[:, c, :], qi.b, 512, gq8.t[:, 0:1], gq8.b, g * 512, qq.t[:, c, :], qq.b)
                    blocks = []
                    for b_ in range(4):
                        blocks.append((CK, b_ * 128, CV, b_, 0, 512, []))
                    for kb in range(max(0, 4 * g - 1), min(32, 4 * g + 5)):
                        qb0 = max(kb - 1, 4 * g)
                        qb1 = min(kb + 1, 4 * g + 3)
                        masks = []
                        for qb in range(qb0, qb1 + 1):
                            if qb == kb + 1:
                                masks.append(((qb - qb0) * 128, mprev))
                            elif qb == kb - 1:
                                masks.append(((qb - qb0) * 128, mnext))
                        blocks.append((KT, kb * 128, VV, kb, (qb0 - 4 * g) * 128, (qb1 - 4 * g + 1) * 128, masks))
                    attend(qq, 512, blocks, g * 512)
                KTp = [sb(st, "KTp", [128, 2, 256], BF16) for _ in range(2)]
                VVp = [sb(st, "VVp", [128, 2, 128], BF16) for _ in range(2)]
                for pi in range(4):
                    t0 = 4096 + pi * 256
                    ki = kin[pi % 2]
                    kt, vv, kn_ = KTp[pi % 2], VVp[pi % 2], kno[pi % 2]
                    S.add("sp", lambda e, ki=ki, t0=t0: e.dma_start(out=ki.t[:, :, 0:256], in_=kT_d[:, :, t0:t0 + 256]), writes=[ki.b], dma=True)
                    S.add("pool", lambda e, vv=vv, t0=t0: e.dma_start(
                        out=vv.t[:], in_=v_d[t0:t0 + 256, :].rearrange("(tb p) n -> p tb n", p=128)), writes=[vv.b], dma=True)
                    for ch in range(2):
                        qk_prep(ki.t[:, ch, 0:256], ki.b, 256, gn.t[:, 1:2], gn.b, None, kt.t[:, ch, :], kt.b,
                                nout=kn_.t[:, ch, :], noutb=kn_.b)
                    S.add("sp", lambda e, kn_=kn_, pi=pi: e.dma_start(
                        out=newk[i, :, :, pi * 256:(pi + 1) * 256].rearrange("k d t -> d k t"), in_=kn_.t[0:64, :, :]),
                        reads=[kn_.b], dma=True)
                    qi = qin[pi % 2]
                    qq = qp[pi % 2]
                    S.add("sp", lambda e, qi=qi, t0=t0: e.dma_start(out=qi.t[:, :, 0:256], in_=qT_d[:, :, t0:t0 + 256]), writes=[qi.b], dma=True)
                    for c in range(4):
                        qk_prep(qi.t[:, c, 0:256], qi.b, 256, gq8.t[:, 0:1], gq8.b, None, qq.t[:, c, 0:256], qq.b)
                    blocks = [(kt, b_ * 128, vv, b_, 0, 256, []) for b_ in range(2)]
                    attend(qq, 256, blocks, t0)
                S.add("sp", lambda e: e.dma_start(out=newv[i], in_=v_d[4096:5120, :]), dma=True)
                S.end_stage(resched=True)

        def ah_outproj(l, i):
            with contextlib.ExitStack() as st:
                X = [sb(st, "x", [128, 8, TM]) for _ in range(2)]
                AY = [sb(st, "ay", [128, 8, TM], BF16) for _ in range(2)]
                W = sb(st, "wout", [128, 8, 1024], BF16)
                for kc2 in range(4):
                    S.add("pool", lambda e, kc2=kc2: e.dma_start(out=W.t[:, 2 * kc2:2 * kc2 + 2, :], in_=wout[i, :, 2 * kc2:2 * kc2 + 2, :],
                                                               max_dma_last_dim=4096), writes=[W.b], dma=True)

                def load(mt):
                    xb, ab = X[mt % 2], AY[mt % 2]
                    S.add("sp", lambda e: e.dma_start(out=xb.t[:], in_=xs[:, :, mt * TM:(mt + 1) * TM]), writes=[xb.b], dma=True)
                    S.add("sp", lambda e: e.dma_start(out=ab.t[:], in_=ay_d[:, :, mt * TM:(mt + 1) * TM]), writes=[ab.b], dma=True)

                load(0)
                for mt in range(NMT):
                    which = 0 if mt < 4 else 1
                    xb, ab = X[mt % 2], AY[mt % 2]
                    if mt + 1 < NMT:
                        load(mt + 1)
                    for m in range(8):
                        for s_ in range(TM // 512):
                            pb = PS[(m * 2 + s_) % 4]
                            for f in range(8):
                                S.add("pe", lambda e, pb=pb, f=f, m=m, s_=s_, ab=ab: e.matmul(
                                    pb.t[:], lhsT=W.t[:, f, m * 128:(m + 1) * 128], rhs=ab.t[:, f, s_ * 512:(s_ + 1) * 512],
                                    start=(f == 0), stop=(f == 7)), reads=[W.b, ab.b], writes=[pb.b])
                            S.add("dve", lambda e, pb=pb, m=m, s_=s_, xb=xb, which=which: e.scalar_tensor_tensor(
                                out=xb.t[:, m, s_ * 512:(s_ + 1) * 512], in0=pb.t[:],
                                scalar=hgT.t[:, 1, m, which:which + 1], in1=xb.t[:, m, s_ * 512:(s_ + 1) * 512],
                                op0=ALU.mult, op1=ALU.add), reads=[pb.b, hgT.b, xb.b], writes=[xb.b])
                    S.add("sp", lambda e, xb=xb, mt=mt: e.dma_start(out=xs[:, :, mt * TM:(mt + 1) * TM], in_=xb.t[:]),
                          reads=[xb.b], dma=True)
                S.end_stage()


        def hy_conv(l, i):
            with contextlib.ExitStack() as st:
                cw = sb(st, "cw", [128, 12, 3])
                cb = sb(st, "cb", [128, 12])
                U = [sb(st, "U", [128, 12, 514]) for _ in range(2)]
                O = [sb(st, "O", [128, 12, 512]) for _ in range(2)]
                S.add("sp", lambda e: e.dma_start(out=cw.t[:], in_=hcw[:, i]), writes=[cw.b], dma=True)
                S.add("sp", lambda e: e.dma_start(out=cb.t[:], in_=hcb[:, i]), writes=[cb.b], dma=True)
                tiles = [(tt * 512, 512, tt == 0, tt == 7) for tt in range(8)] + [(4096 + 256 * pi, 256, True, True) for pi in range(4)]
                for ti, (t0, n, first, lastt) in enumerate(tiles):
                    u, o = U[ti % 2], O[ti % 2]
                    a0 = t0 if first else t0 - 1
                    a1 = t0 + n if lastt else t0 + n + 1
                    c0 = 1 if first else 0
                    S.add("sp", lambda e, u=u, a0=a0, a1=a1, c0=c0: e.dma_start(out=u.t[:, :, c0:c0 + (a1 - a0)], in_=u3T_d[:, :, a0:a1]),
                          writes=[u.b], dma=True)
                    if first:
                        S.add("pool", lambda e, u=u: e.memset(u.t[:, :, 0:1], 0.0), writes=[u.b])
                    if lastt:
                        S.add("pool", lambda e, u=u, n=n: e.memset(u.t[:, :, n + 1:n + 2], 0.0), writes=[u.b])
                    for ch in range(12):
                        en = "dve"
                        S.add(en, lambda e, u=u, o=o, ch=ch, n=n: e.tensor_scalar(
                            out=o.t[:, ch, 0:n], in0=u.t[:, ch, 0:n], scalar1=cw.t[:, ch, 0:1], scalar2=cb.t[:, ch:ch + 1],
                            op0=ALU.mult, op1=ALU.add), reads=[u.b, cw.b, cb.b], writes=[o.b])
                        S.add(en, lambda e, u=u, o=o, ch=ch, n=n: e.scalar_tensor_tensor(
                            out=o.t[:, ch, 0:n], in0=u.t[:, ch, 1:n + 1], scalar=cw.t[:, ch, 1:2], in1=o.t[:, ch, 0:n],
                            op0=ALU.mult, op1=ALU.add), reads=[u.b, cw.b, o.b], writes=[o.b])
                        S.add(en, lambda e, u=u, o=o, ch=ch, n=n: e.scalar_tensor_tensor(
                            out=o.t[:, ch, 0:n], in0=u.t[:, ch, 2:n + 2], scalar=cw.t[:, ch, 2:3], in1=o.t[:, ch, 0:n],
                            op0=ALU.mult, op1=ALU.add), reads=[u.b, cw.b, o.b], writes=[o.b])
                    S.add("sp", lambda e, o=o, t0=t0, n=n: e.dma_start(out=uc_d[:, :, t0:t0 + n], in_=o.t[:, :, 0:n]), reads=[o.b], dma=True)
                S.end_stage(resched=True)

        MAGIC = 12582912.0

        def hy_filter_gen(i, L, rn):
            G = HG[L]
            nb = L // 128
            TT = min(512, L)
            with contextlib.ExitStack() as st:
                zf = sb(st, "zf", [33, L])
                w1 = sb(st, "w1", [33, 64])
                w2 = sb(st, "w2", [64, 64])
                w3 = sb(st, "w3", [64, 2048])
                p1 = sb(st, "p1", [64, 2])
                p2 = sb(st, "p2", [64, 2])
                h1T = sb(st, "h1T", [64, L])
                h2T = sb(st, "h2T", [64, L])
                delt = sb(st, "delt", [128, 2048])
                tl = sb(st, "tl", [128, nb])
                dec = [sb(st, "dec", [128, 2048]) for _ in range(2)]
                hh = [sb(st, "hh", [128, 2048]) for _ in range(2)]
                sqh = [sb(st, "sqh", [128, 2048]) for _ in range(2)]
                stg = [sb(st, "hstg", [128, 2, 1024], BF16) for _ in range(2)]
                uu = [sb(st, "uu", [64, 512]) for _ in range(2)]
                ta = [sb(st, "ta", [64, 512]) for _ in range(2)]
                nr = [sb(st, "nr", [64, 512]) for _ in range(2)]
                sst = sb(st, "sst", [128, 1024])
                for (tt_, src) in ((zf, G["zf"]), (w1, hw1[i]), (w2, hw2[i]), (w3, hw3[i]), (p1, hb1[:, i, :]), (p2, hb2[:, i, :]),
                                  (delt, deltas_d), (tl, G["tl"])):
                    S.add("sp", lambda e, tt_=tt_, src=src: e.dma_start(out=tt_.t[:], in_=src), writes=[tt_.b], dma=True)
                for pp in (p1, p2):
                    S.add("dve", lambda e, pp=pp: e.tensor_scalar(out=pp.t[:, 1:2], in0=pp.t[:, 1:2], scalar1=1.0 / (2 * math.pi), scalar2=None,
                                                                  op0=ALU.mult), reads=[pp.b], writes=[pp.b])
                k = 0
                for (wt, pp, srcT, dstT) in ((w1, p1, zf, h1T), (w2, p2, h1T, h2T)):
                    for tile_ in range(L // TT):
                        c0 = tile_ * TT
                        pb = PS[k % 2]
                        u_, a_, n_ = uu[k % 2], ta[k % 2], nr[k % 2]
                        k += 1
                        kk = wt.t.shape[0]
                        S.add("pe", lambda e, pb=pb, wt=wt, srcT=srcT, c0=c0, kk=kk: e.matmul(
                            pb.t[0:64, 0:TT], lhsT=wt.t[:, :], rhs=srcT.t[0:kk, c0:c0 + TT], start=True, stop=True),
                            reads=[wt.b, srcT.b], writes=[pb.b])
                        S.add("dve", lambda e, pb=pb, u_=u_, pp=pp: e.tensor_scalar(
                            out=u_.t[:, 0:TT], in0=pb.t[0:64, 0:TT], scalar1=pp.t[:, 0:1], scalar2=pp.t[:, 1:2], op0=ALU.add, op1=ALU.mult),
                            reads=[pb.b, pp.b], writes=[u_.b])
                        S.add("dve", lambda e, u_=u_, a_=a_: e.tensor_scalar(
                            out=a_.t[:, 0:TT], in0=u_.t[:, 0:TT], scalar1=MAGIC, scalar2=None, op0=ALU.add), reads=[u_.b], writes=[a_.b])
                        S.add("dve", lambda e, u_=u_, a_=a_, n_=n_: e.scalar_tensor_tensor(
                            out=n_.t[:, 0:TT], in0=a_.t[:, 0:TT], scalar=MAGIC, in1=u_.t[:, 0:TT], op0=ALU.subtract, op1=ALU.subtract),
                            reads=[a_.b, u_.b], writes=[n_.b])
                        S.add("dve", lambda e, n_=n_: e.tensor_scalar(
                            out=n_.t[:, 0:TT], in0=n_.t[:, 0:TT], scalar1=0.49999, scalar2=-0.49999, op0=ALU.min, op1=ALU.max),
                            reads=[n_.b], writes=[n_.b])
                        S.add("act", lambda e, n_=n_, dstT=dstT, c0=c0: e.activation(
                            out=dstT.t[:, c0:c0 + TT], in_=n_.t[:, 0:TT], func=AF.Sin, scale=-2.0 * math.pi), reads=[n_.b], writes=[dstT.b])
                for lb in range(nb):
                    d_, h_, q_, sg_ = dec[lb % 2], hh[lb % 2], sqh[lb % 2], stg[lb % 2]
                    S.add("act", lambda e, d_=d_, lb=lb: e.activation(out=d_.t[:], in_=delt.t[:], func=AF.Exp, scale=tl.t[:, lb:lb + 1]),
                          reads=[delt.b, tl.b], writes=[d_.b])
                    for ct in range(4):
                        pb = PS[ct]
                        S.add("pe", lambda e, pb=pb, lb=lb, ct=ct: e.matmul(
                            pb.t[:], lhsT=h2T.t[:, lb * 128:(lb + 1) * 128], rhs=w3.t[:, ct * 512:(ct + 1) * 512], start=True, stop=True),
                            reads=[h2T.b, w3.b], writes=[pb.b])
                        S.add("dve", lambda e, pb=pb, h_=h_, d_=d_, ct=ct: e.tensor_tensor(
                            out=h_.t[:, ct * 512:(ct + 1) * 512], in0=pb.t[:], in1=d_.t[:, ct * 512:(ct + 1) * 512], op=ALU.mult),
                            reads=[pb.b, d_.b], writes=[h_.b])
                    if lb == 0:
                        S.add("dve", lambda e, h_=h_: e.memset(h_.t[0:1, 1024:2048], 0.0), writes=[h_.b])
                    S.add("act", lambda e, h_=h_, q_=q_: e.activation(out=q_.t[:], in_=h_.t[:], func=AF.Square), reads=[h_.b], writes=[q_.b])
                    for ct in range(4):
                        S.add("pe", lambda e, q_=q_, ct=ct, lb=lb: e.matmul(
                            PS[4 + ct].t[:], lhsT=ones32.t[:], rhs=q_.t[:, ct * 512:(ct + 1) * 512], start=(lb == 0), stop=(lb == nb - 1)),
                            reads=[ones32.b, q_.b], writes=[PS[4 + ct].b])
                    S.add("pool", lambda e, h_=h_, sg_=sg_: e.tensor_tensor(out=sg_.t[:, 0, :], in0=h_.t[:, 0:1024], in1=h_.t[:, 1024:2048], op=ALU.add),
                          reads=[h_.b], writes=[sg_.b])
                    S.add("pool", lambda e, h_=h_, sg_=sg_: e.tensor_tensor(out=sg_.t[:, 1, :], in0=h_.t[:, 1024:2048], in1=h_.t[:, 0:1024], op=ALU.subtract),
                          reads=[h_.b], writes=[sg_.b])
                    S.add("sp", lambda e, sg_=sg_, lb=lb: e.dma_start(out=hsd_d[:, lb].rearrange("s p n -> p s n"), in_=sg_.t[:]), reads=[sg_.b], dma=True)
                for o in range(2):
                    S.add("dve", lambda e, o=o: e.tensor_copy(out=sst.t[:, o * 512:(o + 1) * 512], in_=PS[4 + o].t[:]), reads=[PS[4 + o].b], writes=[sst.b])
                    S.add("dve", lambda e, o=o: e.tensor_tensor(out=sst.t[:, o * 512:(o + 1) * 512], in0=sst.t[:, o * 512:(o + 1) * 512],
                                                                in1=PS[6 + o].t[:], op=ALU.add), reads=[sst.b, PS[6 + o].b], writes=[sst.b])
                S.add("act", lambda e: e.activation(out=rn.t[:], in_=sst.t[:], func=AF.Sqrt, bias=epsT.t[:, 0:1]), reads=[sst.b, epsT.b], writes=[rn.b])
                S.add("dve", lambda e: e.reciprocal(out=rn.t[:], in_=rn.t[:]), reads=[rn.b], writes=[rn.b])
                S.end_stage(resched=True)

        def hy_filter_dft(L, rn):
            G = HG[L]
            nb = L // 128
            with contextlib.ExitStack() as st:
                hs = sb(st, "hs", [128, nb, 1024], BF16)
                hd = sb(st, "hd", [128, nb, 1024], BF16)
                tC = [sb(st, "tC", [128, nb, 128], BF16) for _ in range(2)]
                tS = [sb(st, "tS", [128, nb, 128], BF16) for _ in range(2)]
                stg = [sb(st, "Hstg", [128, 512]) for _ in range(4)]
                step = max(1, nb // 4)
                for lb0 in range(0, nb, step):
                    S.add("sp", lambda e, lb0=lb0: e.dma_start(out=hs.t[:, lb0:lb0 + step, :], in_=hsd_d[0, lb0:lb0 + step].rearrange("l p n -> p l n")),
                          writes=[hs.b], dma=True)
                    S.add("sp", lambda e, lb0=lb0: e.dma_start(out=hd.t[:, lb0:lb0 + step, :], in_=hsd_d[1, lb0:lb0 + step].rearrange("l p n -> p l n")),
                          writes=[hd.b], dma=True)
                k = 0
                for kb in range(nb):
                    c_, s_ = tC[kb % 2], tS[kb % 2]
                    S.add("sp", lambda e, c_=c_, kb=kb: e.dma_start(out=c_.t[:], in_=G["F"][0, kb]), writes=[c_.b], dma=True)
                    S.add("pool", lambda e, s_=s_, kb=kb: e.dma_start(out=s_.t[:], in_=G["F"][1, kb]), writes=[s_.b], dma=True)
                    for o in range(2):
                        for ri, (tab, src) in enumerate(((c_, hs), (s_, hd))):
                            pb = PS[k % 8]
                            sg_ = stg[k % 4]
                            k += 1
                            for lb in range(nb):
                                S.add("pe", lambda e, pb=pb, tab=tab, src=src, lb=lb, o=o: e.matmul(
                                    pb.t[:], lhsT=tab.t[:, lb, :], rhs=src.t[:, lb, o * 512:(o + 1) * 512], start=(lb == 0), stop=(lb == nb - 1)),
                                    reads=[tab.b, src.b], writes=[pb.b])
                            S.add("dve", lambda e, pb=pb, sg_=sg_, o=o: e.tensor_tensor(out=sg_.t[:], in0=pb.t[:], in1=rn.t[:, o * 512:(o + 1) * 512], op=ALU.mult),
                                  reads=[pb.b, rn.b], writes=[sg_.b])
                            S.add("sp", lambda e, sg_=sg_, o=o, ri=ri, kb=kb: e.dma_start(out=H_d[o, ri, kb], in_=sg_.t[:]), reads=[sg_.b], dma=True)
                S.end_stage()

        def hy_order(l, i, L, offs, o, ident, hbs, Z, YR, YI):
            G = HG[L]
            nb = L // 128
            TT = min(512, L)
            ntt = L // TT
            nsub = TT // 128
            with contextlib.ExitStack() as st:
                zin = [sb(st, "zin", [128, 512]) for _ in range(3)]
                k = 0
                for si, t0 in enumerate(offs):
                    for tt in range(ntt):
                        for cc in range(4):
                            zi = zin[k % 3]
                            pb = PS[k % 4]
                            k += 1
                            src = uc_d[:, 8 + cc, t0 + tt * TT:t0 + (tt + 1) * TT] if o == 0 else z1T_d[:, cc, t0 + tt * TT:t0 + (tt + 1) * TT]
                            S.add("sp", lambda e, zi=zi, src=src: e.dma_start(out=zi.t[:, 0:TT], in_=src), writes=[zi.b], dma=True)
                            for j in range(nsub):
                                S.add("pe", lambda e, pb=pb, zi=zi, j=j: e.transpose(pb.t[:, j * 128:(j + 1) * 128], zi.t[:, j * 128:(j + 1) * 128], ident.t[:]),
                                      reads=[zi.b, ident.b], writes=[pb.b])
                            zt = Z[si]
                            if k % 2 == 0:
                                S.add("act", lambda e, pb=pb, zt=zt, tt=tt, cc=cc: e.activation(
                                    out=zt.t[:, tt * nsub:(tt + 1) * nsub, cc * 128:(cc + 1) * 128],
                                    in_=pb.t[:, 0:TT].rearrange("p (a b) -> p a b", b=128), func=AF.Identity), reads=[pb.b], writes=[zt.b])
                            else:
                                S.add("dve", lambda e, pb=pb, zt=zt, tt=tt, cc=cc: e.tensor_copy(
                                    out=zt.t[:, tt * nsub:(tt + 1) * nsub, cc * 128:(cc + 1) * 128],
                                    in_=pb.t[:, 0:TT].rearrange("p (a b) -> p a b", b=128)), reads=[pb.b], writes=[zt.b])
                S.end_stage(resched=True)
            with contextlib.ExitStack() as st:
                tC = [sb(st, "tC", [128, nb, 128], BF16) for _ in range(3)]
                tS = [sb(st, "tS", [128, nb, 128], BF16) for _ in range(3)]
                hr = [sb(st, "hr", [128, 512]) for _ in range(2)]
                hi = [sb(st, "hi", [128, 512]) for _ in range(2)]
                m_ = [[sb(st, "m", [128, 512]) for _ in range(2)] for _ in range(4)]
                k = 0
                for kb in range(nb):
                    c_, s_ = tC[kb % 3], tS[kb % 3]
                    hr_, hi_ = hr[kb % 2], hi[kb % 2]
                    S.add("sp", lambda e, c_=c_, kb=kb: e.dma_start(out=c_.t[:], in_=G["F"][0, kb]), writes=[c_.b], dma=True)
                    S.add("pool", lambda e, s_=s_, kb=kb: e.dma_start(out=s_.t[:], in_=G["F"][1, kb]), writes=[s_.b], dma=True)
                    S.add("sp", lambda e, hr_=hr_, kb=kb: e.dma_start(out=hr_.t[:], in_=H_d[o, 0, kb]), writes=[hr_.b], dma=True)
                    S.add("sp", lambda e, hi_=hi_, kb=kb: e.dma_start(out=hi_.t[:], in_=H_d[o, 1, kb]), writes=[hi_.b], dma=True)
                    for si in range(len(offs)):
                        zt, yr, yi = Z[si], YR[si], YI[si]
                        Pc, Ps = PS[(k % 4) * 2], PS[(k % 4) * 2 + 1]
                        mm = [m_[q][k % 2] for q in range(4)]
                        k += 1
                        for tb in range(nb):
                            S.add("pe", lambda e, Pc=Pc, c_=c_, zt=zt, tb=tb: e.matmul(Pc.t[:], lhsT=c_.t[:, tb, :], rhs=zt.t[:, tb, :],
                                                                                 start=(tb == 0), stop=(tb == nb - 1)), reads=[c_.b, zt.b], writes=[Pc.b])
                        for tb in range(nb):
                            S.add("pe", lambda e, Ps=Ps, s_=s_, zt=zt, tb=tb: e.matmul(Ps.t[:], lhsT=s_.t[:, tb, :], rhs=zt.t[:, tb, :],
                                                                                 start=(tb == 0), stop=(tb == nb - 1)), reads=[s_.b, zt.b], writes=[Ps.b])
                        for q, (hh_, pp_) in enumerate(((hr_, Pc), (hi_, Ps), (hr_, Ps), (hi_, Pc))):
                            S.add("dve", lambda e, q=q, hh_=hh_, pp_=pp_, mm=mm: e.tensor_tensor(out=mm[q].t[:], in0=pp_.t[:], in1=hh_.t[:], op=ALU.mult),
                                  reads=[hh_.b, pp_.b], writes=[mm[q].b])
                        S.add("pool", lambda e, mm=mm, yr=yr, kb=kb: e.tensor_tensor(out=yr.t[:, kb, :], in0=mm[0].t[:], in1=mm[1].t[:], op=ALU.add),
                              reads=[mm[0].b, mm[1].b], writes=[yr.b])
                        S.add("pool", lambda e, mm=mm, yi=yi, kb=kb: e.tensor_tensor(out=yi.t[:, kb, :], in0=mm[2].t[:], in1=mm[3].t[:], op=ALU.subtract),
                              reads=[mm[2].b, mm[3].b], writes=[yi.b])
                S.end_stage(resched=True)
            with contextlib.ExitStack() as st:
                KQ = min(8, nb)
                nkq = nb // KQ
                iC = [sb(st, "iC", [128, KQ, TT], BF16) for _ in range(3)]
                iS = [sb(st, "iS", [128, KQ, TT], BF16) for _ in range(3)]
                zt_ = [sb(st, "zt", [128, 512]) for _ in range(3)]
                gt_ = [sb(st, "gt", [128, 512]) for _ in range(3)]
                ab_ = [sb(st, "ab", [128, 512]) for _ in range(3)]
                of_ = [sb(st, "of", [128, 512]) for _ in range(3)]
                ob_ = [sb(st, "ob", [128, 512], BF16) for _ in range(3)]
                kt = 0
                ke = 0
                kk = 0
                for si, t0 in enumerate(offs):
                    yr, yi = YR[si], YI[si]
                    for tt in range(ntt):
                        banks = [PS[(kk % 2) * 4 + cc] for cc in range(4)]
                        kk += 1
                        for kq in range(nkq):
                            c_, s_ = iC[kt % 3], iS[kt % 3]
                            kt += 1
                            S.add("sp", lambda e, c_=c_, tt=tt, kq=kq: e.dma_start(out=c_.t[:], in_=G["I"][0, tt, :, kq * KQ:(kq + 1) * KQ, :]), writes=[c_.b], dma=True)
                            S.add("pool", lambda e, s_=s_, tt=tt, kq=kq: e.dma_start(out=s_.t[:], in_=G["I"][1, tt, :, kq * KQ:(kq + 1) * KQ, :]), writes=[s_.b], dma=True)
                            for cc in range(4):
                                for kbl in range(KQ):
                                    kb = kq * KQ + kbl
                                    S.add("pe", lambda e, bk=banks[cc], yr=yr, c_=c_, kb=kb, kbl=kbl, cc=cc: e.matmul(
                                        bk.t[:, 0:TT], lhsT=yr.t[:, kb, cc * 128:(cc + 1) * 128], rhs=c_.t[:, kbl, :], start=(kb == 0), stop=False),
                                        reads=[yr.b, c_.b], writes=[banks[cc].b])
                                    S.add("pe", lambda e, bk=banks[cc], yi=yi, s_=s_, kb=kb, kbl=kbl, cc=cc: e.matmul(
                                        bk.t[:, 0:TT], lhsT=yi.t[:, kb, cc * 128:(cc + 1) * 128], rhs=s_.t[:, kbl, :], start=False, stop=(kb == nb - 1)),
                                        reads=[yi.b, s_.b], writes=[banks[cc].b])
                        for cc in range(4):
                            z_, g_, a_, f_, b_ = zt_[ke % 3], gt_[ke % 3], ab_[ke % 3], of_[ke % 3], ob_[ke % 3]
                            ke += 1
                            tok = slice(t0 + tt * TT, t0 + (tt + 1) * TT)
                            zsrc = uc_d[:, 8 + cc, tok] if o == 0 else z1T_d[:, cc, tok]
                            S.add("sp", lambda e, z_=z_, zsrc=zsrc: e.dma_start(out=z_.t[:, 0:TT], in_=zsrc), writes=[z_.b], dma=True)
                            S.add("sp", lambda e, g_=g_, cc=cc, tok=tok: e.dma_start(out=g_.t[:, 0:TT], in_=uc_d[:, o * 4 + cc, tok]), writes=[g_.b], dma=True)
                            S.add("pool", lambda e, z_=z_, a_=a_, cc=cc: e.tensor_scalar(out=a_.t[:, 0:TT], in0=z_.t[:, 0:TT], scalar1=hbs.t[:, o, cc:cc + 1],
                                                                                    scalar2=None, op0=ALU.mult), reads=[z_.b, hbs.b], writes=[a_.b])
                            S.add("dve", lambda e, bk=banks[cc], a_=a_: e.scalar_tensor_tensor(out=a_.t[:, 0:TT], in0=bk.t[:, 0:TT], scalar=2.0 / (2 * L),
                                                                                            in1=a_.t[:, 0:TT], op0=ALU.mult, op1=ALU.add),
                                  reads=[banks[cc].b, a_.b], writes=[a_.b])
                            if o == 0:
                                S.add("pool", lambda e, a_=a_, g_=g_, f_=f_: e.tensor_tensor(out=f_.t[:, 0:TT], in0=a_.t[:, 0:TT], in1=g_.t[:, 0:TT], op=ALU.mult),
                                      reads=[a_.b, g_.b], writes=[f_.b])
                                S.add("sp", lambda e, f_=f_, cc=cc, tok=tok: e.dma_start(out=z1T_d[:, cc, tok], in_=f_.t[:, 0:TT]), reads=[f_.b], dma=True)
                            else:
                                S.add("pool", lambda e, a_=a_, g_=g_, b_=b_: e.tensor_tensor(out=b_.t[:, 0:TT], in0=a_.t[:, 0:TT], in1=g_.t[:, 0:TT], op=ALU.mult),
                                      reads=[a_.b, g_.b], writes=[b_.b])
                                S.add("sp", lambda e, b_=b_, cc=cc, tok=tok: e.dma_start(out=ay_d[:, 4 + cc, tok], in_=b_.t[:, 0:TT]), reads=[b_.b], dma=True)
                S.end_stage(resched=True)

        def ah_hyena(l, i):
            hy_conv(l, i)
            with contextlib.ExitStack() as st:
                ident = sb(st, "ident", [128, 128])
                hbs = sb(st, "hbs", [128, 2, 4])
                rn = sb(st, "rn", [128, 1024])
                S.add("sp", lambda e: e.dma_start(out=ident.t[:], in_=ident_d), writes=[ident.b], dma=True)
                S.add("sp", lambda e: e.dma_start(out=hbs.t[:], in_=hbias[:, i]), writes=[hbs.b], dma=True)
                S.end_stage()
                for (L, offs) in ((4096, [0]), (256, [4096 + 256 * pi for pi in range(4)])):
                    nb = L // 128
                    with contextlib.ExitStack() as st2:
                        hy_filter_gen(i, L, rn)
                        hy_filter_dft(L, rn)
                        Z = [sb(st2, "Z", [128, nb, 512], BF16) for _ in offs]
                        YR = [sb(st2, "YR", [128, nb, 512], BF16) for _ in offs]
                        YI = [sb(st2, "YI", [128, nb, 512], BF16) for _ in offs]
                        for o in range(2):
                            hy_order(l, i, L, offs, o, ident, hbs, Z, YR, YI)

        class Rot:
            def __init__(self, st, name, shape, dt, n):
                self.ts = [sb(st, name, shape, dt) for _ in range(n)]
                self.k = 0

            def next(self):
                t = self.ts[self.k % len(self.ts)]
                self.k += 1
                return t

        def dn_inproj(l, i):
            with contextlib.ExitStack() as st:
                X = [sb(st, "x", [128, 8, TM]) for _ in range(2)]
                Hh = [sb(st, "h", [128, 8, TM], BF16) for _ in range(2)]
                sq = [sb(st, "sq", [128, TM]) for _ in range(2)]
                tmp = [sb(st, "tmp", [128, TM]) for _ in range(2)]
                rstd = sb(st, "rstd", [128, TM])
                W = sb(st, "dwin", [128, 8, 4128], BF16)
                stg = [sb(st, "stg", [128, 512]) for _ in range(4)]
                vst = [sb(st, "vst", [128, 8, 32]) for _ in range(2)]
                for kc in range(8):
                    S.add("pool", lambda e, kc=kc: e.dma_start(out=W.t[:, kc, :], in_=dwin[i, :, kc, :], max_dma_last_dim=4096),
                          writes=[W.b], dma=True)

                def load_x(mt):
                    xb = X[mt % 2]
                    S.add("sp", lambda e: e.dma_start(out=xb.t[:], in_=xs[:, :, mt * TM:(mt + 1) * TM]), writes=[xb.b], dma=True)

                def stage_a(mt):
                    which = 0 if mt < 4 else 1
                    norm_mod((sq, rstd, tmp), X[mt % 2], Hh[mt % 2], 1, which, (PS[4], PS[5]))

                load_x(0)
                stage_a(0)
                ctr = [0]
                for mt in range(NMT):
                    hb = Hh[mt % 2]
                    if mt + 1 < NMT:
                        load_x(mt + 1)
                    for oc in range(32):
                        dst, ch = (qkvT_d, oc) if oc < 24 else (zT_d, oc - 24)
                        for s_ in range(TM // 512):
                            k = ctr[0]
                            ctr[0] += 1
                            pb = PS[k % 4]
                            sg_ = stg[k % 4]
                            for c in range(8):
                                S.add("pe", lambda e, pb=pb, c=c, s_=s_, oc=oc, hb=hb: e.matmul(
                                    pb.t[:], lhsT=W.t[:, c, oc * 128:(oc + 1) * 128],
                                    rhs=hb.t[:, c, s_ * 512:(s_ + 1) * 512], start=(c == 0), stop=(c == 7)),
                                    reads=[W.b, hb.b], writes=[pb.b])
                            if k % 2 == 0:
                                S.add("act", lambda e, pb=pb, sg_=sg_: e.activation(out=sg_.t[:], in_=pb.t[:], func=AF.Identity),
                                      reads=[pb.b], writes=[sg_.b])
                            else:
                                S.add("dve", lambda e, pb=pb, sg_=sg_: e.tensor_copy(out=sg_.t[:], in_=pb.t[:]),
                                      reads=[pb.b], writes=[sg_.b])
                            t0 = mt * TM + s_ * 512
                            S.add("sp", lambda e, dst=dst, ch=ch, t0=t0, sg_=sg_: e.dma_start(
                                out=dst[:, ch, t0:t0 + 512], in_=sg_.t[:]), reads=[sg_.b], dma=True)
                        if oc == 15 and mt + 1 < NMT:
                            stage_a(mt + 1)
                    vs_ = vst[mt % 2]
                    for tb in range(TM // 128):
                        pb = PS[6 + tb % 2]
                        for c in range(8):
                            S.add("pe", lambda e, pb=pb, c=c, tb=tb, hb=hb: e.matmul(
                                pb.t[:, 0:32], lhsT=hb.t[:, c, tb * 128:(tb + 1) * 128], rhs=W.t[:, c, 4096:4128],
                                start=(c == 0), stop=(c == 7)), reads=[W.b, hb.b], writes=[pb.b])
                        S.add("dve", lambda e, pb=pb, tb=tb, vs_=vs_: e.tensor_copy(out=vs_.t[:, tb, :], in_=pb.t[:, 0:32]),
                              reads=[pb.b], writes=[vs_.b])
                    S.add("sp", lambda e, mt=mt, vs_=vs_: e.dma_start(
                        out=ba_d[mt * TM:(mt + 1) * TM, :].rearrange("(tb p) n -> p tb n", p=128), in_=vs_.t[:]),
                        reads=[vs_.b], dma=True)
                S.end_stage()

        def dn_conv(l, i):
            with contextlib.ExitStack() as st:
                cw = sb(st, "dcw", [128, 24, 3])
                U = [sb(st, "U", [128, 12, 514]) for _ in range(2)]
                O = [sb(st, "O", [128, 12, 512]) for _ in range(2)]
                SQ = Rot(st, "dsq", [128, 512], F32, 8)
                RS = Rot(st, "drs", [128, 512], F32, 8)
                S.add("sp", lambda e: e.dma_start(out=cw.t[:], in_=dcw[:, i]), writes=[cw.b], dma=True)
                tiles = [(tt * 512, 512, tt == 0, tt == 7) for tt in range(8)] + [(4096 + 256 * pi, 256, True, True) for pi in range(4)]
                ti = 0
                for (t0, n, first, lastt) in tiles:
                    for half in range(2):
                        u, o = U[ti % 2], O[ti % 2]
                        ti += 1
                        a0 = t0 if first else t0 - 1
                        a1 = t0 + n if lastt else t0 + n + 1
                        c0 = 1 if first else 0
                        S.add("sp", lambda e, u=u, a0=a0, a1=a1, c0=c0, half=half: e.dma_start(
                            out=u.t[:, :, c0:c0 + (a1 - a0)], in_=qkvT_d[:, half * 12:(half + 1) * 12, a0:a1]), writes=[u.b], dma=True)
                        if first:
                            S.add("pool", lambda e, u=u: e.memset(u.t[:, :, 0:1], 0.0), writes=[u.b])
                        if lastt:
                            S.add("pool", lambda e, u=u, n=n: e.memset(u.t[:, :, n + 1:n + 2], 0.0), writes=[u.b])
                        for c12 in range(12):
                            ch = half * 12 + c12
                            S.add("dve", lambda e, u=u, o=o, ch=ch, c12=c12, n=n: e.tensor_scalar(
                                out=o.t[:, c12, 0:n], in0=u.t[:, c12, 0:n], scalar1=cw.t[:, ch, 0:1], scalar2=None, op0=ALU.mult),
                                reads=[u.b, cw.b], writes=[o.b])
                            for tap in (1, 2):
                                S.add("dve", lambda e, u=u, o=o, ch=ch, c12=c12, n=n, tap=tap: e.scalar_tensor_tensor(
                                    out=o.t[:, c12, 0:n], in0=u.t[:, c12, tap:n + tap], scalar=cw.t[:, ch, tap:tap + 1], in1=o.t[:, c12, 0:n],
                                    op0=ALU.mult, op1=ALU.add), reads=[u.b, cw.b, o.b], writes=[o.b])
                        S.add("act", lambda e, o=o, n=n: e.activation(out=o.t[:, :, 0:n], in_=o.t[:, :, 0:n], func=AF.Silu), reads=[o.b], writes=[o.b])
                        for c12 in range(12):
                            ch = half * 12 + c12
                            if ch >= 16:
                                continue
                            q_, r_ = SQ.next(), RS.next()
                            pb = PS[ch % 8]
                            S.add("act", lambda e, o=o, q_=q_, c12=c12, n=n: e.activation(out=q_.t[:, 0:n], in_=o.t[:, c12, 0:n], func=AF.Square),
                                  reads=[o.b], writes=[q_.b])
                            S.add("pe", lambda e, pb=pb, q_=q_, n=n: e.matmul(pb.t[:, 0:n], lhsT=ones32.t[:], rhs=q_.t[:, 0:n], start=True, stop=True),
                                  reads=[ones32.b, q_.b], writes=[pb.b])
                            S.add("act", lambda e, pb=pb, r_=r_, n=n: e.activation(out=r_.t[:, 0:n], in_=pb.t[:, 0:n], func=AF.Sqrt, bias=epsT.t[:, 0:1]),
                                  reads=[pb.b, epsT.b], writes=[r_.b])
                            S.add("dve", lambda e, r_=r_, n=n: e.reciprocal(out=r_.t[:, 0:n], in_=r_.t[:, 0:n]), reads=[r_.b], writes=[r_.b])
                            sc = (128.0 ** -0.5) if ch < 8 else 1.0
                            S.add("dve", lambda e, o=o, r_=r_, c12=c12, n=n, sc=sc: e.scalar_tensor_tensor(
                                out=o.t[:, c12, 0:n], in0=o.t[:, c12, 0:n], scalar=sc, in1=r_.t[:, 0:n], op0=ALU.mult, op1=ALU.mult),
                                reads=[o.b, r_.b], writes=[o.b])
                        S.add("sp", lambda e, o=o, t0=t0, n=n, half=half: e.dma_start(out=qkvn_d[:, half * 12:(half + 1) * 12, t0:t0 + n], in_=o.t[:, :, 0:n]),
                              reads=[o.b], dma=True)
                S.end_stage(resched=True)

        def dn_chunks(l, i):
            with contextlib.ExitStack() as st:
                msk = sb(st, "msk", [64, 5, 64])
                ident = sb(st, "ident", [128, 128])
                ones64 = sb(st, "ones64", [64, 128])
                prm = sb(st, "prm", [64, 32])
                ba = sb(st, "ba", [64, NCH, 32])
                gall = sb(st, "gall", [64, NCH, 16])
                ball = sb(st, "ball", [64, NCH, 16])
                KT_ = sb(st, "KTt", [128, 8, 256])
                QT_ = sb(st, "QTt", [128, 8, 256])
                VT_ = sb(st, "VTt", [128, 8, 256])
                Ktm = sb(st, "Ktm", [64, 8, 128])
                Vtm = sb(st, "Vtm", [64, 8, 128])
                r512L = [Rot(st, "r512", [128, 512], F32, 10) for _ in range(2)]
                ratr = Rot(st, "ratr", [64, 512], F32, 2)
                b512L = [Rot(st, "b512", [64, 512], BF16, 16) for _ in range(2)]
                batnL = [Rot(st, "batn", [64, 512], BF16, 4) for _ in range(2)]
                lvm_t = sb(st, "lvm", [64, 12, 64])
                S.add("sp", lambda e: e.dma_start(out=lvm_t.t[:], in_=lvmask_d), writes=[lvm_t.b], dma=True)
                b1kL = [Rot(st, "b1k", [64, 8, 128], BF16, 3) for _ in range(2)]
                bqL = [Rot(st, "bq", [128, 512], BF16, 2) for _ in range(2)]
                u1kL = [Rot(st, "u1k", [64, 8, 128], F32, 1) for _ in range(2)]
                smL = [Rot(st, "sm", [128, 8], F32, 10) for _ in range(2)]
                S.add("sp", lambda e: e.dma_start(out=msk.t[:], in_=dmask_d), writes=[msk.b], dma=True)
                S.add("sp", lambda e: e.dma_start(out=ident.t[:], in_=ident_d), writes=[ident.b], dma=True)
                S.add("sp", lambda e: e.dma_start(out=prm.t[:], in_=dprm[0:64, i, :]), writes=[prm.b], dma=True)
                S.add("sp", lambda e: e.dma_start(out=ba.t[:], in_=ba_d.rearrange("(c p) n -> p c n", p=64)), writes=[ba.b], dma=True)
                S.add("dve", lambda e: e.memset(ones64.t[:], 1.0), writes=[ones64.b])
                S.add("act", lambda e: e.activation(out=ball.t[:], in_=ba.t[:, :, 0:16], func=AF.Sigmoid), reads=[ba.b], writes=[ball.b])
                S.add("dve", lambda e: e.tensor_tensor(out=gall.t[:], in0=ba.t[:, :, 16:32],
                                                       in1=prm.t[:, 16:32].unsqueeze(1).broadcast_to([64, NCH, 16]), op=ALU.add),
                      reads=[ba.b, prm.b], writes=[gall.b])
                S.add("act", lambda e: e.activation(out=gall.t[:], in_=gall.t[:], func=AF.Exp), reads=[gall.b], writes=[gall.b])
                S.add("act", lambda e: e.activation(out=gall.t[:], in_=gall.t[:], func=AF.Ln, bias=1.0), reads=[gall.b], writes=[gall.b])
                S.add("act", lambda e: e.activation(out=prm.t[:, 0:16], in_=prm.t[:, 0:16], func=AF.Exp), reads=[prm.b], writes=[prm.b])
                S.add("dve", lambda e: e.scalar_tensor_tensor(out=gall.t[:], in0=gall.t[:], scalar=-1.0,
                                                              in1=prm.t[:, 0:16].unsqueeze(1).broadcast_to([64, NCH, 16]), op0=ALU.mult, op1=ALU.mult),
                      reads=[gall.b, prm.b], writes=[gall.b])
                Uf, Ub, Sf, Sb, I64 = (msk.t[:, k_, :] for k_ in range(5))

                def bc_h(m):
                    return m.unsqueeze(1).broadcast_to([64, 8, 64])

                def v3(t_, p=64):
                    return t_[0:p, :].rearrange("p (h i) -> p h i", i=64)

                def chunk_body(c):
                    cl = c % 4
                    if cl == 0:
                        t0 = c * 64
                        for (tile_, ch0) in ((QT_, 0), (KT_, 8), (VT_, 16)):
                            S.add("sp", lambda e, tile_=tile_, ch0=ch0, t0=t0: e.dma_start(out=tile_.t[:], in_=qkvn_d[:, ch0:ch0 + 8, t0:t0 + 256]),
                                  writes=[tile_.b], dma=True)
                    csl = slice(cl * 64, (cl + 1) * 64)
                    Kc, Qc, Vc = KT_.t[:, :, csl], QT_.t[:, :, csl], VT_.t[:, :, csl]
                    for (src, srcb, dst, bk0) in ((Kc, KT_.b, Ktm, 0), (Vc, VT_.b, Vtm, 4)):
                        for h in range(8):
                            pb = PS[bk0 + h // 4]
                            S.add("pe", lambda e, pb=pb, src=src, h=h: e.transpose(pb.t[0:64, (h % 4) * 128:(h % 4 + 1) * 128], src[:, h, :], ident.t[:]),
                                  reads=[srcb, ident.b], writes=[pb.b])
                        for hb_ in range(2):
                            S.add("act", lambda e, dst=dst, hb_=hb_, bk0=bk0: e.activation(
                                out=dst.t[:, hb_ * 4:(hb_ + 1) * 4, :], in_=PS[bk0 + hb_].t[0:64, :].rearrange("p (h d) -> p h d", d=128), func=AF.Identity),
                                reads=[PS[bk0 + hb_].b], writes=[dst.b])
                    for h in range(8):
                        S.add("pe", lambda e, h=h, Kc=Kc, Qc=Qc: e.matmul(PS[2].t[0:64, h * 64:(h + 1) * 64], lhsT=Kc[:, h, :], rhs=Qc[:, h, :], start=True, stop=True),
                              reads=[KT_.b, QT_.b], writes=[PS[2].b])
                    atraw = ratr.next()
                    S.add("dve", lambda e, atraw=atraw: e.tensor_copy(out=atraw.t[0:64, :], in_=PS[2].t[0:64, :]), reads=[PS[2].b], writes=[atraw.b])
                    def unit(d):
                        r512, b512, batn, b1k, bq, u1k, sm = r512L[d], b512L[d], batnL[d], b1kL[d], bqL[d], u1kL[d], smL[d]
                        B0, B1, B2, B3 = (PS[4 * d + k_] for k_ in range(4))
                        Ud, Sd, SdT = (Uf, Sf, Sb) if d == 0 else (Ub, Sb, Sf)
                        g_ = gall.t[:, c, d * 8:(d + 1) * 8]
                        b_ = ball.t[:, c, d * 8:(d + 1) * 8]
                        ug, ib = r512.next(), r512.next()
                        S.add("pool", lambda e, ug=ug, Ud=Ud, g_=g_: e.tensor_tensor(out=v3(ug.t), in0=bc_h(Ud), in1=g_.unsqueeze(2).broadcast_to([64, 8, 64]), op=ALU.mult),
                              reads=[msk.b, gall.b], writes=[ug.b])
                        S.add("pool", lambda e, ib=ib, b_=b_: e.tensor_tensor(out=v3(ib.t), in0=bc_h(I64), in1=b_.unsqueeze(2).broadcast_to([64, 8, 64]), op=ALU.mult),
                              reads=[msk.b, ball.b], writes=[ib.b])
                        S.add("pe", lambda e, Ud=Ud, g_=g_: e.matmul(B0.t[0:64, 0:8], lhsT=Ud, rhs=g_, start=True, stop=True),
                              reads=[msk.b, gall.b], writes=[B0.b])
                        S.add("pe", lambda e, g_=g_: e.matmul(B0.t[:, 8:16], lhsT=ones64.t[:], rhs=g_, start=True, stop=True),
                              reads=[ones64.b, gall.b], writes=[B0.b])
                        S.add("pe", lambda e, ug=ug: e.matmul(B1.t[:], lhsT=ones64.t[:], rhs=ug.t[0:64, :], start=True, stop=True),
                              reads=[ones64.b, ug.b], writes=[B1.b])
                        S.add("pe", lambda e, ib=ib: e.matmul(B2.t[:], lhsT=ones64.t[:], rhs=ib.t[0:64, :], start=True, stop=True),
                              reads=[ones64.b, ib.b], writes=[B2.b])
                        gcc, egl, egc, bg, kds = sm.next(), sm.next(), sm.next(), sm.next(), sm.next()
                        S.add("dve", lambda e, gcc=gcc: e.tensor_copy(out=gcc.t[0:64, :], in_=B0.t[0:64, 0:8]), reads=[B0.b], writes=[gcc.b])
                        S.add("act", lambda e, egl=egl: e.activation(out=egl.t[:], in_=B0.t[:, 8:16], func=AF.Exp), reads=[B0.b], writes=[egl.b])
                        S.add("sp", lambda e, egl=egl, d=d, c=c: e.dma_start(out=egl_d[d, c], in_=egl.t[:]), reads=[egl.b], dma=True)
                        S.add("act", lambda e, egc=egc, gcc=gcc: e.activation(out=egc.t[0:64, :], in_=gcc.t[0:64, :], func=AF.Exp), reads=[gcc.b], writes=[egc.b])
                        S.add("dve", lambda e, bg=bg, egc=egc, b_=b_: e.tensor_tensor(out=bg.t[0:64, :], in0=egc.t[0:64, :], in1=b_, op=ALU.mult),
                              reads=[egc.b, ball.b], writes=[bg.b])
                        S.add("dve", lambda e, kds=kds, gcc=gcc: e.tensor_tensor(out=kds.t[0:64, :], in0=B0.t[0:64, 8:16], in1=gcc.t[0:64, :], op=ALU.subtract),
                              reads=[B0.b, gcc.b], writes=[kds.b])
                        S.add("act", lambda e, kds=kds: e.activation(out=kds.t[0:64, :], in_=kds.t[0:64, :], func=AF.Exp), reads=[kds.b], writes=[kds.b])
                        Dt, E1, E2 = r512.next(), r512.next(), r512.next()
                        S.add("dve", lambda e, Dt=Dt, gcc=gcc: e.tensor_tensor(out=v3(Dt.t), in0=v3(B1.t), in1=gcc.t[0:64, :].unsqueeze(2).broadcast_to([64, 8, 64]),
                                                                        op=ALU.subtract), reads=[B1.b, gcc.b], writes=[Dt.b])
                        S.add("dve", lambda e, Dt=Dt, E1=E1: e.tensor_scalar(out=E1.t[0:64, :], in0=Dt.t[0:64, :], scalar1=0.0, scalar2=None, op0=ALU.min),
                              reads=[Dt.b], writes=[E1.b])
                        S.add("dve", lambda e, Dt=Dt, E2=E2: e.tensor_scalar(out=E2.t[0:64, :], in0=Dt.t[0:64, :], scalar1=-1.0, scalar2=0.0, op0=ALU.mult, op1=ALU.min),
                              reads=[Dt.b], writes=[E2.b])
                        S.add("act", lambda e, E1=E1: e.activation(out=E1.t[0:64, :], in_=E1.t[0:64, :], func=AF.Exp), reads=[E1.b], writes=[E1.b])
                        S.add("act", lambda e, E2=E2: e.activation(out=E2.t[0:64, :], in_=E2.t[0:64, :], func=AF.Exp), reads=[E2.b], writes=[E2.b])
                        decI, decS, decN = r512.next(), r512.next(), r512.next()
                        S.add("pool", lambda e, decI=decI, E1=E1, Ud=Ud: e.tensor_tensor(out=v3(decI.t), in0=v3(E1.t), in1=bc_h(Ud), op=ALU.mult),
                              reads=[E1.b, msk.b], writes=[decI.b])
                        S.add("pool", lambda e, decS=decS, E1=E1, Sd=Sd: e.tensor_tensor(out=v3(decS.t), in0=v3(E1.t), in1=bc_h(Sd), op=ALU.mult),
                              reads=[E1.b, msk.b], writes=[decS.b])
                        S.add("pool", lambda e, decN=decN, E2=E2, SdT=SdT: e.tensor_tensor(out=v3(decN.t), in0=v3(E2.t), in1=bc_h(SdT), op=ALU.mult),
                              reads=[E2.b, msk.b], writes=[decN.b])
                        eg = r512.next()
                        qg = bq.next()
                        S.add("act", lambda e, eg=eg: e.activation(out=eg.t[:], in_=B1.t[:], func=AF.Exp), reads=[B1.b], writes=[eg.b])
                        S.add("pool", lambda e, eg=eg, qg=qg, Qc=Qc: e.tensor_tensor(out=v3(qg.t, 128), in0=Qc, in1=v3(eg.t, 128), op=ALU.mult),
                              reads=[eg.b, QT_.b], writes=[qg.b])
                        S.add("sp", lambda e, qg=qg, d=d, c=c: e.dma_start(out=QgT_d[d, c], in_=v3(qg.t, 128)), reads=[qg.b], dma=True)
                        kbT = r512.next()
                        S.add("dve", lambda e, kbT=kbT, Kc=Kc: e.tensor_tensor(out=v3(kbT.t, 128), in0=Kc, in1=v3(B2.t, 128), op=ALU.mult),
                              reads=[B2.b, KT_.b], writes=[kbT.b])
                        for h in range(8):
                            S.add("pe", lambda e, h=h, Kc=Kc, kbT=kbT: e.matmul(B3.t[0:64, h * 64:(h + 1) * 64], lhsT=Kc[:, h, :], rhs=kbT.t[:, h * 64:(h + 1) * 64],
                                                                              start=True, stop=True), reads=[KT_.b, kbT.b], writes=[B3.b])
                        for h in range(8):
                            S.add("pe", lambda e, h=h, Kc=Kc, kbT=kbT: e.matmul(B0.t[0:64, h * 64:(h + 1) * 64], lhsT=kbT.t[:, h * 64:(h + 1) * 64], rhs=Kc[:, h, :],
                                                                              start=True, stop=True), reads=[KT_.b, kbT.b], writes=[B0.b])
                        AT, AN = batn.next(), batn.next()
                        S.add("dve", lambda e, AT=AT, decS=decS: e.scalar_tensor_tensor(out=AT.t[:], in0=B3.t[0:64, :], scalar=-1.0, in1=decS.t[0:64, :],
                                                                                    op0=ALU.mult, op1=ALU.mult), reads=[B3.b, decS.b], writes=[AT.b])
                        S.add("dve", lambda e, AN=AN, decN=decN: e.scalar_tensor_tensor(out=AN.t[:], in0=B0.t[0:64, :], scalar=-1.0, in1=decN.t[0:64, :],
                                                                                    op0=ALU.mult, op1=ALU.mult), reads=[B0.b, decN.b], writes=[AN.b])
                        def lvm(lv, tr):
                            k_ = 2 * lv + (tr if d == 0 else 1 - tr)
                            return bc_h(lvm_t.t[:, k_, :])
                        TN, TT = b512.next(), b512.next()
                        for (dst_, src_, tr) in ((TN, AN, 0), (TT, AT, 1)):
                            mk0 = lvm(0, tr)
                            S.add("pool", lambda e, dst_=dst_, src_=src_, mk0=mk0: e.tensor_tensor(out=v3(dst_.t), in0=v3(src_.t), in1=mk0, op=ALU.mult),
                                  reads=[src_.b, lvm_t.b], writes=[dst_.b])
                            S.add("pool", lambda e, dst_=dst_: e.tensor_tensor(out=v3(dst_.t), in0=v3(dst_.t), in1=bc_h(I64), op=ALU.add),
                                  reads=[dst_.b, msk.b], writes=[dst_.b])
                        for lv in range(1, 6):
                            LoN, LoT, M1, M2, TT2 = b512.next(), b512.next(), b512.next(), b512.next(), b512.next()
                            mkn, mkt = lvm(lv, 0), lvm(lv, 1)
                            S.add("pool", lambda e, LoN=LoN, AN=AN, mkn=mkn: e.tensor_tensor(out=v3(LoN.t), in0=v3(AN.t), in1=mkn, op=ALU.mult),
                                  reads=[AN.b, lvm_t.b], writes=[LoN.b])
                            S.add("pool", lambda e, LoT=LoT, AT=AT, mkt=mkt: e.tensor_tensor(out=v3(LoT.t), in0=v3(AT.t), in1=mkt, op=ALU.mult),
                                  reads=[AT.b, lvm_t.b], writes=[LoT.b])
                            for h in range(8):
                                hs_ = slice(h * 64, (h + 1) * 64)
                                S.add("pe", lambda e, hs_=hs_, LoN=LoN, TT=TT: e.matmul(B1.t[0:64, hs_], lhsT=LoN.t[:, hs_], rhs=TT.t[:, hs_], start=True, stop=True),
                                      reads=[LoN.b, TT.b], writes=[B1.b])
                            S.add("act", lambda e, M1=M1: e.activation(out=M1.t[:], in_=B1.t[0:64, :], func=AF.Identity), reads=[B1.b], writes=[M1.b])
                            if lv < 5:
                                for h in range(8):
                                    hs_ = slice(h * 64, (h + 1) * 64)
                                    S.add("pe", lambda e, hs_=hs_, LoT=LoT, TN=TN: e.matmul(B2.t[0:64, hs_], lhsT=LoT.t[:, hs_], rhs=TN.t[:, hs_], start=True, stop=True),
                                          reads=[LoT.b, TN.b], writes=[B2.b])
                                S.add("act", lambda e, M2=M2: e.activation(out=M2.t[:], in_=B2.t[0:64, :], func=AF.Identity), reads=[B2.b], writes=[M2.b])
                            for h in range(8):
                                hs_ = slice(h * 64, (h + 1) * 64)
                                S.add("pe", lambda e, hs_=hs_, TN=TN, M1=M1: e.matmul(B3.t[0:64, hs_], lhsT=TN.t[:, hs_], rhs=M1.t[:, hs_], start=True, stop=True),
                                      reads=[TN.b, M1.b], writes=[B3.b])
                            S.add("dve", lambda e, TT2=TT2, TT=TT: e.tensor_tensor(out=TT2.t[:], in0=B3.t[0:64, :], in1=TT.t[:], op=ALU.add),
                                  reads=[B3.b, TT.b], writes=[TT2.b])
                            if lv < 5:
                                TN2 = b512.next()
                                for h in range(8):
                                    hs_ = slice(h * 64, (h + 1) * 64)
                                    S.add("pe", lambda e, hs_=hs_, TT=TT, M2=M2: e.matmul(B0.t[0:64, hs_], lhsT=TT.t[:, hs_], rhs=M2.t[:, hs_], start=True, stop=True),
                                          reads=[TT.b, M2.b], writes=[B0.b])
                                S.add("dve", lambda e, TN2=TN2, TN=TN: e.tensor_tensor(out=TN2.t[:], in0=B0.t[0:64, :], in1=TN.t[:], op=ALU.add),
                                      reads=[B0.b, TN.b], writes=[TN2.b])
                                TN = TN2
                            TT = TT2
                        P = TT
                        ato = b512.next()
                        S.add("pool", lambda e, ato=ato, atraw=atraw, decI=decI: e.tensor_tensor(out=ato.t[:], in0=atraw.t[0:64, :], in1=decI.t[0:64, :], op=ALU.mult),
                              reads=[atraw.b, decI.b], writes=[ato.b])
                        S.add("sp", lambda e, ato=ato, d=d, c=c: e.dma_start(out=AT_d[d, c], in_=v3(ato.t)), reads=[ato.b], dma=True)
                        Vb, KBg, Kdd = b1k.next(), b1k.next(), b1k.next()
                        S.add("pool", lambda e, Vb=Vb, b_=b_: e.tensor_tensor(out=Vb.t[:], in0=Vtm.t[:], in1=b_.unsqueeze(2).broadcast_to([64, 8, 128]), op=ALU.mult),
                              reads=[Vtm.b, ball.b], writes=[Vb.b])
                        S.add("pool", lambda e, KBg=KBg, bg=bg: e.tensor_tensor(out=KBg.t[:], in0=Ktm.t[:], in1=bg.t[0:64, :].unsqueeze(2).broadcast_to([64, 8, 128]), op=ALU.mult),
                              reads=[Ktm.b, bg.b], writes=[KBg.b])
                        S.add("pool", lambda e, Kdd=Kdd, kds=kds: e.tensor_tensor(out=Kdd.t[:], in0=Ktm.t[:], in1=kds.t[0:64, :].unsqueeze(2).broadcast_to([64, 8, 128]), op=ALU.mult),
                              reads=[Ktm.b, kds.b], writes=[Kdd.b])
                        S.add("sp", lambda e, Kdd=Kdd, d=d, c=c: e.dma_start(out=Kd_d[d, c], in_=Kdd.t[:]), reads=[Kdd.b], dma=True)
                        for h in range(8):
                            pb = (B1, B2)[h // 4]
                            S.add("pe", lambda e, pb=pb, h=h, P=P, Vb=Vb: e.matmul(pb.t[0:64, (h % 4) * 128:(h % 4 + 1) * 128], lhsT=P.t[:, h * 64:(h + 1) * 64], rhs=Vb.t[:, h, :],
                                                                              start=True, stop=True), reads=[P.b, Vb.b], writes=[pb.b])
                        uo = u1k.next()
                        for hb_ in range(2):
                            S.add("act", lambda e, uo=uo, hb_=hb_: e.activation(out=uo.t[:, hb_ * 4:(hb_ + 1) * 4, :],
                                                                           in_=(B1, B2)[hb_].t[0:64, :].rearrange("p (h d) -> p h d", d=128), func=AF.Identity),
                                  reads=[(B1, B2)[hb_].b], writes=[uo.b])
                        S.add("sp", lambda e, uo=uo, d=d, c=c: e.dma_start(out=u_d[d, c], in_=uo.t[:]), reads=[uo.b], dma=True)
                        for h in range(8):
                            S.add("pe", lambda e, h=h, P=P, KBg=KBg: e.matmul(B3.t[:, h * 64:(h + 1) * 64], lhsT=KBg.t[:, h, :], rhs=P.t[:, h * 64:(h + 1) * 64],
                                                                            start=True, stop=True), reads=[P.b, KBg.b], writes=[B3.b])
                        wo = bq.next()
                        S.add("dve", lambda e, wo=wo: e.tensor_copy(out=wo.t[:], in_=B3.t[:]), reads=[B3.b], writes=[wo.b])
                        S.add("sp", lambda e, wo=wo, d=d, c=c: e.dma_start(out=wT_d[d, c], in_=v3(wo.t, 128)), reads=[wo.b], dma=True)
                    for d_ in range(2):
                        unit(d_)

                for c_ in range(NCH):
                    chunk_body(c_)
                S.end_stage(resched=True)

        def dn_scan(l, i):
            with contextlib.ExitStack() as st:
                Sf = [sb(st, "S", [128, 8, 128]) for _ in range(2)]
                Sbf = [sb(st, "Sbf", [128, 8, 128], BF16) for _ in range(2)]
                uL = [Rot(st, "uL", [64, 8, 128], F32, 2) for _ in range(2)]
                wL = [Rot(st, "wL", [128, 8, 64], BF16, 2) for _ in range(2)]
                qL = [Rot(st, "qL", [128, 8, 64], BF16, 2) for _ in range(2)]
                aL = [Rot(st, "aL", [128, 8, 64], BF16, 2) for _ in range(2)]
                kL = [Rot(st, "kL", [64, 8, 128], BF16, 2) for _ in range(2)]
                eL = [Rot(st, "eL", [128, 8], F32, 2) for _ in range(2)]
                vn = [Rot(st, "vn", [128, 8, 128], BF16, 2) for _ in range(2)]
                for d in range(2):
                    for t_ in aL[d].ts + vn[d].ts:
                        S.add("pool", lambda e, t_=t_: e.memset(t_.t[:], 0.0), writes=[t_.b])
                oS = [Rot(st, "oS", [128, 8, 64], F32, 2) for _ in range(2)]
                seqs = [(0, 64, None)] + [(64 + 4 * pi, 4, pi) for pi in range(4)]
                for (c0, nch, pi) in seqs:
                    for d in range(2):
                        if pi is None:
                            S.add("sp", lambda e, d=d: e.dma_start(out=Sf[d].t[:], in_=sf0[d, i]), writes=[Sf[d].b], dma=True)
                        else:
                            S.add("pool", lambda e, d=d: e.memset(Sf[d].t[:], 0.0), writes=[Sf[d].b])
                        S.add("act", lambda e, d=d: e.activation(out=Sbf[d].t[:], in_=Sf[d].t[:], func=AF.Identity), reads=[Sf[d].b], writes=[Sbf[d].b])
                    for step in range(nch):
                        for d in range(2):
                            c = c0 + step if d == 0 else c0 + nch - 1 - step
                            sF, sB = Sf[d], Sbf[d]
                            u_, w_, q_, a_, k_, e_ = uL[d].next(), wL[d].next(), qL[d].next(), aL[d].next(), kL[d].next(), eL[d].next()
                            q1 = "sp" if d == 0 else "act"
                            for (tile_, src) in ((u_, u_d[d, c]), (w_, wT_d[d, c]), (q_, QgT_d[d, c]), (a_, AT_d[d, c]), (k_, Kd_d[d, c]), (e_, egl_d[d, c])):
                                np_ = src.shape[0]
                                S.add("sp", lambda e, tile_=tile_, src=src, np_=np_: e.dma_start(out=tile_.t[0:np_], in_=src), writes=[tile_.b], dma=True)
                            pw = (PS[0], PS[1]) if d == 0 else (PS[4], PS[5])
                            po = PS[2] if d == 0 else PS[6]
                            for h in range(8):
                                pb = pw[h // 4]
                                S.add("pe", lambda e, pb=pb, h=h, w_=w_, sB=sB: e.matmul(pb.t[0:64, (h % 4) * 128:(h % 4 + 1) * 128], lhsT=w_.t[:, h, :], rhs=sB.t[:, h, :],
                                                                                    start=True, stop=True), reads=[w_.b, sB.b], writes=[pb.b])
                            v_ = vn[d].next()
                            for hb_ in range(2):
                                S.add("dve", lambda e, v_=v_, u_=u_, hb_=hb_, pw=pw: e.tensor_tensor(
                                    out=v_.t[0:64, hb_ * 4:(hb_ + 1) * 4, :], in0=u_.t[:, hb_ * 4:(hb_ + 1) * 4, :],
                                    in1=pw[hb_].t[0:64, :].rearrange("p (h d) -> p h d", d=128), op=ALU.subtract),
                                    reads=[u_.b, pw[hb_].b], writes=[v_.b])
                            for h in range(8):
                                S.add("pe", lambda e, po=po, h=h, q_=q_, sB=sB: e.matmul(po.t[:, h * 64:(h + 1) * 64], lhsT=sB.t[:, h, :], rhs=q_.t[:, h, :],
                                                                                    start=True, stop=False), reads=[q_.b, sB.b], writes=[po.b])
                                S.add("pe", lambda e, po=po, h=h, a_=a_, v_=v_: e.matmul(po.t[:, h * 64:(h + 1) * 64], lhsT=v_.t[:, h, :], rhs=a_.t[:, h, :],
                                                                                    start=False, stop=True), reads=[a_.b, v_.b], writes=[po.b])
                            o_ = oS[d].next()
                            S.add("act", lambda e, o_=o_, po=po: e.activation(out=o_.t[:], in_=po.t[:].rearrange("p (h i) -> p h i", i=64), func=AF.Identity),
                                  reads=[po.b], writes=[o_.b])
                            S.add("sp", lambda e, o_=o_, d=d, c=c: e.dma_start(out=oT_d[d, :, :, c * 64:(c + 1) * 64], in_=o_.t[:]), reads=[o_.b], dma=True)
                            for h in range(8):
                                pb = pw[h // 4]
                                S.add("pe", lambda e, pb=pb, h=h, k_=k_, v_=v_: e.matmul(pb.t[:, (h % 4) * 128:(h % 4 + 1) * 128], lhsT=k_.t[:, h, :], rhs=v_.t[0:64, h, :],
                                                                                    start=True, stop=True), reads=[k_.b, v_.b], writes=[pb.b])
                            S.add("pool", lambda e, sF=sF, e_=e_: e.tensor_tensor(out=sF.t[:], in0=sF.t[:], in1=e_.t[:].unsqueeze(2).broadcast_to([128, 8, 128]), op=ALU.mult),
                                  reads=[sF.b, e_.b], writes=[sF.b])
                            for hb_ in range(2):
                                S.add("dve", lambda e, sF=sF, hb_=hb_, pw=pw: e.tensor_tensor(
                                    out=sF.t[:, hb_ * 4:(hb_ + 1) * 4, :], in0=sF.t[:, hb_ * 4:(hb_ + 1) * 4, :],
                                    in1=pw[hb_].t[:].rearrange("p (h d) -> p h d", d=128), op=ALU.add), reads=[sF.b, pw[hb_].b], writes=[sF.b])
                            S.add("act", lambda e, sF=sF, sB=sB: e.activation(out=sB.t[:], in_=sF.t[:], func=AF.Identity), reads=[sF.b], writes=[sB.b])
                    if pi is not None:
                        for d in range(2):
                            S.add("sp", lambda e, d=d, pi=pi: e.dma_start(out=nst[d, i, pi], in_=Sf[d].t[:]), reads=[Sf[d].b], dma=True)
                S.end_stage(resched=True)

        def dn_final(l, i):
            with contextlib.ExitStack() as st:
                W = sb(st, "dwout", [128, 8, 1024], BF16)
                gn2 = sb(st, "dng", [128, 2])
                OF = Rot(st, "OF", [128, 8, 512], F32, 2)
                OB = Rot(st, "OB", [128, 8, 512], F32, 2)
                ZZ = Rot(st, "ZZ", [128, 8, 512], F32, 2)
                XX = Rot(st, "XX", [128, 8, 512], F32, 2)
                OG = Rot(st, "OG", [128, 8, 512], BF16, 2)
                SQ = Rot(st, "fsq", [128, 512], F32, 4)
                RS = Rot(st, "frs", [128, 512], F32, 4)
                for kc2 in range(4):
                    S.add("pool", lambda e, kc2=kc2: e.dma_start(out=W.t[:, 2 * kc2:2 * kc2 + 2, :], in_=dwout[i, :, 2 * kc2:2 * kc2 + 2, :],
                                                               max_dma_last_dim=4096), writes=[W.b], dma=True)
                S.add("sp", lambda e: e.dma_start(out=gn2.t[:], in_=dng), writes=[gn2.b], dma=True)
                for tt in range(NTOK // 512):
                    which = 0 if tt < 8 else 1
                    tok = slice(tt * 512, (tt + 1) * 512)
                    of_, ob_, zz, xx, og = OF.next(), OB.next(), ZZ.next(), XX.next(), OG.next()
                    S.add("sp", lambda e, of_=of_, tok=tok: e.dma_start(out=of_.t[:], in_=oT_d[0, :, :, tok]), writes=[of_.b], dma=True)
                    S.add("sp", lambda e, ob_=ob_, tok=tok: e.dma_start(out=ob_.t[:], in_=oT_d[1, :, :, tok]), writes=[ob_.b], dma=True)
                    S.add("sp", lambda e, zz=zz, tok=tok: e.dma_start(out=zz.t[:], in_=zT_d[:, :, tok]), writes=[zz.b], dma=True)
                    S.add("sp", lambda e, xx=xx, tok=tok: e.dma_start(out=xx.t[:], in_=xs[:, :, tok]), writes=[xx.b], dma=True)
                    S.add("pool", lambda e, of_=of_, ob_=ob_: e.tensor_tensor(out=of_.t[:], in0=of_.t[:], in1=ob_.t[:], op=ALU.add),
                          reads=[of_.b, ob_.b], writes=[of_.b])
                    S.add("act", lambda e, zz=zz: e.activation(out=zz.t[:], in_=zz.t[:], func=AF.Silu), reads=[zz.b], writes=[zz.b])
                    for h in range(8):
                        q_, r_ = SQ.next(), RS.next()
                        pb = PS[h % 4]
                        S.add("act", lambda e, q_=q_, of_=of_, h=h: e.activation(out=q_.t[:], in_=of_.t[:, h, :], func=AF.Square), reads=[of_.b], writes=[q_.b])
                        S.add("pe", lambda e, pb=pb, q_=q_: e.matmul(pb.t[:], lhsT=ones32.t[:], rhs=q_.t[:], start=True, stop=True),
                              reads=[ones32.b, q_.b], writes=[pb.b])
                        S.add("act", lambda e, pb=pb, r_=r_: e.activation(out=r_.t[:], in_=pb.t[:], func=AF.Sqrt, scale=1.0 / 128, bias=epsT.t[:, 0:1]),
                              reads=[pb.b, epsT.b], writes=[r_.b])
                        S.add("dve", lambda e, r_=r_: e.reciprocal(out=r_.t[:], in_=r_.t[:]), reads=[r_.b], writes=[r_.b])
                        S.add("dve", lambda e, r_=r_, of_=of_, h=h: e.scalar_tensor_tensor(out=r_.t[:], in0=of_.t[:, h, :], scalar=gn2.t[:, i:i + 1], in1=r_.t[:],
                                                                                      op0=ALU.mult, op1=ALU.mult), reads=[r_.b, of_.b, gn2.b], writes=[r_.b])
                        S.add("pool", lambda e, r_=r_, zz=zz, og=og, h=h: e.tensor_tensor(out=og.t[:, h, :], in0=r_.t[:], in1=zz.t[:, h, :], op=ALU.mult),
                              reads=[r_.b, zz.b], writes=[og.b])
                    for m in range(8):
                        pb = PS[4 + m % 4]
                        for f in range(8):
                            S.add("pe", lambda e, pb=pb, f=f, m=m, og=og: e.matmul(pb.t[:], lhsT=W.t[:, f, m * 128:(m + 1) * 128], rhs=og.t[:, f, :],
                                                                              start=(f == 0), stop=(f == 7)), reads=[W.b, og.b], writes=[pb.b])
                        S.add("dve", lambda e, pb=pb, m=m, xx=xx, which=which: e.scalar_tensor_tensor(
                            out=xx.t[:, m, :], in0=pb.t[:], scalar=hgT.t[:, 1, m, which:which + 1], in1=xx.t[:, m, :], op0=ALU.mult, op1=ALU.add),
                            reads=[pb.b, hgT.b, xx.b], writes=[xx.b])
                    S.add("sp", lambda e, xx=xx, tok=tok: e.dma_start(out=xs[:, :, tok], in_=xx.t[:]), reads=[xx.b], dma=True)
                S.end_stage(resched=True)

        def dn_mixer(l, i):
            nst_ = cfg.get("dn_stages", 5)
            for k_, fn in enumerate((dn_inproj, dn_conv, dn_chunks, dn_scan, dn_final)):
                if k_ < nst_:
                    fn(l, i)

        def ah_hyena_zero(l, i):
            with contextlib.ExitStack() as st:
                z = sb(st, "zz", [128, 4, 1024], BF16)
                S.add("dve", lambda e: e.memset(z.t[:], 0.0), writes=[z.b])
                for mt in range(NMT):
                    S.add("sp", lambda e, mt=mt: e.dma_start(out=ay_d[:, 4:8, mt * TM:(mt + 1) * TM], in_=z.t[:]), reads=[z.b], dma=True)
                S.end_stage()

        cur = xT
        layer_list = cfg.get("layers", None)
        layer_list = list(layer_list) if layer_list is not None else list(range(nlayers))
        do_ffn = cfg.get("ffn", True)

        def copy_stage(src, dst):
            for mt in range(NMT):
                S.add("sp", lambda e, mt=mt: e.dma_start(out=dst[:, :, mt * TM:(mt + 1) * TM], in_=src[:, :, mt * TM:(mt + 1) * TM]), dma=True)
            S.end_stage()

        for li, l in enumerate(layer_list):
            modulation_stage(l)
            if do_ffn:
                ffn_stage(l, 0, cur, xs)
            else:
                copy_stage(cur, xs)
            cur = xs
            if do_mix and l % 2 == 0:
                ah_inproj(l, l // 2)
                ah_attention(l, l // 2)
                if cfg.get("hyena", 1):
                    ah_hyena(l, l // 2)
                else:
                    ah_hyena_zero(l, l // 2)
                ah_outproj(l, l // 2)
            if do_mix and l % 2 == 1:
                dn_mixer(l, l // 2)
            last = (li == len(layer_list) - 1)
            if do_ffn:
                ffn_stage(l, 1, cur, yT if last else xs)
            elif last:
                copy_stage(xs, yT)
        print("stages", S.nstage, "ops", S.ninstr)
    return nc


def host_layout(inputs, core):
    f = lambda a: np.ascontiguousarray(a, dtype=np.float32)
    xs_ = np.asarray(inputs["x_sample"][core])
    xp_ = np.asarray(inputs["x_prompt"][4 * core:4 * core + 4]).reshape(1024, D)
    x = np.concatenate([xs_, xp_], axis=0)
    m = {}
    m["xT"] = f(x.T.reshape(8, 128, NTOK).transpose(1, 0, 2))
    cond = np.stack([np.asarray(inputs["c"][core]), np.asarray(inputs["c_ctx"])], axis=1)
    m["condT"] = f(cond.reshape(8, 128, 2).transpose(1, 0, 2))
    ck = np.asarray(inputs["cache_k"][core])
    ckT = ck.transpose(0, 3, 2, 1)
    m["ckT"] = f(np.concatenate([ckT, ckT], axis=1))
    cv = np.asarray(inputs["cache_v"][core]).reshape(2, 4, 128, 128)
    m["cvv"] = f(cv.transpose(0, 2, 1, 3))
    st_ = np.stack([np.asarray(inputs["state_fwd"][core]), np.asarray(inputs["state_bwd"][core])], axis=0)
    m["sf0"] = f(st_.transpose(0, 1, 3, 2, 4))
    return m


def shared_layout(inputs):
    f = lambda a: np.ascontiguousarray(a, dtype=np.float32)
    m = {}
    aw = np.asarray(inputs["ada_w"])
    m["ada_w"] = f(aw.reshape(DEPTH, 8, 128, 9, 1024).transpose(0, 3, 2, 1, 4))
    ab = np.asarray(inputs["ada_b"]).reshape(DEPTH, 72, 128).transpose(2, 0, 1)
    m["ada_b"] = f(np.repeat(ab[..., None], 2, axis=-1))
    g = np.asarray(inputs["norm_g"]).reshape(DEPTH, 3, 8, 128).transpose(3, 0, 1, 2)
    m["norm_g"] = f(np.repeat(g[..., None], 2, axis=-1))
    w13 = np.asarray(inputs["ffn_w13"]).reshape(DEPTH * 2, 8, 128, 2, NFC, 128)
    m["w13"] = f(w13.transpose(0, 4, 2, 1, 3, 5).reshape(DEPTH * 2, NFC, 128, 8, 256))
    w2 = np.asarray(inputs["ffn_w2"]).reshape(DEPTH * 2, NFC, 128, 8, 128)
    m["w2"] = f(w2.transpose(0, 3, 2, 1, 4))
    wi = np.asarray(inputs["mx_w_in"])
    wcat = np.concatenate([wi[:, :, 0:512], wi[:, :, 512:576], wi[:, :, 512:576], wi[:, :, 576:640], wi[:, :, 576:640],
                           wi[:, :, 768:2304], wi[:, :, 640:768]], axis=2)
    m["win"] = f(wcat.reshape(2, 8, 128, 2432).transpose(0, 2, 1, 3))
    wo = np.asarray(inputs["mx_w_out"])
    m["wout"] = f(wo.reshape(2, 8, 128, 1024).transpose(0, 2, 1, 3))
    qn = np.asarray(inputs["q_norm"])
    kn = np.asarray(inputs["k_norm"])
    pidx = np.arange(128)
    m["qkn"] = f(np.stack([qn[:, pidx % 64].T, kn[:, pidx % 64].T], axis=-1))
    sk = np.asarray(inputs["attn_sink"])
    hidx = 2 * np.arange(4)[None, :] + (pidx // 64)[:, None]
    m["sinkT"] = f(sk[:, hidx].transpose(1, 0, 2))
    cw = np.asarray(inputs["hy_conv_w"]).reshape(2, 3, 12, 128)
    m["hcw"] = f(cw.transpose(3, 0, 2, 1))
    cb = np.asarray(inputs["hy_conv_b"]).reshape(2, 12, 128)
    m["hcb"] = f(cb.transpose(2, 0, 1))
    m["hw1"] = f(inputs["hy_w1"])
    m["hb1"] = f(np.stack([np.asarray(inputs["hy_b1"]).T, np.asarray(inputs["hy_freq1"]).T], axis=-1))
    m["hw2"] = f(inputs["hy_w2"])
    m["hb2"] = f(np.stack([np.asarray(inputs["hy_b2"]).T, np.asarray(inputs["hy_freq2"]).T], axis=-1))
    m["hw3"] = f(inputs["hy_w3"])
    hbz = np.asarray(inputs["hy_bias"]).reshape(2, 2, 4, 128)
    m["hbias"] = f(hbz.transpose(3, 0, 1, 2))
    dwi = np.asarray(inputs["dn_w_in"])
    m["dwin"] = f(dwi.reshape(2, 8, 128, 4128).transpose(0, 2, 1, 3))
    dwo = np.asarray(inputs["dn_w_out"])
    m["dwout"] = f(dwo.reshape(2, 8, 128, 1024).transpose(0, 2, 1, 3))
    dc = np.asarray(inputs["dn_conv_w"]).reshape(2, 3, 24, 128)
    m["dcw"] = f(dc.transpose(3, 0, 2, 1))
    prm = np.concatenate([np.asarray(inputs["dn_a_log"]).reshape(2, 16), np.asarray(inputs["dn_dt_bias"]).reshape(2, 16)], axis=1)
    m["dprm"] = f(np.broadcast_to(prm[None], (128, 2, 32)))
    m["dng"] = f(np.asarray(inputs["dn_norm_g"]).T)
    m.update(const_tables())
    return m


_CONST = {}


def const_tables():
    if _CONST:
        return _CONST
    f = lambda a: np.ascontiguousarray(a, dtype=np.float32)
    pidx = np.arange(128)
    a = (pidx % 64) % 32
    inv = (np.float32(10000.0) ** (-np.arange(0, 32, 2, dtype=np.float32) / np.float32(32))).astype(np.float32)
    t = np.arange(4096)
    r = (t // 64).astype(np.float32)
    col = (t % 64).astype(np.float32)
    ang = np.where((a < 16)[:, None], r[None, :] * inv[a % 16][:, None], col[None, :] * inv[a % 16][:, None]).astype(np.float32)
    _CONST["cosT"] = f(np.cos(ang))
    _CONST["sinT"] = f(np.sin(ang))
    rot = np.zeros((128, 128), np.float32)
    for d_out in range(128):
        dd = d_out % 64
        base = d_out - dd
        if dd < 32:
            rot[base + dd + 32, d_out] = -1.0
        else:
            rot[base + dd - 32, d_out] = 1.0
    _CONST["rotT"] = rot
    blk = np.zeros((128, 128), np.float32)
    blk[:64, :64] = 1.0
    blk[64:, 64:] = 1.0
    _CONST["blk1"] = blk
    ko = np.arange(128)[:, None]
    qo = np.arange(128)[None, :]
    _CONST["mprev"] = f(ko >= qo)
    _CONST["mnext"] = f(ko <= qo)
    _CONST["ident"] = np.eye(128, dtype=np.float32)
    pp = np.arange(64)[:, None]
    ff = np.arange(64)[None, :]
    _CONST["dmask"] = f(np.stack([pp <= ff, pp >= ff, pp < ff, pp > ff, pp == ff], axis=1))
    lvm = []
    for lv in range(6):
        b_ = 2 ** lv
        mn = (pp // (2 * b_) == ff // (2 * b_)) & (pp % (2 * b_) >= b_) & (ff % (2 * b_) < b_)
        lvm.append(mn)
        lvm.append(mn.T)
    _CONST["lvmask"] = f(np.stack(lvm, axis=1))
    HY_MIN = math.log(1e-2) / 1.5
    HY_MAX = math.log(1e-2) / 0.3
    dl = np.abs(np.linspace(HY_MIN, HY_MAX, 2048, dtype=np.float32)).astype(np.float32)
    _CONST["deltas"] = f(np.broadcast_to(dl[None, :], (128, 2048)))
    for L in (4096, 256):
        N = 2 * L
        nb = L // 128
        TT = min(512, L)
        k = np.arange(L, dtype=np.int64)
        t = np.arange(L, dtype=np.int64)
        mm = ((2 * k[:, None] + 1) * t[None, :]) % (2 * N)
        ang = mm.astype(np.float64) * (np.pi / N)
        Ckt = np.cos(ang).astype(np.float32)
        Skt = np.sin(ang).astype(np.float32)
        del mm, ang
        F = np.empty((2, nb, 128, nb, 128), ml_dtypes.bfloat16)
        I = np.empty((2, L // TT, 128, nb, TT), ml_dtypes.bfloat16)
        for ci, tab in enumerate((Ckt, Skt)):
            t4 = tab.reshape(nb, 128, nb, 128)
            F[ci] = t4.transpose(0, 3, 2, 1).astype(ml_dtypes.bfloat16)
            t5 = tab.reshape(nb, 128, L // TT, TT)
            I[ci] = t5.transpose(2, 1, 0, 3).astype(ml_dtypes.bfloat16)
        _CONST["dftF%d" % L] = F
        _CONST["dftI%d" % L] = I
        tt_ = np.linspace(0.0, 1.0, L, dtype=np.float32)
        w = (np.float32(2 * math.pi) * np.arange(L, dtype=np.float32) / np.float32(L)).astype(np.float32)
        fr = np.linspace(1e-4, 15.0, 16, dtype=np.float32)
        fw = (fr[None, :] * w[:, None]).astype(np.float32)
        z = np.concatenate([tt_[:, None], np.cos(fw), -np.sin(fw)], axis=-1).astype(np.float32)
        _CONST["zfeat%d" % L] = f(z.T)
        _CONST["tlag%d" % L] = f(-tt_.reshape(nb, 128).T)
    return _CONST


_CACHE = {}


def run(inputs, cfg, ncores=8, cores=None):
    key = tuple(sorted(cfg.items()))
    if key not in _CACHE:
        _CACHE[key] = build_program(cfg)
    nc = _CACHE[key]
    shared = shared_layout(inputs)
    in_maps = []
    for core in (cores if cores is not None else range(ncores)):
        m = dict(shared)
        m.update(host_layout(inputs, core))
        in_maps.append(m)
    res = run_bass_kernel_spmd(nc, in_maps, core_ids=list(range(ncores)))
    return res


def assemble(res, ncores=8):
    yp = np.zeros((32, 256, D), np.float32)
    ys = np.zeros((8, 4096, D), np.float32)
    nk = np.zeros((32, 2, 256, 2, 64), np.float32)
    nv = np.zeros((32, 2, 256, 2, 64), np.float32)
    nsf = np.zeros((32, 2, 8, 128, 128), np.float32)
    nsb = np.zeros((32, 2, 8, 128, 128), np.float32)
    for core in range(ncores):
        r = res.results[core]
        y = r["yT"].transpose(1, 0, 2).reshape(D, NTOK).T
        ys[core] = y[:4096]
        yp[4 * core:4 * core + 4] = y[4096:].reshape(4, 256, D)
        k = r["newk"].reshape(2, 2, 64, 4, 256)
        nk[4 * core:4 * core + 4] = k.transpose(3, 0, 4, 1, 2)
        v = r["newv"].reshape(2, 4, 256, 2, 64)
        nv[4 * core:4 * core + 4] = v.transpose(1, 0, 2, 3, 4)
        s_ = r["nst"]
        nsf[4 * core:4 * core + 4] = s_[0].transpose(1, 0, 3, 2, 4)
        nsb[4 * core:4 * core + 4] = s_[1].transpose(1, 0, 3, 2, 4)
    return yp, ys, nk, nv, nsf, nsb


def kernel(**inputs):
    res = run(inputs, {})
    return assemble(res)
```

```python
import contextlib
import math
import numpy as np
import ml_dtypes
import concourse.bass as bass
import concourse.mybir as mybir
from concourse.bass_utils import run_bass_kernel_spmd

F32 = mybir.dt.float32
BF16 = mybir.dt.bfloat16
AF = mybir.ActivationFunctionType
ALU = mybir.AluOpType

D = 1024
NTOK = 5120
TM = 1024
NMT = NTOK // TM
DFF = 2816
NFC = DFF // 128
DEPTH = 4
EPS = 1e-6

ENGS = ("pe", "dve", "act", "pool", "sp")
NDMASEM = 8


class Buf:
    __slots__ = ("name", "w", "rs", "excl")

    def __init__(self, name="", excl=False):
        self.name = name
        self.w = None
        self.rs = []
        self.excl = excl


class Op:
    __slots__ = ("eng", "fn", "deps", "dma", "sig", "cnt", "semi", "use", "gid", "cost")


class Sched:
    def __init__(self, nc, st):
        self.nc = nc
        self.csem = {e: st.enter_context(nc.semaphore("c_" + e)) for e in ENGS}
        self.dsem = {e: [st.enter_context(nc.semaphore("d_%s%d" % (e, i))) for i in range(NDMASEM)]
                     for e in ("sp", "pool", "act")}
        self.cnt = {e: 0 for e in ENGS}
        self.ndma = {e: 0 for e in ENGS}
        self.ops = []
        self.bufs = []
        self.nstage = 0
        self.ninstr = 0
        self.xlat = 2.0

    def buf(self, name=""):
        return Buf(name)

    COST = {"pe": 0.12, "dve": 0.6, "act": 0.7, "pool": 0.9, "sp": 2.5}

    def add(self, eng, fn, reads=(), writes=(), dma=False, cost=None):
        op = Op()
        op.cost = cost if cost is not None else (2.5 if dma else self.COST[eng])
        op.eng = eng
        op.fn = fn
        op.dma = dma
        op.sig = False
        op.gid = len(self.ops)
        deps = set()
        ex = [b for b in reads if b.excl]
        if ex:
            reads = [b for b in reads if not b.excl]
            writes = list(writes) + [b for b in ex if b not in writes]
        for b in reads:
            if b.w is not None:
                deps.add(b.w)
        for b in writes:
            if b.w is not None:
                deps.add(b.w)
            deps.update(b.rs)
        deps.discard(op.gid)
        op.deps = deps
        for b in reads:
            b.rs.append(op.gid)
            self.bufs.append(b)
        for b in writes:
            b.w = op.gid
            b.rs = []
            self.bufs.append(b)
        self.ops.append(op)
        return op

    def _resched(self, ops):
        import heapq
        n = len(ops)
        succ = [[] for _ in range(n)]
        indeg = [0] * n
        for op in ops:
            indeg[op.gid] = len(op.deps)
            for d in op.deps:
                succ[d].append(op.gid)
        ready_t = [0.0] * n
        fin = [0.0] * n
        heaps = {e: [] for e in ENGS}
        free = {e: 0.0 for e in ENGS}
        for op in ops:
            if indeg[op.gid] == 0:
                heapq.heappush(heaps[op.eng], (0.0, op.gid))
        order = []
        while len(order) < n:
            best = None
            for e in ENGS:
                h = heaps[e]
                if not h:
                    continue
                rt, gid = h[0]
                st_ = max(free[e], rt)
                if best is None or (st_, gid) < (best[0], best[1]):
                    best = (st_, gid, e)
            st_, gid, e = best
            heapq.heappop(heaps[e])
            op = ops[gid]
            if op.dma:
                free[e] = st_ + 0.15
                fin[gid] = st_ + op.cost
            else:
                free[e] = st_ + op.cost
                fin[gid] = st_ + op.cost
            order.append(gid)
            for s_ in succ[gid]:
                so = ops[s_]
                lat = 0.05 if (so.eng == op.eng and not op.dma) else self.xlat
                ready_t[s_] = max(ready_t[s_], fin[gid] + lat)
                indeg[s_] -= 1
                if indeg[s_] == 0:
                    heapq.heappush(heaps[so.eng], (ready_t[s_], s_))
        return order

    def end_stage(self, resched=False):
        nc = self.nc
        ops = self.ops
        if resched and len(ops) > 2:
            order = self._resched(ops)
            remap = {g: k for k, g in enumerate(order)}
            ops = [ops[g] for g in order]
            for k, op in enumerate(ops):
                op.gid = k
                op.deps = {remap[d] for d in op.deps}
        for op in ops:
            if op.dma:
                k = self.ndma[op.eng]
                self.ndma[op.eng] = k + 1
                op.semi = k % NDMASEM
                op.use = k // NDMASEM + 1
        for op in ops:
            nd = set()
            for d in op.deps:
                p = ops[d]
                if (not p.dma) and (not op.dma) and p.eng == "pe" and op.eng == "pe":
                    continue
                nd.add(d)
                p.sig = True
            op.deps = nd
        for op in ops:
            if (not op.dma) and op.sig:
                self.cnt[op.eng] += 1
                op.cnt = self.cnt[op.eng]
        per = {e: [op for op in ops if op.eng == e] for e in ENGS}
        csem, dsem = self.csem, self.dsem
        self.ninstr += len(ops)
        with nc.Block() as block:
            def gen(ename):
                def body(eng):
                    seen_c = {}
                    seen_d = {}
                    for op in per[ename]:
                        need_c = {}
                        need_d = {}
                        for d in op.deps:
                            p = ops[d]
                            if p.dma:
                                key = (p.eng, p.semi)
                                need_d[key] = max(need_d.get(key, 0), 16 * p.use)
                            else:
                                need_c[p.eng] = max(need_c.get(p.eng, 0), p.cnt)
                        if op.dma and op.use > 1:
                            key = (op.eng, op.semi)
                            need_d[key] = max(need_d.get(key, 0), 16 * (op.use - 1))
                        for e, v in need_c.items():
                            if v > seen_c.get(e, 0):
                                eng.wait_ge(csem[e], v)
                                seen_c[e] = v
                        for key, v in need_d.items():
                            if v > seen_d.get(key, 0):
                                eng.wait_ge(dsem[key[0]][key[1]], v)
                                seen_d[key] = v
                        ins = op.fn(eng)
                        if op.dma:
                            ins.then_inc(dsem[op.eng][op.semi], 16)
                        elif op.sig:
                            ins.then_inc(csem[op.eng], 1)
                    last = {}
                    for op in per[ename]:
                        if op.dma:
                            last[op.semi] = op.use
                    for semi, use in last.items():
                        eng.wait_ge(dsem[ename][semi], 16 * use)
                return body

            block.tensor(gen("pe"))
            block.vector(gen("dve"))
            block.scalar(gen("act"))
            block.gpsimd(gen("pool"))
            block.sync(gen("sp"))
        for b in self.bufs:
            b.w = None
            b.rs = []
        self.bufs = []
        self.ops = []
        self.nstage += 1


class T:
    def __init__(self, t, name=""):
        self.t = t
        self.b = Buf(name)


def build_program(cfg):
    nlayers = cfg.get("nlayers", DEPTH)
    do_mix = cfg.get("mixers", True)
    nc = bass.Bass("TRN2", target_bir_lowering=False)

    def din(name, shape, dt=F32):
        return nc.dram_tensor(name, list(shape), dt, kind="ExternalInput").ap()

    def dout(name, shape, dt=F32):
        return nc.dram_tensor(name, list(shape), dt, kind="ExternalOutput").ap()

    dbg = cfg.get("debug", ())

    def dscr(name, shape, dt=F32):
        kind = "ExternalOutput" if name in dbg else "Internal"
        return nc.dram_tensor(name, list(shape), dt, kind=kind).ap()

    xT = din("xT", [128, 8, NTOK])
    condT = din("condT", [128, 8, 2])
    ada_w = din("ada_w", [DEPTH, 9, 128, 8, 1024])
    ada_b = din("ada_b", [128, DEPTH, 72, 2])
    norm_g = din("norm_g", [128, DEPTH, 3, 8, 2])
    w13 = din("w13", [DEPTH * 2, NFC, 128, 8, 256])
    w2 = din("w2", [DEPTH * 2, 8, 128, NFC, 128])
    yT = dout("yT", [128, 8, NTOK])
    xs = dscr("xs", [128, 8, NTOK])
    win = din("win", [2, 128, 8, 2432])
    wout = din("wout", [2, 128, 8, 1024])
    qkn = din("qkn", [128, 2, 2])
    sinkT = din("sinkT", [128, 2, 4])
    cosT_d = din("cosT", [128, 4096])
    sinT_d = din("sinT", [128, 4096])
    rotT_d = din("rotT", [128, 128])
    blk1_d = din("blk1", [128, 128])
    mprev_d = din("mprev", [128, 128])
    mnext_d = din("mnext", [128, 128])
    ckT = din("ckT", [2, 128, 2, 512])
    cvv = din("cvv", [2, 128, 4, 128])
    newk = dout("newk", [2, 2, 64, 1024])
    newv = dout("newv", [2, 1024, 128])
    qT_d = dscr("qT_d", [128, 4, NTOK])
    kT_d = dscr("kT_d", [128, 2, NTOK])
    u3T_d = dscr("u3T_d", [128, 12, NTOK])
    v_d = dscr("v_d", [NTOK, 128])
    ay_d = dscr("ay_d", [128, 8, NTOK], BF16)
    hcw = din("hcw", [128, 2, 12, 3])
    hcb = din("hcb", [128, 2, 12])
    hw1 = din("hw1", [2, 33, 64])
    hb1 = din("hb1", [64, 2, 2])
    hw2 = din("hw2", [2, 64, 64])
    hb2 = din("hb2", [64, 2, 2])
    hw3 = din("hw3", [2, 64, 2048])
    hbias = din("hbias", [128, 2, 2, 4])
    ident_d = din("ident", [128, 128])
    deltas_d = din("deltas", [128, 2048])
    HG = {}
    for L_ in (4096, 256):
        nb_ = L_ // 128
        TT_ = min(512, L_)
        HG[L_] = dict(
            F=din("dftF%d" % L_, [2, nb_, 128, nb_, 128], BF16),
            I=din("dftI%d" % L_, [2, L_ // TT_, 128, nb_, TT_], BF16),
            zf=din("zfeat%d" % L_, [33, L_]),
            tl=din("tlag%d" % L_, [128, nb_]))
    uc_d = dscr("uc_d", [128, 12, NTOK])
    z1T_d = dscr("z1T_d", [128, 4, NTOK])
    hsd_d = dscr("hsd_d", [2, 32, 128, 1024], BF16)
    H_d = dscr("H_d", [2, 2, 32, 128, 512])
    dwin = din("dwin", [2, 128, 8, 4128])
    dwout = din("dwout", [2, 128, 8, 1024])
    dcw = din("dcw", [128, 2, 24, 3])
    dprm = din("dprm", [128, 2, 32])
    dng = din("dng", [128, 2])
    dmask_d = din("dmask", [64, 5, 64])
    lvmask_d = din("lvmask", [64, 12, 64])
    sf0 = din("sf0", [2, 2, 128, 8, 128])
    nst = dout("nst", [2, 2, 4, 128, 8, 128])
    qkvT_d = dscr("qkvT_d", [128, 24, NTOK])
    qkvn_d = dscr("qkvn_d", [128, 24, NTOK])
    zT_d = dscr("zT_d", [128, 8, NTOK])
    ba_d = dscr("ba_d", [NTOK, 32])
    NCH = NTOK // 64
    u_d = dscr("u_d", [2, NCH, 64, 8, 128])
    wT_d = dscr("wT_d", [2, NCH, 128, 8, 64], BF16)
    QgT_d = dscr("QgT_d", [2, NCH, 128, 8, 64], BF16)
    AT_d = dscr("AT_d", [2, NCH, 64, 8, 64], BF16)
    Kd_d = dscr("Kd_d", [2, NCH, 64, 8, 128], BF16)
    egl_d = dscr("egl_d", [2, NCH, 128, 8])
    oT_d = dscr("oT_d", [2, 128, 8, NTOK])

    with contextlib.ExitStack() as top:
        S = Sched(nc, top)
        S.xlat = cfg.get("xlat", 2.0)

        uid = [0]

        def sb(st, name, shape, dt=F32):
            uid[0] += 1
            name = "%s_%d" % (name, uid[0])
            return T(st.enter_context(nc.sbuf_tensor(name, list(shape), dt)), name)

        PS = [T(top.enter_context(nc.psum_tensor("ps%d" % i, [128, 512], F32)), "ps%d" % i) for i in range(8)]
        for p_ in PS:
            p_.b.excl = True
        ones32 = sb(top, "ones32", [128, 128])
        epsT = sb(top, "epsT", [128, 1])
        scond = sb(top, "scond", [128, 8, 2], BF16)
        modT = sb(top, "modT", [128, 72, 2])
        gsT = sb(top, "gsT", [128, 3, 8, 2])
        hgT = sb(top, "hgT", [128, 3, 8, 2])
        adab = sb(top, "adab", [128, DEPTH, 72, 2])
        ng = sb(top, "ng", [128, DEPTH, 3, 8, 2])

        with contextlib.ExitStack() as st:
            cnd = sb(st, "cnd", [128, 8, 2])
            S.add("dve", lambda e: e.memset(ones32.t[:], 1.0), writes=[ones32.b])
            S.add("dve", lambda e: e.memset(epsT.t[:], EPS), writes=[epsT.b])
            S.add("sp", lambda e: e.dma_start(out=cnd.t[:], in_=condT), writes=[cnd.b], dma=True)
            S.add("sp", lambda e: e.dma_start(out=adab.t[:], in_=ada_b), writes=[adab.b], dma=True)
            S.add("sp", lambda e: e.dma_start(out=ng.t[:], in_=norm_g), writes=[ng.b], dma=True)
            S.add("act", lambda e: e.activation(out=scond.t[:], in_=cnd.t[:], func=AF.Silu),
                  reads=[cnd.b], writes=[scond.b])
            S.end_stage()

        def modulation_stage(l):
            with contextlib.ExitStack() as st:
                wa = [sb(st, "wa%d" % i, [128, 8, 1024], BF16) for i in range(2)]
                for blk in range(9):
                    w = wa[blk % 2]
                    S.add("pool", lambda e, w=w, blk=blk: e.dma_start(out=w.t[:], in_=ada_w[l, blk]),
                          writes=[w.b], dma=True)
                    ps = PS[blk % 2]
                    for cc in range(8):
                        for kc in range(8):
                            S.add("pe", lambda e, w=w, ps=ps, cc=cc, kc=kc: e.matmul(
                                ps.t[:, cc * 2:cc * 2 + 2], lhsT=w.t[:, kc, cc * 128:(cc + 1) * 128],
                                rhs=scond.t[:, kc, :], start=(kc == 0), stop=(kc == 7)),
                                reads=[w.b, scond.b], writes=[ps.b])
                    S.add("dve", lambda e, ps=ps, blk=blk: e.tensor_tensor(
                        out=modT.t[:, blk * 8:(blk + 1) * 8, :],
                        in0=ps.t[:, 0:16].rearrange("p (a b) -> p a b", b=2),
                        in1=adab.t[:, l, blk * 8:(blk + 1) * 8, :], op=ALU.add),
                        reads=[ps.b, adab.b], writes=[modT.b])
                for j in range(3):
                    S.add("dve", lambda e, j=j: e.scalar_tensor_tensor(
                        out=gsT.t[:, j], in0=modT.t[:, (3 * j + 1) * 8:(3 * j + 2) * 8, :], scalar=1.0,
                        in1=ng.t[:, l, j], op0=ALU.add, op1=ALU.mult),
                        reads=[modT.b, ng.b], writes=[gsT.b])
                    S.add("dve", lambda e, j=j: e.tensor_scalar(
                        out=hgT.t[:, j], in0=modT.t[:, (3 * j + 2) * 8:(3 * j + 3) * 8, :],
                        scalar1=(1.0 if j == 1 else 0.5), scalar2=None, op0=ALU.mult),
                        reads=[modT.b], writes=[hgT.b])
                S.end_stage(resched=cfg.get("rs_x", True))

        def norm_mod(st_tiles, xb, hb, j, which, ps_pair):
            sq, rstd, tmp = st_tiles
            for c in range(8):
                q = sq[c % 2]
                S.add("act", lambda e, q=q, c=c: e.activation(out=q.t[:], in_=xb.t[:, c, :], func=AF.Square),
                      reads=[xb.b], writes=[q.b])
                for s in range(TM // 512):
                    S.add("pe", lambda e, q=q, c=c, s=s: e.matmul(
                        ps_pair[s].t[:], lhsT=ones32.t[:], rhs=q.t[:, s * 512:(s + 1) * 512],
                        start=(c == 0), stop=(c == 7)), reads=[q.b, ones32.b], writes=[ps_pair[s].b])
            for s in range(TM // 512):
                S.add("act", lambda e, s=s: e.activation(
                    out=rstd.t[:, s * 512:(s + 1) * 512], in_=ps_pair[s].t[:], func=AF.Sqrt,
                    scale=1.0 / D, bias=epsT.t[:, 0:1]), reads=[ps_pair[s].b, epsT.b], writes=[rstd.b])
            S.add("dve", lambda e: e.reciprocal(out=rstd.t[:], in_=rstd.t[:]), reads=[rstd.b], writes=[rstd.b])
            for c in range(8):
                tp = tmp[c % 2]
                S.add("dve", lambda e, tp=tp, c=c: e.tensor_tensor(
                    out=tp.t[:], in0=xb.t[:, c, :], in1=rstd.t[:], op=ALU.mult),
                    reads=[xb.b, rstd.b], writes=[tp.b])
                S.add("act", lambda e, tp=tp, c=c: e.activation(
                    out=hb.t[:, c, :], in_=tp.t[:], func=AF.Identity,
                    scale=gsT.t[:, j, c, which:which + 1], bias=modT.t[:, 3 * j * 8 + c, which:which + 1]),
                    reads=[tp.b, gsT.b, modT.b], writes=[hb.b])

        def ffn_stage(l, hf, src, dst):
            j = 0 if hf == 0 else 2
            lh = l * 2 + hf
            with contextlib.ExitStack() as st:
                X = [sb(st, "x%d" % i, [128, 8, TM]) for i in range(2)]
                Hh = [sb(st, "h%d" % i, [128, 8, TM], BF16) for i in range(2)]
                sq = [sb(st, "sq%d" % i, [128, TM]) for i in range(2)]
                tmp = [sb(st, "tmp%d" % i, [128, TM]) for i in range(2)]
                rstd = sb(st, "rstd", [128, TM])
                actT = sb(st, "actT", [128, NFC, TM], BF16)
                sg = [sb(st, "sg%d" % i, [128, 512]) for i in range(2)]
                wp = [sb(st, "wp%d" % i, [128, 8, 256], BF16) for i in range(4)]
                w2t = [sb(st, "w2t%d" % i, [128, NFC, 128], BF16) for i in range(2)]
                dsrc = [Buf() for _ in range(NMT)]
                wctr = [0, 0]

                def load_x(mt):
                    xb = X[mt % 2]
                    S.add("sp", lambda e: e.dma_start(out=xb.t[:], in_=src[:, :, mt * TM:(mt + 1) * TM]),
                          reads=[dsrc[mt]], writes=[xb.b], dma=True)

                def stage_a(mt):
                    which = 0 if mt < 4 else 1
                    norm_mod((sq, rstd, tmp), X[mt % 2], Hh[mt % 2], j, which, (PS[4], PS[5]))

                load_x(0)
                stage_a(0)
                for mt in range(NMT):
                    which = 0 if mt < 4 else 1
                    xb = X[mt % 2]
                    hb = Hh[mt % 2]
                    if mt + 1 < NMT:
                        load_x(mt + 1)
                    for jp in range(NFC):
                        w = wp[wctr[0] % 4]
                        wctr[0] += 1
                        S.add("pool", lambda e, w=w, jp=jp: e.dma_start(out=w.t[:], in_=w13[lh, jp]),
                              writes=[w.b], dma=True)
                        for s in range(TM // 512):
                            k = jp * 2 + s
                            pg, pu = PS[k % 2], PS[2 + k % 2]
                            for half, pb in ((0, pg), (1, pu)):
                                for c in range(8):
                                    S.add("pe", lambda e, w=w, pb=pb, c=c, s=s, half=half, hb=hb: e.matmul(
                                        pb.t[:], lhsT=w.t[:, c, half * 128:(half + 1) * 128],
                                        rhs=hb.t[:, c, s * 512:(s + 1) * 512], start=(c == 0), stop=(c == 7)),
                                        reads=[w.b, hb.b], writes=[pb.b])
                            g = sg[k % 2]
                            S.add("act", lambda e, g=g, pg=pg: e.activation(out=g.t[:], in_=pg.t[:], func=AF.Silu),
                                  reads=[pg.b], writes=[g.b])
                            S.add("dve", lambda e, g=g, pu=pu, jp=jp, s=s: e.tensor_tensor(
                                out=actT.t[:, jp, s * 512:(s + 1) * 512], in0=g.t[:], in1=pu.t[:], op=ALU.mult),
                                reads=[g.b, pu.b], writes=[actT.b])
                        if jp == 11 and mt + 1 < NMT:
                            stage_a(mt + 1)
                    for m in range(8):
                        w = w2t[wctr[1] % 2]
                        wctr[1] += 1
                        S.add("pool", lambda e, w=w, m=m: e.dma_start(out=w.t[:], in_=w2[lh, m], max_dma_last_dim=4096),
                              writes=[w.b], dma=True)
                        for s in range(TM // 512):
                            pb = PS[6 + (m * 2 + s) % 2]
                            for f in range(NFC):
                                S.add("pe", lambda e, w=w, pb=pb, f=f, s=s: e.matmul(
                                    pb.t[:], lhsT=w.t[:, f, :], rhs=actT.t[:, f, s * 512:(s + 1) * 512],
                                    start=(f == 0), stop=(f == NFC - 1)), reads=[w.b, actT.b], writes=[pb.b])
                            S.add("dve", lambda e, pb=pb, m=m, s=s, xb=xb, which=which: e.scalar_tensor_tensor(
                                out=xb.t[:, m, s * 512:(s + 1) * 512], in0=pb.t[:],
                                scalar=hgT.t[:, j, m, which:which + 1], in1=xb.t[:, m, s * 512:(s + 1) * 512],
                                op0=ALU.mult, op1=ALU.add), reads=[pb.b, hgT.b, xb.b], writes=[xb.b])
                    S.add("sp", lambda e, xb=xb, mt=mt: e.dma_start(out=dst[:, :, mt * TM:(mt + 1) * TM], in_=xb.t[:]),
                          reads=[xb.b], writes=[dsrc[mt]], dma=True)
                S.end_stage(resched=cfg.get("rs_ffn", False))

        def ah_inproj(l, i):
            with contextlib.ExitStack() as st:
                X = [sb(st, "x", [128, 8, TM]) for _ in range(2)]
                Hh = [sb(st, "h", [128, 8, TM], BF16) for _ in range(2)]
                sq = [sb(st, "sq", [128, TM]) for _ in range(2)]
                tmp = [sb(st, "tmp", [128, TM]) for _ in range(2)]
                rstd = sb(st, "rstd", [128, TM])
                W = sb(st, "win", [128, 8, 2432], BF16)
                stg = [sb(st, "stg", [128, 512]) for _ in range(4)]
                vst = [sb(st, "vst", [128, 8, 128]) for _ in range(2)]
                for kc2 in range(4):
                    S.add("pool", lambda e, kc2=kc2: e.dma_start(out=W.t[:, 2 * kc2:2 * kc2 + 2, :], in_=win[i, :, 2 * kc2:2 * kc2 + 2, :],
                                                               max_dma_last_dim=4096), writes=[W.b], dma=True)

                def load_x(mt):
                    xb = X[mt % 2]
                    S.add("sp", lambda e: e.dma_start(out=xb.t[:], in_=xs[:, :, mt * TM:(mt + 1) * TM]),
                          writes=[xb.b], dma=True)

                def stage_a(mt):
                    which = 0 if mt < 4 else 1
                    norm_mod((sq, rstd, tmp), X[mt % 2], Hh[mt % 2], 1, which, (PS[4], PS[5]))

                load_x(0)
                stage_a(0)
                ctr = [0]
                for mt in range(NMT):
                    hb = Hh[mt % 2]
                    if mt + 1 < NMT:
                        load_x(mt + 1)
                    for oc in range(18):
                        if oc < 4:
                            dst, ch = qT_d, oc
                        elif oc < 6:
                            dst, ch = kT_d, oc - 4
                        else:
                            dst, ch = u3T_d, oc - 6
                        for s_ in range(TM // 512):
                            k = ctr[0]
                            ctr[0] += 1
                            pb = PS[k % 4]
                            sg_ = stg[k % 4]
                            for c in range(8):
                                S.add("pe", lambda e, pb=pb, c=c, s_=s_, oc=oc, hb=hb: e.matmul(
                                    pb.t[:], lhsT=W.t[:, c, oc * 128:(oc + 1) * 128],
                                    rhs=hb.t[:, c, s_ * 512:(s_ + 1) * 512], start=(c == 0), stop=(c == 7)),
                                    reads=[W.b, hb.b], writes=[pb.b])
                            if k % 2 == 0:
                                S.add("act", lambda e, pb=pb, sg_=sg_: e.activation(out=sg_.t[:], in_=pb.t[:], func=AF.Identity),
                                      reads=[pb.b], writes=[sg_.b])
                            else:
                                S.add("dve", lambda e, pb=pb, sg_=sg_: e.tensor_copy(out=sg_.t[:], in_=pb.t[:]),
                                      reads=[pb.b], writes=[sg_.b])
                            t0 = mt * TM + s_ * 512
                            S.add("sp", lambda e, dst=dst, ch=ch, t0=t0, sg_=sg_: e.dma_start(
                                out=dst[:, ch, t0:t0 + 512], in_=sg_.t[:]), reads=[sg_.b], dma=True)
                        if oc == 9 and mt + 1 < NMT:
                            stage_a(mt + 1)
                    vs_ = vst[mt % 2]
                    for tb in range(TM // 128):
                        pb = PS[6 + tb % 2]
                        for c in range(8):
                            S.add("pe", lambda e, pb=pb, c=c, tb=tb, hb=hb: e.matmul(
                                pb.t[:, 0:128], lhsT=hb.t[:, c, tb * 128:(tb + 1) * 128], rhs=W.t[:, c, 2304:2432],
                                start=(c == 0), stop=(c == 7)), reads=[W.b, hb.b], writes=[pb.b])
                        S.add("dve", lambda e, pb=pb, tb=tb, vs_=vs_: e.tensor_copy(out=vs_.t[:, tb, :], in_=pb.t[:, 0:128]),
                              reads=[pb.b], writes=[vs_.b])
                    S.add("sp", lambda e, mt=mt, vs_=vs_: e.dma_start(
                        out=v_d[mt * TM:(mt + 1) * TM, :].rearrange("(tb p) n -> p tb n", p=128), in_=vs_.t[:]),
                        reads=[vs_.b], dma=True)
                S.end_stage(resched=cfg.get("rs_x", True))

        def ah_attention(l, i):
            with contextlib.ExitStack() as st:
                cosT = sb(st, "cosT", [128, 4096])
                sinT = sb(st, "sinT", [128, 4096])
                rotT = sb(st, "rotT", [128, 128])
                blk1 = sb(st, "blk1", [128, 128])
                mprev = sb(st, "mprev", [128, 128], BF16)
                mnext = sb(st, "mnext", [128, 128], BF16)
                onesb = sb(st, "onesb", [128, 64], BF16)
                gn = sb(st, "gn", [128, 2])
                gq8 = sb(st, "gq8", [128, 1])
                esink = sb(st, "esink", [128, 4])
                KT = sb(st, "KT", [128, 2, 4096], BF16)
                VV = sb(st, "VV", [128, 32, 128], BF16)
                CK = sb(st, "CK", [128, 2, 512], BF16)
                CV = sb(st, "CV", [128, 4, 128], BF16)
                kin = [sb(st, "kin", [128, 2, 512]) for _ in range(2)]
                qin = [sb(st, "qin", [128, 4, 512]) for _ in range(2)]
                qp = [sb(st, "qp", [128, 4, 512], BF16) for _ in range(2)]
                sqq = [sb(st, "sqq", [128, 512]) for _ in range(2)]
                rs_ = [sb(st, "rs", [128, 512]) for _ in range(2)]
                kg_ = [sb(st, "kg", [128, 512]) for _ in range(2)]
                t1_ = [sb(st, "t1", [128, 512]) for _ in range(2)]
                t2_ = [sb(st, "t2", [128, 512]) for _ in range(2)]
                pt = [sb(st, "pt", [128, 512], BF16) for _ in range(3)]
                den = [sb(st, "den", [128, 512]) for _ in range(2)]
                aout = [sb(st, "aout", [128, 4, 512], BF16) for _ in range(2)]
                kno = [sb(st, "kno", [128, 2, 256]) for _ in range(2)]

                S.add("sp", lambda e: e.dma_start(out=cosT.t[:], in_=cosT_d), writes=[cosT.b], dma=True)
                S.add("sp", lambda e: e.dma_start(out=sinT.t[:], in_=sinT_d), writes=[sinT.b], dma=True)
                S.add("sp", lambda e: e.dma_start(out=rotT.t[:], in_=rotT_d), writes=[rotT.b], dma=True)
                S.add("sp", lambda e: e.dma_start(out=blk1.t[:], in_=blk1_d), writes=[blk1.b], dma=True)
                S.add("pool", lambda e: e.dma_start(out=mprev.t[:], in_=mprev_d), writes=[mprev.b], dma=True)
                S.add("pool", lambda e: e.dma_start(out=mnext.t[:], in_=mnext_d), writes=[mnext.b], dma=True)
                S.add("sp", lambda e: e.dma_start(out=gn.t[:], in_=qkn[:, i, :]), writes=[gn.b], dma=True)
                S.add("sp", lambda e: e.dma_start(out=esink.t[:], in_=sinkT[:, i, :]), writes=[esink.b], dma=True)
                S.add("pool", lambda e: e.dma_start(out=CK.t[:], in_=ckT[i]), writes=[CK.b], dma=True)
                S.add("pool", lambda e: e.dma_start(out=CV.t[:], in_=cvv[i]), writes=[CV.b], dma=True)
                S.add("dve", lambda e: e.memset(onesb.t[:], 1.0), writes=[onesb.b])
                S.add("dve", lambda e: e.tensor_scalar(out=gq8.t[:], in0=gn.t[:, 0:1], scalar1=0.125, scalar2=None, op0=ALU.mult),
                      reads=[gn.b], writes=[gq8.b])
                S.add("act", lambda e: e.activation(out=esink.t[:], in_=esink.t[:], func=AF.Exp), reads=[esink.b], writes=[esink.b])
                pctr = [0]

                def qk_prep(src, srcb, n, gain, gainb, rope_t0, out, outb, nout=None, noutb=None):
                    k = pctr[0]
                    pctr[0] += 1
                    sq_, r_, g_, a_, b_ = sqq[k % 2], rs_[k % 2], kg_[k % 2], t1_[k % 2], t2_[k % 2]
                    pb = PS[3]
                    S.add("act", lambda e: e.activation(out=sq_.t[:, 0:n], in_=src, func=AF.Square), reads=[srcb], writes=[sq_.b])
                    S.add("pe", lambda e: e.matmul(pb.t[:, 0:n], lhsT=blk1.t[:], rhs=sq_.t[:, 0:n], start=True, stop=True),
                          reads=[blk1.b, sq_.b], writes=[pb.b])
                    S.add("act", lambda e: e.activation(out=r_.t[:, 0:n], in_=pb.t[:, 0:n], func=AF.Sqrt, scale=1.0 / 64,
                                                        bias=epsT.t[:, 0:1]), reads=[pb.b, epsT.b], writes=[r_.b])
                    S.add("dve", lambda e: e.reciprocal(out=r_.t[:, 0:n], in_=r_.t[:, 0:n]), reads=[r_.b], writes=[r_.b])
                    if rope_t0 is None:
                        if nout is not None:
                            S.add("dve", lambda e: e.scalar_tensor_tensor(out=nout, in0=src, scalar=gain, in1=r_.t[:, 0:n],
                                                                          op0=ALU.mult, op1=ALU.mult),
                                  reads=[srcb, gainb, r_.b], writes=[noutb])
                        S.add("dve", lambda e: e.scalar_tensor_tensor(out=out, in0=src, scalar=gain, in1=r_.t[:, 0:n],
                                                                      op0=ALU.mult, op1=ALU.mult),
                              reads=[srcb, gainb, r_.b], writes=[outb])
                        return
                    S.add("dve", lambda e: e.scalar_tensor_tensor(out=g_.t[:, 0:n], in0=src, scalar=gain, in1=r_.t[:, 0:n],
                                                                  op0=ALU.mult, op1=ALU.mult),
                          reads=[srcb, gainb, r_.b], writes=[g_.b])
                    S.add("pe", lambda e: e.matmul(pb.t[:, 0:n], lhsT=rotT.t[:], rhs=g_.t[:, 0:n], start=True, stop=True),
                          reads=[rotT.b, g_.b], writes=[pb.b])
                    S.add("dve", lambda e: e.tensor_tensor(out=b_.t[:, 0:n], in0=pb.t[:, 0:n], in1=sinT.t[:, rope_t0:rope_t0 + n], op=ALU.mult),
                          reads=[pb.b, sinT.b], writes=[b_.b])
                    S.add("pool", lambda e: e.tensor_tensor(out=a_.t[:, 0:n], in0=g_.t[:, 0:n], in1=cosT.t[:, rope_t0:rope_t0 + n], op=ALU.mult),
                          reads=[g_.b, cosT.b], writes=[a_.b])
                    S.add("dve", lambda e: e.tensor_tensor(out=out, in0=a_.t[:, 0:n], in1=b_.t[:, 0:n], op=ALU.add),
                          reads=[a_.b, b_.b], writes=[outb])

                stc = [0]
                gctr = [0]

                def attend(qpt, nq, blocks, dst_t0):
                    gi = gctr[0]
                    gctr[0] += 1
                    ao = aout[gi % 2]
                    for c in range(4):
                        par = (gi * 4 + c) % 2
                        PA, PB = PS[4 + 2 * par], PS[5 + 2 * par]
                        for hh in range(2):
                            h = 2 * c + hh
                            kvh = h // 4
                            lo = hh * 64
                            nb = len(blocks)
                            for idx, (Kt, kcol, Vt, vblk, q0, q1, masks) in enumerate(blocks):
                                n = q1 - q0
                                k = stc[0]
                                stc[0] += 1
                                ST = PS[k % 3]
                                P_ = pt[k % 3]
                                S.add("pe", lambda e, ST=ST, Kt=Kt, kcol=kcol, q0=q0, q1=q1, n=n, lo=lo, kvh=kvh, c=c: e.matmul(
                                    ST.t[:, 0:n], lhsT=Kt.t[lo:lo + 64, kvh, kcol:kcol + 128], rhs=qpt.t[lo:lo + 64, c, q0:q1],
                                    start=True, stop=True), reads=[Kt.b, qpt.b], writes=[ST.b])
                                S.add("act", lambda e, ST=ST, P_=P_, n=n: e.activation(out=P_.t[:, 0:n], in_=ST.t[:, 0:n], func=AF.Exp),
                                      reads=[ST.b], writes=[P_.b])
                                for (moff, mt_) in masks:
                                    S.add("dve", lambda e, P_=P_, moff=moff, mt_=mt_: e.tensor_tensor(
                                        out=P_.t[:, moff:moff + 128], in0=P_.t[:, moff:moff + 128], in1=mt_.t[:], op=ALU.mult),
                                        reads=[P_.b, mt_.b], writes=[P_.b])
                                S.add("pe", lambda e, PA=PA, Vt=Vt, vblk=vblk, P_=P_, n=n, q0=q0, q1=q1, lo=lo, kvh=kvh, idx=idx, nb=nb: e.matmul(
                                    PA.t[lo:lo + 64, q0:q1], lhsT=Vt.t[:, vblk, kvh * 64:(kvh + 1) * 64], rhs=P_.t[:, 0:n],
                                    start=(idx == 0), stop=(idx == nb - 1)), reads=[Vt.b, P_.b], writes=[PA.b])
                                S.add("pe", lambda e, PB=PB, P_=P_, n=n, q0=q0, q1=q1, lo=lo, idx=idx, nb=nb: e.matmul(
                                    PB.t[lo:lo + 64, q0:q1], lhsT=onesb.t[:, 0:64], rhs=P_.t[:, 0:n],
                                    start=(idx == 0), stop=(idx == nb - 1)), reads=[onesb.b, P_.b], writes=[PB.b])
                        dn = den[(gi * 4 + c) % 2]
                        S.add("dve", lambda e, dn=dn, PB=PB, c=c: e.tensor_scalar(out=dn.t[:, 0:nq], in0=PB.t[:, 0:nq], scalar1=esink.t[:, c:c + 1],
                                                                             scalar2=None, op0=ALU.add), reads=[PB.b, esink.b], writes=[dn.b])
                        S.add("dve", lambda e, dn=dn: e.reciprocal(out=dn.t[:, 0:nq], in_=dn.t[:, 0:nq]), reads=[dn.b], writes=[dn.b])
                        S.add("dve", lambda e, dn=dn, PA=PA, c=c: e.tensor_tensor(out=ao.t[:, c, 0:nq], in0=PA.t[:, 0:nq], in1=dn.t[:, 0:nq], op=ALU.mult),
                              reads=[PA.b, dn.b], writes=[ao.b])
                    S.add("sp", lambda e: e.dma_start(out=ay_d[:, 0:4, dst_t0:dst_t0 + nq], in_=ao.t[:, :, 0:nq]), reads=[ao.b], dma=True)

                for tt in range(8):
                    ki = kin[tt % 2]
                    S.add("sp", lambda e, ki=ki, tt=tt: e.dma_start(out=ki.t[:], in_=kT_d[:, :, tt * 512:(tt + 1) * 512]), writes=[ki.b], dma=True)
                    S.add("pool", lambda e, tt=tt: e.dma_start(
                        out=VV.t[:, tt * 4:(tt + 1) * 4, :], in_=v_d[tt * 512:(tt + 1) * 512, :].rearrange("(tb p) n -> p tb n", p=128)),
                        writes=[VV.b], dma=True)
                    for ch in range(2):
                        qk_prep(ki.t[:, ch, :], ki.b, 512, gn.t[:, 1:2], gn.b, tt * 512, KT.t[:, ch, tt * 512:(tt + 1) * 512], KT.b)
                for g in range(8):
                    qi = qin[g % 2]
                    qq = qp[g % 2]
                    S.add("sp", lambda e, qi=qi, g=g: e.dma_start(out=qi.t[:], in_=qT_d[:, :, g * 512:(g + 1) * 512]), writes=[qi.b], dma=True)
                    for c in range(4):
                        qk_prep(qi.t[:, c, :], qi.b, 512, gq8.t[:, 0:1], gq8.b, g * 512, qq.t[:, c, :], qq.b)
                    blocks = []
                    for b_ in range(4):
                        blocks.append((CK, b_ * 128, CV, b_, 0, 512, []))
                    for kb in range(max(0, 4 * g - 1), min(32, 4 * g + 5)):
                        qb0 = max(kb - 1, 4 * g)
                        qb1 = min(kb + 1, 4 * g + 3)
                        masks = []
                        for qb in range(qb0, qb1 + 1):
                            if qb == kb + 1:
                                masks.append(((qb - qb0) * 128, mprev))
                            elif qb == kb - 1:
                                masks.append(((qb - qb0) * 128, mnext))
                        blocks.append((KT, kb * 128, VV, kb, (qb0 - 4 * g) * 128, (qb1 - 4 * g + 1) * 128, masks))
                    attend(qq, 512, blocks, g * 512)
                KTp = [sb(st, "KTp", [128, 2, 256], BF16) for _ in range(2)]
                VVp = [sb(st, "VVp", [128, 2, 128], BF16) for _ in range(2)]
                for pi in range(4):
                    t0 = 4096 + pi * 256
                    ki = kin[pi % 2]
                    kt, vv, kn_ = KTp[pi % 2], VVp[pi % 2], kno[pi % 2]
                    S.add("sp", lambda e, ki=ki, t0=t0: e.dma_start(out=ki.t[:, :, 0:256], in_=kT_d[:, :, t0:t0 + 256]), writes=[ki.b], dma=True)
                    S.add("pool", lambda e, vv=vv, t0=t0: e.dma_start(
                        out=vv.t[:], in_=v_d[t0:t0 + 256, :].rearrange("(tb p) n -> p tb n", p=128)), writes=[vv.b], dma=True)
                    for ch in range(2):
                        qk_prep(ki.t[:, ch, 0:256], ki.b, 256, gn.t[:, 1:2], gn.b, None, kt.t[:, ch, :], kt.b,
                                nout=kn_.t[:, ch, :], noutb=kn_.b)
                    S.add("sp", lambda e, kn_=kn_, pi=pi: e.dma_start(
                        out=newk[i, :, :, pi * 256:(pi + 1) * 256].rearrange("k d t -> d k t"), in_=kn_.t[0:64, :, :]),
                        reads=[kn_.b], dma=True)
                    qi = qin[pi % 2]
                    qq = qp[pi % 2]
                    S.add("sp", lambda e, qi=qi, t0=t0: e.dma_start(out=qi.t[:, :, 0:256], in_=qT_d[:, :, t0:t0 + 256]), writes=[qi.b], dma=True)
                    for c in range(4):
                        qk_prep(qi.t[:, c, 0:256], qi.b, 256, gq8.t[:, 0:1], gq8.b, None, qq.t[:, c, 0:256], qq.b)
                    blocks = [(kt, b_ * 128, vv, b_, 0, 256, []) for b_ in range(2)]
                    attend(qq, 256, blocks, t0)
                S.add("sp", lambda e: e.dma_start(out=newv[i], in_=v_d[4096:5120, :]), dma=True)
                S.end_stage(resched=True)

        def ah_outproj(l, i):
            with contextlib.ExitStack() as st:
                X = [sb(st, "x", [128, 8, TM]) for _ in range(2)]
                AY = [sb(st, "ay", [128, 8, TM], BF16) for _ in range(2)]
                W = sb(st, "wout", [128, 8, 1024], BF16)
                for kc2 in range(4):
                    S.add("pool", lambda e, kc2=kc2: e.dma_start(out=W.t[:, 2 * kc2:2 * kc2 + 2, :], in_=wout[i, :, 2 * kc2:2 * kc2 + 2, :],
                                                               max_dma_last_dim=4096), writes=[W.b], dma=True)

                def load(mt):
                    xb, ab = X[mt % 2], AY[mt % 2]
                    S.add("sp", lambda e: e.dma_start(out=xb.t[:], in_=xs[:, :, mt * TM:(mt + 1) * TM]), writes=[xb.b], dma=True)
                    S.add("sp", lambda e: e.dma_start(out=ab.t[:], in_=ay_d[:, :, mt * TM:(mt + 1) * TM]), writes=[ab.b], dma=True)

                load(0)
                for mt in range(NMT):
                    which = 0 if mt < 4 else 1
                    xb, ab = X[mt % 2], AY[mt % 2]
                    if mt + 1 < NMT:
                        load(mt + 1)
                    for m in range(8):
                        for s_ in range(TM // 512):
                            pb = PS[(m * 2 + s_) % 4]
                            for f in range(8):
                                S.add("pe", lambda e, pb=pb, f=f, m=m, s_=s_, ab=ab: e.matmul(
                                    pb.t[:], lhsT=W.t[:, f, m * 128:(m + 1) * 128], rhs=ab.t[:, f, s_ * 512:(s_ + 1) * 512],
                                    start=(f == 0), stop=(f == 7)), reads=[W.b, ab.b], writes=[pb.b])
                            S.add("dve", lambda e, pb=pb, m=m, s_=s_, xb=xb, which=which: e.scalar_tensor_tensor(
                                out=xb.t[:, m, s_ * 512:(s_ + 1) * 512], in0=pb.t[:],
                                scalar=hgT.t[:, 1, m, which:which + 1], in1=xb.t[:, m, s_ * 512:(s_ + 1) * 512],
                                op0=ALU.mult, op1=ALU.add), reads=[pb.b, hgT.b, xb.b], writes=[xb.b])
                    S.add("sp", lambda e, xb=xb, mt=mt: e.dma_start(out=xs[:, :, mt * TM:(mt + 1) * TM], in_=xb.t[:]),
                          reads=[xb.b], dma=True)
                S.end_stage(resched=cfg.get("rs_x", True))


        def hy_conv(l, i):
            with contextlib.ExitStack() as st:
                cw = sb(st, "cw", [128, 12, 3])
                cb = sb(st, "cb", [128, 12])
                U = [sb(st, "U", [128, 12, 514]) for _ in range(2)]
                O = [sb(st, "O", [128, 12, 512]) for _ in range(2)]
                S.add("sp", lambda e: e.dma_start(out=cw.t[:], in_=hcw[:, i]), writes=[cw.b], dma=True)
                S.add("sp", lambda e: e.dma_start(out=cb.t[:], in_=hcb[:, i]), writes=[cb.b], dma=True)
                tiles = [(tt * 512, 512, tt == 0, tt == 7) for tt in range(8)] + [(4096 + 256 * pi, 256, True, True) for pi in range(4)]
                for ti, (t0, n, first, lastt) in enumerate(tiles):
                    u, o = U[ti % 2], O[ti % 2]
                    a0 = t0 if first else t0 - 1
                    a1 = t0 + n if lastt else t0 + n + 1
                    c0 = 1 if first else 0
                    S.add("sp", lambda e, u=u, a0=a0, a1=a1, c0=c0: e.dma_start(out=u.t[:, :, c0:c0 + (a1 - a0)], in_=u3T_d[:, :, a0:a1]),
                          writes=[u.b], dma=True)
                    if first:
                        S.add("pool", lambda e, u=u: e.memset(u.t[:, :, 0:1], 0.0), writes=[u.b])
                    if lastt:
                        S.add("pool", lambda e, u=u, n=n: e.memset(u.t[:, :, n + 1:n + 2], 0.0), writes=[u.b])
                    for ch in range(12):
                        en = "dve"
                        S.add(en, lambda e, u=u, o=o, ch=ch, n=n: e.tensor_scalar(
                            out=o.t[:, ch, 0:n], in0=u.t[:, ch, 0:n], scalar1=cw.t[:, ch, 0:1], scalar2=cb.t[:, ch:ch + 1],
                            op0=ALU.mult, op1=ALU.add), reads=[u.b, cw.b, cb.b], writes=[o.b])
                        S.add(en, lambda e, u=u, o=o, ch=ch, n=n: e.scalar_tensor_tensor(
                            out=o.t[:, ch, 0:n], in0=u.t[:, ch, 1:n + 1], scalar=cw.t[:, ch, 1:2], in1=o.t[:, ch, 0:n],
                            op0=ALU.mult, op1=ALU.add), reads=[u.b, cw.b, o.b], writes=[o.b])
                        S.add(en, lambda e, u=u, o=o, ch=ch, n=n: e.scalar_tensor_tensor(
                            out=o.t[:, ch, 0:n], in0=u.t[:, ch, 2:n + 2], scalar=cw.t[:, ch, 2:3], in1=o.t[:, ch, 0:n],
                            op0=ALU.mult, op1=ALU.add), reads=[u.b, cw.b, o.b], writes=[o.b])
                    S.add("sp", lambda e, o=o, t0=t0, n=n: e.dma_start(out=uc_d[:, :, t0:t0 + n], in_=o.t[:, :, 0:n]), reads=[o.b], dma=True)
                S.end_stage(resched=True)

        MAGIC = 12582912.0

        def hy_filter_gen(i, L, rn):
            G = HG[L]
            nb = L // 128
            TT = min(512, L)
            with contextlib.ExitStack() as st:
                zf = sb(st, "zf", [33, L])
                w1 = sb(st, "w1", [33, 64])
                w2 = sb(st, "w2", [64, 64])
                w3 = sb(st, "w3", [64, 2048])
                p1 = sb(st, "p1", [64, 2])
                p2 = sb(st, "p2", [64, 2])
                h1T = sb(st, "h1T", [64, L])
                h2T = sb(st, "h2T", [64, L])
                delt = sb(st, "delt", [128, 2048])
                tl = sb(st, "tl", [128, nb])
                dec = [sb(st, "dec", [128, 2048]) for _ in range(2)]
                hh = [sb(st, "hh", [128, 2048]) for _ in range(2)]
                sqh = [sb(st, "sqh", [128, 2048]) for _ in range(2)]
                stg = [sb(st, "hstg", [128, 2, 1024], BF16) for _ in range(2)]
                uu = [sb(st, "uu", [64, 512]) for _ in range(2)]
                ta = [sb(st, "ta", [64, 512]) for _ in range(2)]
                nr = [sb(st, "nr", [64, 512]) for _ in range(2)]
                sst = sb(st, "sst", [128, 1024])
                for (tt_, src) in ((zf, G["zf"]), (w1, hw1[i]), (w2, hw2[i]), (w3, hw3[i]), (p1, hb1[:, i, :]), (p2, hb2[:, i, :]),
                                  (delt, deltas_d), (tl, G["tl"])):
                    S.add("sp", lambda e, tt_=tt_, src=src: e.dma_start(out=tt_.t[:], in_=src), writes=[tt_.b], dma=True)
                for pp in (p1, p2):
                    S.add("dve", lambda e, pp=pp: e.tensor_scalar(out=pp.t[:, 1:2], in0=pp.t[:, 1:2], scalar1=1.0 / (2 * math.pi), scalar2=None,
                                                                  op0=ALU.mult), reads=[pp.b], writes=[pp.b])
                k = 0
                for (wt, pp, srcT, dstT) in ((w1, p1, zf, h1T), (w2, p2, h1T, h2T)):
                    for tile_ in range(L // TT):
                        c0 = tile_ * TT
                        pb = PS[k % 2]
                        u_, a_, n_ = uu[k % 2], ta[k % 2], nr[k % 2]
                        k += 1
                        kk = wt.t.shape[0]
                        S.add("pe", lambda e, pb=pb, wt=wt, srcT=srcT, c0=c0, kk=kk: e.matmul(
                            pb.t[0:64, 0:TT], lhsT=wt.t[:, :], rhs=srcT.t[0:kk, c0:c0 + TT], start=True, stop=True),
                            reads=[wt.b, srcT.b], writes=[pb.b])
                        S.add("dve", lambda e, pb=pb, u_=u_, pp=pp: e.tensor_scalar(
                            out=u_.t[:, 0:TT], in0=pb.t[0:64, 0:TT], scalar1=pp.t[:, 0:1], scalar2=pp.t[:, 1:2], op0=ALU.add, op1=ALU.mult),
                            reads=[pb.b, pp.b], writes=[u_.b])
                        S.add("dve", lambda e, u_=u_, a_=a_: e.tensor_scalar(
                            out=a_.t[:, 0:TT], in0=u_.t[:, 0:TT], scalar1=MAGIC, scalar2=None, op0=ALU.add), reads=[u_.b], writes=[a_.b])
                        S.add("dve", lambda e, u_=u_, a_=a_, n_=n_: e.scalar_tensor_tensor(
                            out=n_.t[:, 0:TT], in0=a_.t[:, 0:TT], scalar=MAGIC, in1=u_.t[:, 0:TT], op0=ALU.subtract, op1=ALU.subtract),
                            reads=[a_.b, u_.b], writes=[n_.b])
                        S.add("dve", lambda e, n_=n_: e.tensor_scalar(
                            out=n_.t[:, 0:TT], in0=n_.t[:, 0:TT], scalar1=0.49999, scalar2=-0.49999, op0=ALU.min, op1=ALU.max),
                            reads=[n_.b], writes=[n_.b])
                        S.add("act", lambda e, n_=n_, dstT=dstT, c0=c0: e.activation(
                            out=dstT.t[:, c0:c0 + TT], in_=n_.t[:, 0:TT], func=AF.Sin, scale=-2.0 * math.pi), reads=[n_.b], writes=[dstT.b])
                for lb in range(nb):
                    d_, h_, q_, sg_ = dec[lb % 2], hh[lb % 2], sqh[lb % 2], stg[lb % 2]
                    S.add("act", lambda e, d_=d_, lb=lb: e.activation(out=d_.t[:], in_=delt.t[:], func=AF.Exp, scale=tl.t[:, lb:lb + 1]),
                          reads=[delt.b, tl.b], writes=[d_.b])
                    for ct in range(4):
                        pb = PS[ct]
                        S.add("pe", lambda e, pb=pb, lb=lb, ct=ct: e.matmul(
                            pb.t[:], lhsT=h2T.t[:, lb * 128:(lb + 1) * 128], rhs=w3.t[:, ct * 512:(ct + 1) * 512], start=True, stop=True),
                            reads=[h2T.b, w3.b], writes=[pb.b])
                        S.add("dve", lambda e, pb=pb, h_=h_, d_=d_, ct=ct: e.tensor_tensor(
                            out=h_.t[:, ct * 512:(ct + 1) * 512], in0=pb.t[:], in1=d_.t[:, ct * 512:(ct + 1) * 512], op=ALU.mult),
                            reads=[pb.b, d_.b], writes=[h_.b])
                    if lb == 0:
                        S.add("dve", lambda e, h_=h_: e.memset(h_.t[0:1, 1024:2048], 0.0), writes=[h_.b])
                    S.add("act", lambda e, h_=h_, q_=q_: e.activation(out=q_.t[:], in_=h_.t[:], func=AF.Square), reads=[h_.b], writes=[q_.b])
                    for ct in range(4):
                        S.add("pe", lambda e, q_=q_, ct=ct, lb=lb: e.matmul(
                            PS[4 + ct].t[:], lhsT=ones32.t[:], rhs=q_.t[:, ct * 512:(ct + 1) * 512], start=(lb == 0), stop=(lb == nb - 1)),
                            reads=[ones32.b, q_.b], writes=[PS[4 + ct].b])
                    S.add("pool", lambda e, h_=h_, sg_=sg_: e.tensor_tensor(out=sg_.t[:, 0, :], in0=h_.t[:, 0:1024], in1=h_.t[:, 1024:2048], op=ALU.add),
                          reads=[h_.b], writes=[sg_.b])
                    S.add("pool", lambda e, h_=h_, sg_=sg_: e.tensor_tensor(out=sg_.t[:, 1, :], in0=h_.t[:, 1024:2048], in1=h_.t[:, 0:1024], op=ALU.subtract),
                          reads=[h_.b], writes=[sg_.b])
                    S.add("sp", lambda e, sg_=sg_, lb=lb: e.dma_start(out=hsd_d[:, lb].rearrange("s p n -> p s n"), in_=sg_.t[:]), reads=[sg_.b], dma=True)
                for o in range(2):
                    S.add("dve", lambda e, o=o: e.tensor_copy(out=sst.t[:, o * 512:(o + 1) * 512], in_=PS[4 + o].t[:]), reads=[PS[4 + o].b], writes=[sst.b])
                    S.add("dve", lambda e, o=o: e.tensor_tensor(out=sst.t[:, o * 512:(o + 1) * 512], in0=sst.t[:, o * 512:(o + 1) * 512],
                                                                in1=PS[6 + o].t[:], op=ALU.add), reads=[sst.b, PS[6 + o].b], writes=[sst.b])
                S.add("act", lambda e: e.activation(out=rn.t[:], in_=sst.t[:], func=AF.Sqrt, bias=epsT.t[:, 0:1]), reads=[sst.b, epsT.b], writes=[rn.b])
                S.add("dve", lambda e: e.reciprocal(out=rn.t[:], in_=rn.t[:]), reads=[rn.b], writes=[rn.b])
                S.end_stage(resched=True)

        def hy_filter_dft(L, rn):
            G = HG[L]
            nb = L // 128
            with contextlib.ExitStack() as st:
                hs = sb(st, "hs", [128, nb, 1024], BF16)
                hd = sb(st, "hd", [128, nb, 1024], BF16)
                tC = [sb(st, "tC", [128, nb, 128], BF16) for _ in range(2)]
                tS = [sb(st, "tS", [128, nb, 128], BF16) for _ in range(2)]
                stg = [sb(st, "Hstg", [128, 512]) for _ in range(4)]
                step = max(1, nb // 4)
                for lb0 in range(0, nb, step):
                    S.add("sp", lambda e, lb0=lb0: e.dma_start(out=hs.t[:, lb0:lb0 + step, :], in_=hsd_d[0, lb0:lb0 + step].rearrange("l p n -> p l n")),
                          writes=[hs.b], dma=True)
                    S.add("sp", lambda e, lb0=lb0: e.dma_start(out=hd.t[:, lb0:lb0 + step, :], in_=hsd_d[1, lb0:lb0 + step].rearrange("l p n -> p l n")),
                          writes=[hd.b], dma=True)
                k = 0
                for kb in range(nb):
                    c_, s_ = tC[kb % 2], tS[kb % 2]
                    S.add("sp", lambda e, c_=c_, kb=kb: e.dma_start(out=c_.t[:], in_=G["F"][0, kb]), writes=[c_.b], dma=True)
                    S.add("pool", lambda e, s_=s_, kb=kb: e.dma_start(out=s_.t[:], in_=G["F"][1, kb]), writes=[s_.b], dma=True)
                    for o in range(2):
                        for ri, (tab, src) in enumerate(((c_, hs), (s_, hd))):
                            pb = PS[k % 8]
                            sg_ = stg[k % 4]
                            k += 1
                            for lb in range(nb):
                                S.add("pe", lambda e, pb=pb, tab=tab, src=src, lb=lb, o=o: e.matmul(
                                    pb.t[:], lhsT=tab.t[:, lb, :], rhs=src.t[:, lb, o * 512:(o + 1) * 512], start=(lb == 0), stop=(lb == nb - 1)),
                                    reads=[tab.b, src.b], writes=[pb.b])
                            S.add("dve", lambda e, pb=pb, sg_=sg_, o=o: e.tensor_tensor(out=sg_.t[:], in0=pb.t[:], in1=rn.t[:, o * 512:(o + 1) * 512], op=ALU.mult),
                                  reads=[pb.b, rn.b], writes=[sg_.b])
                            S.add("sp", lambda e, sg_=sg_, o=o, ri=ri, kb=kb: e.dma_start(out=H_d[o, ri, kb], in_=sg_.t[:]), reads=[sg_.b], dma=True)
                S.end_stage(resched=cfg.get("rs_x", True))

        def hy_order(l, i, L, offs, o, ident, hbs, Z, YR, YI):
            G = HG[L]
            nb = L // 128
            TT = min(512, L)
            ntt = L // TT
            nsub = TT // 128
            with contextlib.ExitStack() as st:
                zin = [sb(st, "zin", [128, 512]) for _ in range(3)]
                k = 0
                for si, t0 in enumerate(offs):
                    for tt in range(ntt):
                        for cc in range(4):
                            zi = zin[k % 3]
                            pb = PS[k % 4]
                            k += 1
                            src = uc_d[:, 8 + cc, t0 + tt * TT:t0 + (tt + 1) * TT] if o == 0 else z1T_d[:, cc, t0 + tt * TT:t0 + (tt + 1) * TT]
                            S.add("sp", lambda e, zi=zi, src=src: e.dma_start(out=zi.t[:, 0:TT], in_=src), writes=[zi.b], dma=True)
                            for j in range(nsub):
                                S.add("pe", lambda e, pb=pb, zi=zi, j=j: e.transpose(pb.t[:, j * 128:(j + 1) * 128], zi.t[:, j * 128:(j + 1) * 128], ident.t[:]),
                                      reads=[zi.b, ident.b], writes=[pb.b])
                            zt = Z[si]
                            if k % 2 == 0:
                                S.add("act", lambda e, pb=pb, zt=zt, tt=tt, cc=cc: e.activation(
                                    out=zt.t[:, tt * nsub:(tt + 1) * nsub, cc * 128:(cc + 1) * 128],
                                    in_=pb.t[:, 0:TT].rearrange("p (a b) -> p a b", b=128), func=AF.Identity), reads=[pb.b], writes=[zt.b])
                            else:
                                S.add("dve", lambda e, pb=pb, zt=zt, tt=tt, cc=cc: e.tensor_copy(
                                    out=zt.t[:, tt * nsub:(tt + 1) * nsub, cc * 128:(cc + 1) * 128],
                                    in_=pb.t[:, 0:TT].rearrange("p (a b) -> p a b", b=128)), reads=[pb.b], writes=[zt.b])
                S.end_stage(resched=True)
            with contextlib.ExitStack() as st:
                tC = [sb(st, "tC", [128, nb, 128], BF16) for _ in range(3)]
                tS = [sb(st, "tS", [128, nb, 128], BF16) for _ in range(3)]
                hr = [sb(st, "hr", [128, 512]) for _ in range(2)]
                hi = [sb(st, "hi", [128, 512]) for _ in range(2)]
                m_ = [[sb(st, "m", [128, 512]) for _ in range(2)] for _ in range(4)]
                k = 0
                for kb in range(nb):
                    c_, s_ = tC[kb % 3], tS[kb % 3]
                    hr_, hi_ = hr[kb % 2], hi[kb % 2]
                    S.add("sp", lambda e, c_=c_, kb=kb: e.dma_start(out=c_.t[:], in_=G["F"][0, kb]), writes=[c_.b], dma=True)
                    S.add("pool", lambda e, s_=s_, kb=kb: e.dma_start(out=s_.t[:], in_=G["F"][1, kb]), writes=[s_.b], dma=True)
                    S.add("sp", lambda e, hr_=hr_, kb=kb: e.dma_start(out=hr_.t[:], in_=H_d[o, 0, kb]), writes=[hr_.b], dma=True)
                    S.add("sp", lambda e, hi_=hi_, kb=kb: e.dma_start(out=hi_.t[:], in_=H_d[o, 1, kb]), writes=[hi_.b], dma=True)
                    for si in range(len(offs)):
                        zt, yr, yi = Z[si], YR[si], YI[si]
                        Pc, Ps = PS[(k % 4) * 2], PS[(k % 4) * 2 + 1]
                        mm = [m_[q][k % 2] for q in range(4)]
                        k += 1
                        for tb in range(nb):
                            S.add("pe", lambda e, Pc=Pc, c_=c_, zt=zt, tb=tb: e.matmul(Pc.t[:], lhsT=c_.t[:, tb, :], rhs=zt.t[:, tb, :],
                                                                                 start=(tb == 0), stop=(tb == nb - 1)), reads=[c_.b, zt.b], writes=[Pc.b])
                        for tb in range(nb):
                            S.add("pe", lambda e, Ps=Ps, s_=s_, zt=zt, tb=tb: e.matmul(Ps.t[:], lhsT=s_.t[:, tb, :], rhs=zt.t[:, tb, :],
                                                                                 start=(tb == 0), stop=(tb == nb - 1)), reads=[s_.b, zt.b], writes=[Ps.b])
                        for q, (hh_, pp_) in enumerate(((hr_, Pc), (hi_, Ps), (hr_, Ps), (hi_, Pc))):
                            S.add("dve", lambda e, q=q, hh_=hh_, pp_=pp_, mm=mm: e.tensor_tensor(out=mm[q].t[:], in0=pp_.t[:], in1=hh_.t[:], op=ALU.mult),
                                  reads=[hh_.b, pp_.b], writes=[mm[q].b])
                        S.add("pool", lambda e, mm=mm, yr=yr, kb=kb: e.tensor_tensor(out=yr.t[:, kb, :], in0=mm[0].t[:], in1=mm[1].t[:], op=ALU.add),
                              reads=[mm[0].b, mm[1].b], writes=[yr.b])
                        S.add("pool", lambda e, mm=mm, yi=yi, kb=kb: e.tensor_tensor(out=yi.t[:, kb, :], in0=mm[2].t[:], in1=mm[3].t[:], op=ALU.subtract),
                              reads=[mm[2].b, mm[3].b], writes=[yi.b])
                S.end_stage(resched=True)
            with contextlib.ExitStack() as st:
                KQ = min(8, nb)
                nkq = nb // KQ
                iC = [sb(st, "iC", [128, KQ, TT], BF16) for _ in range(3)]
                iS = [sb(st, "iS", [128, KQ, TT], BF16) for _ in range(3)]
                zt_ = [sb(st, "zt", [128, 512]) for _ in range(3)]
                gt_ = [sb(st, "gt", [128, 512]) for _ in range(3)]
                ab_ = [sb(st, "ab", [128, 512]) for _ in range(3)]
                of_ = [sb(st, "of", [128, 512]) for _ in range(3)]
                ob_ = [sb(st, "ob", [128, 512], BF16) for _ in range(3)]
                kt = 0
                ke = 0
                kk = 0
                for si, t0 in enumerate(offs):
                    yr, yi = YR[si], YI[si]
                    for tt in range(ntt):
                        banks = [PS[(kk % 2) * 4 + cc] for cc in range(4)]
                        kk += 1
                        for kq in range(nkq):
                            c_, s_ = iC[kt % 3], iS[kt % 3]
                            kt += 1
                            S.add("sp", lambda e, c_=c_, tt=tt, kq=kq: e.dma_start(out=c_.t[:], in_=G["I"][0, tt, :, kq * KQ:(kq + 1) * KQ, :]), writes=[c_.b], dma=True)
                            S.add("pool", lambda e, s_=s_, tt=tt, kq=kq: e.dma_start(out=s_.t[:], in_=G["I"][1, tt, :, kq * KQ:(kq + 1) * KQ, :]), writes=[s_.b], dma=True)
                            for cc in range(4):
                                for kbl in range(KQ):
                                    kb = kq * KQ + kbl
                                    S.add("pe", lambda e, bk=banks[cc], yr=yr, c_=c_, kb=kb, kbl=kbl, cc=cc: e.matmul(
                                        bk.t[:, 0:TT], lhsT=yr.t[:, kb, cc * 128:(cc + 1) * 128], rhs=c_.t[:, kbl, :], start=(kb == 0), stop=False),
                                        reads=[yr.b, c_.b], writes=[banks[cc].b])
                                    S.add("pe", lambda e, bk=banks[cc], yi=yi, s_=s_, kb=kb, kbl=kbl, cc=cc: e.matmul(
                                        bk.t[:, 0:TT], lhsT=yi.t[:, kb, cc * 128:(cc + 1) * 128], rhs=s_.t[:, kbl, :], start=False, stop=(kb == nb - 1)),
                                        reads=[yi.b, s_.b], writes=[banks[cc].b])
                        for cc in range(4):
                            z_, g_, a_, f_, b_ = zt_[ke % 3], gt_[ke % 3], ab_[ke % 3], of_[ke % 3], ob_[ke % 3]
                            ke += 1
                            tok = slice(t0 + tt * TT, t0 + (tt + 1) * TT)
                            zsrc = uc_d[:, 8 + cc, tok] if o == 0 else z1T_d[:, cc, tok]
                            S.add("sp", lambda e, z_=z_, zsrc=zsrc: e.dma_start(out=z_.t[:, 0:TT], in_=zsrc), writes=[z_.b], dma=True)
                            S.add("sp", lambda e, g_=g_, cc=cc, tok=tok: e.dma_start(out=g_.t[:, 0:TT], in_=uc_d[:, o * 4 + cc, tok]), writes=[g_.b], dma=True)
                            S.add("pool", lambda e, z_=z_, a_=a_, cc=cc: e.tensor_scalar(out=a_.t[:, 0:TT], in0=z_.t[:, 0:TT], scalar1=hbs.t[:, o, cc:cc + 1],
                                                                                    scalar2=None, op0=ALU.mult), reads=[z_.b, hbs.b], writes=[a_.b])
                            S.add("dve", lambda e, bk=banks[cc], a_=a_: e.scalar_tensor_tensor(out=a_.t[:, 0:TT], in0=bk.t[:, 0:TT], scalar=2.0 / (2 * L),
                                                                                            in1=a_.t[:, 0:TT], op0=ALU.mult, op1=ALU.add),
                                  reads=[banks[cc].b, a_.b], writes=[a_.b])
                            if o == 0:
                                S.add("pool", lambda e, a_=a_, g_=g_, f_=f_: e.tensor_tensor(out=f_.t[:, 0:TT], in0=a_.t[:, 0:TT], in1=g_.t[:, 0:TT], op=ALU.mult),
                                      reads=[a_.b, g_.b], writes=[f_.b])
                                S.add("sp", lambda e, f_=f_, cc=cc, tok=tok: e.dma_start(out=z1T_d[:, cc, tok], in_=f_.t[:, 0:TT]), reads=[f_.b], dma=True)
                            else:
                                S.add("pool", lambda e, a_=a_, g_=g_, b_=b_: e.tensor_tensor(out=b_.t[:, 0:TT], in0=a_.t[:, 0:TT], in1=g_.t[:, 0:TT], op=ALU.mult),
                                      reads=[a_.b, g_.b], writes=[b_.b])
                                S.add("sp", lambda e, b_=b_, cc=cc, tok=tok: e.dma_start(out=ay_d[:, 4 + cc, tok], in_=b_.t[:, 0:TT]), reads=[b_.b], dma=True)
                S.end_stage(resched=True)

        def ah_hyena(l, i):
            hy_conv(l, i)
            with contextlib.ExitStack() as st:
                ident = sb(st, "ident", [128, 128])
                hbs = sb(st, "hbs", [128, 2, 4])
                rn = sb(st, "rn", [128, 1024])
                S.add("sp", lambda e: e.dma_start(out=ident.t[:], in_=ident_d), writes=[ident.b], dma=True)
                S.add("sp", lambda e: e.dma_start(out=hbs.t[:], in_=hbias[:, i]), writes=[hbs.b], dma=True)
                S.end_stage()
                for (L, offs) in ((4096, [0]), (256, [4096 + 256 * pi for pi in range(4)])):
                    nb = L // 128
                    with contextlib.ExitStack() as st2:
                        hy_filter_gen(i, L, rn)
                        hy_filter_dft(L, rn)
                        Z = [sb(st2, "Z", [128, nb, 512], BF16) for _ in offs]
                        YR = [sb(st2, "YR", [128, nb, 512], BF16) for _ in offs]
                        YI = [sb(st2, "YI", [128, nb, 512], BF16) for _ in offs]
                        for o in range(2):
                            hy_order(l, i, L, offs, o, ident, hbs, Z, YR, YI)

        class Rot:
            def __init__(self, st, name, shape, dt, n):
                self.ts = [sb(st, name, shape, dt) for _ in range(n)]
                self.k = 0

            def next(self):
                t = self.ts[self.k % len(self.ts)]
                self.k += 1
                return t

        def dn_inproj(l, i):
            with contextlib.ExitStack() as st:
                X = [sb(st, "x", [128, 8, TM]) for _ in range(2)]
                Hh = [sb(st, "h", [128, 8, TM], BF16) for _ in range(2)]
                sq = [sb(st, "sq", [128, TM]) for _ in range(2)]
                tmp = [sb(st, "tmp", [128, TM]) for _ in range(2)]
                rstd = sb(st, "rstd", [128, TM])
                W = sb(st, "dwin", [128, 8, 4128], BF16)
                stg = [sb(st, "stg", [128, 512]) for _ in range(4)]
                vst = [sb(st, "vst", [128, 8, 32]) for _ in range(2)]
                for kc in range(8):
                    S.add("pool", lambda e, kc=kc: e.dma_start(out=W.t[:, kc, :], in_=dwin[i, :, kc, :], max_dma_last_dim=4096),
                          writes=[W.b], dma=True)

                def load_x(mt):
                    xb = X[mt % 2]
                    S.add("sp", lambda e: e.dma_start(out=xb.t[:], in_=xs[:, :, mt * TM:(mt + 1) * TM]), writes=[xb.b], dma=True)

                def stage_a(mt):
                    which = 0 if mt < 4 else 1
                    norm_mod((sq, rstd, tmp), X[mt % 2], Hh[mt % 2], 1, which, (PS[4], PS[5]))

                load_x(0)
                stage_a(0)
                ctr = [0]
                for mt in range(NMT):
                    hb = Hh[mt % 2]
                    if mt + 1 < NMT:
                        load_x(mt + 1)
                    for oc in range(32):
                        dst, ch = (qkvT_d, oc) if oc < 24 else (zT_d, oc - 24)
                        for s_ in range(TM // 512):
                            k = ctr[0]
                            ctr[0] += 1
                            pb = PS[k % 4]
                            sg_ = stg[k % 4]
                            for c in range(8):
                                S.add("pe", lambda e, pb=pb, c=c, s_=s_, oc=oc, hb=hb: e.matmul(
                                    pb.t[:], lhsT=W.t[:, c, oc * 128:(oc + 1) * 128],
                                    rhs=hb.t[:, c, s_ * 512:(s_ + 1) * 512], start=(c == 0), stop=(c == 7)),
                                    reads=[W.b, hb.b], writes=[pb.b])
                            if k % 2 == 0:
                                S.add("act", lambda e, pb=pb, sg_=sg_: e.activation(out=sg_.t[:], in_=pb.t[:], func=AF.Identity),
                                      reads=[pb.b], writes=[sg_.b])
                            else:
                                S.add("dve", lambda e, pb=pb, sg_=sg_: e.tensor_copy(out=sg_.t[:], in_=pb.t[:]),
                                      reads=[pb.b], writes=[sg_.b])
                            t0 = mt * TM + s_ * 512
                            S.add("sp", lambda e, dst=dst, ch=ch, t0=t0, sg_=sg_: e.dma_start(
                                out=dst[:, ch, t0:t0 + 512], in_=sg_.t[:]), reads=[sg_.b], dma=True)
                        if oc == 15 and mt + 1 < NMT:
                            stage_a(mt + 1)
                    vs_ = vst[mt % 2]
                    for tb in range(TM // 128):
                        pb = PS[6 + tb % 2]
                        for c in range(8):
                            S.add("pe", lambda e, pb=pb, c=c, tb=tb, hb=hb: e.matmul(
                                pb.t[:, 0:32], lhsT=hb.t[:, c, tb * 128:(tb + 1) * 128], rhs=W.t[:, c, 4096:4128],
                                start=(c == 0), stop=(c == 7)), reads=[W.b, hb.b], writes=[pb.b])
                        S.add("dve", lambda e, pb=pb, tb=tb, vs_=vs_: e.tensor_copy(out=vs_.t[:, tb, :], in_=pb.t[:, 0:32]),
                              reads=[pb.b], writes=[vs_.b])
                    S.add("sp", lambda e, mt=mt, vs_=vs_: e.dma_start(
                        out=ba_d[mt * TM:(mt + 1) * TM, :].rearrange("(tb p) n -> p tb n", p=128), in_=vs_.t[:]),
                        reads=[vs_.b], dma=True)
                S.end_stage(resched=cfg.get("rs_x", True))

        def dn_conv(l, i):
            with contextlib.ExitStack() as st:
                cw = sb(st, "dcw", [128, 24, 3])
                U = [sb(st, "U", [128, 12, 514]) for _ in range(2)]
                O = [sb(st, "O", [128, 12, 512]) for _ in range(2)]
                SQ = Rot(st, "dsq", [128, 512], F32, 8)
                RS = Rot(st, "drs", [128, 512], F32, 8)
                S.add("sp", lambda e: e.dma_start(out=cw.t[:], in_=dcw[:, i]), writes=[cw.b], dma=True)
                tiles = [(tt * 512, 512, tt == 0, tt == 7) for tt in range(8)] + [(4096 + 256 * pi, 256, True, True) for pi in range(4)]
                ti = 0
                for (t0, n, first, lastt) in tiles:
                    for half in range(2):
                        u, o = U[ti % 2], O[ti % 2]
                        ti += 1
                        a0 = t0 if first else t0 - 1
                        a1 = t0 + n if lastt else t0 + n + 1
                        c0 = 1 if first else 0
                        S.add("sp", lambda e, u=u, a0=a0, a1=a1, c0=c0, half=half: e.dma_start(
                            out=u.t[:, :, c0:c0 + (a1 - a0)], in_=qkvT_d[:, half * 12:(half + 1) * 12, a0:a1]), writes=[u.b], dma=True)
                        if first:
                            S.add("pool", lambda e, u=u: e.memset(u.t[:, :, 0:1], 0.0), writes=[u.b])
                        if lastt:
                            S.add("pool", lambda e, u=u, n=n: e.memset(u.t[:, :, n + 1:n + 2], 0.0), writes=[u.b])
                        for c12 in range(12):
                            ch = half * 12 + c12
                            S.add("dve", lambda e, u=u, o=o, ch=ch, c12=c12, n=n: e.tensor_scalar(
                                out=o.t[:, c12, 0:n], in0=u.t[:, c12, 0:n], scalar1=cw.t[:, ch, 0:1], scalar2=None, op0=ALU.mult),
                                reads=[u.b, cw.b], writes=[o.b])
                            for tap in (1, 2):
                                S.add("dve", lambda e, u=u, o=o, ch=ch, c12=c12, n=n, tap=tap: e.scalar_tensor_tensor(
                                    out=o.t[:, c12, 0:n], in0=u.t[:, c12, tap:n + tap], scalar=cw.t[:, ch, tap:tap + 1], in1=o.t[:, c12, 0:n],
                                    op0=ALU.mult, op1=ALU.add), reads=[u.b, cw.b, o.b], writes=[o.b])
                        S.add("act", lambda e, o=o, n=n: e.activation(out=o.t[:, :, 0:n], in_=o.t[:, :, 0:n], func=AF.Silu), reads=[o.b], writes=[o.b])
                        for c12 in range(12):
                            ch = half * 12 + c12
                            if ch >= 16:
                                continue
                            q_, r_ = SQ.next(), RS.next()
                            pb = PS[ch % 8]
                            S.add("act", lambda e, o=o, q_=q_, c12=c12, n=n: e.activation(out=q_.t[:, 0:n], in_=o.t[:, c12, 0:n], func=AF.Square),
                                  reads=[o.b], writes=[q_.b])
                            S.add("pe", lambda e, pb=pb, q_=q_, n=n: e.matmul(pb.t[:, 0:n], lhsT=ones32.t[:], rhs=q_.t[:, 0:n], start=True, stop=True),
                                  reads=[ones32.b, q_.b], writes=[pb.b])
                            S.add("act", lambda e, pb=pb, r_=r_, n=n: e.activation(out=r_.t[:, 0:n], in_=pb.t[:, 0:n], func=AF.Sqrt, bias=epsT.t[:, 0:1]),
                                  reads=[pb.b, epsT.b], writes=[r_.b])
                            S.add("dve", lambda e, r_=r_, n=n: e.reciprocal(out=r_.t[:, 0:n], in_=r_.t[:, 0:n]), reads=[r_.b], writes=[r_.b])
                            sc = (128.0 ** -0.5) if ch < 8 else 1.0
                            S.add("dve", lambda e, o=o, r_=r_, c12=c12, n=n, sc=sc: e.scalar_tensor_tensor(
                                out=o.t[:, c12, 0:n], in0=o.t[:, c12, 0:n], scalar=sc, in1=r_.t[:, 0:n], op0=ALU.mult, op1=ALU.mult),
                                reads=[o.b, r_.b], writes=[o.b])
                        S.add("sp", lambda e, o=o, t0=t0, n=n, half=half: e.dma_start(out=qkvn_d[:, half * 12:(half + 1) * 12, t0:t0 + n], in_=o.t[:, :, 0:n]),
                              reads=[o.b], dma=True)
                S.end_stage(resched=True)

        def dn_chunks(l, i):
            with contextlib.ExitStack() as st:
                msk = sb(st, "msk", [64, 5, 64])
                ident = sb(st, "ident", [128, 128])
                ones64 = sb(st, "ones64", [64, 128])
                prm = sb(st, "prm", [64, 32])
                ba = sb(st, "ba", [64, NCH, 32])
                gall = sb(st, "gall", [64, NCH, 16])
                ball = sb(st, "ball", [64, NCH, 16])
                KT_ = sb(st, "KTt", [128, 8, 256])
                QT_ = sb(st, "QTt", [128, 8, 256])
                VT_ = sb(st, "VTt", [128, 8, 256])
                Ktm = sb(st, "Ktm", [64, 8, 128])
                Vtm = sb(st, "Vtm", [64, 8, 128])
                r512L = [Rot(st, "r512", [128, 512], F32, 10) for _ in range(2)]
                ratr = Rot(st, "ratr", [64, 512], F32, 2)
                b512L = [Rot(st, "b512", [64, 512], BF16, 16) for _ in range(2)]
                batnL = [Rot(st, "batn", [64, 512], BF16, 4) for _ in range(2)]
                lvm_t = sb(st, "lvm", [64, 12, 64])
                S.add("sp", lambda e: e.dma_start(out=lvm_t.t[:], in_=lvmask_d), writes=[lvm_t.b], dma=True)
                b1kL = [Rot(st, "b1k", [64, 8, 128], BF16, 3) for _ in range(2)]
                bqL = [Rot(st, "bq", [128, 512], BF16, 2) for _ in range(2)]
                u1kL = [Rot(st, "u1k", [64, 8, 128], F32, 1) for _ in range(2)]
                smL = [Rot(st, "sm", [128, 8], F32, 10) for _ in range(2)]
                S.add("sp", lambda e: e.dma_start(out=msk.t[:], in_=dmask_d), writes=[msk.b], dma=True)
                S.add("sp", lambda e: e.dma_start(out=ident.t[:], in_=ident_d), writes=[ident.b], dma=True)
                S.add("sp", lambda e: e.dma_start(out=prm.t[:], in_=dprm[0:64, i, :]), writes=[prm.b], dma=True)
                S.add("sp", lambda e: e.dma_start(out=ba.t[:], in_=ba_d.rearrange("(c p) n -> p c n", p=64)), writes=[ba.b], dma=True)
                S.add("dve", lambda e: e.memset(ones64.t[:], 1.0), writes=[ones64.b])
                S.add("act", lambda e: e.activation(out=ball.t[:], in_=ba.t[:, :, 0:16], func=AF.Sigmoid), reads=[ba.b], writes=[ball.b])
                S.add("dve", lambda e: e.tensor_tensor(out=gall.t[:], in0=ba.t[:, :, 16:32],
                                                       in1=prm.t[:, 16:32].unsqueeze(1).broadcast_to([64, NCH, 16]), op=ALU.add),
                      reads=[ba.b, prm.b], writes=[gall.b])
                S.add("act", lambda e: e.activation(out=gall.t[:], in_=gall.t[:], func=AF.Exp), reads=[gall.b], writes=[gall.b])
                S.add("act", lambda e: e.activation(out=gall.t[:], in_=gall.t[:], func=AF.Ln, bias=1.0), reads=[gall.b], writes=[gall.b])
                S.add("act", lambda e: e.activation(out=prm.t[:, 0:16], in_=prm.t[:, 0:16], func=AF.Exp), reads=[prm.b], writes=[prm.b])
                S.add("dve", lambda e: e.scalar_tensor_tensor(out=gall.t[:], in0=gall.t[:], scalar=-1.0,
                                                              in1=prm.t[:, 0:16].unsqueeze(1).broadcast_to([64, NCH, 16]), op0=ALU.mult, op1=ALU.mult),
                      reads=[gall.b, prm.b], writes=[gall.b])
                Uf, Ub, Sf, Sb, I64 = (msk.t[:, k_, :] for k_ in range(5))

                def bc_h(m):
                    return m.unsqueeze(1).broadcast_to([64, 8, 64])

                def v3(t_, p=64):
                    return t_[0:p, :].rearrange("p (h i) -> p h i", i=64)

                def chunk_body(c):
                    cl = c % 4
                    if cl == 0:
                        t0 = c * 64
                        for (tile_, ch0) in ((QT_, 0), (KT_, 8), (VT_, 16)):
                            S.add("sp", lambda e, tile_=tile_, ch0=ch0, t0=t0: e.dma_start(out=tile_.t[:], in_=qkvn_d[:, ch0:ch0 + 8, t0:t0 + 256]),
                                  writes=[tile_.b], dma=True)
                    csl = slice(cl * 64, (cl + 1) * 64)
                    Kc, Qc, Vc = KT_.t[:, :, csl], QT_.t[:, :, csl], VT_.t[:, :, csl]
                    for (src, srcb, dst, bk0) in ((Kc, KT_.b, Ktm, 0), (Vc, VT_.b, Vtm, 4)):
                        for h in range(8):
                            pb = PS[bk0 + h // 4]
                            S.add("pe", lambda e, pb=pb, src=src, h=h: e.transpose(pb.t[0:64, (h % 4) * 128:(h % 4 + 1) * 128], src[:, h, :], ident.t[:]),
                                  reads=[srcb, ident.b], writes=[pb.b])
                        for hb_ in range(2):
                            S.add("act", lambda e, dst=dst, hb_=hb_, bk0=bk0: e.activation(
                                out=dst.t[:, hb_ * 4:(hb_ + 1) * 4, :], in_=PS[bk0 + hb_].t[0:64, :].rearrange("p (h d) -> p h d", d=128), func=AF.Identity),
                                reads=[PS[bk0 + hb_].b], writes=[dst.b])
                    for h in range(8):
                        S.add("pe", lambda e, h=h, Kc=Kc, Qc=Qc: e.matmul(PS[2].t[0:64, h * 64:(h + 1) * 64], lhsT=Kc[:, h, :], rhs=Qc[:, h, :], start=True, stop=True),
                              reads=[KT_.b, QT_.b], writes=[PS[2].b])
                    atraw = ratr.next()
                    S.add("dve", lambda e, atraw=atraw: e.tensor_copy(out=atraw.t[0:64, :], in_=PS[2].t[0:64, :]), reads=[PS[2].b], writes=[atraw.b])
                    def unit(d):
                        r512, b512, batn, b1k, bq, u1k, sm = r512L[d], b512L[d], batnL[d], b1kL[d], bqL[d], u1kL[d], smL[d]
                        B0, B1, B2, B3 = (PS[4 * d + k_] for k_ in range(4))
                        Ud, Sd, SdT = (Uf, Sf, Sb) if d == 0 else (Ub, Sb, Sf)
                        g_ = gall.t[:, c, d * 8:(d + 1) * 8]
                        b_ = ball.t[:, c, d * 8:(d + 1) * 8]
                        ug, ib = r512.next(), r512.next()
                        S.add("pool", lambda e, ug=ug, Ud=Ud, g_=g_: e.tensor_tensor(out=v3(ug.t), in0=bc_h(Ud), in1=g_.unsqueeze(2).broadcast_to([64, 8, 64]), op=ALU.mult),
                              reads=[msk.b, gall.b], writes=[ug.b])
                        S.add("pool", lambda e, ib=ib, b_=b_: e.tensor_tensor(out=v3(ib.t), in0=bc_h(I64), in1=b_.unsqueeze(2).broadcast_to([64, 8, 64]), op=ALU.mult),
                              reads=[msk.b, ball.b], writes=[ib.b])
                        S.add("pe", lambda e, Ud=Ud, g_=g_: e.matmul(B0.t[0:64, 0:8], lhsT=Ud, rhs=g_, start=True, stop=True),
                              reads=[msk.b, gall.b], writes=[B0.b])
                        S.add("pe", lambda e, g_=g_: e.matmul(B0.t[:, 8:16], lhsT=ones64.t[:], rhs=g_, start=True, stop=True),
                              reads=[ones64.b, gall.b], writes=[B0.b])
                        S.add("pe", lambda e, ug=ug: e.matmul(B1.t[:], lhsT=ones64.t[:], rhs=ug.t[0:64, :], start=True, stop=True),
                              reads=[ones64.b, ug.b], writes=[B1.b])
                        S.add("pe", lambda e, ib=ib: e.matmul(B2.t[:], lhsT=ones64.t[:], rhs=ib.t[0:64, :], start=True, stop=True),
                              reads=[ones64.b, ib.b], writes=[B2.b])
                        gcc, egl, egc, bg, kds = sm.next(), sm.next(), sm.next(), sm.next(), sm.next()
                        S.add("dve", lambda e, gcc=gcc: e.tensor_copy(out=gcc.t[0:64, :], in_=B0.t[0:64, 0:8]), reads=[B0.b], writes=[gcc.b])
                        S.add("act", lambda e, egl=egl: e.activation(out=egl.t[:], in_=B0.t[:, 8:16], func=AF.Exp), reads=[B0.b], writes=[egl.b])
                        S.add("sp", lambda e, egl=egl, d=d, c=c: e.dma_start(out=egl_d[d, c], in_=egl.t[:]), reads=[egl.b], dma=True)
                        S.add("act", lambda e, egc=egc, gcc=gcc: e.activation(out=egc.t[0:64, :], in_=gcc.t[0:64, :], func=AF.Exp), reads=[gcc.b], writes=[egc.b])
                        S.add("dve", lambda e, bg=bg, egc=egc, b_=b_: e.tensor_tensor(out=bg.t[0:64, :], in0=egc.t[0:64, :], in1=b_, op=ALU.mult),
                              reads=[egc.b, ball.b], writes=[bg.b])
                        S.add("dve", lambda e, kds=kds, gcc=gcc: e.tensor_tensor(out=kds.t[0:64, :], in0=B0.t[0:64, 8:16], in1=gcc.t[0:64, :], op=ALU.subtract),
                              reads=[B0.b, gcc.b], writes=[kds.b])
                        S.add("act", lambda e, kds=kds: e.activation(out=kds.t[0:64, :], in_=kds.t[0:64, :], func=AF.Exp), reads=[kds.b], writes=[kds.b])
                        Dt, E1, E2 = r512.next(), r512.next(), r512.next()
                        S.add("dve", lambda e, Dt=Dt, gcc=gcc: e.tensor_tensor(out=v3(Dt.t), in0=v3(B1.t), in1=gcc.t[0:64, :].unsqueeze(2).broadcast_to([64, 8, 64]),
                                                                        op=ALU.subtract), reads=[B1.b, gcc.b], writes=[Dt.b])
                        S.add("dve", lambda e, Dt=Dt, E1=E1: e.tensor_scalar(out=E1.t[0:64, :], in0=Dt.t[0:64, :], scalar1=0.0, scalar2=None, op0=ALU.min),
                              reads=[Dt.b], writes=[E1.b])
                        S.add("dve", lambda e, Dt=Dt, E2=E2: e.tensor_scalar(out=E2.t[0:64, :], in0=Dt.t[0:64, :], scalar1=-1.0, scalar2=0.0, op0=ALU.mult, op1=ALU.min),
                              reads=[Dt.b], writes=[E2.b])
                        S.add("act", lambda e, E1=E1: e.activation(out=E1.t[0:64, :], in_=E1.t[0:64, :], func=AF.Exp), reads=[E1.b], writes=[E1.b])
                        S.add("act", lambda e, E2=E2: e.activation(out=E2.t[0:64, :], in_=E2.t[0:64, :], func=AF.Exp), reads=[E2.b], writes=[E2.b])
                        decI, decS, decN = r512.next(), r512.next(), r512.next()
                        S.add("pool", lambda e, decI=decI, E1=E1, Ud=Ud: e.tensor_tensor(out=v3(decI.t), in0=v3(E1.t), in1=bc_h(Ud), op=ALU.mult),
                              reads=[E1.b, msk.b], writes=[decI.b])
                        S.add("pool", lambda e, decS=decS, E1=E1, Sd=Sd: e.tensor_tensor(out=v3(decS.t), in0=v3(E1.t), in1=bc_h(Sd), op=ALU.mult),
                              reads=[E1.b, msk.b], writes=[decS.b])
                        S.add("pool", lambda e, decN=decN, E2=E2, SdT=SdT: e.tensor_tensor(out=v3(decN.t), in0=v3(E2.t), in1=bc_h(SdT), op=ALU.mult),
                              reads=[E2.b, msk.b], writes=[decN.b])
                        eg = r512.next()
                        qg = bq.next()
                        S.add("act", lambda e, eg=eg: e.activation(out=eg.t[:], in_=B1.t[:], func=AF.Exp), reads=[B1.b], writes=[eg.b])
                        S.add("pool", lambda e, eg=eg, qg=qg, Qc=Qc: e.tensor_tensor(out=v3(qg.t, 128), in0=Qc, in1=v3(eg.t, 128), op=ALU.mult),
                              reads=[eg.b, QT_.b], writes=[qg.b])
                        S.add("sp", lambda e, qg=qg, d=d, c=c: e.dma_start(out=QgT_d[d, c], in_=v3(qg.t, 128)), reads=[qg.b], dma=True)
                        kbT = r512.next()
                        S.add("dve", lambda e, kbT=kbT, Kc=Kc: e.tensor_tensor(out=v3(kbT.t, 128), in0=Kc, in1=v3(B2.t, 128), op=ALU.mult),
                              reads=[B2.b, KT_.b], writes=[kbT.b])
                        for h in range(8):
                            S.add("pe", lambda e, h=h, Kc=Kc, kbT=kbT: e.matmul(B3.t[0:64, h * 64:(h + 1) * 64], lhsT=Kc[:, h, :], rhs=kbT.t[:, h * 64:(h + 1) * 64],
                                                                              start=True, stop=True), reads=[KT_.b, kbT.b], writes=[B3.b])
                        for h in range(8):
                            S.add("pe", lambda e, h=h, Kc=Kc, kbT=kbT: e.matmul(B0.t[0:64, h * 64:(h + 1) * 64], lhsT=kbT.t[:, h * 64:(h + 1) * 64], rhs=Kc[:, h, :],
                                                                              start=True, stop=True), reads=[KT_.b, kbT.b], writes=[B0.b])
                        AT, AN = batn.next(), batn.next()
                        S.add("dve", lambda e, AT=AT, decS=decS: e.scalar_tensor_tensor(out=AT.t[:], in0=B3.t[0:64, :], scalar=-1.0, in1=decS.t[0:64, :],
                                                                                    op0=ALU.mult, op1=ALU.mult), reads=[B3.b, decS.b], writes=[AT.b])
                        S.add("dve", lambda e, AN=AN, decN=decN: e.scalar_tensor_tensor(out=AN.t[:], in0=B0.t[0:64, :], scalar=-1.0, in1=decN.t[0:64, :],
                                                                                    op0=ALU.mult, op1=ALU.mult), reads=[B0.b, decN.b], writes=[AN.b])
                        def lvm(lv, tr):
                            k_ = 2 * lv + (tr if d == 0 else 1 - tr)
                            return bc_h(lvm_t.t[:, k_, :])
                        TN, TT = b512.next(), b512.next()
                        for (dst_, src_, tr) in ((TN, AN, 0), (TT, AT, 1)):
                            mk0 = lvm(0, tr)
                            S.add("pool", lambda e, dst_=dst_, src_=src_, mk0=mk0: e.tensor_tensor(out=v3(dst_.t), in0=v3(src_.t), in1=mk0, op=ALU.mult),
                                  reads=[src_.b, lvm_t.b], writes=[dst_.b])
                            S.add("pool", lambda e, dst_=dst_: e.tensor_tensor(out=v3(dst_.t), in0=v3(dst_.t), in1=bc_h(I64), op=ALU.add),
                                  reads=[dst_.b, msk.b], writes=[dst_.b])
                        for lv in range(1, 6):
                            LoN, LoT, M1, M2, TT2 = b512.next(), b512.next(), b512.next(), b512.next(), b512.next()
                            mkn, mkt = lvm(lv, 0), lvm(lv, 1)
                            S.add("pool", lambda e, LoN=LoN, AN=AN, mkn=mkn: e.tensor_tensor(out=v3(LoN.t), in0=v3(AN.t), in1=mkn, op=ALU.mult),
                                  reads=[AN.b, lvm_t.b], writes=[LoN.b])
                            S.add("pool", lambda e, LoT=LoT, AT=AT, mkt=mkt: e.tensor_tensor(out=v3(LoT.t), in0=v3(AT.t), in1=mkt, op=ALU.mult),
                                  reads=[AT.b, lvm_t.b], writes=[LoT.b])
                            for h in range(8):
                                hs_ = slice(h * 64, (h + 1) * 64)
                                S.add("pe", lambda e, hs_=hs_, LoN=LoN, TT=TT: e.matmul(B1.t[0:64, hs_], lhsT=LoN.t[:, hs_], rhs=TT.t[:, hs_], start=True, stop=True),
                                      reads=[LoN.b, TT.b], writes=[B1.b])
                            S.add("act", lambda e, M1=M1: e.activation(out=M1.t[:], in_=B1.t[0:64, :], func=AF.Identity), reads=[B1.b], writes=[M1.b])
                            if lv < 5:
                                for h in range(8):
                                    hs_ = slice(h * 64, (h + 1) * 64)
                                    S.add("pe", lambda e, hs_=hs_, LoT=LoT, TN=TN: e.matmul(B2.t[0:64, hs_], lhsT=LoT.t[:, hs_], rhs=TN.t[:, hs_], start=True, stop=True),
                                          reads=[LoT.b, TN.b], writes=[B2.b])
                                S.add("act", lambda e, M2=M2: e.activation(out=M2.t[:], in_=B2.t[0:64, :], func=AF.Identity), reads=[B2.b], writes=[M2.b])
                            for h in range(8):
                                hs_ = slice(h * 64, (h + 1) * 64)
                                S.add("pe", lambda e, hs_=hs_, TN=TN, M1=M1: e.matmul(B3.t[0:64, hs_], lhsT=TN.t[:, hs_], rhs=M1.t[:, hs_], start=True, stop=True),
                                      reads=[TN.b, M1.b], writes=[B3.b])
                            S.add("dve", lambda e, TT2=TT2, TT=TT: e.tensor_tensor(out=TT2.t[:], in0=B3.t[0:64, :], in1=TT.t[:], op=ALU.add),
                                  reads=[B3.b, TT.b], writes=[TT2.b])
                            if lv < 5:
                                TN2 = b512.next()
                                for h in range(8):
                                    hs_ = slice(h * 64, (h + 1) * 64)
                                    S.add("pe", lambda e, hs_=hs_, TT=TT, M2=M2: e.matmul(B0.t[0:64, hs_], lhsT=TT.t[:, hs_], rhs=M2.t[:, hs_], start=True, stop=True),
                                          reads=[TT.b, M2.b], writes=[B0.b])
                                S.add("dve", lambda e, TN2=TN2, TN=TN: e.tensor_tensor(out=TN2.t[:], in0=B0.t[0:64, :], in1=TN.t[:], op=ALU.add),
                                      reads=[B0.b, TN.b], writes=[TN2.b])
                                TN = TN2
                            TT = TT2
                        P = TT
                        ato = b512.next()
                        S.add("pool", lambda e, ato=ato, atraw=atraw, decI=decI: e.tensor_tensor(out=ato.t[:], in0=atraw.t[0:64, :], in1=decI.t[0:64, :], op=ALU.mult),
                              reads=[atraw.b, decI.b], writes=[ato.b])
                        S.add("sp", lambda e, ato=ato, d=d, c=c: e.dma_start(out=AT_d[d, c], in_=v3(ato.t)), reads=[ato.b], dma=True)
                        Vb, KBg, Kdd = b1k.next(), b1k.next(), b1k.next()
                        S.add("pool", lambda e, Vb=Vb, b_=b_: e.tensor_tensor(out=Vb.t[:], in0=Vtm.t[:], in1=b_.unsqueeze(2).broadcast_to([64, 8, 128]), op=ALU.mult),
                              reads=[Vtm.b, ball.b], writes=[Vb.b])
                        S.add("pool", lambda e, KBg=KBg, bg=bg: e.tensor_tensor(out=KBg.t[:], in0=Ktm.t[:], in1=bg.t[0:64, :].unsqueeze(2).broadcast_to([64, 8, 128]), op=ALU.mult),
                              reads=[Ktm.b, bg.b], writes=[KBg.b])
                        S.add("pool", lambda e, Kdd=Kdd, kds=kds: e.tensor_tensor(out=Kdd.t[:], in0=Ktm.t[:], in1=kds.t[0:64, :].unsqueeze(2).broadcast_to([64, 8, 128]), op=ALU.mult),
                              reads=[Ktm.b, kds.b], writes=[Kdd.b])
                        S.add("sp", lambda e, Kdd=Kdd, d=d, c=c: e.dma_start(out=Kd_d[d, c], in_=Kdd.t[:]), reads=[Kdd.b], dma=True)
                        for h in range(8):
                            pb = (B1, B2)[h // 4]
                            S.add("pe", lambda e, pb=pb, h=h, P=P, Vb=Vb: e.matmul(pb.t[0:64, (h % 4) * 128:(h % 4 + 1) * 128], lhsT=P.t[:, h * 64:(h + 1) * 64], rhs=Vb.t[:, h, :],
                                                                              start=True, stop=True), reads=[P.b, Vb.b], writes=[pb.b])
                        uo = u1k.next()
                        for hb_ in range(2):
                            S.add("act", lambda e, uo=uo, hb_=hb_: e.activation(out=uo.t[:, hb_ * 4:(hb_ + 1) * 4, :],
                                                                           in_=(B1, B2)[hb_].t[0:64, :].rearrange("p (h d) -> p h d", d=128), func=AF.Identity),
                                  reads=[(B1, B2)[hb_].b], writes=[uo.b])
                        S.add("sp", lambda e, uo=uo, d=d, c=c: e.dma_start(out=u_d[d, c], in_=uo.t[:]), reads=[uo.b], dma=True)
                        for h in range(8):
                            S.add("pe", lambda e, h=h, P=P, KBg=KBg: e.matmul(B3.t[:, h * 64:(h + 1) * 64], lhsT=KBg.t[:, h, :], rhs=P.t[:, h * 64:(h + 1) * 64],
                                                                            start=True, stop=True), reads=[P.b, KBg.b], writes=[B3.b])
                        wo = bq.next()
                        S.add("dve", lambda e, wo=wo: e.tensor_copy(out=wo.t[:], in_=B3.t[:]), reads=[B3.b], writes=[wo.b])
                        S.add("sp", lambda e, wo=wo, d=d, c=c: e.dma_start(out=wT_d[d, c], in_=v3(wo.t, 128)), reads=[wo.b], dma=True)
                    for d_ in range(2):
                        unit(d_)

                for c_ in range(NCH):
                    chunk_body(c_)
                S.end_stage(resched=True)

        def dn_scan(l, i):
            with contextlib.ExitStack() as st:
                Sf = [sb(st, "S", [128, 8, 128]) for _ in range(2)]
                Sbf = [sb(st, "Sbf", [128, 8, 128], BF16) for _ in range(2)]
                uL = [Rot(st, "uL", [64, 8, 128], F32, 2) for _ in range(2)]
                wL = [Rot(st, "wL", [128, 8, 64], BF16, 2) for _ in range(2)]
                qL = [Rot(st, "qL", [128, 8, 64], BF16, 2) for _ in range(2)]
                aL = [Rot(st, "aL", [128, 8, 64], BF16, 2) for _ in range(2)]
                kL = [Rot(st, "kL", [64, 8, 128], BF16, 2) for _ in range(2)]
                eL = [Rot(st, "eL", [128, 8], F32, 2) for _ in range(2)]
                vn = [Rot(st, "vn", [128, 8, 128], BF16, 2) for _ in range(2)]
                for d in range(2):
                    for t_ in aL[d].ts + vn[d].ts:
                        S.add("pool", lambda e, t_=t_: e.memset(t_.t[:], 0.0), writes=[t_.b])
                oS = [Rot(st, "oS", [128, 8, 64], F32, 2) for _ in range(2)]
                seqs = [(0, 64, None)] + [(64 + 4 * pi, 4, pi) for pi in range(4)]
                for (c0, nch, pi) in seqs:
                    for d in range(2):
                        if pi is None:
                            S.add("sp", lambda e, d=d: e.dma_start(out=Sf[d].t[:], in_=sf0[d, i]), writes=[Sf[d].b], dma=True)
                        else:
                            S.add("pool", lambda e, d=d: e.memset(Sf[d].t[:], 0.0), writes=[Sf[d].b])
                        S.add("act", lambda e, d=d: e.activation(out=Sbf[d].t[:], in_=Sf[d].t[:], func=AF.Identity), reads=[Sf[d].b], writes=[Sbf[d].b])
                    for step in range(nch):
                        for d in range(2):
                            c = c0 + step if d == 0 else c0 + nch - 1 - step
                            sF, sB = Sf[d], Sbf[d]
                            u_, w_, q_, a_, k_, e_ = uL[d].next(), wL[d].next(), qL[d].next(), aL[d].next(), kL[d].next(), eL[d].next()
                            q1 = "sp" if d == 0 else "act"
                            for (tile_, src) in ((u_, u_d[d, c]), (w_, wT_d[d, c]), (q_, QgT_d[d, c]), (a_, AT_d[d, c]), (k_, Kd_d[d, c]), (e_, egl_d[d, c])):
                                np_ = src.shape[0]
                                S.add("sp", lambda e, tile_=tile_, src=src, np_=np_: e.dma_start(out=tile_.t[0:np_], in_=src), writes=[tile_.b], dma=True)
                            pw = (PS[0], PS[1]) if d == 0 else (PS[4], PS[5])
                            po = PS[2] if d == 0 else PS[6]
                            for h in range(8):
                                pb = pw[h // 4]
                                S.add("pe", lambda e, pb=pb, h=h, w_=w_, sB=sB: e.matmul(pb.t[0:64, (h % 4) * 128:(h % 4 + 1) * 128], lhsT=w_.t[:, h, :], rhs=sB.t[:, h, :],
                                                                                    start=True, stop=True), reads=[w_.b, sB.b], writes=[pb.b])
                            v_ = vn[d].next()
                            for hb_ in range(2):
                                S.add("dve", lambda e, v_=v_, u_=u_, hb_=hb_, pw=pw: e.tensor_tensor(
                                    out=v_.t[0:64, hb_ * 4:(hb_ + 1) * 4, :], in0=u_.t[:, hb_ * 4:(hb_ + 1) * 4, :],
                                    in1=pw[hb_].t[0:64, :].rearrange("p (h d) -> p h d", d=128), op=ALU.subtract),
                                    reads=[u_.b, pw[hb_].b], writes=[v_.b])
                            for h in range(8):
                                S.add("pe", lambda e, po=po, h=h, q_=q_, sB=sB: e.matmul(po.t[:, h * 64:(h + 1) * 64], lhsT=sB.t[:, h, :], rhs=q_.t[:, h, :],
                                                                                    start=True, stop=False), reads=[q_.b, sB.b], writes=[po.b])
                                S.add("pe", lambda e, po=po, h=h, a_=a_, v_=v_: e.matmul(po.t[:, h * 64:(h + 1) * 64], lhsT=v_.t[:, h, :], rhs=a_.t[:, h, :],
                                                                                    start=False, stop=True), reads=[a_.b, v_.b], writes=[po.b])
                            o_ = oS[d].next()
                            S.add("act", lambda e, o_=o_, po=po: e.activation(out=o_.t[:], in_=po.t[:].rearrange("p (h i) -> p h i", i=64), func=AF.Identity),
                                  reads=[po.b], writes=[o_.b])
                            S.add("sp", lambda e, o_=o_, d=d, c=c: e.dma_start(out=oT_d[d, :, :, c * 64:(c + 1) * 64], in_=o_.t[:]), reads=[o_.b], dma=True)
                            for h in range(8):
                                pb = pw[h // 4]
                                S.add("pe", lambda e, pb=pb, h=h, k_=k_, v_=v_: e.matmul(pb.t[:, (h % 4) * 128:(h % 4 + 1) * 128], lhsT=k_.t[:, h, :], rhs=v_.t[0:64, h, :],
                                                                                    start=True, stop=True), reads=[k_.b, v_.b], writes=[pb.b])
                            S.add("pool", lambda e, sF=sF, e_=e_: e.tensor_tensor(out=sF.t[:], in0=sF.t[:], in1=e_.t[:].unsqueeze(2).broadcast_to([128, 8, 128]), op=ALU.mult),
                                  reads=[sF.b, e_.b], writes=[sF.b])
                            for hb_ in range(2):
                                S.add("dve", lambda e, sF=sF, hb_=hb_, pw=pw: e.tensor_tensor(
                                    out=sF.t[:, hb_ * 4:(hb_ + 1) * 4, :], in0=sF.t[:, hb_ * 4:(hb_ + 1) * 4, :],
                                    in1=pw[hb_].t[:].rearrange("p (h d) -> p h d", d=128), op=ALU.add), reads=[sF.b, pw[hb_].b], writes=[sF.b])
                            S.add("act", lambda e, sF=sF, sB=sB: e.activation(out=sB.t[:], in_=sF.t[:], func=AF.Identity), reads=[sF.b], writes=[sB.b])
                    if pi is not None:
                        for d in range(2):
                            S.add("sp", lambda e, d=d, pi=pi: e.dma_start(out=nst[d, i, pi], in_=Sf[d].t[:]), reads=[Sf[d].b], dma=True)
                S.end_stage(resched=True)

        def dn_final(l, i):
            with contextlib.ExitStack() as st:
                W = sb(st, "dwout", [128, 8, 1024], BF16)
                gn2 = sb(st, "dng", [128, 2])
                OF = Rot(st, "OF", [128, 8, 512], F32, 2)
                OB = Rot(st, "OB", [128, 8, 512], F32, 2)
                ZZ = Rot(st, "ZZ", [128, 8, 512], F32, 2)
                XX = Rot(st, "XX", [128, 8, 512], F32, 2)
                OG = Rot(st, "OG", [128, 8, 512], BF16, 2)
                SQ = Rot(st, "fsq", [128, 512], F32, 4)
                RS = Rot(st, "frs", [128, 512], F32, 4)
                for kc2 in range(4):
                    S.add("pool", lambda e, kc2=kc2: e.dma_start(out=W.t[:, 2 * kc2:2 * kc2 + 2, :], in_=dwout[i, :, 2 * kc2:2 * kc2 + 2, :],
                                                               max_dma_last_dim=4096), writes=[W.b], dma=True)
                S.add("sp", lambda e: e.dma_start(out=gn2.t[:], in_=dng), writes=[gn2.b], dma=True)
                for tt in range(NTOK // 512):
                    which = 0 if tt < 8 else 1
                    tok = slice(tt * 512, (tt + 1) * 512)
                    of_, ob_, zz, xx, og = OF.next(), OB.next(), ZZ.next(), XX.next(), OG.next()
                    S.add("sp", lambda e, of_=of_, tok=tok: e.dma_start(out=of_.t[:], in_=oT_d[0, :, :, tok]), writes=[of_.b], dma=True)
                    S.add("sp", lambda e, ob_=ob_, tok=tok: e.dma_start(out=ob_.t[:], in_=oT_d[1, :, :, tok]), writes=[ob_.b], dma=True)
                    S.add("sp", lambda e, zz=zz, tok=tok: e.dma_start(out=zz.t[:], in_=zT_d[:, :, tok]), writes=[zz.b], dma=True)
                    S.add("sp", lambda e, xx=xx, tok=tok: e.dma_start(out=xx.t[:], in_=xs[:, :, tok]), writes=[xx.b], dma=True)
                    S.add("pool", lambda e, of_=of_, ob_=ob_: e.tensor_tensor(out=of_.t[:], in0=of_.t[:], in1=ob_.t[:], op=ALU.add),
                          reads=[of_.b, ob_.b], writes=[of_.b])
                    S.add("act", lambda e, zz=zz: e.activation(out=zz.t[:], in_=zz.t[:], func=AF.Silu), reads=[zz.b], writes=[zz.b])
                    for h in range(8):
                        q_, r_ = SQ.next(), RS.next()
                        pb = PS[h % 4]
                        S.add("act", lambda e, q_=q_, of_=of_, h=h: e.activation(out=q_.t[:], in_=of_.t[:, h, :], func=AF.Square), reads=[of_.b], writes=[q_.b])
                        S.add("pe", lambda e, pb=pb, q_=q_: e.matmul(pb.t[:], lhsT=ones32.t[:], rhs=q_.t[:], start=True, stop=True),
                              reads=[ones32.b, q_.b], writes=[pb.b])
                        S.add("act", lambda e, pb=pb, r_=r_: e.activation(out=r_.t[:], in_=pb.t[:], func=AF.Sqrt, scale=1.0 / 128, bias=epsT.t[:, 0:1]),
                              reads=[pb.b, epsT.b], writes=[r_.b])
                        S.add("dve", lambda e, r_=r_: e.reciprocal(out=r_.t[:], in_=r_.t[:]), reads=[r_.b], writes=[r_.b])
                        S.add("dve", lambda e, r_=r_, of_=of_, h=h: e.scalar_tensor_tensor(out=r_.t[:], in0=of_.t[:, h, :], scalar=gn2.t[:, i:i + 1], in1=r_.t[:],
                                                                                      op0=ALU.mult, op1=ALU.mult), reads=[r_.b, of_.b, gn2.b], writes=[r_.b])
                        S.add("pool", lambda e, r_=r_, zz=zz, og=og, h=h: e.tensor_tensor(out=og.t[:, h, :], in0=r_.t[:], in1=zz.t[:, h, :], op=ALU.mult),
                              reads=[r_.b, zz.b], writes=[og.b])
                    for m in range(8):
                        pb = PS[4 + m % 4]
                        for f in range(8):
                            S.add("pe", lambda e, pb=pb, f=f, m=m, og=og: e.matmul(pb.t[:], lhsT=W.t[:, f, m * 128:(m + 1) * 128], rhs=og.t[:, f, :],
                                                                              start=(f == 0), stop=(f == 7)), reads=[W.b, og.b], writes=[pb.b])
                        S.add("dve", lambda e, pb=pb, m=m, xx=xx, which=which: e.scalar_tensor_tensor(
                            out=xx.t[:, m, :], in0=pb.t[:], scalar=hgT.t[:, 1, m, which:which + 1], in1=xx.t[:, m, :], op0=ALU.mult, op1=ALU.add),
                            reads=[pb.b, hgT.b, xx.b], writes=[xx.b])
                    S.add("sp", lambda e, xx=xx, tok=tok: e.dma_start(out=xs[:, :, tok], in_=xx.t[:]), reads=[xx.b], dma=True)
                S.end_stage(resched=True)

        def dn_mixer(l, i):
            nst_ = cfg.get("dn_stages", 5)
            for k_, fn in enumerate((dn_inproj, dn_conv, dn_chunks, dn_scan, dn_final)):
                if k_ < nst_:
                    fn(l, i)

        def ah_hyena_zero(l, i):
            with contextlib.ExitStack() as st:
                z = sb(st, "zz", [128, 4, 1024], BF16)
                S.add("dve", lambda e: e.memset(z.t[:], 0.0), writes=[z.b])
                for mt in range(NMT):
                    S.add("sp", lambda e, mt=mt: e.dma_start(out=ay_d[:, 4:8, mt * TM:(mt + 1) * TM], in_=z.t[:]), reads=[z.b], dma=True)
                S.end_stage()

        cur = xT
        layer_list = cfg.get("layers", None)
        layer_list = list(layer_list) if layer_list is not None else list(range(nlayers))
        do_ffn = cfg.get("ffn", True)

        def copy_stage(src, dst):
            for mt in range(NMT):
                S.add("sp", lambda e, mt=mt: e.dma_start(out=dst[:, :, mt * TM:(mt + 1) * TM], in_=src[:, :, mt * TM:(mt + 1) * TM]), dma=True)
            S.end_stage()

        for li, l in enumerate(layer_list):
            modulation_stage(l)
            if do_ffn:
                ffn_stage(l, 0, cur, xs)
            else:
                copy_stage(cur, xs)
            cur = xs
            if do_mix and l % 2 == 0:
                ah_inproj(l, l // 2)
                ah_attention(l, l // 2)
                if cfg.get("hyena", 1):
                    ah_hyena(l, l // 2)
                else:
                    ah_hyena_zero(l, l // 2)
                ah_outproj(l, l // 2)
            if do_mix and l % 2 == 1:
                dn_mixer(l, l // 2)
            last = (li == len(layer_list) - 1)
            if do_ffn:
                ffn_stage(l, 1, cur, yT if last else xs)
            elif last:
                copy_stage(xs, yT)
        print("stages", S.nstage, "ops", S.ninstr)
    return nc


def host_layout(inputs, core):
    f = lambda a: np.ascontiguousarray(a, dtype=np.float32)
    xs_ = np.asarray(inputs["x_sample"][core])
    xp_ = np.asarray(inputs["x_prompt"][4 * core:4 * core + 4]).reshape(1024, D)
    x = np.concatenate([xs_, xp_], axis=0)
    m = {}
    m["xT"] = f(x.T.reshape(8, 128, NTOK).transpose(1, 0, 2))
    cond = np.stack([np.asarray(inputs["c"][core]), np.asarray(inputs["c_ctx"])], axis=1)
    m["condT"] = f(cond.reshape(8, 128, 2).transpose(1, 0, 2))
    ck = np.asarray(inputs["cache_k"][core])
    ckT = ck.transpose(0, 3, 2, 1)
    m["ckT"] = f(np.concatenate([ckT, ckT], axis=1))
    cv = np.asarray(inputs["cache_v"][core]).reshape(2, 4, 128, 128)
    m["cvv"] = f(cv.transpose(0, 2, 1, 3))
    st_ = np.stack([np.asarray(inputs["state_fwd"][core]), np.asarray(inputs["state_bwd"][core])], axis=0)
    m["sf0"] = f(st_.transpose(0, 1, 3, 2, 4))
    return m


def shared_layout(inputs):
    f = lambda a: np.ascontiguousarray(a, dtype=np.float32)
    m = {}
    aw = np.asarray(inputs["ada_w"])
    m["ada_w"] = f(aw.reshape(DEPTH, 8, 128, 9, 1024).transpose(0, 3, 2, 1, 4))
    ab = np.asarray(inputs["ada_b"]).reshape(DEPTH, 72, 128).transpose(2, 0, 1)
    m["ada_b"] = f(np.repeat(ab[..., None], 2, axis=-1))
    g = np.asarray(inputs["norm_g"]).reshape(DEPTH, 3, 8, 128).transpose(3, 0, 1, 2)
    m["norm_g"] = f(np.repeat(g[..., None], 2, axis=-1))
    w13 = np.asarray(inputs["ffn_w13"]).reshape(DEPTH * 2, 8, 128, 2, NFC, 128)
    m["w13"] = f(w13.transpose(0, 4, 2, 1, 3, 5).reshape(DEPTH * 2, NFC, 128, 8, 256))
    w2 = np.asarray(inputs["ffn_w2"]).reshape(DEPTH * 2, NFC, 128, 8, 128)
    m["w2"] = f(w2.transpose(0, 3, 2, 1, 4))
    wi = np.asarray(inputs["mx_w_in"])
    wcat = np.concatenate([wi[:, :, 0:512], wi[:, :, 512:576], wi[:, :, 512:576], wi[:, :, 576:640], wi[:, :, 576:640],
                           wi[:, :, 768:2304], wi[:, :, 640:768]], axis=2)
    m["win"] = f(wcat.reshape(2, 8, 128, 2432).transpose(0, 2, 1, 3))
    wo = np.asarray(inputs["mx_w_out"])
    m["wout"] = f(wo.reshape(2, 8, 128, 1024).transpose(0, 2, 1, 3))
    qn = np.asarray(inputs["q_norm"])
    kn = np.asarray(inputs["k_norm"])
    pidx = np.arange(128)
    m["qkn"] = f(np.stack([qn[:, pidx % 64].T, kn[:, pidx % 64].T], axis=-1))
    sk = np.asarray(inputs["attn_sink"])
    hidx = 2 * np.arange(4)[None, :] + (pidx // 64)[:, None]
    m["sinkT"] = f(sk[:, hidx].transpose(1, 0, 2))
    cw = np.asarray(inputs["hy_conv_w"]).reshape(2, 3, 12, 128)
    m["hcw"] = f(cw.transpose(3, 0, 2, 1))
    cb = np.asarray(inputs["hy_conv_b"]).reshape(2, 12, 128)
    m["hcb"] = f(cb.transpose(2, 0, 1))
    m["hw1"] = f(inputs["hy_w1"])
    m["hb1"] = f(np.stack([np.asarray(inputs["hy_b1"]).T, np.asarray(inputs["hy_freq1"]).T], axis=-1))
    m["hw2"] = f(inputs["hy_w2"])
    m["hb2"] = f(np.stack([np.asarray(inputs["hy_b2"]).T, np.asarray(inputs["hy_freq2"]).T], axis=-1))
    m["hw3"] = f(inputs["hy_w3"])
    hbz = np.asarray(inputs["hy_bias"]).reshape(2, 2, 4, 128)
    m["hbias"] = f(hbz.transpose(3, 0, 1, 2))
    dwi = np.asarray(inputs["dn_w_in"])
    m["dwin"] = f(dwi.reshape(2, 8, 128, 4128).transpose(0, 2, 1, 3))
    dwo = np.asarray(inputs["dn_w_out"])
    m["dwout"] = f(dwo.reshape(2, 8, 128, 1024).transpose(0, 2, 1, 3))
    dc = np.asarray(inputs["dn_conv_w"]).reshape(2, 3, 24, 128)
    m["dcw"] = f(dc.transpose(3, 0, 2, 1))
    prm = np.concatenate([np.asarray(inputs["dn_a_log"]).reshape(2, 16), np.asarray(inputs["dn_dt_bias"]).reshape(2, 16)], axis=1)
    m["dprm"] = f(np.broadcast_to(prm[None], (128, 2, 32)))
    m["dng"] = f(np.asarray(inputs["dn_norm_g"]).T)
    m.update(const_tables())
    return m


_CONST = {}


def const_tables():
    if _CONST:
        return _CONST
    f = lambda a: np.ascontiguousarray(a, dtype=np.float32)
    pidx = np.arange(128)
    a = (pidx % 64) % 32
    inv = (np.float32(10000.0) ** (-np.arange(0, 32, 2, dtype=np.float32) / np.float32(32))).astype(np.float32)
    t = np.arange(4096)
    r = (t // 64).astype(np.float32)
    col = (t % 64).astype(np.float32)
    ang = np.where((a < 16)[:, None], r[None, :] * inv[a % 16][:, None], col[None, :] * inv[a % 16][:, None]).astype(np.float32)
    _CONST["cosT"] = f(np.cos(ang))
    _CONST["sinT"] = f(np.sin(ang))
    rot = np.zeros((128, 128), np.float32)
    for d_out in range(128):
        dd = d_out % 64
        base = d_out - dd
        if dd < 32:
            rot[base + dd + 32, d_out] = -1.0
        else:
            rot[base + dd - 32, d_out] = 1.0
    _CONST["rotT"] = rot
    blk = np.zeros((128, 128), np.float32)
    blk[:64, :64] = 1.0
    blk[64:, 64:] = 1.0
    _CONST["blk1"] = blk
    ko = np.arange(128)[:, None]
    qo = np.arange(128)[None, :]
    _CONST["mprev"] = f(ko >= qo)
    _CONST["mnext"] = f(ko <= qo)
    _CONST["ident"] = np.eye(128, dtype=np.float32)
    pp = np.arange(64)[:, None]
    ff = np.arange(64)[None, :]
    _CONST["dmask"] = f(np.stack([pp <= ff, pp >= ff, pp < ff, pp > ff, pp == ff], axis=1))
    lvm = []
    for lv in range(6):
        b_ = 2 ** lv
        mn = (pp // (2 * b_) == ff // (2 * b_)) & (pp % (2 * b_) >= b_) & (ff % (2 * b_) < b_)
        lvm.append(mn)
        lvm.append(mn.T)
    _CONST["lvmask"] = f(np.stack(lvm, axis=1))
    HY_MIN = math.log(1e-2) / 1.5
    HY_MAX = math.log(1e-2) / 0.3
    dl = np.abs(np.linspace(HY_MIN, HY_MAX, 2048, dtype=np.float32)).astype(np.float32)
    _CONST["deltas"] = f(np.broadcast_to(dl[None, :], (128, 2048)))
    for L in (4096, 256):
        N = 2 * L
        nb = L // 128
        TT = min(512, L)
        k = np.arange(L, dtype=np.int64)
        t = np.arange(L, dtype=np.int64)
        mm = ((2 * k[:, None] + 1) * t[None, :]) % (2 * N)
        ang = mm.astype(np.float64) * (np.pi / N)
        Ckt = np.cos(ang).astype(np.float32)
        Skt = np.sin(ang).astype(np.float32)
        del mm, ang
        F = np.empty((2, nb, 128, nb, 128), ml_dtypes.bfloat16)
        I = np.empty((2, L // TT, 128, nb, TT), ml_dtypes.bfloat16)
        for ci, tab in enumerate((Ckt, Skt)):
            t4 = tab.reshape(nb, 128, nb, 128)
            F[ci] = t4.transpose(0, 3, 2, 1).astype(ml_dtypes.bfloat16)
            t5 = tab.reshape(nb, 128, L // TT, TT)
            I[ci] = t5.transpose(2, 1, 0, 3).astype(ml_dtypes.bfloat16)
        _CONST["dftF%d" % L] = F
        _CONST["dftI%d" % L] = I
        tt_ = np.linspace(0.0, 1.0, L, dtype=np.float32)
        w = (np.float32(2 * math.pi) * np.arange(L, dtype=np.float32) / np.float32(L)).astype(np.float32)
        fr = np.linspace(1e-4, 15.0, 16, dtype=np.float32)
        fw = (fr[None, :] * w[:, None]).astype(np.float32)
        z = np.concatenate([tt_[:, None], np.cos(fw), -np.sin(fw)], axis=-1).astype(np.float32)
        _CONST["zfeat%d" % L] = f(z.T)
        _CONST["tlag%d" % L] = f(-tt_.reshape(nb, 128).T)
    return _CONST


_CACHE = {}


def run(inputs, cfg, ncores=8, cores=None):
    key = tuple(sorted(cfg.items()))
    if key not in _CACHE:
        _CACHE[key] = build_program(cfg)
    nc = _CACHE[key]
    shared = shared_layout(inputs)
    in_maps = []
    for core in (cores if cores is not None else range(ncores)):
        m = dict(shared)
        m.update(host_layout(inputs, core))
        in_maps.append(m)
    res = run_bass_kernel_spmd(nc, in_maps, core_ids=list(range(ncores)))
    return res


def assemble(res, ncores=8):
    yp = np.zeros((32, 256, D), np.float32)
    ys = np.zeros((8, 4096, D), np.float32)
    nk = np.zeros((32, 2, 256, 2, 64), np.float32)
    nv = np.zeros((32, 2, 256, 2, 64), np.float32)
    nsf = np.zeros((32, 2, 8, 128, 128), np.float32)
    nsb = np.zeros((32, 2, 8, 128, 128), np.float32)
    for core in range(ncores):
        r = res.results[core]
        y = r["yT"].transpose(1, 0, 2).reshape(D, NTOK).T
        ys[core] = y[:4096]
        yp[4 * core:4 * core + 4] = y[4096:].reshape(4, 256, D)
        k = r["newk"].reshape(2, 2, 64, 4, 256)
        nk[4 * core:4 * core + 4] = k.transpose(3, 0, 4, 1, 2)
        v = r["newv"].reshape(2, 4, 256, 2, 64)
        nv[4 * core:4 * core + 4] = v.transpose(1, 0, 2, 3, 4)
        s_ = r["nst"]
        nsf[4 * core:4 * core + 4] = s_[0].transpose(1, 0, 3, 2, 4)
        nsb[4 * core:4 * core + 4] = s_[1].transpose(1, 0, 3, 2, 4)
    return yp, ys, nk, nv, nsf, nsb


def kernel(**inputs):
    res = run(inputs, {})
    return assemble(res)
```

```python
import contextlib
import math
import numpy as np
import ml_dtypes
import concourse.bass as bass
import concourse.mybir as mybir
from concourse.bass_utils import run_bass_kernel_spmd

F32 = mybir.dt.float32
BF16 = mybir.dt.bfloat16
AF = mybir.ActivationFunctionType
ALU = mybir.AluOpType

D = 1024
NTOK = 5120
TM = 1024
NMT = NTOK // TM
DFF = 2816
NFC = DFF // 128
DEPTH = 4
EPS = 1e-6

ENGS = ("pe", "dve", "act", "pool", "sp")
NDMASEM = 8


class Buf:
    __slots__ = ("name", "w", "rs", "excl")

    def __init__(self, name="", excl=False):
        self.name = name
        self.w = None
        self.rs = []
        self.excl = excl


class Op:
    __slots__ = ("eng", "fn", "deps", "dma", "sig", "cnt", "semi", "use", "gid", "cost")


class Sched:
    def __init__(self, nc, st):
        self.nc = nc
        self.csem = {e: st.enter_context(nc.semaphore("c_" + e)) for e in ENGS}
        self.dsem = {e: [st.enter_context(nc.semaphore("d_%s%d" % (e, i))) for i in range(NDMASEM)]
                     for e in ("sp", "pool", "act")}
        self.cnt = {e: 0 for e in ENGS}
        self.ndma = {e: 0 for e in ENGS}
        self.ops = []
        self.bufs = []
        self.nstage = 0
        self.ninstr = 0
        self.xlat = 2.0

    def buf(self, name=""):
        return Buf(name)

    COST = {"pe": 0.12, "dve": 0.6, "act": 0.7, "pool": 0.9, "sp": 2.5}

    def add(self, eng, fn, reads=(), writes=(), dma=False, cost=None):
        op = Op()
        op.cost = cost if cost is not None else (2.5 if dma else self.COST[eng])
        op.eng = eng
        op.fn = fn
        op.dma = dma
        op.sig = False
        op.gid = len(self.ops)
        deps = set()
        ex = [b for b in reads if b.excl]
        if ex:
            reads = [b for b in reads if not b.excl]
            writes = list(writes) + [b for b in ex if b not in writes]
        for b in reads:
            if b.w is not None:
                deps.add(b.w)
        for b in writes:
            if b.w is not None:
                deps.add(b.w)
            deps.update(b.rs)
        deps.discard(op.gid)
        op.deps = deps
        for b in reads:
            b.rs.append(op.gid)
            self.bufs.append(b)
        for b in writes:
            b.w = op.gid
            b.rs = []
            self.bufs.append(b)
        self.ops.append(op)
        return op

    def _resched(self, ops):
        import heapq
        n = len(ops)
        succ = [[] for _ in range(n)]
        indeg = [0] * n
        for op in ops:
            indeg[op.gid] = len(op.deps)
            for d in op.deps:
                succ[d].append(op.gid)
        ready_t = [0.0] * n
        fin = [0.0] * n
        heaps = {e: [] for e in ENGS}
        free = {e: 0.0 for e in ENGS}
        for op in ops:
            if indeg[op.gid] == 0:
                heapq.heappush(heaps[op.eng], (0.0, op.gid))
        order = []
        while len(order) < n:
            best = None
            for e in ENGS:
                h = heaps[e]
                if not h:
                    continue
                rt, gid = h[0]
                st_ = max(free[e], rt)
                if best is None or (st_, gid) < (best[0], best[1]):
                    best = (st_, gid, e)
            st_, gid, e = best
            heapq.heappop(heaps[e])
            op = ops[gid]
            if op.dma:
                free[e] = st_ + 0.15
                fin[gid] = st_ + op.cost
            else:
                free[e] = st_ + op.cost
                fin[gid] = st_ + op.cost
            order.append(gid)
            for s_ in succ[gid]:
                so = ops[s_]
                lat = 0.05 if (so.eng == op.eng and not op.dma) else self.xlat
                ready_t[s_] = max(ready_t[s_], fin[gid] + lat)
                indeg[s_] -= 1
                if indeg[s_] == 0:
                    heapq.heappush(heaps[so.eng], (ready_t[s_], s_))
        return order

    def end_stage(self, resched=False):
        nc = self.nc
        ops = self.ops
        if resched and len(ops) > 2:
            order = self._resched(ops)
            remap = {g: k for k, g in enumerate(order)}
            ops = [ops[g] for g in order]
            for k, op in enumerate(ops):
                op.gid = k
                op.deps = {remap[d] for d in op.deps}
        for op in ops:
            if op.dma:
                k = self.ndma[op.eng]
                self.ndma[op.eng] = k + 1
                op.semi = k % NDMASEM
                op.use = k // NDMASEM + 1
        for op in ops:
            nd = set()
            for d in op.deps:
                p = ops[d]
                if (not p.dma) and (not op.dma) and p.eng == "pe" and op.eng == "pe":
                    continue
                nd.add(d)
                p.sig = True
            op.deps = nd
        for op in ops:
            if (not op.dma) and op.sig:
                self.cnt[op.eng] += 1
                op.cnt = self.cnt[op.eng]
        per = {e: [op for op in ops if op.eng == e] for e in ENGS}
        csem, dsem = self.csem, self.dsem
        self.ninstr += len(ops)
        with nc.Block() as block:
            def gen(ename):
                def body(eng):
                    seen_c = {}
                    seen_d = {}
                    for op in per[ename]:
                        need_c = {}
                        need_d = {}
                        for d in op.deps:
                            p = ops[d]
                            if p.dma:
                                key = (p.eng, p.semi)
                                need_d[key] = max(need_d.get(key, 0), 16 * p.use)
                            else:
                                need_c[p.eng] = max(need_c.get(p.eng, 0), p.cnt)
                        if op.dma and op.use > 1:
                            key = (op.eng, op.semi)
                            need_d[key] = max(need_d.get(key, 0), 16 * (op.use - 1))
                        for e, v in need_c.items():
                            if v > seen_c.get(e, 0):
                                eng.wait_ge(csem[e], v)
                                seen_c[e] = v
                        for key, v in need_d.items():
                            if v > seen_d.get(key, 0):
                                eng.wait_ge(dsem[key[0]][key[1]], v)
                                seen_d[key] = v
                        ins = op.fn(eng)
                        if op.dma:
                            ins.then_inc(dsem[op.eng][op.semi], 16)
                        elif op.sig:
                            ins.then_inc(csem[op.eng], 1)
                    last = {}
                    for op in per[ename]:
                        if op.dma:
                            last[op.semi] = op.use
                    for semi, use in last.items():
                        eng.wait_ge(dsem[ename][semi], 16 * use)
                return body

            block.tensor(gen("pe"))
            block.vector(gen("dve"))
            block.scalar(gen("act"))
            block.gpsimd(gen("pool"))
            block.sync(gen("sp"))
        for b in self.bufs:
            b.w = None
            b.rs = []
        self.bufs = []
        self.ops = []
        self.nstage += 1


class T:
    def __init__(self, t, name=""):
        self.t = t
        self.b = Buf(name)


def build_program(cfg):
    nlayers = cfg.get("nlayers", DEPTH)
    do_mix = cfg.get("mixers", True)
    nc = bass.Bass("TRN2", target_bir_lowering=False)

    def din(name, shape, dt=F32):
        return nc.dram_tensor(name, list(shape), dt, kind="ExternalInput").ap()

    def dout(name, shape, dt=F32):
        return nc.dram_tensor(name, list(shape), dt, kind="ExternalOutput").ap()

    dbg = cfg.get("debug", ())

    def dscr(name, shape, dt=F32):
        kind = "ExternalOutput" if name in dbg else "Internal"
        return nc.dram_tensor(name, list(shape), dt, kind=kind).ap()

    xT = din("xT", [128, 8, NTOK])
    condT = din("condT", [128, 8, 2])
    ada_w = din("ada_w", [DEPTH, 9, 128, 8, 1024])
    ada_b = din("ada_b", [128, DEPTH, 72, 2])
    norm_g = din("norm_g", [128, DEPTH, 3, 8, 2])
    w13 = din("w13", [DEPTH * 2, NFC, 128, 8, 256])
    w2 = din("w2", [DEPTH * 2, 8, 128, NFC, 128])
    yT = dout("yT", [128, 8, NTOK])
    xs = dscr("xs", [128, 8, NTOK])
    win = din("win", [2, 128, 8, 2432])
    wout = din("wout", [2, 128, 8, 1024])
    qkn = din("qkn", [128, 2, 2])
    sinkT = din("sinkT", [128, 2, 4])
    cosT_d = din("cosT", [128, 4096])
    sinT_d = din("sinT", [128, 4096])
    rotT_d = din("rotT", [128, 128])
    blk1_d = din("blk1", [128, 128])
    mprev_d = din("mprev", [128, 128])
    mnext_d = din("mnext", [128, 128])
    ckT = din("ckT", [2, 128, 2, 512])
    cvv = din("cvv", [2, 128, 4, 128])
    newk = dout("newk", [2, 2, 64, 1024])
    newv = dout("newv", [2, 1024, 128])
    qT_d = dscr("qT_d", [128, 4, NTOK])
    kT_d = dscr("kT_d", [128, 2, NTOK])
    u3T_d = dscr("u3T_d", [128, 12, NTOK])
    v_d = dscr("v_d", [NTOK, 128])
    ay_d = dscr("ay_d", [128, 8, NTOK], BF16)
    hcw = din("hcw", [128, 2, 12, 3])
    hcb = din("hcb", [128, 2, 12])
    hw1 = din("hw1", [2, 33, 64])
    hb1 = din("hb1", [64, 2, 2])
    hw2 = din("hw2", [2, 64, 64])
    hb2 = din("hb2", [64, 2, 2])
    hw3 = din("hw3", [2, 64, 2048])
    hbias = din("hbias", [128, 2, 2, 4])
    ident_d = din("ident", [128, 128])
    deltas_d = din("deltas", [128, 2048])
    HG = {}
    for L_ in (4096, 256):
        nb_ = L_ // 128
        TT_ = min(512, L_)
        HG[L_] = dict(
            F=din("dftF%d" % L_, [2, nb_, 128, nb_, 128], BF16),
            I=din("dftI%d" % L_, [2, L_ // TT_, 128, nb_, TT_], BF16),
            zf=din("zfeat%d" % L_, [33, L_]),
            tl=din("tlag%d" % L_, [128, nb_]))
    uc_d = dscr("uc_d", [128, 12, NTOK])
    z1T_d = dscr("z1T_d", [128, 4, NTOK])
    hsd_d = dscr("hsd_d", [2, 32, 128, 1024], BF16)
    H_d = dscr("H_d", [2, 2, 32, 128, 512])
    dwin = din("dwin", [2, 128, 8, 4128])
    dwout = din("dwout", [2, 128, 8, 1024])
    dcw = din("dcw", [128, 2, 24, 3])
    dprm = din("dprm", [128, 2, 32])
    dng = din("dng", [128, 2])
    dmask_d = din("dmask", [64, 5, 64])
    lvmask_d = din("lvmask", [64, 12, 64])
    sf0 = din("sf0", [2, 2, 128, 8, 128])
    nst = dout("nst", [2, 2, 4, 128, 8, 128])
    qkvT_d = dscr("qkvT_d", [128, 24, NTOK])
    qkvn_d = dscr("qkvn_d", [128, 24, NTOK])
    zT_d = dscr("zT_d", [128, 8, NTOK])
    ba_d = dscr("ba_d", [NTOK, 32])
    NCH = NTOK // 64
    u_d = dscr("u_d", [2, NCH, 64, 8, 128])
    wT_d = dscr("wT_d", [2, NCH, 128, 8, 64], BF16)
    QgT_d = dscr("QgT_d", [2, NCH, 128, 8, 64], BF16)
    AT_d = dscr("AT_d", [2, NCH, 64, 8, 64], BF16)
    Kd_d = dscr("Kd_d", [2, NCH, 64, 8, 128], BF16)
    egl_d = dscr("egl_d", [2, NCH, 128, 8])
    oT_d = dscr("oT_d", [2, 128, 8, NTOK])

    with contextlib.ExitStack() as top:
        S = Sched(nc, top)
        S.xlat = cfg.get("xlat", 2.0)

        uid = [0]

        def sb(st, name, shape, dt=F32):
            uid[0] += 1
            name = "%s_%d" % (name, uid[0])
            return T(st.enter_context(nc.sbuf_tensor(name, list(shape), dt)), name)

        PS = [T(top.enter_context(nc.psum_tensor("ps%d" % i, [128, 512], F32)), "ps%d" % i) for i in range(8)]
        for p_ in PS:
            p_.b.excl = True
        ones32 = sb(top, "ones32", [128, 128])
        epsT = sb(top, "epsT", [128, 1])
        scond = sb(top, "scond", [128, 8, 2], BF16)
        modT = sb(top, "modT", [128, 72, 2])
        gsT = sb(top, "gsT", [128, 3, 8, 2])
        hgT = sb(top, "hgT", [128, 3, 8, 2])
        adab = sb(top, "adab", [128, DEPTH, 72, 2])
        ng = sb(top, "ng", [128, DEPTH, 3, 8, 2])

        with contextlib.ExitStack() as st:
            cnd = sb(st, "cnd", [128, 8, 2])
            S.add("dve", lambda e: e.memset(ones32.t[:], 1.0), writes=[ones32.b])
            S.add("dve", lambda e: e.memset(epsT.t[:], EPS), writes=[epsT.b])
            S.add("sp", lambda e: e.dma_start(out=cnd.t[:], in_=condT), writes=[cnd.b], dma=True)
            S.add("sp", lambda e: e.dma_start(out=adab.t[:], in_=ada_b), writes=[adab.b], dma=True)
            S.add("sp", lambda e: e.dma_start(out=ng.t[:], in_=norm_g), writes=[ng.b], dma=True)
            S.add("act", lambda e: e.activation(out=scond.t[:], in_=cnd.t[:], func=AF.Silu),
                  reads=[cnd.b], writes=[scond.b])
            S.end_stage()

        def modulation_stage(l):
            with contextlib.ExitStack() as st:
                wa = [sb(st, "wa%d" % i, [128, 8, 1024], BF16) for i in range(2)]
                for blk in range(9):
                    w = wa[blk % 2]
                    S.add("pool", lambda e, w=w, blk=blk: e.dma_start(out=w.t[:], in_=ada_w[l, blk]),
                          writes=[w.b], dma=True)
                    ps = PS[blk % 2]
                    for cc in range(8):
                        for kc in range(8):
                            S.add("pe", lambda e, w=w, ps=ps, cc=cc, kc=kc: e.matmul(
                                ps.t[:, cc * 2:cc * 2 + 2], lhsT=w.t[:, kc, cc * 128:(cc + 1) * 128],
                                rhs=scond.t[:, kc, :], start=(kc == 0), stop=(kc == 7)),
                                reads=[w.b, scond.b], writes=[ps.b])
                    S.add("dve", lambda e, ps=ps, blk=blk: e.tensor_tensor(
                        out=modT.t[:, blk * 8:(blk + 1) * 8, :],
                        in0=ps.t[:, 0:16].rearrange("p (a b) -> p a b", b=2),
                        in1=adab.t[:, l, blk * 8:(blk + 1) * 8, :], op=ALU.add),
                        reads=[ps.b, adab.b], writes=[modT.b])
                for j in range(3):
                    S.add("dve", lambda e, j=j: e.scalar_tensor_tensor(
                        out=gsT.t[:, j], in0=modT.t[:, (3 * j + 1) * 8:(3 * j + 2) * 8, :], scalar=1.0,
                        in1=ng.t[:, l, j], op0=ALU.add, op1=ALU.mult),
                        reads=[modT.b, ng.b], writes=[gsT.b])
                    S.add("dve", lambda e, j=j: e.tensor_scalar(
                        out=hgT.t[:, j], in0=modT.t[:, (3 * j + 2) * 8:(3 * j + 3) * 8, :],
                        scalar1=(1.0 if j == 1 else 0.5), scalar2=None, op0=ALU.mult),
                        reads=[modT.b], writes=[hgT.b])
                S.end_stage(resched=cfg.get("rs_x", True))

        def norm_mod(st_tiles, xb, hb, j, which, ps_pair):
            sq, rstd, tmp = st_tiles
            for c in range(8):
                q = sq[c % 2]
                S.add("act", lambda e, q=q, c=c: e.activation(out=q.t[:], in_=xb.t[:, c, :], func=AF.Square),
                      reads=[xb.b], writes=[q.b])
                for s in range(TM // 512):
                    S.add("pe", lambda e, q=q, c=c, s=s: e.matmul(
                        ps_pair[s].t[:], lhsT=ones32.t[:], rhs=q.t[:, s * 512:(s + 1) * 512],
                        start=(c == 0), stop=(c == 7)), reads=[q.b, ones32.b], writes=[ps_pair[s].b])
            for s in range(TM // 512):
                S.add("act", lambda e, s=s: e.activation(
                    out=rstd.t[:, s * 512:(s + 1) * 512], in_=ps_pair[s].t[:], func=AF.Sqrt,
                    scale=1.0 / D, bias=epsT.t[:, 0:1]), reads=[ps_pair[s].b, epsT.b], writes=[rstd.b])
            S.add("dve", lambda e: e.reciprocal(out=rstd.t[:], in_=rstd.t[:]), reads=[rstd.b], writes=[rstd.b])
            for c in range(8):
                tp = tmp[c % 2]
                S.add("dve", lambda e, tp=tp, c=c: e.tensor_tensor(
                    out=tp.t[:], in0=xb.t[:, c, :], in1=rstd.t[:], op=ALU.mult),
                    reads=[xb.b, rstd.b], writes=[tp.b])
                S.add("act", lambda e, tp=tp, c=c: e.activation(
                    out=hb.t[:, c, :], in_=tp.t[:], func=AF.Identity,
                    scale=gsT.t[:, j, c, which:which + 1], bias=modT.t[:, 3 * j * 8 + c, which:which + 1]),
                    reads=[tp.b, gsT.b, modT.b], writes=[hb.b])

        def ffn_stage(l, hf, src, dst):
            j = 0 if hf == 0 else 2
            lh = l * 2 + hf
            with contextlib.ExitStack() as st:
                X = [sb(st, "x%d" % i, [128, 8, TM]) for i in range(2)]
                Hh = [sb(st, "h%d" % i, [128, 8, TM], BF16) for i in range(2)]
                sq = [sb(st, "sq%d" % i, [128, TM]) for i in range(2)]
                tmp = [sb(st, "tmp%d" % i, [128, TM]) for i in range(2)]
                rstd = sb(st, "rstd", [128, TM])
                actT = sb(st, "actT", [128, NFC, TM], BF16)
                sg = [sb(st, "sg%d" % i, [128, 512]) for i in range(2)]
                wp = [sb(st, "wp%d" % i, [128, 8, 256], BF16) for i in range(4)]
                w2t = [sb(st, "w2t%d" % i, [128, NFC, 128], BF16) for i in range(2)]
                dsrc = [Buf() for _ in range(NMT)]
                wctr = [0, 0]

                def load_x(mt):
                    xb = X[mt % 2]
                    S.add("sp", lambda e: e.dma_start(out=xb.t[:], in_=src[:, :, mt * TM:(mt + 1) * TM]),
                          reads=[dsrc[mt]], writes=[xb.b], dma=True)

                def stage_a(mt):
                    which = 0 if mt < 4 else 1
                    norm_mod((sq, rstd, tmp), X[mt % 2], Hh[mt % 2], j, which, (PS[4], PS[5]))

                load_x(0)
                stage_a(0)
                for mt in range(NMT):
                    which = 0 if mt < 4 else 1
                    xb = X[mt % 2]
                    hb = Hh[mt % 2]
                    if mt + 1 < NMT:
                        load_x(mt + 1)
                    for jp in range(NFC):
                        w = wp[wctr[0] % 4]
                        wctr[0] += 1
                        S.add("pool", lambda e, w=w, jp=jp: e.dma_start(out=w.t[:], in_=w13[lh, jp]),
                              writes=[w.b], dma=True)
                        for s in range(TM // 512):
                            k = jp * 2 + s
                            pg, pu = PS[k % 2], PS[2 + k % 2]
                            for half, pb in ((0, pg), (1, pu)):
                                for c in range(8):
                                    S.add("pe", lambda e, w=w, pb=pb, c=c, s=s, half=half, hb=hb: e.matmul(
                                        pb.t[:], lhsT=w.t[:, c, half * 128:(half + 1) * 128],
                                        rhs=hb.t[:, c, s * 512:(s + 1) * 512], start=(c == 0), stop=(c == 7)),
                                        reads=[w.b, hb.b], writes=[pb.b])
                            g = sg[k % 2]
                            S.add("act", lambda e, g=g, pg=pg: e.activation(out=g.t[:], in_=pg.t[:], func=AF.Silu),
                                  reads=[pg.b], writes=[g.b])
                            S.add("dve", lambda e, g=g, pu=pu, jp=jp, s=s: e.tensor_tensor(
                                out=actT.t[:, jp, s * 512:(s + 1) * 512], in0=g.t[:], in1=pu.t[:], op=ALU.mult),
                                reads=[g.b, pu.b], writes=[actT.b])
                        if jp == 11 and mt + 1 < NMT:
                            stage_a(mt + 1)
                    for m in range(8):
                        w = w2t[wctr[1] % 2]
                        wctr[1] += 1
                        S.add("pool", lambda e, w=w, m=m: e.dma_start(out=w.t[:], in_=w2[lh, m], max_dma_last_dim=4096),
                              writes=[w.b], dma=True)
                        for s in range(TM // 512):
                            pb = PS[6 + (m * 2 + s) % 2]
                            for f in range(NFC):
                                S.add("pe", lambda e, w=w, pb=pb, f=f, s=s: e.matmul(
                                    pb.t[:], lhsT=w.t[:, f, :], rhs=actT.t[:, f, s * 512:(s + 1) * 512],
                                    start=(f == 0), stop=(f == NFC - 1)), reads=[w.b, actT.b], writes=[pb.b])
                            S.add("dve", lambda e, pb=pb, m=m, s=s, xb=xb, which=which: e.scalar_tensor_tensor(
                                out=xb.t[:, m, s * 512:(s + 1) * 512], in0=pb.t[:],
                                scalar=hgT.t[:, j, m, which:which + 1], in1=xb.t[:, m, s * 512:(s + 1) * 512],
                                op0=ALU.mult, op1=ALU.add), reads=[pb.b, hgT.b, xb.b], writes=[xb.b])
                    S.add("sp", lambda e, xb=xb, mt=mt: e.dma_start(out=dst[:, :, mt * TM:(mt + 1) * TM], in_=xb.t[:]),
                          reads=[xb.b], writes=[dsrc[mt]], dma=True)
                S.end_stage(resched=cfg.get("rs_ffn", False))

        def ah_inproj(l, i):
            with contextlib.ExitStack() as st:
                X = [sb(st, "x", [128, 8, TM]) for _ in range(2)]
                Hh = [sb(st, "h", [128, 8, TM], BF16) for _ in range(2)]
                sq = [sb(st, "sq", [128, TM]) for _ in range(2)]
                tmp = [sb(st, "tmp", [128, TM]) for _ in range(2)]
                rstd = sb(st, "rstd", [128, TM])
                W = sb(st, "win", [128, 8, 2432], BF16)
                stg = [sb(st, "stg", [128, 512]) for _ in range(4)]
                vst = [sb(st, "vst", [128, 8, 128]) for _ in range(2)]
                for kc2 in range(4):
                    S.add("pool", lambda e, kc2=kc2: e.dma_start(out=W.t[:, 2 * kc2:2 * kc2 + 2, :], in_=win[i, :, 2 * kc2:2 * kc2 + 2, :],
                                                               max_dma_last_dim=4096), writes=[W.b], dma=True)

                def load_x(mt):
                    xb = X[mt % 2]
                    S.add("sp", lambda e: e.dma_start(out=xb.t[:], in_=xs[:, :, mt * TM:(mt + 1) * TM]),
                          writes=[xb.b], dma=True)

                def stage_a(mt):
                    which = 0 if mt < 4 else 1
                    norm_mod((sq, rstd, tmp), X[mt % 2], Hh[mt % 2], 1, which, (PS[4], PS[5]))

                load_x(0)
                stage_a(0)
                ctr = [0]
                for mt in range(NMT):
                    hb = Hh[mt % 2]
                    if mt + 1 < NMT:
                        load_x(mt + 1)
                    for oc in range(18):
                        if oc < 4:
                            dst, ch = qT_d, oc
                        elif oc < 6:
                            dst, ch = kT_d, oc - 4
                        else:
                            dst, ch = u3T_d, oc - 6
                        for s_ in range(TM // 512):
                            k = ctr[0]
                            ctr[0] += 1
                            pb = PS[k % 4]
                            sg_ = stg[k % 4]
                            for c in range(8):
                                S.add("pe", lambda e, pb=pb, c=c, s_=s_, oc=oc, hb=hb: e.matmul(
                                    pb.t[:], lhsT=W.t[:, c, oc * 128:(oc + 1) * 128],
                                    rhs=hb.t[:, c, s_ * 512:(s_ + 1) * 512], start=(c == 0), stop=(c == 7)),
                                    reads=[W.b, hb.b], writes=[pb.b])
                            if k % 2 == 0:
                                S.add("act", lambda e, pb=pb, sg_=sg_: e.activation(out=sg_.t[:], in_=pb.t[:], func=AF.Identity),
                                      reads=[pb.b], writes=[sg_.b])
                            else:
                                S.add("dve", lambda e, pb=pb, sg_=sg_: e.tensor_copy(out=sg_.t[:], in_=pb.t[:]),
                                      reads=[pb.b], writes=[sg_.b])
                            t0 = mt * TM + s_ * 512
                            S.add("sp", lambda e, dst=dst, ch=ch, t0=t0, sg_=sg_: e.dma_start(
                                out=dst[:, ch, t0:t0 + 512], in_=sg_.t[:]), reads=[sg_.b], dma=True)
                        if oc == 9 and mt + 1 < NMT:
                            stage_a(mt + 1)
                    vs_ = vst[mt % 2]
                    for tb in range(TM // 128):
                        pb = PS[6 + tb % 2]
                        for c in range(8):
                            S.add("pe", lambda e, pb=pb, c=c, tb=tb, hb=hb: e.matmul(
                                pb.t[:, 0:128], lhsT=hb.t[:, c, tb * 128:(tb + 1) * 128], rhs=W.t[:, c, 2304:2432],
                                start=(c == 0), stop=(c == 7)), reads=[W.b, hb.b], writes=[pb.b])
                        S.add("dve", lambda e, pb=pb, tb=tb, vs_=vs_: e.tensor_copy(out=vs_.t[:, tb, :], in_=pb.t[:, 0:128]),
                              reads=[pb.b], writes=[vs_.b])
                    S.add("sp", lambda e, mt=mt, vs_=vs_: e.dma_start(
                        out=v_d[mt * TM:(mt + 1) * TM, :].rearrange("(tb p) n -> p tb n", p=128), in_=vs_.t[:]),
                        reads=[vs_.b], dma=True)
                S.end_stage(resched=cfg.get("rs_x", True))

        def ah_attention(l, i):
            with contextlib.ExitStack() as st:
                cosT = sb(st, "cosT", [128, 4096])
                sinT = sb(st, "sinT", [128, 4096])
                rotT = sb(st, "rotT", [128, 128])
                blk1 = sb(st, "blk1", [128, 128])
                mprev = sb(st, "mprev", [128, 128], BF16)
                mnext = sb(st, "mnext", [128, 128], BF16)
                onesb = sb(st, "onesb", [128, 64], BF16)
                gn = sb(st, "gn", [128, 2])
                gq8 = sb(st, "gq8", [128, 1])
                esink = sb(st, "esink", [128, 4])
                KT = sb(st, "KT", [128, 2, 4096], BF16)
                VV = sb(st, "VV", [128, 32, 128], BF16)
                CK = sb(st, "CK", [128, 2, 512], BF16)
                CV = sb(st, "CV", [128, 4, 128], BF16)
                kin = [sb(st, "kin", [128, 2, 512]) for _ in range(2)]
                qin = [sb(st, "qin", [128, 4, 512]) for _ in range(2)]
                qp = [sb(st, "qp", [128, 4, 512], BF16) for _ in range(2)]
                sqq = [sb(st, "sqq", [128, 512]) for _ in range(2)]
                rs_ = [sb(st, "rs", [128, 512]) for _ in range(2)]
                kg_ = [sb(st, "kg", [128, 512]) for _ in range(2)]
                t1_ = [sb(st, "t1", [128, 512]) for _ in range(2)]
                t2_ = [sb(st, "t2", [128, 512]) for _ in range(2)]
                pt = [sb(st, "pt", [128, 512], BF16) for _ in range(3)]
                den = [sb(st, "den", [128, 512]) for _ in range(2)]
                aout = [sb(st, "aout", [128, 4, 512], BF16) for _ in range(2)]
                kno = [sb(st, "kno", [128, 2, 256]) for _ in range(2)]

                S.add("sp", lambda e: e.dma_start(out=cosT.t[:], in_=cosT_d), writes=[cosT.b], dma=True)
                S.add("sp", lambda e: e.dma_start(out=sinT.t[:], in_=sinT_d), writes=[sinT.b], dma=True)
                S.add("sp", lambda e: e.dma_start(out=rotT.t[:], in_=rotT_d), writes=[rotT.b], dma=True)
                S.add("sp", lambda e: e.dma_start(out=blk1.t[:], in_=blk1_d), writes=[blk1.b], dma=True)
                S.add("pool", lambda e: e.dma_start(out=mprev.t[:], in_=mprev_d), writes=[mprev.b], dma=True)
                S.add("pool", lambda e: e.dma_start(out=mnext.t[:], in_=mnext_d), writes=[mnext.b], dma=True)
                S.add("sp", lambda e: e.dma_start(out=gn.t[:], in_=qkn[:, i, :]), writes=[gn.b], dma=True)
                S.add("sp", lambda e: e.dma_start(out=esink.t[:], in_=sinkT[:, i, :]), writes=[esink.b], dma=True)
                S.add("pool", lambda e: e.dma_start(out=CK.t[:], in_=ckT[i]), writes=[CK.b], dma=True)
                S.add("pool", lambda e: e.dma_start(out=CV.t[:], in_=cvv[i]), writes=[CV.b], dma=True)
                S.add("dve", lambda e: e.memset(onesb.t[:], 1.0), writes=[onesb.b])
                S.add("dve", lambda e: e.tensor_scalar(out=gq8.t[:], in0=gn.t[:, 0:1], scalar1=0.125, scalar2=None, op0=ALU.mult),
                      reads=[gn.b], writes=[gq8.b])
                S.add("act", lambda e: e.activation(out=esink.t[:], in_=esink.t[:], func=AF.Exp), reads=[esink.b], writes=[esink.b])
                pctr = [0]

                def qk_prep(src, srcb, n, gain, gainb, rope_t0, out, outb, nout=None, noutb=None):
                    k = pctr[0]
                    pctr[0] += 1
                    sq_, r_, g_, a_, b_ = sqq[k % 2], rs_[k % 2], kg_[k % 2], t1_[k % 2], t2_[k % 2]
                    pb = PS[3]
                    S.add("act", lambda e: e.activation(out=sq_.t[:, 0:n], in_=src, func=AF.Square), reads=[srcb], writes=[sq_.b])
                    S.add("pe", lambda e: e.matmul(pb.t[:, 0:n], lhsT=blk1.t[:], rhs=sq_.t[:, 0:n], start=True, stop=True),
                          reads=[blk1.b, sq_.b], writes=[pb.b])
                    S.add("act", lambda e: e.activation(out=r_.t[:, 0:n], in_=pb.t[:, 0:n], func=AF.Sqrt, scale=1.0 / 64,
                                                        bias=epsT.t[:, 0:1]), reads=[pb.b, epsT.b], writes=[r_.b])
                    S.add("dve", lambda e: e.reciprocal(out=r_.t[:, 0:n], in_=r_.t[:, 0:n]), reads=[r_.b], writes=[r_.b])
                    if rope_t0 is None:
                        if nout is not None:
                            S.add("dve", lambda e: e.scalar_tensor_tensor(out=nout, in0=src, scalar=gain, in1=r_.t[:, 0:n],
                                                                          op0=ALU.mult, op1=ALU.mult),
                                  reads=[srcb, gainb, r_.b], writes=[noutb])
                        S.add("dve", lambda e: e.scalar_tensor_tensor(out=out, in0=src, scalar=gain, in1=r_.t[:, 0:n],
                                                                      op0=ALU.mult, op1=ALU.mult),
                              reads=[srcb, gainb, r_.b], writes=[outb])
                        return
                    S.add("dve", lambda e: e.scalar_tensor_tensor(out=g_.t[:, 0:n], in0=src, scalar=gain, in1=r_.t[:, 0:n],
                                                                  op0=ALU.mult, op1=ALU.mult),
                          reads=[srcb, gainb, r_.b], writes=[g_.b])
                    S.add("pe", lambda e: e.matmul(pb.t[:, 0:n], lhsT=rotT.t[:], rhs=g_.t[:, 0:n], start=True, stop=True),
                          reads=[rotT.b, g_.b], writes=[pb.b])
                    S.add("dve", lambda e: e.tensor_tensor(out=b_.t[:, 0:n], in0=pb.t[:, 0:n], in1=sinT.t[:, rope_t0:rope_t0 + n], op=ALU.mult),
                          reads=[pb.b, sinT.b], writes=[b_.b])
                    S.add("pool", lambda e: e.tensor_tensor(out=a_.t[:, 0:n], in0=g_.t[:, 0:n], in1=cosT.t[:, rope_t0:rope_t0 + n], op=ALU.mult),
                          reads=[g_.b, cosT.b], writes=[a_.b])
                    S.add("dve", lambda e: e.tensor_tensor(out=out, in0=a_.t[:, 0:n], in1=b_.t[:, 0:n], op=ALU.add),
                          reads=[a_.b, b_.b], writes=[outb])

                stc = [0]
                gctr = [0]

                def attend(qpt, nq, blocks, dst_t0):
                    gi = gctr[0]
                    gctr[0] += 1
                    ao = aout[gi % 2]
                    for c in range(4):
                        par = (gi * 4 + c) % 2
                        PA, PB = PS[4 + 2 * par], PS[5 + 2 * par]
                        for hh in range(2):
                            h = 2 * c + hh
                            kvh = h // 4
                            lo = hh * 64
                            nb = len(blocks)
                            for idx, (Kt, kcol, Vt, vblk, q0, q1, masks) in enumerate(blocks):
                                n = q1 - q0
                                k = stc[0]
                                stc[0] += 1
                                ST = PS[k % 3]
                                P_ = pt[k % 3]
                                S.add("pe", lambda e, ST=ST, Kt=Kt, kcol=kcol, q0=q0, q1=q1, n=n, lo=lo, kvh=kvh, c=c: e.matmul(
                                    ST.t[:, 0:n], lhsT=Kt.t[lo:lo + 64, kvh, kcol:kcol + 128], rhs=qpt.t[lo:lo + 64, c, q0:q1],
                                    start=True, stop=True), reads=[Kt.b, qpt.b], writes=[ST.b])
                                S.add("act", lambda e, ST=ST, P_=P_, n=n: e.activation(out=P_.t[:, 0:n], in_=ST.t[:, 0:n], func=AF.Exp),
                                      reads=[ST.b], writes=[P_.b])
                                for (moff, mt_) in masks:
                                    S.add("dve", lambda e, P_=P_, moff=moff, mt_=mt_: e.tensor_tensor(
                                        out=P_.t[:, moff:moff + 128], in0=P_.t[:, moff:moff + 128], in1=mt_.t[:], op=ALU.mult),
                                        reads=[P_.b, mt_.b], writes=[P_.b])
                                S.add("pe", lambda e, PA=PA, Vt=Vt, vblk=vblk, P_=P_, n=n, q0=q0, q1=q1, lo=lo, kvh=kvh, idx=idx, nb=nb: e.matmul(
                                    PA.t[lo:lo + 64, q0:q1], lhsT=Vt.t[:, vblk, kvh * 64:(kvh + 1) * 64], rhs=P_.t[:, 0:n],
                                    start=(idx == 0), stop=(idx == nb - 1)), reads=[Vt.b, P_.b], writes=[PA.b])
                                S.add("pe", lambda e, PB=PB, P_=P_, n=n, q0=q0, q1=q1, lo=lo, idx=idx, nb=nb: e.matmul(
                                    PB.t[lo:lo + 64, q0:q1], lhsT=onesb.t[:, 0:64], rhs=P_.t[:, 0:n],
                                    start=(idx == 0), stop=(idx == nb - 1)), reads=[onesb.b, P_.b], writes=[PB.b])
                        dn = den[(gi * 4 + c) % 2]
                        S.add("dve", lambda e, dn=dn, PB=PB, c=c: e.tensor_scalar(out=dn.t[:, 0:nq], in0=PB.t[:, 0:nq], scalar1=esink.t[:, c:c + 1],
                                                                             scalar2=None, op0=ALU.add), reads=[PB.b, esink.b], writes=[dn.b])
                        S.add("dve", lambda e, dn=dn: e.reciprocal(out=dn.t[:, 0:nq], in_=dn.t[:, 0:nq]), reads=[dn.b], writes=[dn.b])
                        S.add("dve", lambda e, dn=dn, PA=PA, c=c: e.tensor_tensor(out=ao.t[:, c, 0:nq], in0=PA.t[:, 0:nq], in1=dn.t[:, 0:nq], op=ALU.mult),
                              reads=[PA.b, dn.b], writes=[ao.b])
                    S.add("sp", lambda e: e.dma_start(out=ay_d[:, 0:4, dst_t0:dst_t0 + nq], in_=ao.t[:, :, 0:nq]), reads=[ao.b], dma=True)

                for tt in range(8):
                    ki = kin[tt % 2]
                    S.add("sp", lambda e, ki=ki, tt=tt: e.dma_start(out=ki.t[:], in_=kT_d[:, :, tt * 512:(tt + 1) * 512]), writes=[ki.b], dma=True)
                    S.add("pool", lambda e, tt=tt: e.dma_start(
                        out=VV.t[:, tt * 4:(tt + 1) * 4, :], in_=v_d[tt * 512:(tt + 1) * 512, :].rearrange("(tb p) n -> p tb n", p=128)),
                        writes=[VV.b], dma=True)
                    for ch in range(2):
                        qk_prep(ki.t[:, ch, :], ki.b, 512, gn.t[:, 1:2], gn.b, tt * 512, KT.t[:, ch, tt * 512:(tt + 1) * 512], KT.b)
                for g in range(8):
                    qi = qin[g % 2]
                    qq = qp[g % 2]
                    S.add("sp", lambda e, qi=qi, g=g: e.dma_start(out=qi.t[:], in_=qT_d[:, :, g * 512:(g + 1) * 512]), writes=[qi.b], dma=True)
                    for c in range(4):
                        qk_prep(qi.t[:, c, :], qi.b, 512, gq8.t[:, 0:1], gq8.b, g * 512, qq.t[:, c, :], qq.b)
                    blocks = []
                    for b_ in range(4):
                        blocks.append((CK, b_ * 128, CV, b_, 0, 512, []))
                    for kb in range(max(0, 4 * g - 1), min(32, 4 * g + 5)):
                        qb0 = max(kb - 1, 4 * g)
                        qb1 = min(kb + 1, 4 * g + 3)
                        masks = []
                        for qb in range(qb0, qb1 + 1):
                            if qb == kb + 1:
                                masks.append(((qb - qb0) * 128, mprev))
                            elif qb == kb - 1:
                                masks.append(((qb - qb0) * 128, mnext))
                        blocks.append((KT, kb * 128, VV, kb, (qb0 - 4 * g) * 128, (qb1 - 4 * g + 1) * 128, masks))
                    attend(qq, 512, blocks, g * 512)
                KTp = [sb(st, "KTp", [128, 2, 256], BF16) for _ in range(2)]
                VVp = [sb(st, "VVp", [128, 2, 128], BF16) for _ in range(2)]
                for pi in range(4):
                    t0 = 4096 + pi * 256
                    ki = kin[pi % 2]
                    kt, vv, kn_ = KTp[pi % 2], VVp[pi % 2], kno[pi % 2]
                    S.add("sp", lambda e, ki=ki, t0=t0: e.dma_start(out=ki.t[:, :, 0:256], in_=kT_d[:, :, t0:t0 + 256]), writes=[ki.b], dma=True)
                    S.add("pool", lambda e, vv=vv, t0=t0: e.dma_start(
                        out=vv.t[:], in_=v_d[t0:t0 + 256, :].rearrange("(tb p) n -> p tb n", p=128)), writes=[vv.b], dma=True)
                    for ch in range(2):
                        qk_prep(ki.t[:, ch, 0:256], ki.b, 256, gn.t[:, 1:2], gn.b, None, kt.t[:, ch, :], kt.b,
                                nout=kn_.t[:, ch, :], noutb=kn_.b)
                    S.add("sp", lambda e, kn_=kn_, pi=pi: e.dma_start(
                        out=newk[i, :, :, pi * 256:(pi + 1) * 256].rearrange("k d t -> d k t"), in_=kn_.t[0:64, :, :]),
                        reads=[kn_.b], dma=True)
                    qi = qin[pi % 2]
                    qq = qp[pi % 2]
                    S.add("sp", lambda e, qi=qi, t0=t0: e.dma_start(out=qi.t[:, :, 0:256], in_=qT_d[:, :, t0:t0 + 256]), writes=[qi.b], dma=True)
                    for c in range(4):
                        qk_prep(qi.t[:, c, 0:256], qi.b, 256, gq8.t[:, 0:1], gq8.b, None, qq.t[:, c, 0:256], qq.b)
                    blocks = [(kt, b_ * 128, vv, b_, 0, 256, []) for b_ in range(2)]
                    attend(qq, 256, blocks, t0)
                S.add("sp", lambda e: e.dma_start(out=newv[i], in_=v_d[4096:5120, :]), dma=True)
                S.end_stage(resched=True)

        def ah_outproj(l, i):
            with contextlib.ExitStack() as st:
                X = [sb(st, "x", [128, 8, TM]) for _ in range(2)]
                AY = [sb(st, "ay", [128, 8, TM], BF16) for _ in range(2)]
                W = sb(st, "wout", [128, 8, 1024], BF16)
                for kc2 in range(4):
                    S.add("pool", lambda e, kc2=kc2: e.dma_start(out=W.t[:, 2 * kc2:2 * kc2 + 2, :], in_=wout[i, :, 2 * kc2:2 * kc2 + 2, :],
                                                               max_dma_last_dim=4096), writes=[W.b], dma=True)

                def load(mt):
                    xb, ab = X[mt % 2], AY[mt % 2]
                    S.add("sp", lambda e: e.dma_start(out=xb.t[:], in_=xs[:, :, mt * TM:(mt + 1) * TM]), writes=[xb.b], dma=True)
                    S.add("sp", lambda e: e.dma_start(out=ab.t[:], in_=ay_d[:, :, mt * TM:(mt + 1) * TM]), writes=[ab.b], dma=True)

                load(0)
                for mt in range(NMT):
                    which = 0 if mt < 4 else 1
                    xb, ab = X[mt % 2], AY[mt % 2]
                    if mt + 1 < NMT:
                        load(mt + 1)
                    for m in range(8):
                        for s_ in range(TM // 512):
                            pb = PS[(m * 2 + s_) % 4]
                            for f in range(8):
                                S.add("pe", lambda e, pb=pb, f=f, m=m, s_=s_, ab=ab: e.matmul(
                                    pb.t[:], lhsT=W.t[:, f, m * 128:(m + 1) * 128], rhs=ab.t[:, f, s_ * 512:(s_ + 1) * 512],
                                    start=(f == 0), stop=(f == 7)), reads=[W.b, ab.b], writes=[pb.b])
                            S.add("dve", lambda e, pb=pb, m=m, s_=s_, xb=xb, which=which: e.scalar_tensor_tensor(
                                out=xb.t[:, m, s_ * 512:(s_ + 1) * 512], in0=pb.t[:],
                                scalar=hgT.t[:, 1, m, which:which + 1], in1=xb.t[:, m, s_ * 512:(s_ + 1) * 512],
                                op0=ALU.mult, op1=ALU.add), reads=[pb.b, hgT.b, xb.b], writes=[xb.b])
                    S.add("sp", lambda e, xb=xb, mt=mt: e.dma_start(out=xs[:, :, mt * TM:(mt + 1) * TM], in_=xb.t[:]),
                          reads=[xb.b], dma=True)
                S.end_stage(resched=cfg.get("rs_x", True))


        def hy_conv(l, i):
            with contextlib.ExitStack() as st:
                cw = sb(st, "cw", [128, 12, 3])
                cb = sb(st, "cb", [128, 12])
                U = [sb(st, "U", [128, 12, 514]) for _ in range(2)]
                O = [sb(st, "O", [128, 12, 512]) for _ in range(2)]
                S.add("sp", lambda e: e.dma_start(out=cw.t[:], in_=hcw[:, i]), writes=[cw.b], dma=True)
                S.add("sp", lambda e: e.dma_start(out=cb.t[:], in_=hcb[:, i]), writes=[cb.b], dma=True)
                tiles = [(tt * 512, 512, tt == 0, tt == 7) for tt in range(8)] + [(4096 + 256 * pi, 256, True, True) for pi in range(4)]
                for ti, (t0, n, first, lastt) in enumerate(tiles):
                    u, o = U[ti % 2], O[ti % 2]
                    a0 = t0 if first else t0 - 1
                    a1 = t0 + n if lastt else t0 + n + 1
                    c0 = 1 if first else 0
                    S.add("sp", lambda e, u=u, a0=a0, a1=a1, c0=c0: e.dma_start(out=u.t[:, :, c0:c0 + (a1 - a0)], in_=u3T_d[:, :, a0:a1]),
                          writes=[u.b], dma=True)
                    if first:
                        S.add("pool", lambda e, u=u: e.memset(u.t[:, :, 0:1], 0.0), writes=[u.b])
                    if lastt:
                        S.add("pool", lambda e, u=u, n=n: e.memset(u.t[:, :, n + 1:n + 2], 0.0), writes=[u.b])
                    for ch in range(12):
                        en = "dve"
                        S.add(en, lambda e, u=u, o=o, ch=ch, n=n: e.tensor_scalar(
                            out=o.t[:, ch, 0:n], in0=u.t[:, ch, 0:n], scalar1=cw.t[:, ch, 0:1], scalar2=cb.t[:, ch:ch + 1],
                            op0=ALU.mult, op1=ALU.add), reads=[u.b, cw.b, cb.b], writes=[o.b])
                        S.add(en, lambda e, u=u, o=o, ch=ch, n=n: e.scalar_tensor_tensor(
                            out=o.t[:, ch, 0:n], in0=u.t[:, ch, 1:n + 1], scalar=cw.t[:, ch, 1:2], in1=o.t[:, ch, 0:n],
                            op0=ALU.mult, op1=ALU.add), reads=[u.b, cw.b, o.b], writes=[o.b])
                        S.add(en, lambda e, u=u, o=o, ch=ch, n=n: e.scalar_tensor_tensor(
                            out=o.t[:, ch, 0:n], in0=u.t[:, ch, 2:n + 2], scalar=cw.t[:, ch, 2:3], in1=o.t[:, ch, 0:n],
                            op0=ALU.mult, op1=ALU.add), reads=[u.b, cw.b, o.b], writes=[o.b])
                    S.add("sp", lambda e, o=o, t0=t0, n=n: e.dma_start(out=uc_d[:, :, t0:t0 + n], in_=o.t[:, :, 0:n]), reads=[o.b], dma=True)
                S.end_stage(resched=True)

        MAGIC = 12582912.0

        def hy_filter_gen(i, L, rn):
            G = HG[L]
            nb = L // 128
            TT = min(512, L)
            with contextlib.ExitStack() as st:
                zf = sb(st, "zf", [33, L])
                w1 = sb(st, "w1", [33, 64])
                w2 = sb(st, "w2", [64, 64])
                w3 = sb(st, "w3", [64, 2048])
                p1 = sb(st, "p1", [64, 2])
                p2 = sb(st, "p2", [64, 2])
                h1T = sb(st, "h1T", [64, L])
                h2T = sb(st, "h2T", [64, L])
                delt = sb(st, "delt", [128, 2048])
                tl = sb(st, "tl", [128, nb])
                dec = [sb(st, "dec", [128, 2048]) for _ in range(2)]
                hh = [sb(st, "hh", [128, 2048]) for _ in range(2)]
                sqh = [sb(st, "sqh", [128, 2048]) for _ in range(2)]
                stg = [sb(st, "hstg", [128, 2, 1024], BF16) for _ in range(2)]
                uu = [sb(st, "uu", [64, 512]) for _ in range(2)]
                ta = [sb(st, "ta", [64, 512]) for _ in range(2)]
                nr = [sb(st, "nr", [64, 512]) for _ in range(2)]
                sst = sb(st, "sst", [128, 1024])
                for (tt_, src) in ((zf, G["zf"]), (w1, hw1[i]), (w2, hw2[i]), (w3, hw3[i]), (p1, hb1[:, i, :]), (p2, hb2[:, i, :]),
                                  (delt, deltas_d), (tl, G["tl"])):
                    S.add("sp", lambda e, tt_=tt_, src=src: e.dma_start(out=tt_.t[:], in_=src), writes=[tt_.b], dma=True)
                for pp in (p1, p2):
                    S.add("dve", lambda e, pp=pp: e.tensor_scalar(out=pp.t[:, 1:2], in0=pp.t[:, 1:2], scalar1=1.0 / (2 * math.pi), scalar2=None,
                                                                  op0=ALU.mult), reads=[pp.b], writes=[pp.b])
                k = 0
                for (wt, pp, srcT, dstT) in ((w1, p1, zf, h1T), (w2, p2, h1T, h2T)):
                    for tile_ in range(L // TT):
                        c0 = tile_ * TT
                        pb = PS[k % 2]
                        u_, a_, n_ = uu[k % 2], ta[k % 2], nr[k % 2]
                        k += 1
                        kk = wt.t.shape[0]
                        S.add("pe", lambda e, pb=pb, wt=wt, srcT=srcT, c0=c0, kk=kk: e.matmul(
                            pb.t[0:64, 0:TT], lhsT=wt.t[:, :], rhs=srcT.t[0:kk, c0:c0 + TT], start=True, stop=True),
                            reads=[wt.b, srcT.b], writes=[pb.b])
                        S.add("dve", lambda e, pb=pb, u_=u_, pp=pp: e.tensor_scalar(
                            out=u_.t[:, 0:TT], in0=pb.t[0:64, 0:TT], scalar1=pp.t[:, 0:1], scalar2=pp.t[:, 1:2], op0=ALU.add, op1=ALU.mult),
                            reads=[pb.b, pp.b], writes=[u_.b])
                        S.add("dve", lambda e, u_=u_, a_=a_: e.tensor_scalar(
                            out=a_.t[:, 0:TT], in0=u_.t[:, 0:TT], scalar1=MAGIC, scalar2=None, op0=ALU.add), reads=[u_.b], writes=[a_.b])
                        S.add("dve", lambda e, u_=u_, a_=a_, n_=n_: e.scalar_tensor_tensor(
                            out=n_.t[:, 0:TT], in0=a_.t[:, 0:TT], scalar=MAGIC, in1=u_.t[:, 0:TT], op0=ALU.subtract, op1=ALU.subtract),
                            reads=[a_.b, u_.b], writes=[n_.b])
                        S.add("dve", lambda e, n_=n_: e.tensor_scalar(
                            out=n_.t[:, 0:TT], in0=n_.t[:, 0:TT], scalar1=0.49999, scalar2=-0.49999, op0=ALU.min, op1=ALU.max),
                            reads=[n_.b], writes=[n_.b])
                        S.add("act", lambda e, n_=n_, dstT=dstT, c0=c0: e.activation(
                            out=dstT.t[:, c0:c0 + TT], in_=n_.t[:, 0:TT], func=AF.Sin, scale=-2.0 * math.pi), reads=[n_.b], writes=[dstT.b])
                for lb in range(nb):
                    d_, h_, q_, sg_ = dec[lb % 2], hh[lb % 2], sqh[lb % 2], stg[lb % 2]
                    S.add("act", lambda e, d_=d_, lb=lb: e.activation(out=d_.t[:], in_=delt.t[:], func=AF.Exp, scale=tl.t[:, lb:lb + 1]),
                          reads=[delt.b, tl.b], writes=[d_.b])
                    for ct in range(4):
                        pb = PS[ct]
                        S.add("pe", lambda e, pb=pb, lb=lb, ct=ct: e.matmul(
                            pb.t[:], lhsT=h2T.t[:, lb * 128:(lb + 1) * 128], rhs=w3.t[:, ct * 512:(ct + 1) * 512], start=True, stop=True),
                            reads=[h2T.b, w3.b], writes=[pb.b])
                        S.add("dve", lambda e, pb=pb, h_=h_, d_=d_, ct=ct: e.tensor_tensor(
                            out=h_.t[:, ct * 512:(ct + 1) * 512], in0=pb.t[:], in1=d_.t[:, ct * 512:(ct + 1) * 512], op=ALU.mult),
                            reads=[pb.b, d_.b], writes=[h_.b])
                    if lb == 0:
                        S.add("dve", lambda e, h_=h_: e.memset(h_.t[0:1, 1024:2048], 0.0), writes=[h_.b])
                    S.add("act", lambda e, h_=h_, q_=q_: e.activation(out=q_.t[:], in_=h_.t[:], func=AF.Square), reads=[h_.b], writes=[q_.b])
                    for ct in range(4):
                        S.add("pe", lambda e, q_=q_, ct=ct, lb=lb: e.matmul(
                            PS[4 + ct].t[:], lhsT=ones32.t[:], rhs=q_.t[:, ct * 512:(ct + 1) * 512], start=(lb == 0), stop=(lb == nb - 1)),
                            reads=[ones32.b, q_.b], writes=[PS[4 + ct].b])
                    S.add("pool", lambda e, h_=h_, sg_=sg_: e.tensor_tensor(out=sg_.t[:, 0, :], in0=h_.t[:, 0:1024], in1=h_.t[:, 1024:2048], op=ALU.add),
                          reads=[h_.b], writes=[sg_.b])
                    S.add("pool", lambda e, h_=h_, sg_=sg_: e.tensor_tensor(out=sg_.t[:, 1, :], in0=h_.t[:, 1024:2048], in1=h_.t[:, 0:1024], op=ALU.subtract),
                          reads=[h_.b], writes=[sg_.b])
                    S.add("sp", lambda e, sg_=sg_, lb=lb: e.dma_start(out=hsd_d[:, lb].rearrange("s p n -> p s n"), in_=sg_.t[:]), reads=[sg_.b], dma=True)
                for o in range(2):
                    S.add("dve", lambda e, o=o: e.tensor_copy(out=sst.t[:, o * 512:(o + 1) * 512], in_=PS[4 + o].t[:]), reads=[PS[4 + o].b], writes=[sst.b])
                    S.add("dve", lambda e, o=o: e.tensor_tensor(out=sst.t[:, o * 512:(o + 1) * 512], in0=sst.t[:, o * 512:(o + 1) * 512],
                                                                in1=PS[6 + o].t[:], op=ALU.add), reads=[sst.b, PS[6 + o].b], writes=[sst.b])
                S.add("act", lambda e: e.activation(out=rn.t[:], in_=sst.t[:], func=AF.Sqrt, bias=epsT.t[:, 0:1]), reads=[sst.b, epsT.b], writes=[rn.b])
                S.add("dve", lambda e: e.reciprocal(out=rn.t[:], in_=rn.t[:]), reads=[rn.b], writes=[rn.b])
                S.end_stage(resched=True)

        def hy_filter_dft(L, rn):
            G = HG[L]
            nb = L // 128
            with contextlib.ExitStack() as st:
                hs = sb(st, "hs", [128, nb, 1024], BF16)
                hd = sb(st, "hd", [128, nb, 1024], BF16)
                tC = [sb(st, "tC", [128, nb, 128], BF16) for _ in range(2)]
                tS = [sb(st, "tS", [128, nb, 128], BF16) for _ in range(2)]
                stg = [sb(st, "Hstg", [128, 512]) for _ in range(4)]
                step = max(1, nb // 4)
                for lb0 in range(0, nb, step):
                    S.add("sp", lambda e, lb0=lb0: e.dma_start(out=hs.t[:, lb0:lb0 + step, :], in_=hsd_d[0, lb0:lb0 + step].rearrange("l p n -> p l n")),
                          writes=[hs.b], dma=True)
                    S.add("sp", lambda e, lb0=lb0: e.dma_start(out=hd.t[:, lb0:lb0 + step, :], in_=hsd_d[1, lb0:lb0 + step].rearrange("l p n -> p l n")),
                          writes=[hd.b], dma=True)
                k = 0
                for kb in range(nb):
                    c_, s_ = tC[kb % 2], tS[kb % 2]
                    S.add("sp", lambda e, c_=c_, kb=kb: e.dma_start(out=c_.t[:], in_=G["F"][0, kb]), writes=[c_.b], dma=True)
                    S.add("pool", lambda e, s_=s_, kb=kb: e.dma_start(out=s_.t[:], in_=G["F"][1, kb]), writes=[s_.b], dma=True)
                    for o in range(2):
                        for ri, (tab, src) in enumerate(((c_, hs), (s_, hd))):
                            pb = PS[k % 8]
                            sg_ = stg[k % 4]
                            k += 1
                            for lb in range(nb):
                                S.add("pe", lambda e, pb=pb, tab=tab, src=src, lb=lb, o=o: e.matmul(
                                    pb.t[:], lhsT=tab.t[:, lb, :], rhs=src.t[:, lb, o * 512:(o + 1) * 512], start=(lb == 0), stop=(lb == nb - 1)),
                                    reads=[tab.b, src.b], writes=[pb.b])
                            S.add("dve", lambda e, pb=pb, sg_=sg_, o=o: e.tensor_tensor(out=sg_.t[:], in0=pb.t[:], in1=rn.t[:, o * 512:(o + 1) * 512], op=ALU.mult),
                                  reads=[pb.b, rn.b], writes=[sg_.b])
                            S.add("sp", lambda e, sg_=sg_, o=o, ri=ri, kb=kb: e.dma_start(out=H_d[o, ri, kb], in_=sg_.t[:]), reads=[sg_.b], dma=True)
                S.end_stage(resched=cfg.get("rs_x", True))

        def hy_order(l, i, L, offs, o, ident, hbs, Z, YR, YI):
            G = HG[L]
            nb = L // 128
            TT = min(512, L)
            ntt = L // TT
            nsub = TT // 128
            with contextlib.ExitStack() as st:
                zin = [sb(st, "zin", [128, 512]) for _ in range(3)]
                k = 0
                for si, t0 in enumerate(offs):
                    for tt in range(ntt):
                        for cc in range(4):
                            zi = zin[k % 3]
                            pb = PS[k % 4]
                            k += 1
                            src = uc_d[:, 8 + cc, t0 + tt * TT:t0 + (tt + 1) * TT] if o == 0 else z1T_d[:, cc, t0 + tt * TT:t0 + (tt + 1) * TT]
                            S.add("sp", lambda e, zi=zi, src=src: e.dma_start(out=zi.t[:, 0:TT], in_=src), writes=[zi.b], dma=True)
                            for j in range(nsub):
                                S.add("pe", lambda e, pb=pb, zi=zi, j=j: e.transpose(pb.t[:, j * 128:(j + 1) * 128], zi.t[:, j * 128:(j + 1) * 128], ident.t[:]),
                                      reads=[zi.b, ident.b], writes=[pb.b])
                            zt = Z[si]
                            if k % 2 == 0:
                                S.add("act", lambda e, pb=pb, zt=zt, tt=tt, cc=cc: e.activation(
                                    out=zt.t[:, tt * nsub:(tt + 1) * nsub, cc * 128:(cc + 1) * 128],
                                    in_=pb.t[:, 0:TT].rearrange("p (a b) -> p a b", b=128), func=AF.Identity), reads=[pb.b], writes=[zt.b])
                            else:
                                S.add("dve", lambda e, pb=pb, zt=zt, tt=tt, cc=cc: e.tensor_copy(
                                    out=zt.t[:, tt * nsub:(tt + 1) * nsub, cc * 128:(cc + 1) * 128],
                                    in_=pb.t[:, 0:TT].rearrange("p (a b) -> p a b", b=128)), reads=[pb.b], writes=[zt.b])
                S.end_stage(resched=True)
            with contextlib.ExitStack() as st:
                tC = [sb(st, "tC", [128, nb, 128], BF16) for _ in range(3)]
                tS = [sb(st, "tS", [128, nb, 128], BF16) for _ in range(3)]
                hr = [sb(st, "hr", [128, 512]) for _ in range(2)]
                hi = [sb(st, "hi", [128, 512]) for _ in range(2)]
                m_ = [[sb(st, "m", [128, 512]) for _ in range(2)] for _ in range(4)]
                k = 0
                for kb in range(nb):
                    c_, s_ = tC[kb % 3], tS[kb % 3]
                    hr_, hi_ = hr[kb % 2], hi[kb % 2]
                    S.add("sp", lambda e, c_=c_, kb=kb: e.dma_start(out=c_.t[:], in_=G["F"][0, kb]), writes=[c_.b], dma=True)
                    S.add("pool", lambda e, s_=s_, kb=kb: e.dma_start(out=s_.t[:], in_=G["F"][1, kb]), writes=[s_.b], dma=True)
                    S.add("sp", lambda e, hr_=hr_, kb=kb: e.dma_start(out=hr_.t[:], in_=H_d[o, 0, kb]), writes=[hr_.b], dma=True)
                    S.add("sp", lambda e, hi_=hi_, kb=kb: e.dma_start(out=hi_.t[:], in_=H_d[o, 1, kb]), writes=[hi_.b], dma=True)
                    for si in range(len(offs)):
                        zt, yr, yi = Z[si], YR[si], YI[si]
                        Pc, Ps = PS[(k % 4) * 2], PS[(k % 4) * 2 + 1]
                        mm = [m_[q][k % 2] for q in range(4)]
                        k += 1
                        for tb in range(nb):
                            S.add("pe", lambda e, Pc=Pc, c_=c_, zt=zt, tb=tb: e.matmul(Pc.t[:], lhsT=c_.t[:, tb, :], rhs=zt.t[:, tb, :],
                                                                                 start=(tb == 0), stop=(tb == nb - 1)), reads=[c_.b, zt.b], writes=[Pc.b])
                        for tb in range(nb):
                            S.add("pe", lambda e, Ps=Ps, s_=s_, zt=zt, tb=tb: e.matmul(Ps.t[:], lhsT=s_.t[:, tb, :], rhs=zt.t[:, tb, :],
                                                                                 start=(tb == 0), stop=(tb == nb - 1)), reads=[s_.b, zt.b], writes=[Ps.b])
                        for q, (hh_, pp_) in enumerate(((hr_, Pc), (hi_, Ps), (hr_, Ps), (hi_, Pc))):
                            S.add("dve", lambda e, q=q, hh_=hh_, pp_=pp_, mm=mm: e.tensor_tensor(out=mm[q].t[:], in0=pp_.t[:], in1=hh_.t[:], op=ALU.mult),
                                  reads=[hh_.b, pp_.b], writes=[mm[q].b])
                        S.add("pool", lambda e, mm=mm, yr=yr, kb=kb: e.tensor_tensor(out=yr.t[:, kb, :], in0=mm[0].t[:], in1=mm[1].t[:], op=ALU.add),
                              reads=[mm[0].b, mm[1].b], writes=[yr.b])
                        S.add("pool", lambda e, mm=mm, yi=yi, kb=kb: e.tensor_tensor(out=yi.t[:, kb, :], in0=mm[2].t[:], in1=mm[3].t[:], op=ALU.subtract),
                              reads=[mm[2].b, mm[3].b], writes=[yi.b])
                S.end_stage(resched=True)
            with contextlib.ExitStack() as st:
                KQ = min(8, nb)
                nkq = nb // KQ
                iC = [sb(st, "iC", [128, KQ, TT], BF16) for _ in range(3)]
                iS = [sb(st, "iS", [128, KQ, TT], BF16) for _ in range(3)]
                zt_ = [sb(st, "zt", [128, 512]) for _ in range(3)]
                gt_ = [sb(st, "gt", [128, 512]) for _ in range(3)]
                ab_ = [sb(st, "ab", [128, 512]) for _ in range(3)]
                of_ = [sb(st, "of", [128, 512]) for _ in range(3)]
                ob_ = [sb(st, "ob", [128, 512], BF16) for _ in range(3)]
                kt = 0
                ke = 0
                kk = 0
                for si, t0 in enumerate(offs):
                    yr, yi = YR[si], YI[si]
                    for tt in range(ntt):
                        banks = [PS[(kk % 2) * 4 + cc] for cc in range(4)]
                        kk += 1
                        for kq in range(nkq):
                            c_, s_ = iC[kt % 3], iS[kt % 3]
                            kt += 1
                            S.add("sp", lambda e, c_=c_, tt=tt, kq=kq: e.dma_start(out=c_.t[:], in_=G["I"][0, tt, :, kq * KQ:(kq + 1) * KQ, :]), writes=[c_.b], dma=True)
                            S.add("pool", lambda e, s_=s_, tt=tt, kq=kq: e.dma_start(out=s_.t[:], in_=G["I"][1, tt, :, kq * KQ:(kq + 1) * KQ, :]), writes=[s_.b], dma=True)
                            for cc in range(4):
                                for kbl in range(KQ):
                                    kb = kq * KQ + kbl
                                    S.add("pe", lambda e, bk=banks[cc], yr=yr, c_=c_, kb=kb, kbl=kbl, cc=cc: e.matmul(
                                        bk.t[:, 0:TT], lhsT=yr.t[:, kb, cc * 128:(cc + 1) * 128], rhs=c_.t[:, kbl, :], start=(kb == 0), stop=False),
                                        reads=[yr.b, c_.b], writes=[banks[cc].b])
                                    S.add("pe", lambda e, bk=banks[cc], yi=yi, s_=s_, kb=kb, kbl=kbl, cc=cc: e.matmul(
                                        bk.t[:, 0:TT], lhsT=yi.t[:, kb, cc * 128:(cc + 1) * 128], rhs=s_.t[:, kbl, :], start=False, stop=(kb == nb - 1)),
                                        reads=[yi.b, s_.b], writes=[banks[cc].b])
                        for cc in range(4):
                            z_, g_, a_, f_, b_ = zt_[ke % 3], gt_[ke % 3], ab_[ke % 3], of_[ke % 3], ob_[ke % 3]
                            ke += 1
                            tok = slice(t0 + tt * TT, t0 + (tt + 1) * TT)
                            zsrc = uc_d[:, 8 + cc, tok] if o == 0 else z1T_d[:, cc, tok]
                            S.add("sp", lambda e, z_=z_, zsrc=zsrc: e.dma_start(out=z_.t[:, 0:TT], in_=zsrc), writes=[z_.b], dma=True)
                            S.add("sp", lambda e, g_=g_, cc=cc, tok=tok: e.dma_start(out=g_.t[:, 0:TT], in_=uc_d[:, o * 4 + cc, tok]), writes=[g_.b], dma=True)
                            S.add("pool", lambda e, z_=z_, a_=a_, cc=cc: e.tensor_scalar(out=a_.t[:, 0:TT], in0=z_.t[:, 0:TT], scalar1=hbs.t[:, o, cc:cc + 1],
                                                                                    scalar2=None, op0=ALU.mult), reads=[z_.b, hbs.b], writes=[a_.b])
                            S.add("dve", lambda e, bk=banks[cc], a_=a_: e.scalar_tensor_tensor(out=a_.t[:, 0:TT], in0=bk.t[:, 0:TT], scalar=2.0 / (2 * L),
                                                                                            in1=a_.t[:, 0:TT], op0=ALU.mult, op1=ALU.add),
                                  reads=[banks[cc].b, a_.b], writes=[a_.b])
                            if o == 0:
                                S.add("pool", lambda e, a_=a_, g_=g_, f_=f_: e.tensor_tensor(out=f_.t[:, 0:TT], in0=a_.t[:, 0:TT], in1=g_.t[:, 0:TT], op=ALU.mult),
                                      reads=[a_.b, g_.b], writes=[f_.b])
                                S.add("sp", lambda e, f_=f_, cc=cc, tok=tok: e.dma_start(out=z1T_d[:, cc, tok], in_=f_.t[:, 0:TT]), reads=[f_.b], dma=True)
                            else:
                                S.add("pool", lambda e, a_=a_, g_=g_, b_=b_: e.tensor_tensor(out=b_.t[:, 0:TT], in0=a_.t[:, 0:TT], in1=g_.t[:, 0:TT], op=ALU.mult),
                                      reads=[a_.b, g_.b], writes=[b_.b])
                                S.add("sp", lambda e, b_=b_, cc=cc, tok=tok: e.dma_start(out=ay_d[:, 4 + cc, tok], in_=b_.t[:, 0:TT]), reads=[b_.b], dma=True)
                S.end_stage(resched=True)

        def ah_hyena(l, i):
            hy_conv(l, i)
            with contextlib.ExitStack() as st:
                ident = sb(st, "ident", [128, 128])
                hbs = sb(st, "hbs", [128, 2, 4])
                rn = sb(st, "rn", [128, 1024])
                S.add("sp", lambda e: e.dma_start(out=ident.t[:], in_=ident_d), writes=[ident.b], dma=True)
                S.add("sp", lambda e: e.dma_start(out=hbs.t[:], in_=hbias[:, i]), writes=[hbs.b], dma=True)
                S.end_stage()
                for (L, offs) in ((4096, [0]), (256, [4096 + 256 * pi for pi in range(4)])):
                    nb = L // 128
                    with contextlib.ExitStack() as st2:
                        hy_filter_gen(i, L, rn)
                        hy_filter_dft(L, rn)
                        Z = [sb(st2, "Z", [128, nb, 512], BF16) for _ in offs]
                        YR = [sb(st2, "YR", [128, nb, 512], BF16) for _ in offs]
                        YI = [sb(st2, "YI", [128, nb, 512], BF16) for _ in offs]
                        for o in range(2):
                            hy_order(l, i, L, offs, o, ident, hbs, Z, YR, YI)

        class Rot:
            def __init__(self, st, name, shape, dt, n):
                self.ts = [sb(st, name, shape, dt) for _ in range(n)]
                self.k = 0

            def next(self):
                t = self.ts[self.k % len(self.ts)]
                self.k += 1
                return t

        def dn_inproj(l, i):
            with contextlib.ExitStack() as st:
                X = [sb(st, "x", [128, 8, TM]) for _ in range(2)]
                Hh = [sb(st, "h", [128, 8, TM], BF16) for _ in range(2)]
                sq = [sb(st, "sq", [128, TM]) for _ in range(2)]
                tmp = [sb(st, "tmp", [128, TM]) for _ in range(2)]
                rstd = sb(st, "rstd", [128, TM])
                W = sb(st, "dwin", [128, 8, 4128], BF16)
                stg = [sb(st, "stg", [128, 512]) for _ in range(4)]
                vst = [sb(st, "vst", [128, 8, 32]) for _ in range(2)]
                for kc in range(8):
                    S.add("pool", lambda e, kc=kc: e.dma_start(out=W.t[:, kc, :], in_=dwin[i, :, kc, :], max_dma_last_dim=4096),
                          writes=[W.b], dma=True)

                def load_x(mt):
                    xb = X[mt % 2]
                    S.add("sp", lambda e: e.dma_start(out=xb.t[:], in_=xs[:, :, mt * TM:(mt + 1) * TM]), writes=[xb.b], dma=True)

                def stage_a(mt):
                    which = 0 if mt < 4 else 1
                    norm_mod((sq, rstd, tmp), X[mt % 2], Hh[mt % 2], 1, which, (PS[4], PS[5]))

                load_x(0)
                stage_a(0)
                ctr = [0]
                for mt in range(NMT):
                    hb = Hh[mt % 2]
                    if mt + 1 < NMT:
                        load_x(mt + 1)
                    for oc in range(32):
                        dst, ch = (qkvT_d, oc) if oc < 24 else (zT_d, oc - 24)
                        for s_ in range(TM // 512):
                            k = ctr[0]
                            ctr[0] += 1
                            pb = PS[k % 4]
                            sg_ = stg[k % 4]
                            for c in range(8):
                                S.add("pe", lambda e, pb=pb, c=c, s_=s_, oc=oc, hb=hb: e.matmul(
                                    pb.t[:], lhsT=W.t[:, c, oc * 128:(oc + 1) * 128],
                                    rhs=hb.t[:, c, s_ * 512:(s_ + 1) * 512], start=(c == 0), stop=(c == 7)),
                                    reads=[W.b, hb.b], writes=[pb.b])
                            if k % 2 == 0:
                                S.add("act", lambda e, pb=pb, sg_=sg_: e.activation(out=sg_.t[:], in_=pb.t[:], func=AF.Identity),
                                      reads=[pb.b], writes=[sg_.b])
                            else:
                                S.add("dve", lambda e, pb=pb, sg_=sg_: e.tensor_copy(out=sg_.t[:], in_=pb.t[:]),
                                      reads=[pb.b], writes=[sg_.b])
                            t0 = mt * TM + s_ * 512
                            S.add("sp", lambda e, dst=dst, ch=ch, t0=t0, sg_=sg_: e.dma_start(
                                out=dst[:, ch, t0:t0 + 512], in_=sg_.t[:]), reads=[sg_.b], dma=True)
                        if oc == 15 and mt + 1 < NMT:
                            stage_a(mt + 1)
                    vs_ = vst[mt % 2]
                    for tb in range(TM // 128):
                        pb = PS[6 + tb % 2]
                        for c in range(8):
                            S.add("pe", lambda e, pb=pb, c=c, tb=tb, hb=hb: e.matmul(
                                pb.t[:, 0:32], lhsT=hb.t[:, c, tb * 128:(tb + 1) * 128], rhs=W.t[:, c, 4096:4128],
                                start=(c == 0), stop=(c == 7)), reads=[W.b, hb.b], writes=[pb.b])
                        S.add("dve", lambda e, pb=pb, tb=tb, vs_=vs_: e.tensor_copy(out=vs_.t[:, tb, :], in_=pb.t[:, 0:32]),
                              reads=[pb.b], writes=[vs_.b])
                    S.add("sp", lambda e, mt=mt, vs_=vs_: e.dma_start(
                        out=ba_d[mt * TM:(mt + 1) * TM, :].rearrange("(tb p) n -> p tb n", p=128), in_=vs_.t[:]),
                        reads=[vs_.b], dma=True)
                S.end_stage(resched=cfg.get("rs_x", True))

        def dn_conv(l, i):
            with contextlib.ExitStack() as st:
                cw = sb(st, "dcw", [128, 24, 3])
                U = [sb(st, "U", [128, 12, 514]) for _ in range(2)]
                O = [sb(st, "O", [128, 12, 512]) for _ in range(2)]
                SQ = Rot(st, "dsq", [128, 512], F32, 8)
                RS = Rot(st, "drs", [128, 512], F32, 8)
                S.add("sp", lambda e: e.dma_start(out=cw.t[:], in_=dcw[:, i]), writes=[cw.b], dma=True)
                tiles = [(tt * 512, 512, tt == 0, tt == 7) for tt in range(8)] + [(4096 + 256 * pi, 256, True, True) for pi in range(4)]
                ti = 0
                for (t0, n, first, lastt) in tiles:
                    for half in range(2):
                        u, o = U[ti % 2], O[ti % 2]
                        ti += 1
                        a0 = t0 if first else t0 - 1
                        a1 = t0 + n if lastt else t0 + n + 1
                        c0 = 1 if first else 0
                        S.add("sp", lambda e, u=u, a0=a0, a1=a1, c0=c0, half=half: e.dma_start(
                            out=u.t[:, :, c0:c0 + (a1 - a0)], in_=qkvT_d[:, half * 12:(half + 1) * 12, a0:a1]), writes=[u.b], dma=True)
                        if first:
                            S.add("pool", lambda e, u=u: e.memset(u.t[:, :, 0:1], 0.0), writes=[u.b])
                        if lastt:
                            S.add("pool", lambda e, u=u, n=n: e.memset(u.t[:, :, n + 1:n + 2], 0.0), writes=[u.b])
                        for c12 in range(12):
                            ch = half * 12 + c12
                            S.add("dve", lambda e, u=u, o=o, ch=ch, c12=c12, n=n: e.tensor_scalar(
                                out=o.t[:, c12, 0:n], in0=u.t[:, c12, 0:n], scalar1=cw.t[:, ch, 0:1], scalar2=None, op0=ALU.mult),
                                reads=[u.b, cw.b], writes=[o.b])
                            for tap in (1, 2):
                                S.add("dve", lambda e, u=u, o=o, ch=ch, c12=c12, n=n, tap=tap: e.scalar_tensor_tensor(
                                    out=o.t[:, c12, 0:n], in0=u.t[:, c12, tap:n + tap], scalar=cw.t[:, ch, tap:tap + 1], in1=o.t[:, c12, 0:n],
                                    op0=ALU.mult, op1=ALU.add), reads=[u.b, cw.b, o.b], writes=[o.b])
                        S.add("act", lambda e, o=o, n=n: e.activation(out=o.t[:, :, 0:n], in_=o.t[:, :, 0:n], func=AF.Silu), reads=[o.b], writes=[o.b])
                        for c12 in range(12):
                            ch = half * 12 + c12
                            if ch >= 16:
                                continue
                            q_, r_ = SQ.next(), RS.next()
                            pb = PS[ch % 8]
                            S.add("act", lambda e, o=o, q_=q_, c12=c12, n=n: e.activation(out=q_.t[:, 0:n], in_=o.t[:, c12, 0:n], func=AF.Square),
                                  reads=[o.b], writes=[q_.b])
                            S.add("pe", lambda e, pb=pb, q_=q_, n=n: e.matmul(pb.t[:, 0:n], lhsT=ones32.t[:], rhs=q_.t[:, 0:n], start=True, stop=True),
                                  reads=[ones32.b, q_.b], writes=[pb.b])
                            S.add("act", lambda e, pb=pb, r_=r_, n=n: e.activation(out=r_.t[:, 0:n], in_=pb.t[:, 0:n], func=AF.Sqrt, bias=epsT.t[:, 0:1]),
                                  reads=[pb.b, epsT.b], writes=[r_.b])
                            S.add("dve", lambda e, r_=r_, n=n: e.reciprocal(out=r_.t[:, 0:n], in_=r_.t[:, 0:n]), reads=[r_.b], writes=[r_.b])
                            sc = (128.0 ** -0.5) if ch < 8 else 1.0
                            S.add("dve", lambda e, o=o, r_=r_, c12=c12, n=n, sc=sc: e.scalar_tensor_tensor(
                                out=o.t[:, c12, 0:n], in0=o.t[:, c12, 0:n], scalar=sc, in1=r_.t[:, 0:n], op0=ALU.mult, op1=ALU.mult),
                                reads=[o.b, r_.b], writes=[o.b])
                        S.add("sp", lambda e, o=o, t0=t0, n=n, half=half: e.dma_start(out=qkvn_d[:, half * 12:(half + 1) * 12, t0:t0 + n], in_=o.t[:, :, 0:n]),
                              reads=[o.b], dma=True)
                S.end_stage(resched=True)

        def dn_chunks(l, i):
            with contextlib.ExitStack() as st:
                msk = sb(st, "msk", [64, 5, 64])
                ident = sb(st, "ident", [128, 128])
                ones64 = sb(st, "ones64", [64, 128])
                prm = sb(st, "prm", [64, 32])
                ba = sb(st, "ba", [64, NCH, 32])
                gall = sb(st, "gall", [64, NCH, 16])
                ball = sb(st, "ball", [64, NCH, 16])
                KT_ = sb(st, "KTt", [128, 8, 256])
                QT_ = sb(st, "QTt", [128, 8, 256])
                VT_ = sb(st, "VTt", [128, 8, 256])
                Ktm = sb(st, "Ktm", [64, 8, 128])
                Vtm = sb(st, "Vtm", [64, 8, 128])
                r512L = [Rot(st, "r512", [128, 512], F32, 10) for _ in range(2)]
                ratr = Rot(st, "ratr", [64, 512], F32, 2)
                b512L = [Rot(st, "b512", [64, 512], BF16, 16) for _ in range(2)]
                batnL = [Rot(st, "batn", [64, 512], BF16, 4) for _ in range(2)]
                lvm_t = sb(st, "lvm", [64, 12, 64])
                S.add("sp", lambda e: e.dma_start(out=lvm_t.t[:], in_=lvmask_d), writes=[lvm_t.b], dma=True)
                b1kL = [Rot(st, "b1k", [64, 8, 128], BF16, 3) for _ in range(2)]
                bqL = [Rot(st, "bq", [128, 512], BF16, 2) for _ in range(2)]
                u1kL = [Rot(st, "u1k", [64, 8, 128], F32, 1) for _ in range(2)]
                smL = [Rot(st, "sm", [128, 8], F32, 10) for _ in range(2)]
                S.add("sp", lambda e: e.dma_start(out=msk.t[:], in_=dmask_d), writes=[msk.b], dma=True)
                S.add("sp", lambda e: e.dma_start(out=ident.t[:], in_=ident_d), writes=[ident.b], dma=True)
                S.add("sp", lambda e: e.dma_start(out=prm.t[:], in_=dprm[0:64, i, :]), writes=[prm.b], dma=True)
                S.add("sp", lambda e: e.dma_start(out=ba.t[:], in_=ba_d.rearrange("(c p) n -> p c n", p=64)), writes=[ba.b], dma=True)
                S.add("dve", lambda e: e.memset(ones64.t[:], 1.0), writes=[ones64.b])
                S.add("act", lambda e: e.activation(out=ball.t[:], in_=ba.t[:, :, 0:16], func=AF.Sigmoid), reads=[ba.b], writes=[ball.b])
                S.add("dve", lambda e: e.tensor_tensor(out=gall.t[:], in0=ba.t[:, :, 16:32],
                                                       in1=prm.t[:, 16:32].unsqueeze(1).broadcast_to([64, NCH, 16]), op=ALU.add),
                      reads=[ba.b, prm.b], writes=[gall.b])
                S.add("act", lambda e: e.activation(out=gall.t[:], in_=gall.t[:], func=AF.Exp), reads=[gall.b], writes=[gall.b])
                S.add("act", lambda e: e.activation(out=gall.t[:], in_=gall.t[:], func=AF.Ln, bias=1.0), reads=[gall.b], writes=[gall.b])
                S.add("act", lambda e: e.activation(out=prm.t[:, 0:16], in_=prm.t[:, 0:16], func=AF.Exp), reads=[prm.b], writes=[prm.b])
                S.add("dve", lambda e: e.scalar_tensor_tensor(out=gall.t[:], in0=gall.t[:], scalar=-1.0,
                                                              in1=prm.t[:, 0:16].unsqueeze(1).broadcast_to([64, NCH, 16]), op0=ALU.mult, op1=ALU.mult),
                      reads=[gall.b, prm.b], writes=[gall.b])
                Uf, Ub, Sf, Sb, I64 = (msk.t[:, k_, :] for k_ in range(5))

                def bc_h(m):
                    return m.unsqueeze(1).broadcast_to([64, 8, 64])

                def v3(t_, p=64):
                    return t_[0:p, :].rearrange("p (h i) -> p h i", i=64)

                def chunk_body(c):
                    cl = c % 4
                    if cl == 0:
                        t0 = c * 64
                        for (tile_, ch0) in ((QT_, 0), (KT_, 8), (VT_, 16)):
                            S.add("sp", lambda e, tile_=tile_, ch0=ch0, t0=t0: e.dma_start(out=tile_.t[:], in_=qkvn_d[:, ch0:ch0 + 8, t0:t0 + 256]),
                                  writes=[tile_.b], dma=True)
                    csl = slice(cl * 64, (cl + 1) * 64)
                    Kc, Qc, Vc = KT_.t[:, :, csl], QT_.t[:, :, csl], VT_.t[:, :, csl]
                    for (src, srcb, dst, bk0) in ((Kc, KT_.b, Ktm, 0), (Vc, VT_.b, Vtm, 4)):
                        for h in range(8):
                            pb = PS[bk0 + h // 4]
                            S.add("pe", lambda e, pb=pb, src=src, h=h: e.transpose(pb.t[0:64, (h % 4) * 128:(h % 4 + 1) * 128], src[:, h, :], ident.t[:]),
                                  reads=[srcb, ident.b], writes=[pb.b])
                        for hb_ in range(2):
                            S.add("act", lambda e, dst=dst, hb_=hb_, bk0=bk0: e.activation(
                                out=dst.t[:, hb_ * 4:(hb_ + 1) * 4, :], in_=PS[bk0 + hb_].t[0:64, :].rearrange("p (h d) -> p h d", d=128), func=AF.Identity),
                                reads=[PS[bk0 + hb_].b], writes=[dst.b])
                    for h in range(8):
                        S.add("pe", lambda e, h=h, Kc=Kc, Qc=Qc: e.matmul(PS[2].t[0:64, h * 64:(h + 1) * 64], lhsT=Kc[:, h, :], rhs=Qc[:, h, :], start=True, stop=True),
                              reads=[KT_.b, QT_.b], writes=[PS[2].b])
                    atraw = ratr.next()
                    S.add("dve", lambda e, atraw=atraw: e.tensor_copy(out=atraw.t[0:64, :], in_=PS[2].t[0:64, :]), reads=[PS[2].b], writes=[atraw.b])
                    def unit(d):
                        r512, b512, batn, b1k, bq, u1k, sm = r512L[d], b512L[d], batnL[d], b1kL[d], bqL[d], u1kL[d], smL[d]
                        B0, B1, B2, B3 = (PS[4 * d + k_] for k_ in range(4))
                        Ud, Sd, SdT = (Uf, Sf, Sb) if d == 0 else (Ub, Sb, Sf)
                        g_ = gall.t[:, c, d * 8:(d + 1) * 8]
                        b_ = ball.t[:, c, d * 8:(d + 1) * 8]
                        ug, ib = r512.next(), r512.next()
                        S.add("pool", lambda e, ug=ug, Ud=Ud, g_=g_: e.tensor_tensor(out=v3(ug.t), in0=bc_h(Ud), in1=g_.unsqueeze(2).broadcast_to([64, 8, 64]), op=ALU.mult),
                              reads=[msk.b, gall.b], writes=[ug.b])
                        S.add("pool", lambda e, ib=ib, b_=b_: e.tensor_tensor(out=v3(ib.t), in0=bc_h(I64), in1=b_.unsqueeze(2).broadcast_to([64, 8, 64]), op=ALU.mult),
                              reads=[msk.b, ball.b], writes=[ib.b])
                        S.add("pe", lambda e, Ud=Ud, g_=g_: e.matmul(B0.t[0:64, 0:8], lhsT=Ud, rhs=g_, start=True, stop=True),
                              reads=[msk.b, gall.b], writes=[B0.b])
                        S.add("pe", lambda e, g_=g_: e.matmul(B0.t[:, 8:16], lhsT=ones64.t[:], rhs=g_, start=True, stop=True),
                              reads=[ones64.b, gall.b], writes=[B0.b])
                        S.add("pe", lambda e, ug=ug: e.matmul(B1.t[:], lhsT=ones64.t[:], rhs=ug.t[0:64, :], start=True, stop=True),
                              reads=[ones64.b, ug.b], writes=[B1.b])
                        S.add("pe", lambda e, ib=ib: e.matmul(B2.t[:], lhsT=ones64.t[:], rhs=ib.t[0:64, :], start=True, stop=True),
                              reads=[ones64.b, ib.b], writes=[B2.b])
                        gcc, egl, egc, bg, kds = sm.next(), sm.next(), sm.next(), sm.next(), sm.next()
                        S.add("dve", lambda e, gcc=gcc: e.tensor_copy(out=gcc.t[0:64, :], in_=B0.t[0:64, 0:8]), reads=[B0.b], writes=[gcc.b])
                        S.add("act", lambda e, egl=egl: e.activation(out=egl.t[:], in_=B0.t[:, 8:16], func=AF.Exp), reads=[B0.b], writes=[egl.b])
                        S.add("sp", lambda e, egl=egl, d=d, c=c: e.dma_start(out=egl_d[d, c], in_=egl.t[:]), reads=[egl.b], dma=True)
                        S.add("act", lambda e, egc=egc, gcc=gcc: e.activation(out=egc.t[0:64, :], in_=gcc.t[0:64, :], func=AF.Exp), reads=[gcc.b], writes=[egc.b])
                        S.add("dve", lambda e, bg=bg, egc=egc, b_=b_: e.tensor_tensor(out=bg.t[0:64, :], in0=egc.t[0:64, :], in1=b_, op=ALU.mult),
                              reads=[egc.b, ball.b], writes=[bg.b])
                        S.add("dve", lambda e, kds=kds, gcc=gcc: e.tensor_tensor(out=kds.t[0:64, :], in0=B0.t[0:64, 8:16], in1=gcc.t[0:64, :], op=ALU.subtract),
                              reads=[B0.b, gcc.b], writes=[kds.b])
                        S.add("act", lambda e, kds=kds: e.activation(out=kds.t[0:64, :], in_=kds.t[0:64, :], func=AF.Exp), reads=[kds.b], writes=[kds.b])
                        Dt, E1, E2 = r512.next(), r512.next(), r512.next()
                        S.add("dve", lambda e, Dt=Dt, gcc=gcc: e.tensor_tensor(out=v3(Dt.t), in0=v3(B1.t), in1=gcc.t[0:64, :].unsqueeze(2).broadcast_to([64, 8, 64]),
                                                                        op=ALU.subtract), reads=[B1.b, gcc.b], writes=[Dt.b])
                        S.add("dve", lambda e, Dt=Dt, E1=E1: e.tensor_scalar(out=E1.t[0:64, :], in0=Dt.t[0:64, :], scalar1=0.0, scalar2=None, op0=ALU.min),
                              reads=[Dt.b], writes=[E1.b])
                        S.add("dve", lambda e, Dt=Dt, E2=E2: e.tensor_scalar(out=E2.t[0:64, :], in0=Dt.t[0:64, :], scalar1=-1.0, scalar2=0.0, op0=ALU.mult, op1=ALU.min),
                              reads=[Dt.b], writes=[E2.b])
                        S.add("act", lambda e, E1=E1: e.activation(out=E1.t[0:64, :], in_=E1.t[0:64, :], func=AF.Exp), reads=[E1.b], writes=[E1.b])
                        S.add("act", lambda e, E2=E2: e.activation(out=E2.t[0:64, :], in_=E2.t[0:64, :], func=AF.Exp), reads=[E2.b], writes=[E2.b])
                        decI, decS, decN = r512.next(), r512.next(), r512.next()
                        S.add("pool", lambda e, decI=decI, E1=E1, Ud=Ud: e.tensor_tensor(out=v3(decI.t), in0=v3(E1.t), in1=bc_h(Ud), op=ALU.mult),
                              reads=[E1.b, msk.b], writes=[decI.b])
                        S.add("pool", lambda e, decS=decS, E1=E1, Sd=Sd: e.tensor_tensor(out=v3(decS.t), in0=v3(E1.t), in1=bc_h(Sd), op=ALU.mult),
                              reads=[E1.b, msk.b], writes=[decS.b])
                        S.add("pool", lambda e, decN=decN, E2=E2, SdT=SdT: e.tensor_tensor(out=v3(decN.t), in0=v3(E2.t), in1=bc_h(SdT), op=ALU.mult),
                              reads=[E2.b, msk.b], writes=[decN.b])
                        eg = r512.next()
                        qg = bq.next()
                        S.add("act", lambda e, eg=eg: e.activation(out=eg.t[:], in_=B1.t[:], func=AF.Exp), reads=[B1.b], writes=[eg.b])
                        S.add("pool", lambda e, eg=eg, qg=qg, Qc=Qc: e.tensor_tensor(out=v3(qg.t, 128), in0=Qc, in1=v3(eg.t, 128), op=ALU.mult),
                              reads=[eg.b, QT_.b], writes=[qg.b])
                        S.add("sp", lambda e, qg=qg, d=d, c=c: e.dma_start(out=QgT_d[d, c], in_=v3(qg.t, 128)), reads=[qg.b], dma=True)
                        kbT = r512.next()
                        S.add("dve", lambda e, kbT=kbT, Kc=Kc: e.tensor_tensor(out=v3(kbT.t, 128), in0=Kc, in1=v3(B2.t, 128), op=ALU.mult),
                              reads=[B2.b, KT_.b], writes=[kbT.b])
                        for h in range(8):
                            S.add("pe", lambda e, h=h, Kc=Kc, kbT=kbT: e.matmul(B3.t[0:64, h * 64:(h + 1) * 64], lhsT=Kc[:, h, :], rhs=kbT.t[:, h * 64:(h + 1) * 64],
                                                                              start=True, stop=True), reads=[KT_.b, kbT.b], writes=[B3.b])
                        for h in range(8):
                            S.add("pe", lambda e, h=h, Kc=Kc, kbT=kbT: e.matmul(B0.t[0:64, h * 64:(h + 1) * 64], lhsT=kbT.t[:, h * 64:(h + 1) * 64], rhs=Kc[:, h, :],
                                                                              start=True, stop=True), reads=[KT_.b, kbT.b], writes=[B0.b])
                        AT, AN = batn.next(), batn.next()
                        S.add("dve", lambda e, AT=AT, decS=decS: e.scalar_tensor_tensor(out=AT.t[:], in0=B3.t[0:64, :], scalar=-1.0, in1=decS.t[0:64, :],
                                                                                    op0=ALU.mult, op1=ALU.mult), reads=[B3.b, decS.b], writes=[AT.b])
                        S.add("dve", lambda e, AN=AN, decN=decN: e.scalar_tensor_tensor(out=AN.t[:], in0=B0.t[0:64, :], scalar=-1.0, in1=decN.t[0:64, :],
                                                                                    op0=ALU.mult, op1=ALU.mult), reads=[B0.b, decN.b], writes=[AN.b])
                        def lvm(lv, tr):
                            k_ = 2 * lv + (tr if d == 0 else 1 - tr)
                            return bc_h(lvm_t.t[:, k_, :])
                        TN, TT = b512.next(), b512.next()
                        for (dst_, src_, tr) in ((TN, AN, 0), (TT, AT, 1)):
                            mk0 = lvm(0, tr)
                            S.add("pool", lambda e, dst_=dst_, src_=src_, mk0=mk0: e.tensor_tensor(out=v3(dst_.t), in0=v3(src_.t), in1=mk0, op=ALU.mult),
                                  reads=[src_.b, lvm_t.b], writes=[dst_.b])
                            S.add("pool", lambda e, dst_=dst_: e.tensor_tensor(out=v3(dst_.t), in0=v3(dst_.t), in1=bc_h(I64), op=ALU.add),
                                  reads=[dst_.b, msk.b], writes=[dst_.b])
                        for lv in range(1, 6):
                            LoN, LoT, M1, M2, TT2 = b512.next(), b512.next(), b512.next(), b512.next(), b512.next()
                            mkn, mkt = lvm(lv, 0), lvm(lv, 1)
                            S.add("pool", lambda e, LoN=LoN, AN=AN, mkn=mkn: e.tensor_tensor(out=v3(LoN.t), in0=v3(AN.t), in1=mkn, op=ALU.mult),
                                  reads=[AN.b, lvm_t.b], writes=[LoN.b])
                            S.add("pool", lambda e, LoT=LoT, AT=AT, mkt=mkt: e.tensor_tensor(out=v3(LoT.t), in0=v3(AT.t), in1=mkt, op=ALU.mult),
                                  reads=[AT.b, lvm_t.b], writes=[LoT.b])
                            for h in range(8):
                                hs_ = slice(h * 64, (h + 1) * 64)
                                S.add("pe", lambda e, hs_=hs_, LoN=LoN, TT=TT: e.matmul(B1.t[0:64, hs_], lhsT=LoN.t[:, hs_], rhs=TT.t[:, hs_], start=True, stop=True),
                                      reads=[LoN.b, TT.b], writes=[B1.b])
                            S.add("act", lambda e, M1=M1: e.activation(out=M1.t[:], in_=B1.t[0:64, :], func=AF.Identity), reads=[B1.b], writes=[M1.b])
                            if lv < 5:
                                for h in range(8):
                                    hs_ = slice(h * 64, (h + 1) * 64)
                                    S.add("pe", lambda e, hs_=hs_, LoT=LoT, TN=TN: e.matmul(B2.t[0:64, hs_], lhsT=LoT.t[:, hs_], rhs=TN.t[:, hs_], start=True, stop=True),
                                          reads=[LoT.b, TN.b], writes=[B2.b])
                                S.add("act", lambda e, M2=M2: e.activation(out=M2.t[:], in_=B2.t[0:64, :], func=AF.Identity), reads=[B2.b], writes=[M2.b])
                            for h in range(8):
                                hs_ = slice(h * 64, (h + 1) * 64)
                                S.add("pe", lambda e, hs_=hs_, TN=TN, M1=M1: e.matmul(B3.t[0:64, hs_], lhsT=TN.t[:, hs_], rhs=M1.t[:, hs_], start=True, stop=True),
                                      reads=[TN.b, M1.b], writes=[B3.b])
                            S.add("dve", lambda e, TT2=TT2, TT=TT: e.tensor_tensor(out=TT2.t[:], in0=B3.t[0:64, :], in1=TT.t[:], op=ALU.add),
                                  reads=[B3.b, TT.b], writes=[TT2.b])
                            if lv < 5:
                                TN2 = b512.next()
                                for h in range(8):
                                    hs_ = slice(h * 64, (h + 1) * 64)
                                    S.add("pe", lambda e, hs_=hs_, TT=TT, M2=M2: e.matmul(B0.t[0:64, hs_], lhsT=TT.t[:, hs_], rhs=M2.t[:, hs_], start=True, stop=True),
                                          reads=[TT.b, M2.b], writes=[B0.b])
                                S.add("dve", lambda e, TN2=TN2, TN=TN: e.tensor_tensor(out=TN2.t[:], in0=B0.t[0:64, :], in1=TN.t[:], op=ALU.add),
                                      reads=[B0.b, TN.b], writes=[TN2.b])
                                TN = TN2
                            TT = TT2
                        P = TT
                        ato = b512.next()
                        S.add("pool", lambda e, ato=ato, atraw=atraw, decI=decI: e.tensor_tensor(out=ato.t[:], in0=atraw.t[0:64, :], in1=decI.t[0:64, :], op=ALU.mult),
                              reads=[atraw.b, decI.b], writes=[ato.b])
                        S.add("sp", lambda e, ato=ato, d=d, c=c: e.dma_start(out=AT_d[d, c], in_=v3(ato.t)), reads=[ato.b], dma=True)
                        Vb, KBg, Kdd = b1k.next(), b1k.next(), b1k.next()
                        S.add("pool", lambda e, Vb=Vb, b_=b_: e.tensor_tensor(out=Vb.t[:], in0=Vtm.t[:], in1=b_.unsqueeze(2).broadcast_to([64, 8, 128]), op=ALU.mult),
                              reads=[Vtm.b, ball.b], writes=[Vb.b])
                        S.add("pool", lambda e, KBg=KBg, bg=bg: e.tensor_tensor(out=KBg.t[:], in0=Ktm.t[:], in1=bg.t[0:64, :].unsqueeze(2).broadcast_to([64, 8, 128]), op=ALU.mult),
                              reads=[Ktm.b, bg.b], writes=[KBg.b])
                        S.add("pool", lambda e, Kdd=Kdd, kds=kds: e.tensor_tensor(out=Kdd.t[:], in0=Ktm.t[:], in1=kds.t[0:64, :].unsqueeze(2).broadcast_to([64, 8, 128]), op=ALU.mult),
                              reads=[Ktm.b, kds.b], writes=[Kdd.b])
                        S.add("sp", lambda e, Kdd=Kdd, d=d, c=c: e.dma_start(out=Kd_d[d, c], in_=Kdd.t[:]), reads=[Kdd.b], dma=True)
                        for h in range(8):
                            pb = (B1, B2)[h // 4]
                            S.add("pe", lambda e, pb=pb, h=h, P=P, Vb=Vb: e.matmul(pb.t[0:64, (h % 4) * 128:(h % 4 + 1) * 128], lhsT=P.t[:, h * 64:(h + 1) * 64], rhs=Vb.t[:, h, :],
                                                                              start=True, stop=True), reads=[P.b, Vb.b], writes=[pb.b])
                        uo = u1k.next()
                        for hb_ in range(2):
                            S.add("act", lambda e, uo=uo, hb_=hb_: e.activation(out=uo.t[:, hb_ * 4:(hb_ + 1) * 4, :],
                                                                           in_=(B1, B2)[hb_].t[0:64, :].rearrange("p (h d) -> p h d", d=128), func=AF.Identity),
                                  reads=[(B1, B2)[hb_].b], writes=[uo.b])
                        S.add("sp", lambda e, uo=uo, d=d, c=c: e.dma_start(out=u_d[d, c], in_=uo.t[:]), reads=[uo.b], dma=True)
                        for h in range(8):
                            S.add("pe", lambda e, h=h, P=P, KBg=KBg: e.matmul(B3.t[:, h * 64:(h + 1) * 64], lhsT=KBg.t[:, h, :], rhs=P.t[:, h * 64:(h + 1) * 64],
                                                                            start=True, stop=True), reads=[P.b, KBg.b], writes=[B3.b])
                        wo = bq.next()
                        S.add("dve", lambda e, wo=wo: e.tensor_copy(out=wo.t[:], in_=B3.t[:]), reads=[B3.b], writes=[wo.b])
                        S.add("sp", lambda e, wo=wo, d=d, c=c: e.dma_start(out=wT_d[d, c], in_=v3(wo.t, 128)), reads=[wo.b], dma=True)
                    for d_ in range(2):
                        unit(d_)

                for c_ in range(NCH):
                    chunk_body(c_)
                S.end_stage(resched=True)

        def dn_scan(l, i):
            with contextlib.ExitStack() as st:
                uL = [Rot(st, "uL", [64, 8, 128], F32, 2) for _ in range(2)]
                wL = [Rot(st, "wL", [128, 8, 64], BF16, 2) for _ in range(2)]
                qL = [Rot(st, "qL", [128, 8, 64], BF16, 2) for _ in range(2)]
                aL = [Rot(st, "aL", [128, 8, 64], BF16, 2) for _ in range(2)]
                kL = [Rot(st, "kL", [64, 8, 128], BF16, 2) for _ in range(2)]
                eL = [Rot(st, "eL", [128, 8], F32, 2) for _ in range(2)]
                vn = [Rot(st, "vn", [128, 8, 128], BF16, 2) for _ in range(2)]
                for d in range(2):
                    for t_ in aL[d].ts + vn[d].ts:
                        S.add("pool", lambda e, t_=t_: e.memset(t_.t[:], 0.0), writes=[t_.b])
                oS = [Rot(st, "oS", [128, 8, 64], F32, 2) for _ in range(2)]
                seqs = [(0, 64, None)] + [(64 + 4 * pi, 4, pi) for pi in range(4)]
                for (c0, nch, pi) in seqs:
                    Sf = [sb(st, "S", [128, 8, 128]) for _ in range(2)]
                    Sbf = [sb(st, "Sbf", [128, 8, 128], BF16) for _ in range(2)]
                    for d in range(2):
                        if pi is None:
                            S.add("sp", lambda e, d=d, Sf=Sf: e.dma_start(out=Sf[d].t[:], in_=sf0[d, i]), writes=[Sf[d].b], dma=True)
                        else:
                            S.add("pool", lambda e, d=d, Sf=Sf: e.memset(Sf[d].t[:], 0.0), writes=[Sf[d].b])
                        S.add("act", lambda e, d=d, Sf=Sf, Sbf=Sbf: e.activation(out=Sbf[d].t[:], in_=Sf[d].t[:], func=AF.Identity), reads=[Sf[d].b], writes=[Sbf[d].b])
                    for step in range(nch):
                        for d in range(2):
                            c = c0 + step if d == 0 else c0 + nch - 1 - step
                            sF, sB = Sf[d], Sbf[d]
                            u_, w_, q_, a_, k_, e_ = uL[d].next(), wL[d].next(), qL[d].next(), aL[d].next(), kL[d].next(), eL[d].next()
                            q1 = "sp" if d == 0 else "act"
                            for (tile_, src) in ((u_, u_d[d, c]), (w_, wT_d[d, c]), (q_, QgT_d[d, c]), (a_, AT_d[d, c]), (k_, Kd_d[d, c]), (e_, egl_d[d, c])):
                                np_ = src.shape[0]
                                S.add("sp", lambda e, tile_=tile_, src=src, np_=np_: e.dma_start(out=tile_.t[0:np_], in_=src), writes=[tile_.b], dma=True)
                            pw = (PS[0], PS[1]) if d == 0 else (PS[4], PS[5])
                            po = PS[2] if d == 0 else PS[6]
                            for h in range(8):
                                pb = pw[h // 4]
                                S.add("pe", lambda e, pb=pb, h=h, w_=w_, sB=sB: e.matmul(pb.t[0:64, (h % 4) * 128:(h % 4 + 1) * 128], lhsT=w_.t[:, h, :], rhs=sB.t[:, h, :],
                                                                                    start=True, stop=True), reads=[w_.b, sB.b], writes=[pb.b])
                            v_ = vn[d].next()
                            for hb_ in range(2):
                                S.add("dve", lambda e, v_=v_, u_=u_, hb_=hb_, pw=pw: e.tensor_tensor(
                                    out=v_.t[0:64, hb_ * 4:(hb_ + 1) * 4, :], in0=u_.t[:, hb_ * 4:(hb_ + 1) * 4, :],
                                    in1=pw[hb_].t[0:64, :].rearrange("p (h d) -> p h d", d=128), op=ALU.subtract),
                                    reads=[u_.b, pw[hb_].b], writes=[v_.b])
                            for h in range(8):
                                S.add("pe", lambda e, po=po, h=h, q_=q_, sB=sB: e.matmul(po.t[:, h * 64:(h + 1) * 64], lhsT=sB.t[:, h, :], rhs=q_.t[:, h, :],
                                                                                    start=True, stop=False), reads=[q_.b, sB.b], writes=[po.b])
                                S.add("pe", lambda e, po=po, h=h, a_=a_, v_=v_: e.matmul(po.t[:, h * 64:(h + 1) * 64], lhsT=v_.t[:, h, :], rhs=a_.t[:, h, :],
                                                                                    start=False, stop=True), reads=[a_.b, v_.b], writes=[po.b])
                            o_ = oS[d].next()
                            S.add("act", lambda e, o_=o_, po=po: e.activation(out=o_.t[:], in_=po.t[:].rearrange("p (h i) -> p h i", i=64), func=AF.Identity),
                                  reads=[po.b], writes=[o_.b])
                            S.add("sp", lambda e, o_=o_, d=d, c=c: e.dma_start(out=oT_d[d, :, :, c * 64:(c + 1) * 64], in_=o_.t[:]), reads=[o_.b], dma=True)
                            for h in range(8):
                                pb = pw[h // 4]
                                S.add("pe", lambda e, pb=pb, h=h, k_=k_, v_=v_: e.matmul(pb.t[:, (h % 4) * 128:(h % 4 + 1) * 128], lhsT=k_.t[:, h, :], rhs=v_.t[0:64, h, :],
                                                                                    start=True, stop=True), reads=[k_.b, v_.b], writes=[pb.b])
                            S.add("pool", lambda e, sF=sF, e_=e_: e.tensor_tensor(out=sF.t[:], in0=sF.t[:], in1=e_.t[:].unsqueeze(2).broadcast_to([128, 8, 128]), op=ALU.mult),
                                  reads=[sF.b, e_.b], writes=[sF.b])
                            for hb_ in range(2):
                                S.add("dve", lambda e, sF=sF, hb_=hb_, pw=pw: e.tensor_tensor(
                                    out=sF.t[:, hb_ * 4:(hb_ + 1) * 4, :], in0=sF.t[:, hb_ * 4:(hb_ + 1) * 4, :],
                                    in1=pw[hb_].t[:].rearrange("p (h d) -> p h d", d=128), op=ALU.add), reads=[sF.b, pw[hb_].b], writes=[sF.b])
                            S.add("act", lambda e, sF=sF, sB=sB: e.activation(out=sB.t[:], in_=sF.t[:], func=AF.Identity), reads=[sF.b], writes=[sB.b])
                    if pi is not None:
                        for d in range(2):
                            S.add("sp", lambda e, d=d, pi=pi, Sf=Sf: e.dma_start(out=nst[d, i, pi], in_=Sf[d].t[:]), reads=[Sf[d].b], dma=True)
                S.end_stage(resched=True)

        def dn_final(l, i):
            with contextlib.ExitStack() as st:
                W = sb(st, "dwout", [128, 8, 1024], BF16)
                gn2 = sb(st, "dng", [128, 2])
                OF = Rot(st, "OF", [128, 8, 512], F32, 2)
                OB = Rot(st, "OB", [128, 8, 512], F32, 2)
                ZZ = Rot(st, "ZZ", [128, 8, 512], F32, 2)
                XX = Rot(st, "XX", [128, 8, 512], F32, 2)
                OG = Rot(st, "OG", [128, 8, 512], BF16, 2)
                SQ = Rot(st, "fsq", [128, 512], F32, 4)
                RS = Rot(st, "frs", [128, 512], F32, 4)
                for kc2 in range(4):
                    S.add("pool", lambda e, kc2=kc2: e.dma_start(out=W.t[:, 2 * kc2:2 * kc2 + 2, :], in_=dwout[i, :, 2 * kc2:2 * kc2 + 2, :],
                                                               max_dma_last_dim=4096), writes=[W.b], dma=True)
                S.add("sp", lambda e: e.dma_start(out=gn2.t[:], in_=dng), writes=[gn2.b], dma=True)
                for tt in range(NTOK // 512):
                    which = 0 if tt < 8 else 1
                    tok = slice(tt * 512, (tt + 1) * 512)
                    of_, ob_, zz, xx, og = OF.next(), OB.next(), ZZ.next(), XX.next(), OG.next()
                    S.add("sp", lambda e, of_=of_, tok=tok: e.dma_start(out=of_.t[:], in_=oT_d[0, :, :, tok]), writes=[of_.b], dma=True)
                    S.add("sp", lambda e, ob_=ob_, tok=tok: e.dma_start(out=ob_.t[:], in_=oT_d[1, :, :, tok]), writes=[ob_.b], dma=True)
                    S.add("sp", lambda e, zz=zz, tok=tok: e.dma_start(out=zz.t[:], in_=zT_d[:, :, tok]), writes=[zz.b], dma=True)
                    S.add("sp", lambda e, xx=xx, tok=tok: e.dma_start(out=xx.t[:], in_=xs[:, :, tok]), writes=[xx.b], dma=True)
                    S.add("pool", lambda e, of_=of_, ob_=ob_: e.tensor_tensor(out=of_.t[:], in0=of_.t[:], in1=ob_.t[:], op=ALU.add),
                          reads=[of_.b, ob_.b], writes=[of_.b])
                    S.add("act", lambda e, zz=zz: e.activation(out=zz.t[:], in_=zz.t[:], func=AF.Silu), reads=[zz.b], writes=[zz.b])
                    for h in range(8):
                        q_, r_ = SQ.next(), RS.next()
                        pb = PS[h % 4]
                        S.add("act", lambda e, q_=q_, of_=of_, h=h: e.activation(out=q_.t[:], in_=of_.t[:, h, :], func=AF.Square), reads=[of_.b], writes=[q_.b])
                        S.add("pe", lambda e, pb=pb, q_=q_: e.matmul(pb.t[:], lhsT=ones32.t[:], rhs=q_.t[:], start=True, stop=True),
                              reads=[ones32.b, q_.b], writes=[pb.b])
                        S.add("act", lambda e, pb=pb, r_=r_: e.activation(out=r_.t[:], in_=pb.t[:], func=AF.Sqrt, scale=1.0 / 128, bias=epsT.t[:, 0:1]),
                              reads=[pb.b, epsT.b], writes=[r_.b])
                        S.add("dve", lambda e, r_=r_: e.reciprocal(out=r_.t[:], in_=r_.t[:]), reads=[r_.b], writes=[r_.b])
                        S.add("dve", lambda e, r_=r_, of_=of_, h=h: e.scalar_tensor_tensor(out=r_.t[:], in0=of_.t[:, h, :], scalar=gn2.t[:, i:i + 1], in1=r_.t[:],
                                                                                      op0=ALU.mult, op1=ALU.mult), reads=[r_.b, of_.b, gn2.b], writes=[r_.b])
                        S.add("pool", lambda e, r_=r_, zz=zz, og=og, h=h: e.tensor_tensor(out=og.t[:, h, :], in0=r_.t[:], in1=zz.t[:, h, :], op=ALU.mult),
                              reads=[r_.b, zz.b], writes=[og.b])
                    for m in range(8):
                        pb = PS[4 + m % 4]
                        for f in range(8):
                            S.add("pe", lambda e, pb=pb, f=f, m=m, og=og: e.matmul(pb.t[:], lhsT=W.t[:, f, m * 128:(m + 1) * 128], rhs=og.t[:, f, :],
                                                                              start=(f == 0), stop=(f == 7)), reads=[W.b, og.b], writes=[pb.b])
                        S.add("dve", lambda e, pb=pb, m=m, xx=xx, which=which: e.scalar_tensor_tensor(
                            out=xx.t[:, m, :], in0=pb.t[:], scalar=hgT.t[:, 1, m, which:which + 1], in1=xx.t[:, m, :], op0=ALU.mult, op1=ALU.add),
                            reads=[pb.b, hgT.b, xx.b], writes=[xx.b])
                    S.add("sp", lambda e, xx=xx, tok=tok: e.dma_start(out=xs[:, :, tok], in_=xx.t[:]), reads=[xx.b], dma=True)
                S.end_stage(resched=True)

        def dn_mixer(l, i):
            nst_ = cfg.get("dn_stages", 5)
            for k_, fn in enumerate((dn_inproj, dn_conv, dn_chunks, dn_scan, dn_final)):
                if k_ < nst_:
                    fn(l, i)

        def ah_hyena_zero(l, i):
            with contextlib.ExitStack() as st:
                z = sb(st, "zz", [128, 4, 1024], BF16)
                S.add("dve", lambda e: e.memset(z.t[:], 0.0), writes=[z.b])
                for mt in range(NMT):
                    S.add("sp", lambda e, mt=mt: e.dma_start(out=ay_d[:, 4:8, mt * TM:(mt + 1) * TM], in_=z.t[:]), reads=[z.b], dma=True)
                S.end_stage()

        cur = xT
        layer_list = cfg.get("layers", None)
        layer_list = list(layer_list) if layer_list is not None else list(range(nlayers))
        do_ffn = cfg.get("ffn", True)

        def copy_stage(src, dst):
            for mt in range(NMT):
                S.add("sp", lambda e, mt=mt: e.dma_start(out=dst[:, :, mt * TM:(mt + 1) * TM], in_=src[:, :, mt * TM:(mt + 1) * TM]), dma=True)
            S.end_stage()

        for li, l in enumerate(layer_list):
            modulation_stage(l)
            if do_ffn:
                ffn_stage(l, 0, cur, xs)
            else:
                copy_stage(cur, xs)
            cur = xs
            if do_mix and l % 2 == 0:
                ah_inproj(l, l // 2)
                ah_attention(l, l // 2)
                if cfg.get("hyena", 1):
                    ah_hyena(l, l // 2)
                else:
                    ah_hyena_zero(l, l // 2)
                ah_outproj(l, l // 2)
            if do_mix and l % 2 == 1:
                dn_mixer(l, l // 2)
            last = (li == len(layer_list) - 1)
            if do_ffn:
                ffn_stage(l, 1, cur, yT if last else xs)
            elif last:
                copy_stage(xs, yT)
        print("stages", S.nstage, "ops", S.ninstr)
    return nc


def host_layout(inputs, core):
    f = lambda a: np.ascontiguousarray(a, dtype=np.float32)
    xs_ = np.asarray(inputs["x_sample"][core])
    xp_ = np.asarray(inputs["x_prompt"][4 * core:4 * core + 4]).reshape(1024, D)
    x = np.concatenate([xs_, xp_], axis=0)
    m = {}
    m["xT"] = f(x.T.reshape(8, 128, NTOK).transpose(1, 0, 2))
    cond = np.stack([np.asarray(inputs["c"][core]), np.asarray(inputs["c_ctx"])], axis=1)
    m["condT"] = f(cond.reshape(8, 128, 2).transpose(1, 0, 2))
    ck = np.asarray(inputs["cache_k"][core])
    ckT = ck.transpose(0, 3, 2, 1)
    m["ckT"] = f(np.concatenate([ckT, ckT], axis=1))
    cv = np.asarray(inputs["cache_v"][core]).reshape(2, 4, 128, 128)
    m["cvv"] = f(cv.transpose(0, 2, 1, 3))
    st_ = np.stack([np.asarray(inputs["state_fwd"][core]), np.asarray(inputs["state_bwd"][core])], axis=0)
    m["sf0"] = f(st_.transpose(0, 1, 3, 2, 4))
    return m


def shared_layout(inputs):
    f = lambda a: np.ascontiguousarray(a, dtype=np.float32)
    m = {}
    aw = np.asarray(inputs["ada_w"])
    m["ada_w"] = f(aw.reshape(DEPTH, 8, 128, 9, 1024).transpose(0, 3, 2, 1, 4))
    ab = np.asarray(inputs["ada_b"]).reshape(DEPTH, 72, 128).transpose(2, 0, 1)
    m["ada_b"] = f(np.repeat(ab[..., None], 2, axis=-1))
    g = np.asarray(inputs["norm_g"]).reshape(DEPTH, 3, 8, 128).transpose(3, 0, 1, 2)
    m["norm_g"] = f(np.repeat(g[..., None], 2, axis=-1))
    w13 = np.asarray(inputs["ffn_w13"]).reshape(DEPTH * 2, 8, 128, 2, NFC, 128)
    m["w13"] = f(w13.transpose(0, 4, 2, 1, 3, 5).reshape(DEPTH * 2, NFC, 128, 8, 256))
    w2 = np.asarray(inputs["ffn_w2"]).reshape(DEPTH * 2, NFC, 128, 8, 128)
    m["w2"] = f(w2.transpose(0, 3, 2, 1, 4))
    wi = np.asarray(inputs["mx_w_in"])
    wcat = np.concatenate([wi[:, :, 0:512], wi[:, :, 512:576], wi[:, :, 512:576], wi[:, :, 576:640], wi[:, :, 576:640],
                           wi[:, :, 768:2304], wi[:, :, 640:768]], axis=2)
    m["win"] = f(wcat.reshape(2, 8, 128, 2432).transpose(0, 2, 1, 3))
    wo = np.asarray(inputs["mx_w_out"])
    m["wout"] = f(wo.reshape(2, 8, 128, 1024).transpose(0, 2, 1, 3))
    qn = np.asarray(inputs["q_norm"])
    kn = np.asarray(inputs["k_norm"])
    pidx = np.arange(128)
    m["qkn"] = f(np.stack([qn[:, pidx % 64].T, kn[:, pidx % 64].T], axis=-1))
    sk = np.asarray(inputs["attn_sink"])
    hidx = 2 * np.arange(4)[None, :] + (pidx // 64)[:, None]
    m["sinkT"] = f(sk[:, hidx].transpose(1, 0, 2))
    cw = np.asarray(inputs["hy_conv_w"]).reshape(2, 3, 12, 128)
    m["hcw"] = f(cw.transpose(3, 0, 2, 1))
    cb = np.asarray(inputs["hy_conv_b"]).reshape(2, 12, 128)
    m["hcb"] = f(cb.transpose(2, 0, 1))
    m["hw1"] = f(inputs["hy_w1"])
    m["hb1"] = f(np.stack([np.asarray(inputs["hy_b1"]).T, np.asarray(inputs["hy_freq1"]).T], axis=-1))
    m["hw2"] = f(inputs["hy_w2"])
    m["hb2"] = f(np.stack([np.asarray(inputs["hy_b2"]).T, np.asarray(inputs["hy_freq2"]).T], axis=-1))
    m["hw3"] = f(inputs["hy_w3"])
    hbz = np.asarray(inputs["hy_bias"]).reshape(2, 2, 4, 128)
    m["hbias"] = f(hbz.transpose(3, 0, 1, 2))
    dwi = np.asarray(inputs["dn_w_in"])
    m["dwin"] = f(dwi.reshape(2, 8, 128, 4128).transpose(0, 2, 1, 3))
    dwo = np.asarray(inputs["dn_w_out"])
    m["dwout"] = f(dwo.reshape(2, 8, 128, 1024).transpose(0, 2, 1, 3))
    dc = np.asarray(inputs["dn_conv_w"]).reshape(2, 3, 24, 128)
    m["dcw"] = f(dc.transpose(3, 0, 2, 1))
    prm = np.concatenate([np.asarray(inputs["dn_a_log"]).reshape(2, 16), np.asarray(inputs["dn_dt_bias"]).reshape(2, 16)], axis=1)
    m["dprm"] = f(np.broadcast_to(prm[None], (128, 2, 32)))
    m["dng"] = f(np.asarray(inputs["dn_norm_g"]).T)
    m.update(const_tables())
    return m


_CONST = {}


def const_tables():
    if _CONST:
        return _CONST
    f = lambda a: np.ascontiguousarray(a, dtype=np.float32)
    pidx = np.arange(128)
    a = (pidx % 64) % 32
    inv = (np.float32(10000.0) ** (-np.arange(0, 32, 2, dtype=np.float32) / np.float32(32))).astype(np.float32)
    t = np.arange(4096)
    r = (t // 64).astype(np.float32)
    col = (t % 64).astype(np.float32)
    ang = np.where((a < 16)[:, None], r[None, :] * inv[a % 16][:, None], col[None, :] * inv[a % 16][:, None]).astype(np.float32)
    _CONST["cosT"] = f(np.cos(ang))
    _CONST["sinT"] = f(np.sin(ang))
    rot = np.zeros((128, 128), np.float32)
    for d_out in range(128):
        dd = d_out % 64
        base = d_out - dd
        if dd < 32:
            rot[base + dd + 32, d_out] = -1.0
        else:
            rot[base + dd - 32, d_out] = 1.0
    _CONST["rotT"] = rot
    blk = np.zeros((128, 128), np.float32)
    blk[:64, :64] = 1.0
    blk[64:, 64:] = 1.0
    _CONST["blk1"] = blk
    ko = np.arange(128)[:, None]
    qo = np.arange(128)[None, :]
    _CONST["mprev"] = f(ko >= qo)
    _CONST["mnext"] = f(ko <= qo)
    _CONST["ident"] = np.eye(128, dtype=np.float32)
    pp = np.arange(64)[:, None]
    ff = np.arange(64)[None, :]
    _CONST["dmask"] = f(np.stack([pp <= ff, pp >= ff, pp < ff, pp > ff, pp == ff], axis=1))
    lvm = []
    for lv in range(6):
        b_ = 2 ** lv
        mn = (pp // (2 * b_) == ff // (2 * b_)) & (pp % (2 * b_) >= b_) & (ff % (2 * b_) < b_)
        lvm.append(mn)
        lvm.append(mn.T)
    _CONST["lvmask"] = f(np.stack(lvm, axis=1))
    HY_MIN = math.log(1e-2) / 1.5
    HY_MAX = math.log(1e-2) / 0.3
    dl = np.abs(np.linspace(HY_MIN, HY_MAX, 2048, dtype=np.float32)).astype(np.float32)
    _CONST["deltas"] = f(np.broadcast_to(dl[None, :], (128, 2048)))
    for L in (4096, 256):
        N = 2 * L
        nb = L // 128
        TT = min(512, L)
        k = np.arange(L, dtype=np.int64)
        t = np.arange(L, dtype=np.int64)
        mm = ((2 * k[:, None] + 1) * t[None, :]) % (2 * N)
        ang = mm.astype(np.float64) * (np.pi / N)
        Ckt = np.cos(ang).astype(np.float32)
        Skt = np.sin(ang).astype(np.float32)
        del mm, ang
        F = np.empty((2, nb, 128, nb, 128), ml_dtypes.bfloat16)
        I = np.empty((2, L // TT, 128, nb, TT), ml_dtypes.bfloat16)
        for ci, tab in enumerate((Ckt, Skt)):
            t4 = tab.reshape(nb, 128, nb, 128)
            F[ci] = t4.transpose(0, 3, 2, 1).astype(ml_dtypes.bfloat16)
            t5 = tab.reshape(nb, 128, L // TT, TT)
            I[ci] = t5.transpose(2, 1, 0, 3).astype(ml_dtypes.bfloat16)
        _CONST["dftF%d" % L] = F
        _CONST["dftI%d" % L] = I
        tt_ = np.linspace(0.0, 1.0, L, dtype=np.float32)
        w = (np.float32(2 * math.pi) * np.arange(L, dtype=np.float32) / np.float32(L)).astype(np.float32)
        fr = np.linspace(1e-4, 15.0, 16, dtype=np.float32)
        fw = (fr[None, :] * w[:, None]).astype(np.float32)
        z = np.concatenate([tt_[:, None], np.cos(fw), -np.sin(fw)], axis=-1).astype(np.float32)
        _CONST["zfeat%d" % L] = f(z.T)
        _CONST["tlag%d" % L] = f(-tt_.reshape(nb, 128).T)
    return _CONST


_CACHE = {}


def run(inputs, cfg, ncores=8, cores=None):
    key = tuple(sorted(cfg.items()))
    if key not in _CACHE:
        _CACHE[key] = build_program(cfg)
    nc = _CACHE[key]
    shared = shared_layout(inputs)
    in_maps = []
    for core in (cores if cores is not None else range(ncores)):
        m = dict(shared)
        m.update(host_layout(inputs, core))
        in_maps.append(m)
    res = run_bass_kernel_spmd(nc, in_maps, core_ids=list(range(ncores)))
    return res


def assemble(res, ncores=8):
    yp = np.zeros((32, 256, D), np.float32)
    ys = np.zeros((8, 4096, D), np.float32)
    nk = np.zeros((32, 2, 256, 2, 64), np.float32)
    nv = np.zeros((32, 2, 256, 2, 64), np.float32)
    nsf = np.zeros((32, 2, 8, 128, 128), np.float32)
    nsb = np.zeros((32, 2, 8, 128, 128), np.float32)
    for core in range(ncores):
        r = res.results[core]
        y = r["yT"].transpose(1, 0, 2).reshape(D, NTOK).T
        ys[core] = y[:4096]
        yp[4 * core:4 * core + 4] = y[4096:].reshape(4, 256, D)
        k = r["newk"].reshape(2, 2, 64, 4, 256)
        nk[4 * core:4 * core + 4] = k.transpose(3, 0, 4, 1, 2)
        v = r["newv"].reshape(2, 4, 256, 2, 64)
        nv[4 * core:4 * core + 4] = v.transpose(1, 0, 2, 3, 4)
        s_ = r["nst"]
        nsf[4 * core:4 * core + 4] = s_[0].transpose(1, 0, 3, 2, 4)
        nsb[4 * core:4 * core + 4] = s_[1].transpose(1, 0, 3, 2, 4)
    return yp, ys, nk, nv, nsf, nsb


def kernel(**inputs):
    res = run(inputs, {})
    return assemble(res)
```

```python
import contextlib
import math
import numpy as np
import ml_dtypes
import concourse.bass as bass
import concourse.mybir as mybir
from concourse.bass_utils import run_bass_kernel_spmd

F32 = mybir.dt.float32
BF16 = mybir.dt.bfloat16
AF = mybir.ActivationFunctionType
ALU = mybir.AluOpType

D = 1024
NTOK = 5120
TM = 1024
NMT = NTOK // TM
DFF = 2816
NFC = DFF // 128
DEPTH = 4
EPS = 1e-6

ENGS = ("pe", "dve", "act", "pool", "sp")
NDMASEM = 8


class Buf:
    __slots__ = ("name", "w", "rs", "excl")

    def __init__(self, name="", excl=False):
        self.name = name
        self.w = None
        self.rs = []
        self.excl = excl


class Op:
    __slots__ = ("eng", "fn", "deps", "dma", "sig", "cnt", "semi", "use", "gid", "cost")


class Sched:
    def __init__(self, nc, st):
        self.nc = nc
        self.csem = {e: st.enter_context(nc.semaphore("c_" + e)) for e in ENGS}
        self.dsem = {e: [st.enter_context(nc.semaphore("d_%s%d" % (e, i))) for i in range(NDMASEM)]
                     for e in ("sp", "pool", "act")}
        self.cnt = {e: 0 for e in ENGS}
        self.ndma = {e: 0 for e in ENGS}
        self.ops = []
        self.bufs = []
        self.nstage = 0
        self.ninstr = 0
        self.xlat = 2.0

    def buf(self, name=""):
        return Buf(name)

    COST = {"pe": 0.12, "dve": 0.6, "act": 0.7, "pool": 0.9, "sp": 2.5}

    def add(self, eng, fn, reads=(), writes=(), dma=False, cost=None):
        op = Op()
        op.cost = cost if cost is not None else (2.5 if dma else self.COST[eng])
        op.eng = eng
        op.fn = fn
        op.dma = dma
        op.sig = False
        op.gid = len(self.ops)
        deps = set()
        ex = [b for b in reads if b.excl]
        if ex:
            reads = [b for b in reads if not b.excl]
            writes = list(writes) + [b for b in ex if b not in writes]
        for b in reads:
            if b.w is not None:
                deps.add(b.w)
        for b in writes:
            if b.w is not None:
                deps.add(b.w)
            deps.update(b.rs)
        deps.discard(op.gid)
        op.deps = deps
        for b in reads:
            b.rs.append(op.gid)
            self.bufs.append(b)
        for b in writes:
            b.w = op.gid
            b.rs = []
            self.bufs.append(b)
        self.ops.append(op)
        return op

    def _resched(self, ops):
        import heapq
        n = len(ops)
        succ = [[] for _ in range(n)]
        indeg = [0] * n
        for op in ops:
            indeg[op.gid] = len(op.deps)
            for d in op.deps:
                succ[d].append(op.gid)
        ready_t = [0.0] * n
        fin = [0.0] * n
        heaps = {e: [] for e in ENGS}
        free = {e: 0.0 for e in ENGS}
        for op in ops:
            if indeg[op.gid] == 0:
                heapq.heappush(heaps[op.eng], (0.0, op.gid))
        order = []
        while len(order) < n:
            best = None
            for e in ENGS:
                h = heaps[e]
                if not h:
                    continue
                rt, gid = h[0]
                st_ = max(free[e], rt)
                if best is None or (st_, gid) < (best[0], best[1]):
                    best = (st_, gid, e)
            st_, gid, e = best
            heapq.heappop(heaps[e])
            op = ops[gid]
            if op.dma:
                free[e] = st_ + 0.15
                fin[gid] = st_ + op.cost
            else:
                free[e] = st_ + op.cost
                fin[gid] = st_ + op.cost
            order.append(gid)
            for s_ in succ[gid]:
                so = ops[s_]
                lat = 0.05 if (so.eng == op.eng and not op.dma) else self.xlat
                ready_t[s_] = max(ready_t[s_], fin[gid] + lat)
                indeg[s_] -= 1
                if indeg[s_] == 0:
                    heapq.heappush(heaps[so.eng], (ready_t[s_], s_))
        return order

    def end_stage(self, resched=False):
        nc = self.nc
        ops = self.ops
        if resched and len(ops) > 2:
            order = self._resched(ops)
            remap = {g: k for k, g in enumerate(order)}
            ops = [ops[g] for g in order]
            for k, op in enumerate(ops):
                op.gid = k
                op.deps = {remap[d] for d in op.deps}
        for op in ops:
            if op.dma:
                k = self.ndma[op.eng]
                self.ndma[op.eng] = k + 1
                op.semi = k % NDMASEM
                op.use = k // NDMASEM + 1
        for op in ops:
            nd = set()
            for d in op.deps:
                p = ops[d]
                if (not p.dma) and (not op.dma) and p.eng == "pe" and op.eng == "pe":
                    continue
                nd.add(d)
                p.sig = True
            op.deps = nd
        for op in ops:
            if (not op.dma) and op.sig:
                self.cnt[op.eng] += 1
                op.cnt = self.cnt[op.eng]
        per = {e: [op for op in ops if op.eng == e] for e in ENGS}
        csem, dsem = self.csem, self.dsem
        self.ninstr += len(ops)
        with nc.Block() as block:
            def gen(ename):
                def body(eng):
                    seen_c = {}
                    seen_d = {}
                    for op in per[ename]:
                        need_c = {}
                        need_d = {}
                        for d in op.deps:
                            p = ops[d]
                            if p.dma:
                                key = (p.eng, p.semi)
                                need_d[key] = max(need_d.get(key, 0), 16 * p.use)
                            else:
                                need_c[p.eng] = max(need_c.get(p.eng, 0), p.cnt)
                        if op.dma and op.use > 1:
                            key = (op.eng, op.semi)
                            need_d[key] = max(need_d.get(key, 0), 16 * (op.use - 1))
                        for e, v in need_c.items():
                            if v > seen_c.get(e, 0):
                                eng.wait_ge(csem[e], v)
                                seen_c[e] = v
                        for key, v in need_d.items():
                            if v > seen_d.get(key, 0):
                                eng.wait_ge(dsem[key[0]][key[1]], v)
                                seen_d[key] = v
                        ins = op.fn(eng)
                        if op.dma:
                            ins.then_inc(dsem[op.eng][op.semi], 16)
                        elif op.sig:
                            ins.then_inc(csem[op.eng], 1)
                    last = {}
                    for op in per[ename]:
                        if op.dma:
                            last[op.semi] = op.use
                    for semi, use in last.items():
                        eng.wait_ge(dsem[ename][semi], 16 * use)
                return body

            block.tensor(gen("pe"))
            block.vector(gen("dve"))
            block.scalar(gen("act"))
            block.gpsimd(gen("pool"))
            block.sync(gen("sp"))
        for b in self.bufs:
            b.w = None
            b.rs = []
        self.bufs = []
        self.ops = []
        self.nstage += 1


class T:
    def __init__(self, t, name=""):
        self.t = t
        self.b = Buf(name)


def build_program(cfg):
    nlayers = cfg.get("nlayers", DEPTH)
    do_mix = cfg.get("mixers", True)
    nc = bass.Bass("TRN2", target_bir_lowering=False)

    def din(name, shape, dt=F32):
        return nc.dram_tensor(name, list(shape), dt, kind="ExternalInput").ap()

    def dout(name, shape, dt=F32):
        return nc.dram_tensor(name, list(shape), dt, kind="ExternalOutput").ap()

    dbg = cfg.get("debug", ())

    def dscr(name, shape, dt=F32):
        kind = "ExternalOutput" if name in dbg else "Internal"
        return nc.dram_tensor(name, list(shape), dt, kind=kind).ap()

    xT = din("xT", [128, 8, NTOK])
    condT = din("condT", [128, 8, 2])
    ada_w = din("ada_w", [DEPTH, 9, 128, 8, 1024])
    ada_b = din("ada_b", [128, DEPTH, 72, 2])
    norm_g = din("norm_g", [128, DEPTH, 3, 8, 2])
    w13 = din("w13", [DEPTH * 2, NFC, 128, 8, 256])
    w2 = din("w2", [DEPTH * 2, 8, 128, NFC, 128])
    yT = dout("yT", [128, 8, NTOK])
    xs = dscr("xs", [128, 8, NTOK])
    win = din("win", [2, 128, 8, 2432])
    wout = din("wout", [2, 128, 8, 1024])
    qkn = din("qkn", [128, 2, 2])
    sinkT = din("sinkT", [128, 2, 4])
    cosT_d = din("cosT", [128, 4096])
    sinT_d = din("sinT", [128, 4096])
    rotT_d = din("rotT", [128, 128])
    blk1_d = din("blk1", [128, 128])
    mprev_d = din("mprev", [128, 128])
    mnext_d = din("mnext", [128, 128])
    ckT = din("ckT", [2, 128, 2, 512])
    cvv = din("cvv", [2, 128, 4, 128])
    newk = dout("newk", [2, 2, 64, 1024])
    newv = dout("newv", [2, 1024, 128])
    qT_d = dscr("qT_d", [128, 4, NTOK])
    kT_d = dscr("kT_d", [128, 2, NTOK])
    u3T_d = dscr("u3T_d", [128, 12, NTOK])
    v_d = dscr("v_d", [NTOK, 128])
    ay_d = dscr("ay_d", [128, 8, NTOK], BF16)
    hcw = din("hcw", [128, 2, 12, 3])
    hcb = din("hcb", [128, 2, 12])
    hw1 = din("hw1", [2, 33, 64])
    hb1 = din("hb1", [64, 2, 2])
    hw2 = din("hw2", [2, 64, 64])
    hb2 = din("hb2", [64, 2, 2])
    hw3 = din("hw3", [2, 64, 2048])
    hbias = din("hbias", [128, 2, 2, 4])
    ident_d = din("ident", [128, 128])
    deltas_d = din("deltas", [128, 2048])
    HG = {}
    for L_ in (4096, 256):
        nb_ = L_ // 128
        TT_ = min(512, L_)
        HG[L_] = dict(
            F=din("dftF%d" % L_, [2, nb_, 128, nb_, 128], BF16),
            I=din("dftI%d" % L_, [2, L_ // TT_, 128, nb_, TT_], BF16),
            zf=din("zfeat%d" % L_, [33, L_]),
            tl=din("tlag%d" % L_, [128, nb_]))
    uc_d = dscr("uc_d", [128, 12, NTOK])
    z1T_d = dscr("z1T_d", [128, 4, NTOK])
    hsd_d = dscr("hsd_d", [2, 32, 128, 1024], BF16)
    H_d = dscr("H_d", [2, 2, 32, 128, 512])
    dwin = din("dwin", [2, 128, 8, 4128])
    dwout = din("dwout", [2, 128, 8, 1024])
    dcw = din("dcw", [128, 2, 24, 3])
    dprm = din("dprm", [128, 2, 32])
    dng = din("dng", [128, 2])
    dmask_d = din("dmask", [64, 5, 64])
    lvmask_d = din("lvmask", [64, 12, 64])
    sf0 = din("sf0", [2, 2, 128, 8, 128])
    nst = dout("nst", [2, 2, 4, 128, 8, 128])
    qkvT_d = dscr("qkvT_d", [128, 24, NTOK])
    qkvn_d = dscr("qkvn_d", [128, 24, NTOK])
    zT_d = dscr("zT_d", [128, 8, NTOK])
    ba_d = dscr("ba_d", [NTOK, 32])
    NCH = NTOK // 64
    u_d = dscr("u_d", [2, NCH, 64, 8, 128])
    wT_d = dscr("wT_d", [2, NCH, 128, 8, 64], BF16)
    QgT_d = dscr("QgT_d", [2, NCH, 128, 8, 64], BF16)
    AT_d = dscr("AT_d", [2, NCH, 64, 8, 64], BF16)
    Kd_d = dscr("Kd_d", [2, NCH, 64, 8, 128], BF16)
    egl_d = dscr("egl_d", [2, NCH, 128, 8])
    oT_d = dscr("oT_d", [2, 128, 8, NTOK])

    with contextlib.ExitStack() as top:
        S = Sched(nc, top)
        S.xlat = cfg.get("xlat", 2.0)

        uid = [0]

        def sb(st, name, shape, dt=F32):
            uid[0] += 1
            name = "%s_%d" % (name, uid[0])
            return T(st.enter_context(nc.sbuf_tensor(name, list(shape), dt)), name)

        PS = [T(top.enter_context(nc.psum_tensor("ps%d" % i, [128, 512], F32)), "ps%d" % i) for i in range(8)]
        for p_ in PS:
            p_.b.excl = True
        ones32 = sb(top, "ones32", [128, 128])
        epsT = sb(top, "epsT", [128, 1])
        scond = sb(top, "scond", [128, 8, 2], BF16)
        modT = sb(top, "modT", [128, 72, 2])
        gsT = sb(top, "gsT", [128, 3, 8, 2])
        hgT = sb(top, "hgT", [128, 3, 8, 2])
        adab = sb(top, "adab", [128, DEPTH, 72, 2])
        ng = sb(top, "ng", [128, DEPTH, 3, 8, 2])

        with contextlib.ExitStack() as st:
            cnd = sb(st, "cnd", [128, 8, 2])
            S.add("dve", lambda e: e.memset(ones32.t[:], 1.0), writes=[ones32.b])
            S.add("dve", lambda e: e.memset(epsT.t[:], EPS), writes=[epsT.b])
            S.add("sp", lambda e: e.dma_start(out=cnd.t[:], in_=condT), writes=[cnd.b], dma=True)
            S.add("sp", lambda e: e.dma_start(out=adab.t[:], in_=ada_b), writes=[adab.b], dma=True)
            S.add("sp", lambda e: e.dma_start(out=ng.t[:], in_=norm_g), writes=[ng.b], dma=True)
            S.add("act", lambda e: e.activation(out=scond.t[:], in_=cnd.t[:], func=AF.Silu),
                  reads=[cnd.b], writes=[scond.b])
            S.end_stage()

        def modulation_stage(l):
            with contextlib.ExitStack() as st:
                wa = [sb(st, "wa%d" % i, [128, 8, 1024], BF16) for i in range(2)]
                for blk in range(9):
                    w = wa[blk % 2]
                    S.add("pool", lambda e, w=w, blk=blk: e.dma_start(out=w.t[:], in_=ada_w[l, blk]),
                          writes=[w.b], dma=True)
                    ps = PS[blk % 2]
                    for cc in range(8):
                        for kc in range(8):
                            S.add("pe", lambda e, w=w, ps=ps, cc=cc, kc=kc: e.matmul(
                                ps.t[:, cc * 2:cc * 2 + 2], lhsT=w.t[:, kc, cc * 128:(cc + 1) * 128],
                                rhs=scond.t[:, kc, :], start=(kc == 0), stop=(kc == 7)),
                                reads=[w.b, scond.b], writes=[ps.b])
                    S.add("dve", lambda e, ps=ps, blk=blk: e.tensor_tensor(
                        out=modT.t[:, blk * 8:(blk + 1) * 8, :],
                        in0=ps.t[:, 0:16].rearrange("p (a b) -> p a b", b=2),
                        in1=adab.t[:, l, blk * 8:(blk + 1) * 8, :], op=ALU.add),
                        reads=[ps.b, adab.b], writes=[modT.b])
                for j in range(3):
                    S.add("dve", lambda e, j=j: e.scalar_tensor_tensor(
                        out=gsT.t[:, j], in0=modT.t[:, (3 * j + 1) * 8:(3 * j + 2) * 8, :], scalar=1.0,
                        in1=ng.t[:, l, j], op0=ALU.add, op1=ALU.mult),
                        reads=[modT.b, ng.b], writes=[gsT.b])
                    S.add("dve", lambda e, j=j: e.tensor_scalar(
                        out=hgT.t[:, j], in0=modT.t[:, (3 * j + 2) * 8:(3 * j + 3) * 8, :],
                        scalar1=(1.0 if j == 1 else 0.5), scalar2=None, op0=ALU.mult),
                        reads=[modT.b], writes=[hgT.b])
                S.end_stage(resched=cfg.get("rs_x", True))

        def norm_mod(st_tiles, xb, hb, j, which, ps_pair):
            sq, rstd, tmp = st_tiles
            acc = sq[2]
            for c in range(8):
                if c == 0:
                    S.add("act", lambda e: e.activation(out=acc.t[:], in_=xb.t[:, 0, :], func=AF.Square),
                          reads=[xb.b], writes=[acc.b])
                    continue
                q = sq[c % 2]
                S.add("act", lambda e, q=q, c=c: e.activation(out=q.t[:], in_=xb.t[:, c, :], func=AF.Square),
                      reads=[xb.b], writes=[q.b])
                S.add("dve", lambda e, q=q: e.tensor_tensor(out=acc.t[:], in0=acc.t[:], in1=q.t[:], op=ALU.add),
                      reads=[acc.b, q.b], writes=[acc.b])
            for s in range(TM // 512):
                S.add("pe", lambda e, s=s: e.matmul(
                    ps_pair[s].t[:], lhsT=ones32.t[:], rhs=acc.t[:, s * 512:(s + 1) * 512],
                    start=True, stop=True), reads=[acc.b, ones32.b], writes=[ps_pair[s].b])
            for s in range(TM // 512):
                S.add("act", lambda e, s=s: e.activation(
                    out=rstd.t[:, s * 512:(s + 1) * 512], in_=ps_pair[s].t[:], func=AF.Sqrt,
                    scale=1.0 / D, bias=epsT.t[:, 0:1]), reads=[ps_pair[s].b, epsT.b], writes=[rstd.b])
            S.add("dve", lambda e: e.reciprocal(out=rstd.t[:], in_=rstd.t[:]), reads=[rstd.b], writes=[rstd.b])
            for c in range(8):
                tp = tmp[c % 2]
                S.add("dve", lambda e, tp=tp, c=c: e.tensor_tensor(
                    out=tp.t[:], in0=xb.t[:, c, :], in1=rstd.t[:], op=ALU.mult),
                    reads=[xb.b, rstd.b], writes=[tp.b])
                S.add("act", lambda e, tp=tp, c=c: e.activation(
                    out=hb.t[:, c, :], in_=tp.t[:], func=AF.Identity,
                    scale=gsT.t[:, j, c, which:which + 1], bias=modT.t[:, 3 * j * 8 + c, which:which + 1]),
                    reads=[tp.b, gsT.b, modT.b], writes=[hb.b])

        def ffn_stage(l, hf, src, dst):
            j = 0 if hf == 0 else 2
            lh = l * 2 + hf
            with contextlib.ExitStack() as st:
                X = [sb(st, "x%d" % i, [128, 8, TM]) for i in range(2)]
                Hh = [sb(st, "h%d" % i, [128, 8, TM], BF16) for i in range(2)]
                sq = [sb(st, "sq%d" % i, [128, TM]) for i in range(3)]
                tmp = [sb(st, "tmp%d" % i, [128, TM]) for i in range(2)]
                rstd = sb(st, "rstd", [128, TM])
                actT = sb(st, "actT", [128, NFC, TM], BF16)
                sg = [sb(st, "sg%d" % i, [128, 512]) for i in range(2)]
                wp = [sb(st, "wp%d" % i, [128, 8, 256], BF16) for i in range(4)]
                w2t = [sb(st, "w2t%d" % i, [128, NFC, 128], BF16) for i in range(2)]
                dsrc = [Buf() for _ in range(NMT)]
                wctr = [0, 0]

                def load_x(mt):
                    xb = X[mt % 2]
                    S.add("sp", lambda e: e.dma_start(out=xb.t[:], in_=src[:, :, mt * TM:(mt + 1) * TM]),
                          reads=[dsrc[mt]], writes=[xb.b], dma=True)

                def stage_a(mt):
                    which = 0 if mt < 4 else 1
                    norm_mod((sq, rstd, tmp), X[mt % 2], Hh[mt % 2], j, which, (PS[4], PS[5]))

                load_x(0)
                stage_a(0)
                for mt in range(NMT):
                    which = 0 if mt < 4 else 1
                    xb = X[mt % 2]
                    hb = Hh[mt % 2]
                    if mt + 1 < NMT:
                        load_x(mt + 1)
                    for jp in range(NFC):
                        w = wp[wctr[0] % 4]
                        wctr[0] += 1
                        S.add("pool", lambda e, w=w, jp=jp: e.dma_start(out=w.t[:], in_=w13[lh, jp]),
                              writes=[w.b], dma=True)
                        for s in range(TM // 512):
                            k = jp * 2 + s
                            pg, pu = PS[k % 2], PS[2 + k % 2]
                            for half, pb in ((0, pg), (1, pu)):
                                for c in range(8):
                                    S.add("pe", lambda e, w=w, pb=pb, c=c, s=s, half=half, hb=hb: e.matmul(
                                        pb.t[:], lhsT=w.t[:, c, half * 128:(half + 1) * 128],
                                        rhs=hb.t[:, c, s * 512:(s + 1) * 512], start=(c == 0), stop=(c == 7)),
                                        reads=[w.b, hb.b], writes=[pb.b])
                            g = sg[k % 2]
                            S.add("act", lambda e, g=g, pg=pg: e.activation(out=g.t[:], in_=pg.t[:], func=AF.Silu),
                                  reads=[pg.b], writes=[g.b])
                            S.add("dve", lambda e, g=g, pu=pu, jp=jp, s=s: e.tensor_tensor(
                                out=actT.t[:, jp, s * 512:(s + 1) * 512], in0=g.t[:], in1=pu.t[:], op=ALU.mult),
                                reads=[g.b, pu.b], writes=[actT.b])
                        if jp == 11 and mt + 1 < NMT:
                            stage_a(mt + 1)
                    for m in range(8):
                        w = w2t[wctr[1] % 2]
                        wctr[1] += 1
                        S.add("pool", lambda e, w=w, m=m: e.dma_start(out=w.t[:], in_=w2[lh, m], max_dma_last_dim=4096),
                              writes=[w.b], dma=True)
                        for s in range(TM // 512):
                            pb = PS[6 + (m * 2 + s) % 2]
                            for f in range(NFC):
                                S.add("pe", lambda e, w=w, pb=pb, f=f, s=s: e.matmul(
                                    pb.t[:], lhsT=w.t[:, f, :], rhs=actT.t[:, f, s * 512:(s + 1) * 512],
                                    start=(f == 0), stop=(f == NFC - 1)), reads=[w.b, actT.b], writes=[pb.b])
                            S.add("dve", lambda e, pb=pb, m=m, s=s, xb=xb, which=which: e.scalar_tensor_tensor(
                                out=xb.t[:, m, s * 512:(s + 1) * 512], in0=pb.t[:],
                                scalar=hgT.t[:, j, m, which:which + 1], in1=xb.t[:, m, s * 512:(s + 1) * 512],
                                op0=ALU.mult, op1=ALU.add), reads=[pb.b, hgT.b, xb.b], writes=[xb.b])
                    S.add("sp", lambda e, xb=xb, mt=mt: e.dma_start(out=dst[:, :, mt * TM:(mt + 1) * TM], in_=xb.t[:]),
                          reads=[xb.b], writes=[dsrc[mt]], dma=True)
                S.end_stage(resched=cfg.get("rs_ffn", False))

        def ah_inproj(l, i):
            with contextlib.ExitStack() as st:
                X = [sb(st, "x", [128, 8, TM]) for _ in range(2)]
                Hh = [sb(st, "h", [128, 8, TM], BF16) for _ in range(2)]
                sq = [sb(st, "sq", [128, TM]) for _ in range(3)]
                tmp = [sb(st, "tmp", [128, TM]) for _ in range(2)]
                rstd = sb(st, "rstd", [128, TM])
                W = sb(st, "win", [128, 8, 2432], BF16)
                stg = [sb(st, "stg", [128, 512]) for _ in range(4)]
                vst = [sb(st, "vst", [128, 8, 128]) for _ in range(2)]
                for kc2 in range(4):
                    S.add("pool", lambda e, kc2=kc2: e.dma_start(out=W.t[:, 2 * kc2:2 * kc2 + 2, :], in_=win[i, :, 2 * kc2:2 * kc2 + 2, :],
                                                               max_dma_last_dim=4096), writes=[W.b], dma=True)

                def load_x(mt):
                    xb = X[mt % 2]
                    S.add("sp", lambda e: e.dma_start(out=xb.t[:], in_=xs[:, :, mt * TM:(mt + 1) * TM]),
                          writes=[xb.b], dma=True)

                def stage_a(mt):
                    which = 0 if mt < 4 else 1
                    norm_mod((sq, rstd, tmp), X[mt % 2], Hh[mt % 2], 1, which, (PS[4], PS[5]))

                load_x(0)
                stage_a(0)
                ctr = [0]
                for mt in range(NMT):
                    hb = Hh[mt % 2]
                    if mt + 1 < NMT:
                        load_x(mt + 1)
                    for oc in range(18):
                        if oc < 4:
                            dst, ch = qT_d, oc
                        elif oc < 6:
                            dst, ch = kT_d, oc - 4
                        else:
                            dst, ch = u3T_d, oc - 6
                        for s_ in range(TM // 512):
                            k = ctr[0]
                            ctr[0] += 1
                            pb = PS[k % 4]
                            sg_ = stg[k % 4]
                            for c in range(8):
                                S.add("pe", lambda e, pb=pb, c=c, s_=s_, oc=oc, hb=hb: e.matmul(
                                    pb.t[:], lhsT=W.t[:, c, oc * 128:(oc + 1) * 128],
                                    rhs=hb.t[:, c, s_ * 512:(s_ + 1) * 512], start=(c == 0), stop=(c == 7)),
                                    reads=[W.b, hb.b], writes=[pb.b])
                            if k % 2 == 0:
                                S.add("act", lambda e, pb=pb, sg_=sg_: e.activation(out=sg_.t[:], in_=pb.t[:], func=AF.Identity),
                                      reads=[pb.b], writes=[sg_.b])
                            else:
                                S.add("dve", lambda e, pb=pb, sg_=sg_: e.tensor_copy(out=sg_.t[:], in_=pb.t[:]),
                                      reads=[pb.b], writes=[sg_.b])
                            t0 = mt * TM + s_ * 512
                            S.add("sp", lambda e, dst=dst, ch=ch, t0=t0, sg_=sg_: e.dma_start(
                                out=dst[:, ch, t0:t0 + 512], in_=sg_.t[:]), reads=[sg_.b], dma=True)
                        if oc == 9 and mt + 1 < NMT:
                            stage_a(mt + 1)
                    vs_ = vst[mt % 2]
                    for tb in range(TM // 128):
                        pb = PS[6 + tb % 2]
                        for c in range(8):
                            S.add("pe", lambda e, pb=pb, c=c, tb=tb, hb=hb: e.matmul(
                                pb.t[:, 0:128], lhsT=hb.t[:, c, tb * 128:(tb + 1) * 128], rhs=W.t[:, c, 2304:2432],
                                start=(c == 0), stop=(c == 7)), reads=[W.b, hb.b], writes=[pb.b])
                        S.add("dve", lambda e, pb=pb, tb=tb, vs_=vs_: e.tensor_copy(out=vs_.t[:, tb, :], in_=pb.t[:, 0:128]),
                              reads=[pb.b], writes=[vs_.b])
                    S.add("sp", lambda e, mt=mt, vs_=vs_: e.dma_start(
                        out=v_d[mt * TM:(mt + 1) * TM, :].rearrange("(tb p) n -> p tb n", p=128), in_=vs_.t[:]),
                        reads=[vs_.b], dma=True)
                S.end_stage(resched=cfg.get("rs_x", True))

        def ah_attention(l, i):
            with contextlib.ExitStack() as st:
                cosT = sb(st, "cosT", [128, 4096])
                sinT = sb(st, "sinT", [128, 4096])
                rotT = sb(st, "rotT", [128, 128])
                blk1 = sb(st, "blk1", [128, 128])
                mprev = sb(st, "mprev", [128, 128], BF16)
                mnext = sb(st, "mnext", [128, 128], BF16)
                onesb = sb(st, "onesb", [128, 64], BF16)
                gn = sb(st, "gn", [128, 2])
                gq8 = sb(st, "gq8", [128, 1])
                esink = sb(st, "esink", [128, 4])
                KT = sb(st, "KT", [128, 2, 4096], BF16)
                VV = sb(st, "VV", [128, 32, 128], BF16)
                CK = sb(st, "CK", [128, 2, 512], BF16)
                CV = sb(st, "CV", [128, 4, 128], BF16)
                kin = [sb(st, "kin", [128, 2, 512]) for _ in range(2)]
                qin = [sb(st, "qin", [128, 4, 512]) for _ in range(2)]
                qp = [sb(st, "qp", [128, 4, 512], BF16) for _ in range(2)]
                sqq = [sb(st, "sqq", [128, 512]) for _ in range(2)]
                rs_ = [sb(st, "rs", [128, 512]) for _ in range(2)]
                kg_ = [sb(st, "kg", [128, 512]) for _ in range(2)]
                t1_ = [sb(st, "t1", [128, 512]) for _ in range(2)]
                t2_ = [sb(st, "t2", [128, 512]) for _ in range(2)]
                pt = [sb(st, "pt", [128, 512], BF16) for _ in range(3)]
                den = [sb(st, "den", [128, 512]) for _ in range(2)]
                aout = [sb(st, "aout", [128, 4, 512], BF16) for _ in range(2)]
                kno = [sb(st, "kno", [128, 2, 256]) for _ in range(2)]

                S.add("sp", lambda e: e.dma_start(out=cosT.t[:], in_=cosT_d), writes=[cosT.b], dma=True)
                S.add("sp", lambda e: e.dma_start(out=sinT.t[:], in_=sinT_d), writes=[sinT.b], dma=True)
                S.add("sp", lambda e: e.dma_start(out=rotT.t[:], in_=rotT_d), writes=[rotT.b], dma=True)
                S.add("sp", lambda e: e.dma_start(out=blk1.t[:], in_=blk1_d), writes=[blk1.b], dma=True)
                S.add("pool", lambda e: e.dma_start(out=mprev.t[:], in_=mprev_d), writes=[mprev.b], dma=True)
                S.add("pool", lambda e: e.dma_start(out=mnext.t[:], in_=mnext_d), writes=[mnext.b], dma=True)
                S.add("sp", lambda e: e.dma_start(out=gn.t[:], in_=qkn[:, i, :]), writes=[gn.b], dma=True)
                S.add("sp", lambda e: e.dma_start(out=esink.t[:], in_=sinkT[:, i, :]), writes=[esink.b], dma=True)
                S.add("pool", lambda e: e.dma_start(out=CK.t[:], in_=ckT[i]), writes=[CK.b], dma=True)
                S.add("pool", lambda e: e.dma_start(out=CV.t[:], in_=cvv[i]), writes=[CV.b], dma=True)
                S.add("dve", lambda e: e.memset(onesb.t[:], 1.0), writes=[onesb.b])
                S.add("dve", lambda e: e.tensor_scalar(out=gq8.t[:], in0=gn.t[:, 0:1], scalar1=0.125, scalar2=None, op0=ALU.mult),
                      reads=[gn.b], writes=[gq8.b])
                S.add("act", lambda e: e.activation(out=esink.t[:], in_=esink.t[:], func=AF.Exp), reads=[esink.b], writes=[esink.b])
                pctr = [0]

                def qk_prep(src, srcb, n, gain, gainb, rope_t0, out, outb, nout=None, noutb=None):
                    k = pctr[0]
                    pctr[0] += 1
                    sq_, r_, g_, a_, b_ = sqq[k % 2], rs_[k % 2], kg_[k % 2], t1_[k % 2], t2_[k % 2]
                    pb = PS[3]
                    S.add("act", lambda e: e.activation(out=sq_.t[:, 0:n], in_=src, func=AF.Square), reads=[srcb], writes=[sq_.b])
                    S.add("pe", lambda e: e.matmul(pb.t[:, 0:n], lhsT=blk1.t[:], rhs=sq_.t[:, 0:n], start=True, stop=True),
                          reads=[blk1.b, sq_.b], writes=[pb.b])
                    S.add("act", lambda e: e.activation(out=r_.t[:, 0:n], in_=pb.t[:, 0:n], func=AF.Sqrt, scale=1.0 / 64,
                                                        bias=epsT.t[:, 0:1]), reads=[pb.b, epsT.b], writes=[r_.b])
                    S.add("dve", lambda e: e.reciprocal(out=r_.t[:, 0:n], in_=r_.t[:, 0:n]), reads=[r_.b], writes=[r_.b])
                    if rope_t0 is None:
                        if nout is not None:
                            S.add("dve", lambda e: e.scalar_tensor_tensor(out=nout, in0=src, scalar=gain, in1=r_.t[:, 0:n],
                                                                          op0=ALU.mult, op1=ALU.mult),
                                  reads=[srcb, gainb, r_.b], writes=[noutb])
                        S.add("dve", lambda e: e.scalar_tensor_tensor(out=out, in0=src, scalar=gain, in1=r_.t[:, 0:n],
                                                                      op0=ALU.mult, op1=ALU.mult),
                              reads=[srcb, gainb, r_.b], writes=[outb])
                        return
                    S.add("dve", lambda e: e.scalar_tensor_tensor(out=g_.t[:, 0:n], in0=src, scalar=gain, in1=r_.t[:, 0:n],
                                                                  op0=ALU.mult, op1=ALU.mult),
                          reads=[srcb, gainb, r_.b], writes=[g_.b])
                    S.add("pe", lambda e: e.matmul(pb.t[:, 0:n], lhsT=rotT.t[:], rhs=g_.t[:, 0:n], start=True, stop=True),
                          reads=[rotT.b, g_.b], writes=[pb.b])
                    S.add("dve", lambda e: e.tensor_tensor(out=b_.t[:, 0:n], in0=pb.t[:, 0:n], in1=sinT.t[:, rope_t0:rope_t0 + n], op=ALU.mult),
                          reads=[pb.b, sinT.b], writes=[b_.b])
                    S.add("pool", lambda e: e.tensor_tensor(out=a_.t[:, 0:n], in0=g_.t[:, 0:n], in1=cosT.t[:, rope_t0:rope_t0 + n], op=ALU.mult),
                          reads=[g_.b, cosT.b], writes=[a_.b])
                    S.add("dve", lambda e: e.tensor_tensor(out=out, in0=a_.t[:, 0:n], in1=b_.t[:, 0:n], op=ALU.add),
                          reads=[a_.b, b_.b], writes=[outb])

                stc = [0]
                gctr = [0]

                def attend(qpt, nq, blocks, dst_t0):
                    gi = gctr[0]
                    gctr[0] += 1
                    ao = aout[gi % 2]
                    for c in range(4):
                        par = (gi * 4 + c) % 2
                        PA, PB = PS[4 + 2 * par], PS[5 + 2 * par]
                        for hh in range(2):
                            h = 2 * c + hh
                            kvh = h // 4
                            lo = hh * 64
                            nb = len(blocks)
                            for idx, (Kt, kcol, Vt, vblk, q0, q1, masks) in enumerate(blocks):
                                n = q1 - q0
                                k = stc[0]
                                stc[0] += 1
                                ST = PS[k % 3]
                                P_ = pt[k % 3]
                                S.add("pe", lambda e, ST=ST, Kt=Kt, kcol=kcol, q0=q0, q1=q1, n=n, lo=lo, kvh=kvh, c=c: e.matmul(
                                    ST.t[:, 0:n], lhsT=Kt.t[lo:lo + 64, kvh, kcol:kcol + 128], rhs=qpt.t[lo:lo + 64, c, q0:q1],
                                    start=True, stop=True), reads=[Kt.b, qpt.b], writes=[ST.b])
                                S.add("act", lambda e, ST=ST, P_=P_, n=n: e.activation(out=P_.t[:, 0:n], in_=ST.t[:, 0:n], func=AF.Exp),
                                      reads=[ST.b], writes=[P_.b])
                                for (moff, mt_) in masks:
                                    S.add("dve", lambda e, P_=P_, moff=moff, mt_=mt_: e.tensor_tensor(
                                        out=P_.t[:, moff:moff + 128], in0=P_.t[:, moff:moff + 128], in1=mt_.t[:], op=ALU.mult),
                                        reads=[P_.b, mt_.b], writes=[P_.b])
                                S.add("pe", lambda e, PA=PA, Vt=Vt, vblk=vblk, P_=P_, n=n, q0=q0, q1=q1, lo=lo, kvh=kvh, idx=idx, nb=nb: e.matmul(
                                    PA.t[lo:lo + 64, q0:q1], lhsT=Vt.t[:, vblk, kvh * 64:(kvh + 1) * 64], rhs=P_.t[:, 0:n],
                                    start=(idx == 0), stop=(idx == nb - 1)), reads=[Vt.b, P_.b], writes=[PA.b])
                                S.add("pe", lambda e, PB=PB, P_=P_, n=n, q0=q0, q1=q1, lo=lo, idx=idx, nb=nb: e.matmul(
                                    PB.t[lo:lo + 64, q0:q1], lhsT=onesb.t[:, 0:64], rhs=P_.t[:, 0:n],
                                    start=(idx == 0), stop=(idx == nb - 1)), reads=[onesb.b, P_.b], writes=[PB.b])
                        dn = den[(gi * 4 + c) % 2]
                        S.add("dve", lambda e, dn=dn, PB=PB, c=c: e.tensor_scalar(out=dn.t[:, 0:nq], in0=PB.t[:, 0:nq], scalar1=esink.t[:, c:c + 1],
                                                                             scalar2=None, op0=ALU.add), reads=[PB.b, esink.b], writes=[dn.b])
                        S.add("dve", lambda e, dn=dn: e.reciprocal(out=dn.t[:, 0:nq], in_=dn.t[:, 0:nq]), reads=[dn.b], writes=[dn.b])
                        S.add("dve", lambda e, dn=dn, PA=PA, c=c: e.tensor_tensor(out=ao.t[:, c, 0:nq], in0=PA.t[:, 0:nq], in1=dn.t[:, 0:nq], op=ALU.mult),
                              reads=[PA.b, dn.b], writes=[ao.b])
                    S.add("sp", lambda e: e.dma_start(out=ay_d[:, 0:4, dst_t0:dst_t0 + nq], in_=ao.t[:, :, 0:nq]), reads=[ao.b], dma=True)

                for tt in range(8):
                    ki = kin[tt % 2]
                    S.add("sp", lambda e, ki=ki, tt=tt: e.dma_start(out=ki.t[:], in_=kT_d[:, :, tt * 512:(tt + 1) * 512]), writes=[ki.b], dma=True)
                    S.add("pool", lambda e, tt=tt: e.dma_start(
                        out=VV.t[:, tt * 4:(tt + 1) * 4, :], in_=v_d[tt * 512:(tt + 1) * 512, :].rearrange("(tb p) n -> p tb n", p=128)),
                        writes=[VV.b], dma=True)
                    for ch in range(2):
                        qk_prep(ki.t[:, ch, :], ki.b, 512, gn.t[:, 1:2], gn.b, tt * 512, KT.t[:, ch, tt * 512:(tt + 1) * 512], KT.b)
                for g in range(8):
                    qi = qin[g % 2]
                    qq = qp[g % 2]
                    S.add("sp", lambda e, qi=qi, g=g: e.dma_start(out=qi.t[:], in_=qT_d[:, :, g * 512:(g + 1) * 512]), writes=[qi.b], dma=True)
                    for c in range(4):
                        qk_prep(qi.t[:, c, :], qi.b, 512, gq8.t[:, 0:1], gq8.b, g * 512, qq.t[:, c, :], qq.b)
                    blocks = []
                    for b_ in range(4):
                        blocks.append((CK, b_ * 128, CV, b_, 0, 512, []))
                    for kb in range(max(0, 4 * g - 1), min(32, 4 * g + 5)):
                        qb0 = max(kb - 1, 4 * g)
                        qb1 = min(kb + 1, 4 * g + 3)
                        masks = []
                        for qb in range(qb0, qb1 + 1):
                            if qb == kb + 1:
                                masks.append(((qb - qb0) * 128, mprev))
                            elif qb == kb - 1:
                                masks.append(((qb - qb0) * 128, mnext))
                        blocks.append((KT, kb * 128, VV, kb, (qb0 - 4 * g) * 128, (qb1 - 4 * g + 1) * 128, masks))
                    attend(qq, 512, blocks, g * 512)
                KTp = [sb(st, "KTp", [128, 2, 256], BF16) for _ in range(2)]
                VVp = [sb(st, "VVp", [128, 2, 128], BF16) for _ in range(2)]
                for pi in range(4):
                    t0 = 4096 + pi * 256
                    ki = kin[pi % 2]
                    kt, vv, kn_ = KTp[pi % 2], VVp[pi % 2], kno[pi % 2]
                    S.add("sp", lambda e, ki=ki, t0=t0: e.dma_start(out=ki.t[:, :, 0:256], in_=kT_d[:, :, t0:t0 + 256]), writes=[ki.b], dma=True)
                    S.add("pool", lambda e, vv=vv, t0=t0: e.dma_start(
                        out=vv.t[:], in_=v_d[t0:t0 + 256, :].rearrange("(tb p) n -> p tb n", p=128)), writes=[vv.b], dma=True)
                    for ch in range(2):
                        qk_prep(ki.t[:, ch, 0:256], ki.b, 256, gn.t[:, 1:2], gn.b, None, kt.t[:, ch, :], kt.b,
                                nout=kn_.t[:, ch, :], noutb=kn_.b)
                    S.add("sp", lambda e, kn_=kn_, pi=pi: e.dma_start(
                        out=newk[i, :, :, pi * 256:(pi + 1) * 256].rearrange("k d t -> d k t"), in_=kn_.t[0:64, :, :]),
                        reads=[kn_.b], dma=True)
                    qi = qin[pi % 2]
                    qq = qp[pi % 2]
                    S.add("sp", lambda e, qi=qi, t0=t0: e.dma_start(out=qi.t[:, :, 0:256], in_=qT_d[:, :, t0:t0 + 256]), writes=[qi.b], dma=True)
                    for c in range(4):
                        qk_prep(qi.t[:, c, 0:256], qi.b, 256, gq8.t[:, 0:1], gq8.b, None, qq.t[:, c, 0:256], qq.b)
                    blocks = [(kt, b_ * 128, vv, b_, 0, 256, []) for b_ in range(2)]
                    attend(qq, 256, blocks, t0)
                S.add("sp", lambda e: e.dma_start(out=newv[i], in_=v_d[4096:5120, :]), dma=True)
                S.end_stage(resched=True)

        def ah_outproj(l, i):
            with contextlib.ExitStack() as st:
                X = [sb(st, "x", [128, 8, TM]) for _ in range(2)]
                AY = [sb(st, "ay", [128, 8, TM], BF16) for _ in range(2)]
                W = sb(st, "wout", [128, 8, 1024], BF16)
                for kc2 in range(4):
                    S.add("pool", lambda e, kc2=kc2: e.dma_start(out=W.t[:, 2 * kc2:2 * kc2 + 2, :], in_=wout[i, :, 2 * kc2:2 * kc2 + 2, :],
                                                               max_dma_last_dim=4096), writes=[W.b], dma=True)

                def load(mt):
                    xb, ab = X[mt % 2], AY[mt % 2]
                    S.add("sp", lambda e: e.dma_start(out=xb.t[:], in_=xs[:, :, mt * TM:(mt + 1) * TM]), writes=[xb.b], dma=True)
                    S.add("sp", lambda e: e.dma_start(out=ab.t[:], in_=ay_d[:, :, mt * TM:(mt + 1) * TM]), writes=[ab.b], dma=True)

                load(0)
                for mt in range(NMT):
                    which = 0 if mt < 4 else 1
                    xb, ab = X[mt % 2], AY[mt % 2]
                    if mt + 1 < NMT:
                        load(mt + 1)
                    for m in range(8):
                        for s_ in range(TM // 512):
                            pb = PS[(m * 2 + s_) % 4]
                            for f in range(8):
                                S.add("pe", lambda e, pb=pb, f=f, m=m, s_=s_, ab=ab: e.matmul(
                                    pb.t[:], lhsT=W.t[:, f, m * 128:(m + 1) * 128], rhs=ab.t[:, f, s_ * 512:(s_ + 1) * 512],
                                    start=(f == 0), stop=(f == 7)), reads=[W.b, ab.b], writes=[pb.b])
                            S.add("dve", lambda e, pb=pb, m=m, s_=s_, xb=xb, which=which: e.scalar_tensor_tensor(
                                out=xb.t[:, m, s_ * 512:(s_ + 1) * 512], in0=pb.t[:],
                                scalar=hgT.t[:, 1, m, which:which + 1], in1=xb.t[:, m, s_ * 512:(s_ + 1) * 512],
                                op0=ALU.mult, op1=ALU.add), reads=[pb.b, hgT.b, xb.b], writes=[xb.b])
                    S.add("sp", lambda e, xb=xb, mt=mt: e.dma_start(out=xs[:, :, mt * TM:(mt + 1) * TM], in_=xb.t[:]),
                          reads=[xb.b], dma=True)
                S.end_stage(resched=cfg.get("rs_x", True))


        def hy_conv(l, i):
            with contextlib.ExitStack() as st:
                cw = sb(st, "cw", [128, 12, 3])
                cb = sb(st, "cb", [128, 12])
                U = [sb(st, "U", [128, 12, 514]) for _ in range(2)]
                O = [sb(st, "O", [128, 12, 512]) for _ in range(2)]
                S.add("sp", lambda e: e.dma_start(out=cw.t[:], in_=hcw[:, i]), writes=[cw.b], dma=True)
                S.add("sp", lambda e: e.dma_start(out=cb.t[:], in_=hcb[:, i]), writes=[cb.b], dma=True)
                tiles = [(tt * 512, 512, tt == 0, tt == 7) for tt in range(8)] + [(4096 + 256 * pi, 256, True, True) for pi in range(4)]
                for ti, (t0, n, first, lastt) in enumerate(tiles):
                    u, o = U[ti % 2], O[ti % 2]
                    a0 = t0 if first else t0 - 1
                    a1 = t0 + n if lastt else t0 + n + 1
                    c0 = 1 if first else 0
                    S.add("sp", lambda e, u=u, a0=a0, a1=a1, c0=c0: e.dma_start(out=u.t[:, :, c0:c0 + (a1 - a0)], in_=u3T_d[:, :, a0:a1]),
                          writes=[u.b], dma=True)
                    if first:
                        S.add("pool", lambda e, u=u: e.memset(u.t[:, :, 0:1], 0.0), writes=[u.b])
                    if lastt:
                        S.add("pool", lambda e, u=u, n=n: e.memset(u.t[:, :, n + 1:n + 2], 0.0), writes=[u.b])
                    for ch in range(12):
                        en = "dve"
                        S.add(en, lambda e, u=u, o=o, ch=ch, n=n: e.tensor_scalar(
                            out=o.t[:, ch, 0:n], in0=u.t[:, ch, 0:n], scalar1=cw.t[:, ch, 0:1], scalar2=cb.t[:, ch:ch + 1],
                            op0=ALU.mult, op1=ALU.add), reads=[u.b, cw.b, cb.b], writes=[o.b])
                        S.add(en, lambda e, u=u, o=o, ch=ch, n=n: e.scalar_tensor_tensor(
                            out=o.t[:, ch, 0:n], in0=u.t[:, ch, 1:n + 1], scalar=cw.t[:, ch, 1:2], in1=o.t[:, ch, 0:n],
                            op0=ALU.mult, op1=ALU.add), reads=[u.b, cw.b, o.b], writes=[o.b])
                        S.add(en, lambda e, u=u, o=o, ch=ch, n=n: e.scalar_tensor_tensor(
                            out=o.t[:, ch, 0:n], in0=u.t[:, ch, 2:n + 2], scalar=cw.t[:, ch, 2:3], in1=o.t[:, ch, 0:n],
                            op0=ALU.mult, op1=ALU.add), reads=[u.b, cw.b, o.b], writes=[o.b])
                    S.add("sp", lambda e, o=o, t0=t0, n=n: e.dma_start(out=uc_d[:, :, t0:t0 + n], in_=o.t[:, :, 0:n]), reads=[o.b], dma=True)
                S.end_stage(resched=True)

        MAGIC = 12582912.0

        def hy_filter_gen(i, L, rn):
            G = HG[L]
            nb = L // 128
            TT = min(512, L)
            with contextlib.ExitStack() as st:
                zf = sb(st, "zf", [33, L])
                w1 = sb(st, "w1", [33, 64])
                w2 = sb(st, "w2", [64, 64])
                w3 = sb(st, "w3", [64, 2048])
                p1 = sb(st, "p1", [64, 2])
                p2 = sb(st, "p2", [64, 2])
                h1T = sb(st, "h1T", [64, L])
                h2T = sb(st, "h2T", [64, L])
                delt = sb(st, "delt", [128, 2048])
                tl = sb(st, "tl", [128, nb])
                dec = [sb(st, "dec", [128, 2048]) for _ in range(2)]
                hh = [sb(st, "hh", [128, 2048]) for _ in range(2)]
                sqh = [sb(st, "sqh", [128, 2048]) for _ in range(2)]
                stg = [sb(st, "hstg", [128, 2, 1024], BF16) for _ in range(2)]
                uu = [sb(st, "uu", [64, 512]) for _ in range(2)]
                ta = [sb(st, "ta", [64, 512]) for _ in range(2)]
                nr = [sb(st, "nr", [64, 512]) for _ in range(2)]
                sst = sb(st, "sst", [128, 1024])
                for (tt_, src) in ((zf, G["zf"]), (w1, hw1[i]), (w2, hw2[i]), (w3, hw3[i]), (p1, hb1[:, i, :]), (p2, hb2[:, i, :]),
                                  (delt, deltas_d), (tl, G["tl"])):
                    S.add("sp", lambda e, tt_=tt_, src=src: e.dma_start(out=tt_.t[:], in_=src), writes=[tt_.b], dma=True)
                for pp in (p1, p2):
                    S.add("dve", lambda e, pp=pp: e.tensor_scalar(out=pp.t[:, 1:2], in0=pp.t[:, 1:2], scalar1=1.0 / (2 * math.pi), scalar2=None,
                                                                  op0=ALU.mult), reads=[pp.b], writes=[pp.b])
                k = 0
                for (wt, pp, srcT, dstT) in ((w1, p1, zf, h1T), (w2, p2, h1T, h2T)):
                    for tile_ in range(L // TT):
                        c0 = tile_ * TT
                        pb = PS[k % 2]
                        u_, a_, n_ = uu[k % 2], ta[k % 2], nr[k % 2]
                        k += 1
                        kk = wt.t.shape[0]
                        S.add("pe", lambda e, pb=pb, wt=wt, srcT=srcT, c0=c0, kk=kk: e.matmul(
                            pb.t[0:64, 0:TT], lhsT=wt.t[:, :], rhs=srcT.t[0:kk, c0:c0 + TT], start=True, stop=True),
                            reads=[wt.b, srcT.b], writes=[pb.b])
                        S.add("dve", lambda e, pb=pb, u_=u_, pp=pp: e.tensor_scalar(
                            out=u_.t[:, 0:TT], in0=pb.t[0:64, 0:TT], scalar1=pp.t[:, 0:1], scalar2=pp.t[:, 1:2], op0=ALU.add, op1=ALU.mult),
                            reads=[pb.b, pp.b], writes=[u_.b])
                        S.add("dve", lambda e, u_=u_, a_=a_: e.tensor_scalar(
                            out=a_.t[:, 0:TT], in0=u_.t[:, 0:TT], scalar1=MAGIC, scalar2=None, op0=ALU.add), reads=[u_.b], writes=[a_.b])
                        S.add("dve", lambda e, u_=u_, a_=a_, n_=n_: e.scalar_tensor_tensor(
                            out=n_.t[:, 0:TT], in0=a_.t[:, 0:TT], scalar=MAGIC, in1=u_.t[:, 0:TT], op0=ALU.subtract, op1=ALU.subtract),
                            reads=[a_.b, u_.b], writes=[n_.b])
                        S.add("dve", lambda e, n_=n_: e.tensor_scalar(
                            out=n_.t[:, 0:TT], in0=n_.t[:, 0:TT], scalar1=0.49999, scalar2=-0.49999, op0=ALU.min, op1=ALU.max),
                            reads=[n_.b], writes=[n_.b])
                        S.add("act", lambda e, n_=n_, dstT=dstT, c0=c0: e.activation(
                            out=dstT.t[:, c0:c0 + TT], in_=n_.t[:, 0:TT], func=AF.Sin, scale=-2.0 * math.pi), reads=[n_.b], writes=[dstT.b])
                for lb in range(nb):
                    d_, h_, q_, sg_ = dec[lb % 2], hh[lb % 2], sqh[lb % 2], stg[lb % 2]
                    S.add("act", lambda e, d_=d_, lb=lb: e.activation(out=d_.t[:], in_=delt.t[:], func=AF.Exp, scale=tl.t[:, lb:lb + 1]),
                          reads=[delt.b, tl.b], writes=[d_.b])
                    for ct in range(4):
                        pb = PS[ct]
                        S.add("pe", lambda e, pb=pb, lb=lb, ct=ct: e.matmul(
                            pb.t[:], lhsT=h2T.t[:, lb * 128:(lb + 1) * 128], rhs=w3.t[:, ct * 512:(ct + 1) * 512], start=True, stop=True),
                            reads=[h2T.b, w3.b], writes=[pb.b])
                        S.add("dve", lambda e, pb=pb, h_=h_, d_=d_, ct=ct: e.tensor_tensor(
                            out=h_.t[:, ct * 512:(ct + 1) * 512], in0=pb.t[:], in1=d_.t[:, ct * 512:(ct + 1) * 512], op=ALU.mult),
                            reads=[pb.b, d_.b], writes=[h_.b])
                    if lb == 0:
                        S.add("dve", lambda e, h_=h_: e.memset(h_.t[0:1, 1024:2048], 0.0), writes=[h_.b])
                    S.add("act", lambda e, h_=h_, q_=q_: e.activation(out=q_.t[:], in_=h_.t[:], func=AF.Square), reads=[h_.b], writes=[q_.b])
                    for ct in range(4):
                        S.add("pe", lambda e, q_=q_, ct=ct, lb=lb: e.matmul(
                            PS[4 + ct].t[:], lhsT=ones32.t[:], rhs=q_.t[:, ct * 512:(ct + 1) * 512], start=(lb == 0), stop=(lb == nb - 1)),
                            reads=[ones32.b, q_.b], writes=[PS[4 + ct].b])
                    S.add("pool", lambda e, h_=h_, sg_=sg_: e.tensor_tensor(out=sg_.t[:, 0, :], in0=h_.t[:, 0:1024], in1=h_.t[:, 1024:2048], op=ALU.add),
                          reads=[h_.b], writes=[sg_.b])
                    S.add("pool", lambda e, h_=h_, sg_=sg_: e.tensor_tensor(out=sg_.t[:, 1, :], in0=h_.t[:, 1024:2048], in1=h_.t[:, 0:1024], op=ALU.subtract),
                          reads=[h_.b], writes=[sg_.b])
                    S.add("sp", lambda e, sg_=sg_, lb=lb: e.dma_start(out=hsd_d[:, lb].rearrange("s p n -> p s n"), in_=sg_.t[:]), reads=[sg_.b], dma=True)
                for o in range(2):
                    S.add("dve", lambda e, o=o: e.tensor_copy(out=sst.t[:, o * 512:(o + 1) * 512], in_=PS[4 + o].t[:]), reads=[PS[4 + o].b], writes=[sst.b])
                    S.add("dve", lambda e, o=o: e.tensor_tensor(out=sst.t[:, o * 512:(o + 1) * 512], in0=sst.t[:, o * 512:(o + 1) * 512],
                                                                in1=PS[6 + o].t[:], op=ALU.add), reads=[sst.b, PS[6 + o].b], writes=[sst.b])
                S.add("act", lambda e: e.activation(out=rn.t[:], in_=sst.t[:], func=AF.Sqrt, bias=epsT.t[:, 0:1]), reads=[sst.b, epsT.b], writes=[rn.b])
                S.add("dve", lambda e: e.reciprocal(out=rn.t[:], in_=rn.t[:]), reads=[rn.b], writes=[rn.b])
                S.end_stage(resched=True)

        def hy_filter_dft(L, rn):
            G = HG[L]
            nb = L // 128
            with contextlib.ExitStack() as st:
                hs = sb(st, "hs", [128, nb, 1024], BF16)
                hd = sb(st, "hd", [128, nb, 1024], BF16)
                tC = [sb(st, "tC", [128, nb, 128], BF16) for _ in range(2)]
                tS = [sb(st, "tS", [128, nb, 128], BF16) for _ in range(2)]
                stg = [sb(st, "Hstg", [128, 512]) for _ in range(4)]
                step = max(1, nb // 4)
                for lb0 in range(0, nb, step):
                    S.add("sp", lambda e, lb0=lb0: e.dma_start(out=hs.t[:, lb0:lb0 + step, :], in_=hsd_d[0, lb0:lb0 + step].rearrange("l p n -> p l n")),
                          writes=[hs.b], dma=True)
                    S.add("sp", lambda e, lb0=lb0: e.dma_start(out=hd.t[:, lb0:lb0 + step, :], in_=hsd_d[1, lb0:lb0 + step].rearrange("l p n -> p l n")),
                          writes=[hd.b], dma=True)
                k = 0
                for kb in range(nb):
                    c_, s_ = tC[kb % 2], tS[kb % 2]
                    S.add("sp", lambda e, c_=c_, kb=kb: e.dma_start(out=c_.t[:], in_=G["F"][0, kb]), writes=[c_.b], dma=True)
                    S.add("pool", lambda e, s_=s_, kb=kb: e.dma_start(out=s_.t[:], in_=G["F"][1, kb]), writes=[s_.b], dma=True)
                    for o in range(2):
                        for ri, (tab, src) in enumerate(((c_, hs), (s_, hd))):
                            pb = PS[k % 8]
                            sg_ = stg[k % 4]
                            k += 1
                            for lb in range(nb):
                                S.add("pe", lambda e, pb=pb, tab=tab, src=src, lb=lb, o=o: e.matmul(
                                    pb.t[:], lhsT=tab.t[:, lb, :], rhs=src.t[:, lb, o * 512:(o + 1) * 512], start=(lb == 0), stop=(lb == nb - 1)),
                                    reads=[tab.b, src.b], writes=[pb.b])
                            S.add("dve", lambda e, pb=pb, sg_=sg_, o=o: e.tensor_tensor(out=sg_.t[:], in0=pb.t[:], in1=rn.t[:, o * 512:(o + 1) * 512], op=ALU.mult),
                                  reads=[pb.b, rn.b], writes=[sg_.b])
                            S.add("sp", lambda e, sg_=sg_, o=o, ri=ri, kb=kb: e.dma_start(out=H_d[o, ri, kb], in_=sg_.t[:]), reads=[sg_.b], dma=True)
                S.end_stage(resched=cfg.get("rs_x", True))

        def hy_order(l, i, L, offs, o, ident, hbs, Z, YR, YI):
            G = HG[L]
            nb = L // 128
            TT = min(512, L)
            ntt = L // TT
            nsub = TT // 128
            with contextlib.ExitStack() as st:
                zin = [sb(st, "zin", [128, 512]) for _ in range(3)]
                k = 0
                for si, t0 in enumerate(offs):
                    for tt in range(ntt):
                        for cc in range(4):
                            zi = zin[k % 3]
                            pb = PS[k % 4]
                            k += 1
                            src = uc_d[:, 8 + cc, t0 + tt * TT:t0 + (tt + 1) * TT] if o == 0 else z1T_d[:, cc, t0 + tt * TT:t0 + (tt + 1) * TT]
                            S.add("sp", lambda e, zi=zi, src=src: e.dma_start(out=zi.t[:, 0:TT], in_=src), writes=[zi.b], dma=True)
                            for j in range(nsub):
                                S.add("pe", lambda e, pb=pb, zi=zi, j=j: e.transpose(pb.t[:, j * 128:(j + 1) * 128], zi.t[:, j * 128:(j + 1) * 128], ident.t[:]),
                                      reads=[zi.b, ident.b], writes=[pb.b])
                            zt = Z[si]
                            if k % 2 == 0:
                                S.add("act", lambda e, pb=pb, zt=zt, tt=tt, cc=cc: e.activation(
                                    out=zt.t[:, tt * nsub:(tt + 1) * nsub, cc * 128:(cc + 1) * 128],
                                    in_=pb.t[:, 0:TT].rearrange("p (a b) -> p a b", b=128), func=AF.Identity), reads=[pb.b], writes=[zt.b])
                            else:
                                S.add("dve", lambda e, pb=pb, zt=zt, tt=tt, cc=cc: e.tensor_copy(
                                    out=zt.t[:, tt * nsub:(tt + 1) * nsub, cc * 128:(cc + 1) * 128],
                                    in_=pb.t[:, 0:TT].rearrange("p (a b) -> p a b", b=128)), reads=[pb.b], writes=[zt.b])
                S.end_stage(resched=True)
            with contextlib.ExitStack() as st:
                tC = [sb(st, "tC", [128, nb, 128], BF16) for _ in range(3)]
                tS = [sb(st, "tS", [128, nb, 128], BF16) for _ in range(3)]
                hr = [sb(st, "hr", [128, 512]) for _ in range(2)]
                hi = [sb(st, "hi", [128, 512]) for _ in range(2)]
                m_ = [[sb(st, "m", [128, 512]) for _ in range(2)] for _ in range(4)]
                k = 0
                for kb in range(nb):
                    c_, s_ = tC[kb % 3], tS[kb % 3]
                    hr_, hi_ = hr[kb % 2], hi[kb % 2]
                    S.add("sp", lambda e, c_=c_, kb=kb: e.dma_start(out=c_.t[:], in_=G["F"][0, kb]), writes=[c_.b], dma=True)
                    S.add("pool", lambda e, s_=s_, kb=kb: e.dma_start(out=s_.t[:], in_=G["F"][1, kb]), writes=[s_.b], dma=True)
                    S.add("sp", lambda e, hr_=hr_, kb=kb: e.dma_start(out=hr_.t[:], in_=H_d[o, 0, kb]), writes=[hr_.b], dma=True)
                    S.add("sp", lambda e, hi_=hi_, kb=kb: e.dma_start(out=hi_.t[:], in_=H_d[o, 1, kb]), writes=[hi_.b], dma=True)
                    for si in range(len(offs)):
                        zt, yr, yi = Z[si], YR[si], YI[si]
                        Pc, Ps = PS[(k % 4) * 2], PS[(k % 4) * 2 + 1]
                        mm = [m_[q][k % 2] for q in range(4)]
                        k += 1
                        for tb in range(nb):
                            S.add("pe", lambda e, Pc=Pc, c_=c_, zt=zt, tb=tb: e.matmul(Pc.t[:], lhsT=c_.t[:, tb, :], rhs=zt.t[:, tb, :],
                                                                                 start=(tb == 0), stop=(tb == nb - 1)), reads=[c_.b, zt.b], writes=[Pc.b])
                        for tb in range(nb):
                            S.add("pe", lambda e, Ps=Ps, s_=s_, zt=zt, tb=tb: e.matmul(Ps.t[:], lhsT=s_.t[:, tb, :], rhs=zt.t[:, tb, :],
                                                                                 start=(tb == 0), stop=(tb == nb - 1)), reads=[s_.b, zt.b], writes=[Ps.b])
                        for q, (hh_, pp_) in enumerate(((hr_, Pc), (hi_, Ps), (hr_, Ps), (hi_, Pc))):
                            S.add("dve", lambda e, q=q, hh_=hh_, pp_=pp_, mm=mm: e.tensor_tensor(out=mm[q].t[:], in0=pp_.t[:], in1=hh_.t[:], op=ALU.mult),
                                  reads=[hh_.b, pp_.b], writes=[mm[q].b])
                        S.add("pool", lambda e, mm=mm, yr=yr, kb=kb: e.tensor_tensor(out=yr.t[:, kb, :], in0=mm[0].t[:], in1=mm[1].t[:], op=ALU.add),
                              reads=[mm[0].b, mm[1].b], writes=[yr.b])
                        S.add("pool", lambda e, mm=mm, yi=yi, kb=kb: e.tensor_tensor(out=yi.t[:, kb, :], in0=mm[2].t[:], in1=mm[3].t[:], op=ALU.subtract),
                              reads=[mm[2].b, mm[3].b], writes=[yi.b])
                S.end_stage(resched=True)
            with contextlib.ExitStack() as st:
                KQ = min(8, nb)
                nkq = nb // KQ
                iC = [sb(st, "iC", [128, KQ, TT], BF16) for _ in range(3)]
                iS = [sb(st, "iS", [128, KQ, TT], BF16) for _ in range(3)]
                zt_ = [sb(st, "zt", [128, 512]) for _ in range(3)]
                gt_ = [sb(st, "gt", [128, 512]) for _ in range(3)]
                ab_ = [sb(st, "ab", [128, 512]) for _ in range(3)]
                of_ = [sb(st, "of", [128, 512]) for _ in range(3)]
                ob_ = [sb(st, "ob", [128, 512], BF16) for _ in range(3)]
                kt = 0
                ke = 0
                kk = 0
                for si, t0 in enumerate(offs):
                    yr, yi = YR[si], YI[si]
                    for tt in range(ntt):
                        banks = [PS[(kk % 2) * 4 + cc] for cc in range(4)]
                        kk += 1
                        for kq in range(nkq):
                            c_, s_ = iC[kt % 3], iS[kt % 3]
                            kt += 1
                            S.add("sp", lambda e, c_=c_, tt=tt, kq=kq: e.dma_start(out=c_.t[:], in_=G["I"][0, tt, :, kq * KQ:(kq + 1) * KQ, :]), writes=[c_.b], dma=True)
                            S.add("pool", lambda e, s_=s_, tt=tt, kq=kq: e.dma_start(out=s_.t[:], in_=G["I"][1, tt, :, kq * KQ:(kq + 1) * KQ, :]), writes=[s_.b], dma=True)
                            for cc in range(4):
                                for kbl in range(KQ):
                                    kb = kq * KQ + kbl
                                    S.add("pe", lambda e, bk=banks[cc], yr=yr, c_=c_, kb=kb, kbl=kbl, cc=cc: e.matmul(
                                        bk.t[:, 0:TT], lhsT=yr.t[:, kb, cc * 128:(cc + 1) * 128], rhs=c_.t[:, kbl, :], start=(kb == 0), stop=False),
                                        reads=[yr.b, c_.b], writes=[banks[cc].b])
                                    S.add("pe", lambda e, bk=banks[cc], yi=yi, s_=s_, kb=kb, kbl=kbl, cc=cc: e.matmul(
                                        bk.t[:, 0:TT], lhsT=yi.t[:, kb, cc * 128:(cc + 1) * 128], rhs=s_.t[:, kbl, :], start=False, stop=(kb == nb - 1)),
                                        reads=[yi.b, s_.b], writes=[banks[cc].b])
                        for cc in range(4):
                            z_, g_, a_, f_, b_ = zt_[ke % 3], gt_[ke % 3], ab_[ke % 3], of_[ke % 3], ob_[ke % 3]
                            ke += 1
                            tok = slice(t0 + tt * TT, t0 + (tt + 1) * TT)
                            zsrc = uc_d[:, 8 + cc, tok] if o == 0 else z1T_d[:, cc, tok]
                            S.add("sp", lambda e, z_=z_, zsrc=zsrc: e.dma_start(out=z_.t[:, 0:TT], in_=zsrc), writes=[z_.b], dma=True)
                            S.add("sp", lambda e, g_=g_, cc=cc, tok=tok: e.dma_start(out=g_.t[:, 0:TT], in_=uc_d[:, o * 4 + cc, tok]), writes=[g_.b], dma=True)
                            S.add("pool", lambda e, z_=z_, a_=a_, cc=cc: e.tensor_scalar(out=a_.t[:, 0:TT], in0=z_.t[:, 0:TT], scalar1=hbs.t[:, o, cc:cc + 1],
                                                                                    scalar2=None, op0=ALU.mult), reads=[z_.b, hbs.b], writes=[a_.b])
                            S.add("dve", lambda e, bk=banks[cc], a_=a_: e.scalar_tensor_tensor(out=a_.t[:, 0:TT], in0=bk.t[:, 0:TT], scalar=2.0 / (2 * L),
                                                                                            in1=a_.t[:, 0:TT], op0=ALU.mult, op1=ALU.add),
                                  reads=[banks[cc].b, a_.b], writes=[a_.b])
                            if o == 0:
                                S.add("pool", lambda e, a_=a_, g_=g_, f_=f_: e.tensor_tensor(out=f_.t[:, 0:TT], in0=a_.t[:, 0:TT], in1=g_.t[:, 0:TT], op=ALU.mult),
                                      reads=[a_.b, g_.b], writes=[f_.b])
                                S.add("sp", lambda e, f_=f_, cc=cc, tok=tok: e.dma_start(out=z1T_d[:, cc, tok], in_=f_.t[:, 0:TT]), reads=[f_.b], dma=True)
                            else:
                                S.add("pool", lambda e, a_=a_, g_=g_, b_=b_: e.tensor_tensor(out=b_.t[:, 0:TT], in0=a_.t[:, 0:TT], in1=g_.t[:, 0:TT], op=ALU.mult),
                                      reads=[a_.b, g_.b], writes=[b_.b])
                                S.add("sp", lambda e, b_=b_, cc=cc, tok=tok: e.dma_start(out=ay_d[:, 4 + cc, tok], in_=b_.t[:, 0:TT]), reads=[b_.b], dma=True)
                S.end_stage(resched=True)

        def ah_hyena(l, i):
            hy_conv(l, i)
            with contextlib.ExitStack() as st:
                ident = sb(st, "ident", [128, 128])
                hbs = sb(st, "hbs", [128, 2, 4])
                rn = sb(st, "rn", [128, 1024])
                S.add("sp", lambda e: e.dma_start(out=ident.t[:], in_=ident_d), writes=[ident.b], dma=True)
                S.add("sp", lambda e: e.dma_start(out=hbs.t[:], in_=hbias[:, i]), writes=[hbs.b], dma=True)
                S.end_stage()
                for (L, offs) in ((4096, [0]), (256, [4096 + 256 * pi for pi in range(4)])):
                    nb = L // 128
                    with contextlib.ExitStack() as st2:
                        hy_filter_gen(i, L, rn)
                        hy_filter_dft(L, rn)
                        Z = [sb(st2, "Z", [128, nb, 512], BF16) for _ in offs]
                        YR = [sb(st2, "YR", [128, nb, 512], BF16) for _ in offs]
                        YI = [sb(st2, "YI", [128, nb, 512], BF16) for _ in offs]
                        for o in range(2):
                            hy_order(l, i, L, offs, o, ident, hbs, Z, YR, YI)

        class Rot:
            def __init__(self, st, name, shape, dt, n):
                self.ts = [sb(st, name, shape, dt) for _ in range(n)]
                self.k = 0

            def next(self):
                t = self.ts[self.k % len(self.ts)]
                self.k += 1
                return t

        def dn_inproj(l, i):
            with contextlib.ExitStack() as st:
                X = [sb(st, "x", [128, 8, TM]) for _ in range(2)]
                Hh = [sb(st, "h", [128, 8, TM], BF16) for _ in range(2)]
                sq = [sb(st, "sq", [128, TM]) for _ in range(3)]
                tmp = [sb(st, "tmp", [128, TM]) for _ in range(2)]
                rstd = sb(st, "rstd", [128, TM])
                W = sb(st, "dwin", [128, 8, 4128], BF16)
                stg = [sb(st, "stg", [128, 512]) for _ in range(4)]
                vst = [sb(st, "vst", [128, 8, 32]) for _ in range(2)]
                for kc in range(8):
                    S.add("pool", lambda e, kc=kc: e.dma_start(out=W.t[:, kc, :], in_=dwin[i, :, kc, :], max_dma_last_dim=4096),
                          writes=[W.b], dma=True)

                def load_x(mt):
                    xb = X[mt % 2]
                    S.add("sp", lambda e: e.dma_start(out=xb.t[:], in_=xs[:, :, mt * TM:(mt + 1) * TM]), writes=[xb.b], dma=True)

                def stage_a(mt):
                    which = 0 if mt < 4 else 1
                    norm_mod((sq, rstd, tmp), X[mt % 2], Hh[mt % 2], 1, which, (PS[4], PS[5]))

                load_x(0)
                stage_a(0)
                ctr = [0]
                for mt in range(NMT):
                    hb = Hh[mt % 2]
                    if mt + 1 < NMT:
                        load_x(mt + 1)
                    for oc in range(32):
                        dst, ch = (qkvT_d, oc) if oc < 24 else (zT_d, oc - 24)
                        for s_ in range(TM // 512):
                            k = ctr[0]
                            ctr[0] += 1
                            pb = PS[k % 4]
                            sg_ = stg[k % 4]
                            for c in range(8):
                                S.add("pe", lambda e, pb=pb, c=c, s_=s_, oc=oc, hb=hb: e.matmul(
                                    pb.t[:], lhsT=W.t[:, c, oc * 128:(oc + 1) * 128],
                                    rhs=hb.t[:, c, s_ * 512:(s_ + 1) * 512], start=(c == 0), stop=(c == 7)),
                                    reads=[W.b, hb.b], writes=[pb.b])
                            if k % 2 == 0:
                                S.add("act", lambda e, pb=pb, sg_=sg_: e.activation(out=sg_.t[:], in_=pb.t[:], func=AF.Identity),
                                      reads=[pb.b], writes=[sg_.b])
                            else:
                                S.add("dve", lambda e, pb=pb, sg_=sg_: e.tensor_copy(out=sg_.t[:], in_=pb.t[:]),
                                      reads=[pb.b], writes=[sg_.b])
                            t0 = mt * TM + s_ * 512
                            S.add("sp", lambda e, dst=dst, ch=ch, t0=t0, sg_=sg_: e.dma_start(
                                out=dst[:, ch, t0:t0 + 512], in_=sg_.t[:]), reads=[sg_.b], dma=True)
                        if oc == 15 and mt + 1 < NMT:
                            stage_a(mt + 1)
                    vs_ = vst[mt % 2]
                    for tb in range(TM // 128):
                        pb = PS[6 + tb % 2]
                        for c in range(8):
                            S.add("pe", lambda e, pb=pb, c=c, tb=tb, hb=hb: e.matmul(
                                pb.t[:, 0:32], lhsT=hb.t[:, c, tb * 128:(tb + 1) * 128], rhs=W.t[:, c, 4096:4128],
                                start=(c == 0), stop=(c == 7)), reads=[W.b, hb.b], writes=[pb.b])
                        S.add("dve", lambda e, pb=pb, tb=tb, vs_=vs_: e.tensor_copy(out=vs_.t[:, tb, :], in_=pb.t[:, 0:32]),
                              reads=[pb.b], writes=[vs_.b])
                    S.add("sp", lambda e, mt=mt, vs_=vs_: e.dma_start(
                        out=ba_d[mt * TM:(mt + 1) * TM, :].rearrange("(tb p) n -> p tb n", p=128), in_=vs_.t[:]),
                        reads=[vs_.b], dma=True)
                S.end_stage(resched=cfg.get("rs_x", True))

        def dn_conv(l, i):
            with contextlib.ExitStack() as st:
                cw = sb(st, "dcw", [128, 24, 3])
                U = [sb(st, "U", [128, 12, 514]) for _ in range(2)]
                O = [sb(st, "O", [128, 12, 512]) for _ in range(2)]
                SQ = Rot(st, "dsq", [128, 512], F32, 8)
                RS = Rot(st, "drs", [128, 512], F32, 8)
                S.add("sp", lambda e: e.dma_start(out=cw.t[:], in_=dcw[:, i]), writes=[cw.b], dma=True)
                tiles = [(tt * 512, 512, tt == 0, tt == 7) for tt in range(8)] + [(4096 + 256 * pi, 256, True, True) for pi in range(4)]
                ti = 0
                for (t0, n, first, lastt) in tiles:
                    for half in range(2):
                        u, o = U[ti % 2], O[ti % 2]
                        ti += 1
                        a0 = t0 if first else t0 - 1
                        a1 = t0 + n if lastt else t0 + n + 1
                        c0 = 1 if first else 0
                        S.add("sp", lambda e, u=u, a0=a0, a1=a1, c0=c0, half=half: e.dma_start(
                            out=u.t[:, :, c0:c0 + (a1 - a0)], in_=qkvT_d[:, half * 12:(half + 1) * 12, a0:a1]), writes=[u.b], dma=True)
                        if first:
                            S.add("pool", lambda e, u=u: e.memset(u.t[:, :, 0:1], 0.0), writes=[u.b])
                        if lastt:
                            S.add("pool", lambda e, u=u, n=n: e.memset(u.t[:, :, n + 1:n + 2], 0.0), writes=[u.b])
                        for c12 in range(12):
                            ch = half * 12 + c12
                            S.add("dve", lambda e, u=u, o=o, ch=ch, c12=c12, n=n: e.tensor_scalar(
                                out=o.t[:, c12, 0:n], in0=u.t[:, c12, 0:n], scalar1=cw.t[:, ch, 0:1], scalar2=None, op0=ALU.mult),
                                reads=[u.b, cw.b], writes=[o.b])
                            for tap in (1, 2):
                                S.add("dve", lambda e, u=u, o=o, ch=ch, c12=c12, n=n, tap=tap: e.scalar_tensor_tensor(
                                    out=o.t[:, c12, 0:n], in0=u.t[:, c12, tap:n + tap], scalar=cw.t[:, ch, tap:tap + 1], in1=o.t[:, c12, 0:n],
                                    op0=ALU.mult, op1=ALU.add), reads=[u.b, cw.b, o.b], writes=[o.b])
                        S.add("act", lambda e, o=o, n=n: e.activation(out=o.t[:, :, 0:n], in_=o.t[:, :, 0:n], func=AF.Silu), reads=[o.b], writes=[o.b])
                        for c12 in range(12):
                            ch = half * 12 + c12
                            if ch >= 16:
                                continue
                            q_, r_ = SQ.next(), RS.next()
                            pb = PS[ch % 8]
                            S.add("act", lambda e, o=o, q_=q_, c12=c12, n=n: e.activation(out=q_.t[:, 0:n], in_=o.t[:, c12, 0:n], func=AF.Square),
                                  reads=[o.b], writes=[q_.b])
                            S.add("pe", lambda e, pb=pb, q_=q_, n=n: e.matmul(pb.t[:, 0:n], lhsT=ones32.t[:], rhs=q_.t[:, 0:n], start=True, stop=True),
                                  reads=[ones32.b, q_.b], writes=[pb.b])
                            S.add("act", lambda e, pb=pb, r_=r_, n=n: e.activation(out=r_.t[:, 0:n], in_=pb.t[:, 0:n], func=AF.Sqrt, bias=epsT.t[:, 0:1]),
                                  reads=[pb.b, epsT.b], writes=[r_.b])
                            S.add("dve", lambda e, r_=r_, n=n: e.reciprocal(out=r_.t[:, 0:n], in_=r_.t[:, 0:n]), reads=[r_.b], writes=[r_.b])
                            sc = (128.0 ** -0.5) if ch < 8 else 1.0
                            S.add("dve", lambda e, o=o, r_=r_, c12=c12, n=n, sc=sc: e.scalar_tensor_tensor(
                                out=o.t[:, c12, 0:n], in0=o.t[:, c12, 0:n], scalar=sc, in1=r_.t[:, 0:n], op0=ALU.mult, op1=ALU.mult),
                                reads=[o.b, r_.b], writes=[o.b])
                        S.add("sp", lambda e, o=o, t0=t0, n=n, half=half: e.dma_start(out=qkvn_d[:, half * 12:(half + 1) * 12, t0:t0 + n], in_=o.t[:, :, 0:n]),
                              reads=[o.b], dma=True)
                S.end_stage(resched=True)

        def dn_chunks(l, i):
            with contextlib.ExitStack() as st:
                msk = sb(st, "msk", [64, 5, 64])
                ident = sb(st, "ident", [128, 128])
                ones64 = sb(st, "ones64", [64, 128])
                prm = sb(st, "prm", [64, 32])
                ba = sb(st, "ba", [64, NCH, 32])
                gall = sb(st, "gall", [64, NCH, 16])
                ball = sb(st, "ball", [64, NCH, 16])
                KT_ = sb(st, "KTt", [128, 8, 256])
                QT_ = sb(st, "QTt", [128, 8, 256])
                VT_ = sb(st, "VTt", [128, 8, 256])
                Ktm = sb(st, "Ktm", [64, 8, 128])
                Vtm = sb(st, "Vtm", [64, 8, 128])
                r512L = [Rot(st, "r512", [128, 512], F32, 10) for _ in range(2)]
                ratr = Rot(st, "ratr", [64, 512], F32, 2)
                b512L = [Rot(st, "b512", [64, 512], BF16, 16) for _ in range(2)]
                batnL = [Rot(st, "batn", [64, 512], BF16, 4) for _ in range(2)]
                lvm_t = sb(st, "lvm", [64, 12, 64])
                S.add("sp", lambda e: e.dma_start(out=lvm_t.t[:], in_=lvmask_d), writes=[lvm_t.b], dma=True)
                b1kL = [Rot(st, "b1k", [64, 8, 128], BF16, 3) for _ in range(2)]
                bqL = [Rot(st, "bq", [128, 512], BF16, 2) for _ in range(2)]
                u1kL = [Rot(st, "u1k", [64, 8, 128], F32, 1) for _ in range(2)]
                smL = [Rot(st, "sm", [128, 8], F32, 10) for _ in range(2)]
                S.add("sp", lambda e: e.dma_start(out=msk.t[:], in_=dmask_d), writes=[msk.b], dma=True)
                S.add("sp", lambda e: e.dma_start(out=ident.t[:], in_=ident_d), writes=[ident.b], dma=True)
                S.add("sp", lambda e: e.dma_start(out=prm.t[:], in_=dprm[0:64, i, :]), writes=[prm.b], dma=True)
                S.add("sp", lambda e: e.dma_start(out=ba.t[:], in_=ba_d.rearrange("(c p) n -> p c n", p=64)), writes=[ba.b], dma=True)
                S.add("dve", lambda e: e.memset(ones64.t[:], 1.0), writes=[ones64.b])
                S.add("act", lambda e: e.activation(out=ball.t[:], in_=ba.t[:, :, 0:16], func=AF.Sigmoid), reads=[ba.b], writes=[ball.b])
                S.add("dve", lambda e: e.tensor_tensor(out=gall.t[:], in0=ba.t[:, :, 16:32],
                                                       in1=prm.t[:, 16:32].unsqueeze(1).broadcast_to([64, NCH, 16]), op=ALU.add),
                      reads=[ba.b, prm.b], writes=[gall.b])
                S.add("act", lambda e: e.activation(out=gall.t[:], in_=gall.t[:], func=AF.Exp), reads=[gall.b], writes=[gall.b])
                S.add("act", lambda e: e.activation(out=gall.t[:], in_=gall.t[:], func=AF.Ln, bias=1.0), reads=[gall.b], writes=[gall.b])
                S.add("act", lambda e: e.activation(out=prm.t[:, 0:16], in_=prm.t[:, 0:16], func=AF.Exp), reads=[prm.b], writes=[prm.b])
                S.add("dve", lambda e: e.scalar_tensor_tensor(out=gall.t[:], in0=gall.t[:], scalar=-1.0,
                                                              in1=prm.t[:, 0:16].unsqueeze(1).broadcast_to([64, NCH, 16]), op0=ALU.mult, op1=ALU.mult),
                      reads=[gall.b, prm.b], writes=[gall.b])
                Uf, Ub, Sf, Sb, I64 = (msk.t[:, k_, :] for k_ in range(5))

                def bc_h(m):
                    return m.unsqueeze(1).broadcast_to([64, 8, 64])

                def v3(t_, p=64):
                    return t_[0:p, :].rearrange("p (h i) -> p h i", i=64)

                def chunk_body(c):
                    cl = c % 4
                    if cl == 0:
                        t0 = c * 64
                        for (tile_, ch0) in ((QT_, 0), (KT_, 8), (VT_, 16)):
                            S.add("sp", lambda e, tile_=tile_, ch0=ch0, t0=t0: e.dma_start(out=tile_.t[:], in_=qkvn_d[:, ch0:ch0 + 8, t0:t0 + 256]),
                                  writes=[tile_.b], dma=True)
                    csl = slice(cl * 64, (cl + 1) * 64)
                    Kc, Qc, Vc = KT_.t[:, :, csl], QT_.t[:, :, csl], VT_.t[:, :, csl]
                    for (src, srcb, dst, bk0) in ((Kc, KT_.b, Ktm, 0), (Vc, VT_.b, Vtm, 4)):
                        for h in range(8):
                            pb = PS[bk0 + h // 4]
                            S.add("pe", lambda e, pb=pb, src=src, h=h: e.transpose(pb.t[0:64, (h % 4) * 128:(h % 4 + 1) * 128], src[:, h, :], ident.t[:]),
                                  reads=[srcb, ident.b], writes=[pb.b])
                        for hb_ in range(2):
                            S.add("act", lambda e, dst=dst, hb_=hb_, bk0=bk0: e.activation(
                                out=dst.t[:, hb_ * 4:(hb_ + 1) * 4, :], in_=PS[bk0 + hb_].t[0:64, :].rearrange("p (h d) -> p h d", d=128), func=AF.Identity),
                                reads=[PS[bk0 + hb_].b], writes=[dst.b])
                    for h in range(8):
                        S.add("pe", lambda e, h=h, Kc=Kc, Qc=Qc: e.matmul(PS[2].t[0:64, h * 64:(h + 1) * 64], lhsT=Kc[:, h, :], rhs=Qc[:, h, :], start=True, stop=True),
                              reads=[KT_.b, QT_.b], writes=[PS[2].b])
                    atraw = ratr.next()
                    S.add("dve", lambda e, atraw=atraw: e.tensor_copy(out=atraw.t[0:64, :], in_=PS[2].t[0:64, :]), reads=[PS[2].b], writes=[atraw.b])
                    def unit(d):
                        r512, b512, batn, b1k, bq, u1k, sm = r512L[d], b512L[d], batnL[d], b1kL[d], bqL[d], u1kL[d], smL[d]
                        B0, B1, B2, B3 = (PS[4 * d + k_] for k_ in range(4))
                        Ud, Sd, SdT = (Uf, Sf, Sb) if d == 0 else (Ub, Sb, Sf)
                        g_ = gall.t[:, c, d * 8:(d + 1) * 8]
                        b_ = ball.t[:, c, d * 8:(d + 1) * 8]
                        ug, ib = r512.next(), r512.next()
                        S.add("pool", lambda e, ug=ug, Ud=Ud, g_=g_: e.tensor_tensor(out=v3(ug.t), in0=bc_h(Ud), in1=g_.unsqueeze(2).broadcast_to([64, 8, 64]), op=ALU.mult),
                              reads=[msk.b, gall.b], writes=[ug.b])
                        S.add("pool", lambda e, ib=ib, b_=b_: e.tensor_tensor(out=v3(ib.t), in0=bc_h(I64), in1=b_.unsqueeze(2).broadcast_to([64, 8, 64]), op=ALU.mult),
                              reads=[msk.b, ball.b], writes=[ib.b])
                        S.add("pe", lambda e, Ud=Ud, g_=g_: e.matmul(B0.t[0:64, 0:8], lhsT=Ud, rhs=g_, start=True, stop=True),
                              reads=[msk.b, gall.b], writes=[B0.b])
                        S.add("pe", lambda e, g_=g_: e.matmul(B0.t[:, 8:16], lhsT=ones64.t[:], rhs=g_, start=True, stop=True),
                              reads=[ones64.b, gall.b], writes=[B0.b])
                        S.add("pe", lambda e, ug=ug: e.matmul(B1.t[:], lhsT=ones64.t[:], rhs=ug.t[0:64, :], start=True, stop=True),
                              reads=[ones64.b, ug.b], writes=[B1.b])
                        S.add("pe", lambda e, ib=ib: e.matmul(B2.t[:], lhsT=ones64.t[:], rhs=ib.t[0:64, :], start=True, stop=True),
                              reads=[ones64.b, ib.b], writes=[B2.b])
                        gcc, egl, egc, bg, kds = sm.next(), sm.next(), sm.next(), sm.next(), sm.next()
                        S.add("dve", lambda e, gcc=gcc: e.tensor_copy(out=gcc.t[0:64, :], in_=B0.t[0:64, 0:8]), reads=[B0.b], writes=[gcc.b])
                        S.add("act", lambda e, egl=egl: e.activation(out=egl.t[:], in_=B0.t[:, 8:16], func=AF.Exp), reads=[B0.b], writes=[egl.b])
                        S.add("sp", lambda e, egl=egl, d=d, c=c: e.dma_start(out=egl_d[d, c], in_=egl.t[:]), reads=[egl.b], dma=True)
                        S.add("act", lambda e, egc=egc, gcc=gcc: e.activation(out=egc.t[0:64, :], in_=gcc.t[0:64, :], func=AF.Exp), reads=[gcc.b], writes=[egc.b])
                        S.add("dve", lambda e, bg=bg, egc=egc, b_=b_: e.tensor_tensor(out=bg.t[0:64, :], in0=egc.t[0:64, :], in1=b_, op=ALU.mult),
                              reads=[egc.b, ball.b], writes=[bg.b])
                        S.add("dve", lambda e, kds=kds, gcc=gcc: e.tensor_tensor(out=kds.t[0:64, :], in0=B0.t[0:64, 8:16], in1=gcc.t[0:64, :], op=ALU.subtract),
                              reads=[B0.b, gcc.b], writes=[kds.b])
                        S.add("act", lambda e, kds=kds: e.activation(out=kds.t[0:64, :], in_=kds.t[0:64, :], func=AF.Exp), reads=[kds.b], writes=[kds.b])
                        Dt, E1, E2 = r512.next(), r512.next(), r512.next()
                        S.add("dve", lambda e, Dt=Dt, gcc=gcc: e.tensor_tensor(out=v3(Dt.t), in0=v3(B1.t), in1=gcc.t[0:64, :].unsqueeze(2).broadcast_to([64, 8, 64]),
                                                                        op=ALU.subtract), reads=[B1.b, gcc.b], writes=[Dt.b])
                        S.add("dve", lambda e, Dt=Dt, E1=E1: e.tensor_scalar(out=E1.t[0:64, :], in0=Dt.t[0:64, :], scalar1=0.0, scalar2=None, op0=ALU.min),
                              reads=[Dt.b], writes=[E1.b])
                        S.add("dve", lambda e, Dt=Dt, E2=E2: e.tensor_scalar(out=E2.t[0:64, :], in0=Dt.t[0:64, :], scalar1=-1.0, scalar2=0.0, op0=ALU.mult, op1=ALU.min),
                              reads=[Dt.b], writes=[E2.b])
                        S.add("act", lambda e, E1=E1: e.activation(out=E1.t[0:64, :], in_=E1.t[0:64, :], func=AF.Exp), reads=[E1.b], writes=[E1.b])
                        S.add("act", lambda e, E2=E2: e.activation(out=E2.t[0:64, :], in_=E2.t[0:64, :], func=AF.Exp), reads=[E2.b], writes=[E2.b])
                        decI, decS, decN = r512.next(), r512.next(), r512.next()
                        S.add("pool", lambda e, decI=decI, E1=E1, Ud=Ud: e.tensor_tensor(out=v3(decI.t), in0=v3(E1.t), in1=bc_h(Ud), op=ALU.mult),
                              reads=[E1.b, msk.b], writes=[decI.b])
                        S.add("pool", lambda e, decS=decS, E1=E1, Sd=Sd: e.tensor_tensor(out=v3(decS.t), in0=v3(E1.t), in1=bc_h(Sd), op=ALU.mult),
                              reads=[E1.b, msk.b], writes=[decS.b])
                        S.add("pool", lambda e, decN=decN, E2=E2, SdT=SdT: e.tensor_tensor(out=v3(decN.t), in0=v3(E2.t), in1=bc_h(SdT), op=ALU.mult),
                              reads=[E2.b, msk.b], writes=[decN.b])
                        eg = r512.next()
                        qg = bq.next()
                        S.add("act", lambda e, eg=eg: e.activation(out=eg.t[:], in_=B1.t[:], func=AF.Exp), reads=[B1.b], writes=[eg.b])
                        S.add("pool", lambda e, eg=eg, qg=qg, Qc=Qc: e.tensor_tensor(out=v3(qg.t, 128), in0=Qc, in1=v3(eg.t, 128), op=ALU.mult),
                              reads=[eg.b, QT_.b], writes=[qg.b])
                        S.add("sp", lambda e, qg=qg, d=d, c=c: e.dma_start(out=QgT_d[d, c], in_=v3(qg.t, 128)), reads=[qg.b], dma=True)
                        kbT = r512.next()
                        S.add("dve", lambda e, kbT=kbT, Kc=Kc: e.tensor_tensor(out=v3(kbT.t, 128), in0=Kc, in1=v3(B2.t, 128), op=ALU.mult),
                              reads=[B2.b, KT_.b], writes=[kbT.b])
                        for h in range(8):
                            S.add("pe", lambda e, h=h, Kc=Kc, kbT=kbT: e.matmul(B3.t[0:64, h * 64:(h + 1) * 64], lhsT=Kc[:, h, :], rhs=kbT.t[:, h * 64:(h + 1) * 64],
                                                                              start=True, stop=True), reads=[KT_.b, kbT.b], writes=[B3.b])
                        for h in range(8):
                            S.add("pe", lambda e, h=h, Kc=Kc, kbT=kbT: e.matmul(B0.t[0:64, h * 64:(h + 1) * 64], lhsT=kbT.t[:, h * 64:(h + 1) * 64], rhs=Kc[:, h, :],
                                                                              start=True, stop=True), reads=[KT_.b, kbT.b], writes=[B0.b])
                        AT, AN = batn.next(), batn.next()
                        S.add("dve", lambda e, AT=AT, decS=decS: e.scalar_tensor_tensor(out=AT.t[:], in0=B3.t[0:64, :], scalar=-1.0, in1=decS.t[0:64, :],
                                                                                    op0=ALU.mult, op1=ALU.mult), reads=[B3.b, decS.b], writes=[AT.b])
                        S.add("dve", lambda e, AN=AN, decN=decN: e.scalar_tensor_tensor(out=AN.t[:], in0=B0.t[0:64, :], scalar=-1.0, in1=decN.t[0:64, :],
                                                                                    op0=ALU.mult, op1=ALU.mult), reads=[B0.b, decN.b], writes=[AN.b])
                        def lvm(lv, tr):
                            k_ = 2 * lv + (tr if d == 0 else 1 - tr)
                            return bc_h(lvm_t.t[:, k_, :])
                        TN, TT = b512.next(), b512.next()
                        for (dst_, src_, tr) in ((TN, AN, 0), (TT, AT, 1)):
                            mk0 = lvm(0, tr)
                            S.add("pool", lambda e, dst_=dst_, src_=src_, mk0=mk0: e.tensor_tensor(out=v3(dst_.t), in0=v3(src_.t), in1=mk0, op=ALU.mult),
                                  reads=[src_.b, lvm_t.b], writes=[dst_.b])
                            S.add("pool", lambda e, dst_=dst_: e.tensor_tensor(out=v3(dst_.t), in0=v3(dst_.t), in1=bc_h(I64), op=ALU.add),
                                  reads=[dst_.b, msk.b], writes=[dst_.b])
                        for lv in range(1, 6):
                            LoN, LoT, M1, M2, TT2 = b512.next(), b512.next(), b512.next(), b512.next(), b512.next()
                            mkn, mkt = lvm(lv, 0), lvm(lv, 1)
                            S.add("pool", lambda e, LoN=LoN, AN=AN, mkn=mkn: e.tensor_tensor(out=v3(LoN.t), in0=v3(AN.t), in1=mkn, op=ALU.mult),
                                  reads=[AN.b, lvm_t.b], writes=[LoN.b])
                            S.add("pool", lambda e, LoT=LoT, AT=AT, mkt=mkt: e.tensor_tensor(out=v3(LoT.t), in0=v3(AT.t), in1=mkt, op=ALU.mult),
                                  reads=[AT.b, lvm_t.b], writes=[LoT.b])
                            for h in range(8):
                                hs_ = slice(h * 64, (h + 1) * 64)
                                S.add("pe", lambda e, hs_=hs_, LoN=LoN, TT=TT: e.matmul(B1.t[0:64, hs_], lhsT=LoN.t[:, hs_], rhs=TT.t[:, hs_], start=True, stop=True),
                                      reads=[LoN.b, TT.b], writes=[B1.b])
                            S.add("act", lambda e, M1=M1: e.activation(out=M1.t[:], in_=B1.t[0:64, :], func=AF.Identity), reads=[B1.b], writes=[M1.b])
                            if lv < 5:
                                for h in range(8):
                                    hs_ = slice(h * 64, (h + 1) * 64)
                                    S.add("pe", lambda e, hs_=hs_, LoT=LoT, TN=TN: e.matmul(B2.t[0:64, hs_], lhsT=LoT.t[:, hs_], rhs=TN.t[:, hs_], start=True, stop=True),
                                          reads=[LoT.b, TN.b], writes=[B2.b])
                                S.add("act", lambda e, M2=M2: e.activation(out=M2.t[:], in_=B2.t[0:64, :], func=AF.Identity), reads=[B2.b], writes=[M2.b])
                            for h in range(8):
                                hs_ = slice(h * 64, (h + 1) * 64)
                                S.add("pe", lambda e, hs_=hs_, TN=TN, M1=M1: e.matmul(B3.t[0:64, hs_], lhsT=TN.t[:, hs_], rhs=M1.t[:, hs_], start=True, stop=True),
                                      reads=[TN.b, M1.b], writes=[B3.b])
                            S.add("dve", lambda e, TT2=TT2, TT=TT: e.tensor_tensor(out=TT2.t[:], in0=B3.t[0:64, :], in1=TT.t[:], op=ALU.add),
                                  reads=[B3.b, TT.b], writes=[TT2.b])
                            if lv < 5:
                                TN2 = b512.next()
                                for h in range(8):
                                    hs_ = slice(h * 64, (h + 1) * 64)
                                    S.add("pe", lambda e, hs_=hs_, TT=TT, M2=M2: e.matmul(B0.t[0:64, hs_], lhsT=TT.t[:, hs_], rhs=M2.t[:, hs_], start=True, stop=True),
                                          reads=[TT.b, M2.b], writes=[B0.b])
                                S.add("dve", lambda e, TN2=TN2, TN=TN: e.tensor_tensor(out=TN2.t[:], in0=B0.t[0:64, :], in1=TN.t[:], op=ALU.add),
                                      reads=[B0.b, TN.b], writes=[TN2.b])
                                TN = TN2
                            TT = TT2
                        P = TT
                        ato = b512.next()
                        S.add("pool", lambda e, ato=ato, atraw=atraw, decI=decI: e.tensor_tensor(out=ato.t[:], in0=atraw.t[0:64, :], in1=decI.t[0:64, :], op=ALU.mult),
                              reads=[atraw.b, decI.b], writes=[ato.b])
                        S.add("sp", lambda e, ato=ato, d=d, c=c: e.dma_start(out=AT_d[d, c], in_=v3(ato.t)), reads=[ato.b], dma=True)
                        Vb, KBg, Kdd = b1k.next(), b1k.next(), b1k.next()
                        S.add("pool", lambda e, Vb=Vb, b_=b_: e.tensor_tensor(out=Vb.t[:], in0=Vtm.t[:], in1=b_.unsqueeze(2).broadcast_to([64, 8, 128]), op=ALU.mult),
                              reads=[Vtm.b, ball.b], writes=[Vb.b])
                        S.add("pool", lambda e, KBg=KBg, bg=bg: e.tensor_tensor(out=KBg.t[:], in0=Ktm.t[:], in1=bg.t[0:64, :].unsqueeze(2).broadcast_to([64, 8, 128]), op=ALU.mult),
                              reads=[Ktm.b, bg.b], writes=[KBg.b])
                        S.add("pool", lambda e, Kdd=Kdd, kds=kds: e.tensor_tensor(out=Kdd.t[:], in0=Ktm.t[:], in1=kds.t[0:64, :].unsqueeze(2).broadcast_to([64, 8, 128]), op=ALU.mult),
                              reads=[Ktm.b, kds.b], writes=[Kdd.b])
                        S.add("sp", lambda e, Kdd=Kdd, d=d, c=c: e.dma_start(out=Kd_d[d, c], in_=Kdd.t[:]), reads=[Kdd.b], dma=True)
                        for h in range(8):
                            pb = (B1, B2)[h // 4]
                            S.add("pe", lambda e, pb=pb, h=h, P=P, Vb=Vb: e.matmul(pb.t[0:64, (h % 4) * 128:(h % 4 + 1) * 128], lhsT=P.t[:, h * 64:(h + 1) * 64], rhs=Vb.t[:, h, :],
                                                                              start=True, stop=True), reads=[P.b, Vb.b], writes=[pb.b])
                        uo = u1k.next()
                        for hb_ in range(2):
                            S.add("act", lambda e, uo=uo, hb_=hb_: e.activation(out=uo.t[:, hb_ * 4:(hb_ + 1) * 4, :],
                                                                           in_=(B1, B2)[hb_].t[0:64, :].rearrange("p (h d) -> p h d", d=128), func=AF.Identity),
                                  reads=[(B1, B2)[hb_].b], writes=[uo.b])
                        S.add("sp", lambda e, uo=uo, d=d, c=c: e.dma_start(out=u_d[d, c], in_=uo.t[:]), reads=[uo.b], dma=True)
                        for h in range(8):
                            S.add("pe", lambda e, h=h, P=P, KBg=KBg: e.matmul(B3.t[:, h * 64:(h + 1) * 64], lhsT=KBg.t[:, h, :], rhs=P.t[:, h * 64:(h + 1) * 64],
                                                                            start=True, stop=True), reads=[P.b, KBg.b], writes=[B3.b])
                        wo = bq.next()
                        S.add("dve", lambda e, wo=wo: e.tensor_copy(out=wo.t[:], in_=B3.t[:]), reads=[B3.b], writes=[wo.b])
                        S.add("sp", lambda e, wo=wo, d=d, c=c: e.dma_start(out=wT_d[d, c], in_=v3(wo.t, 128)), reads=[wo.b], dma=True)
                    for d_ in range(2):
                        unit(d_)

                for c_ in range(NCH):
                    chunk_body(c_)
                S.end_stage(resched=True)

        def dn_scan(l, i):
            with contextlib.ExitStack() as st:
                uL = [Rot(st, "uL", [64, 8, 128], F32, 2) for _ in range(2)]
                wL = [Rot(st, "wL", [128, 8, 64], BF16, 2) for _ in range(2)]
                qL = [Rot(st, "qL", [128, 8, 64], BF16, 2) for _ in range(2)]
                aL = [Rot(st, "aL", [128, 8, 64], BF16, 2) for _ in range(2)]
                kL = [Rot(st, "kL", [64, 8, 128], BF16, 2) for _ in range(2)]
                eL = [Rot(st, "eL", [128, 8], F32, 2) for _ in range(2)]
                vn = [Rot(st, "vn", [128, 8, 128], BF16, 2) for _ in range(2)]
                for d in range(2):
                    for t_ in aL[d].ts + vn[d].ts:
                        S.add("pool", lambda e, t_=t_: e.memset(t_.t[:], 0.0), writes=[t_.b])
                oS = [Rot(st, "oS", [128, 8, 64], F32, 2) for _ in range(2)]
                seqs = [(0, 64, None)] + [(64 + 4 * pi, 4, pi) for pi in range(4)]
                for (c0, nch, pi) in seqs:
                    Sf = [sb(st, "S", [128, 8, 128]) for _ in range(2)]
                    Sbf = [sb(st, "Sbf", [128, 8, 128], BF16) for _ in range(2)]
                    for d in range(2):
                        if pi is None:
                            S.add("sp", lambda e, d=d, Sf=Sf: e.dma_start(out=Sf[d].t[:], in_=sf0[d, i]), writes=[Sf[d].b], dma=True)
                        else:
                            S.add("pool", lambda e, d=d, Sf=Sf: e.memset(Sf[d].t[:], 0.0), writes=[Sf[d].b])
                        S.add("act", lambda e, d=d, Sf=Sf, Sbf=Sbf: e.activation(out=Sbf[d].t[:], in_=Sf[d].t[:], func=AF.Identity), reads=[Sf[d].b], writes=[Sbf[d].b])
                    for step in range(nch):
                        for d in range(2):
                            c = c0 + step if d == 0 else c0 + nch - 1 - step
                            sF, sB = Sf[d], Sbf[d]
                            u_, w_, q_, a_, k_, e_ = uL[d].next(), wL[d].next(), qL[d].next(), aL[d].next(), kL[d].next(), eL[d].next()
                            q1 = "sp" if d == 0 else "act"
                            for (tile_, src) in ((u_, u_d[d, c]), (w_, wT_d[d, c]), (q_, QgT_d[d, c]), (a_, AT_d[d, c]), (k_, Kd_d[d, c]), (e_, egl_d[d, c])):
                                np_ = src.shape[0]
                                S.add("sp", lambda e, tile_=tile_, src=src, np_=np_: e.dma_start(out=tile_.t[0:np_], in_=src), writes=[tile_.b], dma=True)
                            pw = (PS[0], PS[1]) if d == 0 else (PS[4], PS[5])
                            po = PS[2] if d == 0 else PS[6]
                            for h in range(8):
                                pb = pw[h // 4]
                                S.add("pe", lambda e, pb=pb, h=h, w_=w_, sB=sB: e.matmul(pb.t[0:64, (h % 4) * 128:(h % 4 + 1) * 128], lhsT=w_.t[:, h, :], rhs=sB.t[:, h, :],
                                                                                    start=True, stop=True), reads=[w_.b, sB.b], writes=[pb.b])
                            v_ = vn[d].next()
                            for hb_ in range(2):
                                S.add("dve", lambda e, v_=v_, u_=u_, hb_=hb_, pw=pw: e.tensor_tensor(
                                    out=v_.t[0:64, hb_ * 4:(hb_ + 1) * 4, :], in0=u_.t[:, hb_ * 4:(hb_ + 1) * 4, :],
                                    in1=pw[hb_].t[0:64, :].rearrange("p (h d) -> p h d", d=128), op=ALU.subtract),
                                    reads=[u_.b, pw[hb_].b], writes=[v_.b])
                            for h in range(8):
                                S.add("pe", lambda e, po=po, h=h, q_=q_, sB=sB: e.matmul(po.t[:, h * 64:(h + 1) * 64], lhsT=sB.t[:, h, :], rhs=q_.t[:, h, :],
                                                                                    start=True, stop=False), reads=[q_.b, sB.b], writes=[po.b])
                                S.add("pe", lambda e, po=po, h=h, a_=a_, v_=v_: e.matmul(po.t[:, h * 64:(h + 1) * 64], lhsT=v_.t[:, h, :], rhs=a_.t[:, h, :],
                                                                                    start=False, stop=True), reads=[a_.b, v_.b], writes=[po.b])
                            o_ = oS[d].next()
                            S.add("act", lambda e, o_=o_, po=po: e.activation(out=o_.t[:], in_=po.t[:].rearrange("p (h i) -> p h i", i=64), func=AF.Identity),
                                  reads=[po.b], writes=[o_.b])
                            S.add("sp", lambda e, o_=o_, d=d, c=c: e.dma_start(out=oT_d[d, :, :, c * 64:(c + 1) * 64], in_=o_.t[:]), reads=[o_.b], dma=True)
                            for h in range(8):
                                pb = pw[h // 4]
                                S.add("pe", lambda e, pb=pb, h=h, k_=k_, v_=v_: e.matmul(pb.t[:, (h % 4) * 128:(h % 4 + 1) * 128], lhsT=k_.t[:, h, :], rhs=v_.t[0:64, h, :],
                                                                                    start=True, stop=True), reads=[k_.b, v_.b], writes=[pb.b])
                            S.add("pool", lambda e, sF=sF, e_=e_: e.tensor_tensor(out=sF.t[:], in0=sF.t[:], in1=e_.t[:].unsqueeze(2).broadcast_to([128, 8, 128]), op=ALU.mult),
                                  reads=[sF.b, e_.b], writes=[sF.b])
                            for hb_ in range(2):
                                S.add("dve", lambda e, sF=sF, hb_=hb_, pw=pw: e.tensor_tensor(
                                    out=sF.t[:, hb_ * 4:(hb_ + 1) * 4, :], in0=sF.t[:, hb_ * 4:(hb_ + 1) * 4, :],
                                    in1=pw[hb_].t[:].rearrange("p (h d) -> p h d", d=128), op=ALU.add), reads=[sF.b, pw[hb_].b], writes=[sF.b])
                            S.add("act", lambda e, sF=sF, sB=sB: e.activation(out=sB.t[:], in_=sF.t[:], func=AF.Identity), reads=[sF.b], writes=[sB.b])
                    if pi is not None:
                        for d in range(2):
                            S.add("sp", lambda e, d=d, pi=pi, Sf=Sf: e.dma_start(out=nst[d, i, pi], in_=Sf[d].t[:]), reads=[Sf[d].b], dma=True)
                S.end_stage(resched=True)

        def dn_final(l, i):
            with contextlib.ExitStack() as st:
                W = sb(st, "dwout", [128, 8, 1024], BF16)
                gn2 = sb(st, "dng", [128, 2])
                OF = Rot(st, "OF", [128, 8, 512], F32, 2)
                OB = Rot(st, "OB", [128, 8, 512], F32, 2)
                ZZ = Rot(st, "ZZ", [128, 8, 512], F32, 2)
                XX = Rot(st, "XX", [128, 8, 512], F32, 2)
                OG = Rot(st, "OG", [128, 8, 512], BF16, 2)
                SQ = Rot(st, "fsq", [128, 512], F32, 4)
                RS = Rot(st, "frs", [128, 512], F32, 4)
                for kc2 in range(4):
                    S.add("pool", lambda e, kc2=kc2: e.dma_start(out=W.t[:, 2 * kc2:2 * kc2 + 2, :], in_=dwout[i, :, 2 * kc2:2 * kc2 + 2, :],
                                                               max_dma_last_dim=4096), writes=[W.b], dma=True)
                S.add("sp", lambda e: e.dma_start(out=gn2.t[:], in_=dng), writes=[gn2.b], dma=True)
                for tt in range(NTOK // 512):
                    which = 0 if tt < 8 else 1
                    tok = slice(tt * 512, (tt + 1) * 512)
                    of_, ob_, zz, xx, og = OF.next(), OB.next(), ZZ.next(), XX.next(), OG.next()
                    S.add("sp", lambda e, of_=of_, tok=tok: e.dma_start(out=of_.t[:], in_=oT_d[0, :, :, tok]), writes=[of_.b], dma=True)
                    S.add("sp", lambda e, ob_=ob_, tok=tok: e.dma_start(out=ob_.t[:], in_=oT_d[1, :, :, tok]), writes=[ob_.b], dma=True)
                    S.add("sp", lambda e, zz=zz, tok=tok: e.dma_start(out=zz.t[:], in_=zT_d[:, :, tok]), writes=[zz.b], dma=True)
                    S.add("sp", lambda e, xx=xx, tok=tok: e.dma_start(out=xx.t[:], in_=xs[:, :, tok]), writes=[xx.b], dma=True)
                    S.add("pool", lambda e, of_=of_, ob_=ob_: e.tensor_tensor(out=of_.t[:], in0=of_.t[:], in1=ob_.t[:], op=ALU.add),
                          reads=[of_.b, ob_.b], writes=[of_.b])
                    S.add("act", lambda e, zz=zz: e.activation(out=zz.t[:], in_=zz.t[:], func=AF.Silu), reads=[zz.b], writes=[zz.b])
                    for h in range(8):
                        q_, r_ = SQ.next(), RS.next()
                        pb = PS[h % 4]
                        S.add("act", lambda e, q_=q_, of_=of_, h=h: e.activation(out=q_.t[:], in_=of_.t[:, h, :], func=AF.Square), reads=[of_.b], writes=[q_.b])
                        S.add("pe", lambda e, pb=pb, q_=q_: e.matmul(pb.t[:], lhsT=ones32.t[:], rhs=q_.t[:], start=True, stop=True),
                              reads=[ones32.b, q_.b], writes=[pb.b])
                        S.add("act", lambda e, pb=pb, r_=r_: e.activation(out=r_.t[:], in_=pb.t[:], func=AF.Sqrt, scale=1.0 / 128, bias=epsT.t[:, 0:1]),
                              reads=[pb.b, epsT.b], writes=[r_.b])
                        S.add("dve", lambda e, r_=r_: e.reciprocal(out=r_.t[:], in_=r_.t[:]), reads=[r_.b], writes=[r_.b])
                        S.add("dve", lambda e, r_=r_, of_=of_, h=h: e.scalar_tensor_tensor(out=r_.t[:], in0=of_.t[:, h, :], scalar=gn2.t[:, i:i + 1], in1=r_.t[:],
                                                                                      op0=ALU.mult, op1=ALU.mult), reads=[r_.b, of_.b, gn2.b], writes=[r_.b])
                        S.add("pool", lambda e, r_=r_, zz=zz, og=og, h=h: e.tensor_tensor(out=og.t[:, h, :], in0=r_.t[:], in1=zz.t[:, h, :], op=ALU.mult),
                              reads=[r_.b, zz.b], writes=[og.b])
                    for m in range(8):
                        pb = PS[4 + m % 4]
                        for f in range(8):
                            S.add("pe", lambda e, pb=pb, f=f, m=m, og=og: e.matmul(pb.t[:], lhsT=W.t[:, f, m * 128:(m + 1) * 128], rhs=og.t[:, f, :],
                                                                              start=(f == 0), stop=(f == 7)), reads=[W.b, og.b], writes=[pb.b])
                        S.add("dve", lambda e, pb=pb, m=m, xx=xx, which=which: e.scalar_tensor_tensor(
                            out=xx.t[:, m, :], in0=pb.t[:], scalar=hgT.t[:, 1, m, which:which + 1], in1=xx.t[:, m, :], op0=ALU.mult, op1=ALU.add),
                            reads=[pb.b, hgT.b, xx.b], writes=[xx.b])
                    S.add("sp", lambda e, xx=xx, tok=tok: e.dma_start(out=xs[:, :, tok], in_=xx.t[:]), reads=[xx.b], dma=True)
                S.end_stage(resched=True)

        def dn_mixer(l, i):
            nst_ = cfg.get("dn_stages", 5)
            for k_, fn in enumerate((dn_inproj, dn_conv, dn_chunks, dn_scan, dn_final)):
                if k_ < nst_:
                    fn(l, i)

        def ah_hyena_zero(l, i):
            with contextlib.ExitStack() as st:
                z = sb(st, "zz", [128, 4, 1024], BF16)
                S.add("dve", lambda e: e.memset(z.t[:], 0.0), writes=[z.b])
                for mt in range(NMT):
                    S.add("sp", lambda e, mt=mt: e.dma_start(out=ay_d[:, 4:8, mt * TM:(mt + 1) * TM], in_=z.t[:]), reads=[z.b], dma=True)
                S.end_stage()

        cur = xT
        layer_list = cfg.get("layers", None)
        layer_list = list(layer_list) if layer_list is not None else list(range(nlayers))
        do_ffn = cfg.get("ffn", True)

        def copy_stage(src, dst):
            for mt in range(NMT):
                S.add("sp", lambda e, mt=mt: e.dma_start(out=dst[:, :, mt * TM:(mt + 1) * TM], in_=src[:, :, mt * TM:(mt + 1) * TM]), dma=True)
            S.end_stage()

        for li, l in enumerate(layer_list):
            modulation_stage(l)
            if do_ffn:
                ffn_stage(l, 0, cur, xs)
            else:
                copy_stage(cur, xs)
            cur = xs
            if do_mix and l % 2 == 0:
                ah_inproj(l, l // 2)
                ah_attention(l, l // 2)
                if cfg.get("hyena", 1):
                    ah_hyena(l, l // 2)
                else:
                    ah_hyena_zero(l, l // 2)
                ah_outproj(l, l // 2)
            if do_mix and l % 2 == 1:
                dn_mixer(l, l // 2)
            last = (li == len(layer_list) - 1)
            if do_ffn:
                ffn_stage(l, 1, cur, yT if last else xs)
            elif last:
                copy_stage(xs, yT)
        print("stages", S.nstage, "ops", S.ninstr)
    return nc


def host_layout(inputs, core):
    f = lambda a: np.ascontiguousarray(a, dtype=np.float32)
    xs_ = np.asarray(inputs["x_sample"][core])
    xp_ = np.asarray(inputs["x_prompt"][4 * core:4 * core + 4]).reshape(1024, D)
    x = np.concatenate([xs_, xp_], axis=0)
    m = {}
    m["xT"] = f(x.T.reshape(8, 128, NTOK).transpose(1, 0, 2))
    cond = np.stack([np.asarray(inputs["c"][core]), np.asarray(inputs["c_ctx"])], axis=1)
    m["condT"] = f(cond.reshape(8, 128, 2).transpose(1, 0, 2))
    ck = np.asarray(inputs["cache_k"][core])
    ckT = ck.transpose(0, 3, 2, 1)
    m["ckT"] = f(np.concatenate([ckT, ckT], axis=1))
    cv = np.asarray(inputs["cache_v"][core]).reshape(2, 4, 128, 128)
    m["cvv"] = f(cv.transpose(0, 2, 1, 3))
    st_ = np.stack([np.asarray(inputs["state_fwd"][core]), np.asarray(inputs["state_bwd"][core])], axis=0)
    m["sf0"] = f(st_.transpose(0, 1, 3, 2, 4))
    return m


def shared_layout(inputs):
    f = lambda a: np.ascontiguousarray(a, dtype=np.float32)
    m = {}
    aw = np.asarray(inputs["ada_w"])
    m["ada_w"] = f(aw.reshape(DEPTH, 8, 128, 9, 1024).transpose(0, 3, 2, 1, 4))
    ab = np.asarray(inputs["ada_b"]).reshape(DEPTH, 72, 128).transpose(2, 0, 1)
    m["ada_b"] = f(np.repeat(ab[..., None], 2, axis=-1))
    g = np.asarray(inputs["norm_g"]).reshape(DEPTH, 3, 8, 128).transpose(3, 0, 1, 2)
    m["norm_g"] = f(np.repeat(g[..., None], 2, axis=-1))
    w13 = np.asarray(inputs["ffn_w13"]).reshape(DEPTH * 2, 8, 128, 2, NFC, 128)
    m["w13"] = f(w13.transpose(0, 4, 2, 1, 3, 5).reshape(DEPTH * 2, NFC, 128, 8, 256))
    w2 = np.asarray(inputs["ffn_w2"]).reshape(DEPTH * 2, NFC, 128, 8, 128)
    m["w2"] = f(w2.transpose(0, 3, 2, 1, 4))
    wi = np.asarray(inputs["mx_w_in"])
    wcat = np.concatenate([wi[:, :, 0:512], wi[:, :, 512:576], wi[:, :, 512:576], wi[:, :, 576:640], wi[:, :, 576:640],
                           wi[:, :, 768:2304], wi[:, :, 640:768]], axis=2)
    m["win"] = f(wcat.reshape(2, 8, 128, 2432).transpose(0, 2, 1, 3))
    wo = np.asarray(inputs["mx_w_out"])
    m["wout"] = f(wo.reshape(2, 8, 128, 1024).transpose(0, 2, 1, 3))
    qn = np.asarray(inputs["q_norm"])
    kn = np.asarray(inputs["k_norm"])
    pidx = np.arange(128)
    m["qkn"] = f(np.stack([qn[:, pidx % 64].T, kn[:, pidx % 64].T], axis=-1))
    sk = np.asarray(inputs["attn_sink"])
    hidx = 2 * np.arange(4)[None, :] + (pidx // 64)[:, None]
    m["sinkT"] = f(sk[:, hidx].transpose(1, 0, 2))
    cw = np.asarray(inputs["hy_conv_w"]).reshape(2, 3, 12, 128)
    m["hcw"] = f(cw.transpose(3, 0, 2, 1))
    cb = np.asarray(inputs["hy_conv_b"]).reshape(2, 12, 128)
    m["hcb"] = f(cb.transpose(2, 0, 1))
    m["hw1"] = f(inputs["hy_w1"])
    m["hb1"] = f(np.stack([np.asarray(inputs["hy_b1"]).T, np.asarray(inputs["hy_freq1"]).T], axis=-1))
    m["hw2"] = f(inputs["hy_w2"])
    m["hb2"] = f(np.stack([np.asarray(inputs["hy_b2"]).T, np.asarray(inputs["hy_freq2"]).T], axis=-1))
    m["hw3"] = f(inputs["hy_w3"])
    hbz = np.asarray(inputs["hy_bias"]).reshape(2, 2, 4, 128)
    m["hbias"] = f(hbz.transpose(3, 0, 1, 2))
    dwi = np.asarray(inputs["dn_w_in"])
    m["dwin"] = f(dwi.reshape(2, 8, 128, 4128).transpose(0, 2, 1, 3))
    dwo = np.asarray(inputs["dn_w_out"])
    m["dwout"] = f(dwo.reshape(2, 8, 128, 1024).transpose(0, 2, 1, 3))
    dc = np.asarray(inputs["dn_conv_w"]).reshape(2, 3, 24, 128)
    m["dcw"] = f(dc.transpose(3, 0, 2, 1))
    prm = np.concatenate([np.asarray(inputs["dn_a_log"]).reshape(2, 16), np.asarray(inputs["dn_dt_bias"]).reshape(2, 16)], axis=1)
    m["dprm"] = f(np.broadcast_to(prm[None], (128, 2, 32)))
    m["dng"] = f(np.asarray(inputs["dn_norm_g"]).T)
    m.update(const_tables())
    return m


_CONST = {}


def const_tables():
    if _CONST:
        return _CONST
    f = lambda a: np.ascontiguousarray(a, dtype=np.float32)
    pidx = np.arange(128)
    a = (pidx % 64) % 32
    inv = (np.float32(10000.0) ** (-np.arange(0, 32, 2, dtype=np.float32) / np.float32(32))).astype(np.float32)
    t = np.arange(4096)
    r = (t // 64).astype(np.float32)
    col = (t % 64).astype(np.float32)
    ang = np.where((a < 16)[:, None], r[None, :] * inv[a % 16][:, None], col[None, :] * inv[a % 16][:, None]).astype(np.float32)
    _CONST["cosT"] = f(np.cos(ang))
    _CONST["sinT"] = f(np.sin(ang))
    rot = np.zeros((128, 128), np.float32)
    for d_out in range(128):
        dd = d_out % 64
        base = d_out - dd
        if dd < 32:
            rot[base + dd + 32, d_out] = -1.0
        else:
            rot[base + dd - 32, d_out] = 1.0
    _CONST["rotT"] = rot
    blk = np.zeros((128, 128), np.float32)
    blk[:64, :64] = 1.0
    blk[64:, 64:] = 1.0
    _CONST["blk1"] = blk
    ko = np.arange(128)[:, None]
    qo = np.arange(128)[None, :]
    _CONST["mprev"] = f(ko >= qo)
    _CONST["mnext"] = f(ko <= qo)
    _CONST["ident"] = np.eye(128, dtype=np.float32)
    pp = np.arange(64)[:, None]
    ff = np.arange(64)[None, :]
    _CONST["dmask"] = f(np.stack([pp <= ff, pp >= ff, pp < ff, pp > ff, pp == ff], axis=1))
    lvm = []
    for lv in range(6):
        b_ = 2 ** lv
        mn = (pp // (2 * b_) == ff // (2 * b_)) & (pp % (2 * b_) >= b_) & (ff % (2 * b_) < b_)
        lvm.append(mn)
        lvm.append(mn.T)
    _CONST["lvmask"] = f(np.stack(lvm, axis=1))
    HY_MIN = math.log(1e-2) / 1.5
    HY_MAX = math.log(1e-2) / 0.3
    dl = np.abs(np.linspace(HY_MIN, HY_MAX, 2048, dtype=np.float32)).astype(np.float32)
    _CONST["deltas"] = f(np.broadcast_to(dl[None, :], (128, 2048)))
    for L in (4096, 256):
        N = 2 * L
        nb = L // 128
        TT = min(512, L)
        k = np.arange(L, dtype=np.int64)
        t = np.arange(L, dtype=np.int64)
        mm = ((2 * k[:, None] + 1) * t[None, :]) % (2 * N)
        ang = mm.astype(np.float64) * (np.pi / N)
        Ckt = np.cos(ang).astype(np.float32)
        Skt = np.sin(ang).astype(np.float32)
        del mm, ang
        F = np.empty((2, nb, 128, nb, 128), ml_dtypes.bfloat16)
        I = np.empty((2, L // TT, 128, nb, TT), ml_dtypes.bfloat16)
        for ci, tab in enumerate((Ckt, Skt)):
            t4 = tab.reshape(nb, 128, nb, 128)
            F[ci] = t4.transpose(0, 3, 2, 1).astype(ml_dtypes.bfloat16)
            t5 = tab.reshape(nb, 128, L // TT, TT)
            I[ci] = t5.transpose(2, 1, 0, 3).astype(ml_dtypes.bfloat16)
        _CONST["dftF%d" % L] = F
        _CONST["dftI%d" % L] = I
        tt_ = np.linspace(0.0, 1.0, L, dtype=np.float32)
        w = (np.float32(2 * math.pi) * np.arange(L, dtype=np.float32) / np.float32(L)).astype(np.float32)
        fr = np.linspace(1e-4, 15.0, 16, dtype=np.float32)
        fw = (fr[None, :] * w[:, None]).astype(np.float32)
        z = np.concatenate([tt_[:, None], np.cos(fw), -np.sin(fw)], axis=-1).astype(np.float32)
        _CONST["zfeat%d" % L] = f(z.T)
        _CONST["tlag%d" % L] = f(-tt_.reshape(nb, 128).T)
    return _CONST


_CACHE = {}


def run(inputs, cfg, ncores=8, cores=None):
    key = tuple(sorted(cfg.items()))
    if key not in _CACHE:
        _CACHE[key] = build_program(cfg)
    nc = _CACHE[key]
    shared = shared_layout(inputs)
    in_maps = []
    for core in (cores if cores is not None else range(ncores)):
        m = dict(shared)
        m.update(host_layout(inputs, core))
        in_maps.append(m)
    res = run_bass_kernel_spmd(nc, in_maps, core_ids=list(range(ncores)))
    return res


def assemble(res, ncores=8):
    yp = np.zeros((32, 256, D), np.float32)
    ys = np.zeros((8, 4096, D), np.float32)
    nk = np.zeros((32, 2, 256, 2, 64), np.float32)
    nv = np.zeros((32, 2, 256, 2, 64), np.float32)
    nsf = np.zeros((32, 2, 8, 128, 128), np.float32)
    nsb = np.zeros((32, 2, 8, 128, 128), np.float32)
    for core in range(ncores):
        r = res.results[core]
        y = r["yT"].transpose(1, 0, 2).reshape(D, NTOK).T
        ys[core] = y[:4096]
        yp[4 * core:4 * core + 4] = y[4096:].reshape(4, 256, D)
        k = r["newk"].reshape(2, 2, 64, 4, 256)
        nk[4 * core:4 * core + 4] = k.transpose(3, 0, 4, 1, 2)
        v = r["newv"].reshape(2, 4, 256, 2, 64)
        nv[4 * core:4 * core + 4] = v.transpose(1, 0, 2, 3, 4)
        s_ = r["nst"]
        nsf[4 * core:4 * core + 4] = s_[0].transpose(1, 0, 3, 2, 4)
        nsb[4 * core:4 * core + 4] = s_[1].transpose(1, 0, 3, 2, 4)
    return yp, ys, nk, nv, nsf, nsb


def kernel(**inputs):
    res = run(inputs, {})
    return assemble(res)
```

```python
import contextlib
import math
import numpy as np
import ml_dtypes
import concourse.bass as bass
import concourse.mybir as mybir
from concourse.bass_utils import run_bass_kernel_spmd

F32 = mybir.dt.float32
BF16 = mybir.dt.bfloat16
AF = mybir.ActivationFunctionType
ALU = mybir.AluOpType

D = 1024
NTOK = 5120
TM = 1024
NMT = NTOK // TM
DFF = 2816
NFC = DFF // 128
DEPTH = 4
EPS = 1e-6

ENGS = ("pe", "dve", "act", "pool", "sp")
NDMASEM = 8


class Buf:
    __slots__ = ("name", "w", "rs", "excl")

    def __init__(self, name="", excl=False):
        self.name = name
        self.w = None
        self.rs = []
        self.excl = excl


class Op:
    __slots__ = ("eng", "fn", "deps", "dma", "sig", "cnt", "semi", "use", "gid", "cost")


class Sched:
    def __init__(self, nc, st):
        self.nc = nc
        self.csem = {e: st.enter_context(nc.semaphore("c_" + e)) for e in ENGS}
        self.dsem = {e: [st.enter_context(nc.semaphore("d_%s%d" % (e, i))) for i in range(NDMASEM)]
                     for e in ("sp", "pool", "act")}
        self.cnt = {e: 0 for e in ENGS}
        self.ndma = {e: 0 for e in ENGS}
        self.ops = []
        self.bufs = []
        self.nstage = 0
        self.ninstr = 0
        self.xlat = 3.0

    def buf(self, name=""):
        return Buf(name)

    COST = {"pe": 0.12, "dve": 0.6, "act": 0.7, "pool": 0.9, "sp": 2.5}

    def add(self, eng, fn, reads=(), writes=(), dma=False, cost=None):
        op = Op()
        op.cost = cost if cost is not None else (2.5 if dma else self.COST[eng])
        op.eng = eng
        op.fn = fn
        op.dma = dma
        op.sig = False
        op.gid = len(self.ops)
        deps = set()
        ex = [b for b in reads if b.excl]
        if ex:
            reads = [b for b in reads if not b.excl]
            writes = list(writes) + [b for b in ex if b not in writes]
        for b in reads:
            if b.w is not None:
                deps.add(b.w)
        for b in writes:
            if b.w is not None:
                deps.add(b.w)
            deps.update(b.rs)
        deps.discard(op.gid)
        op.deps = deps
        for b in reads:
            b.rs.append(op.gid)
            self.bufs.append(b)
        for b in writes:
            b.w = op.gid
            b.rs = []
            self.bufs.append(b)
        self.ops.append(op)
        return op

    def _resched(self, ops):
        import heapq
        n = len(ops)
        succ = [[] for _ in range(n)]
        indeg = [0] * n
        for op in ops:
            indeg[op.gid] = len(op.deps)
            for d in op.deps:
                succ[d].append(op.gid)
        ready_t = [0.0] * n
        fin = [0.0] * n
        heaps = {e: [] for e in ENGS}
        free = {e: 0.0 for e in ENGS}
        for op in ops:
            if indeg[op.gid] == 0:
                heapq.heappush(heaps[op.eng], (0.0, op.gid))
        order = []
        while len(order) < n:
            best = None
            for e in ENGS:
                h = heaps[e]
                if not h:
                    continue
                rt, gid = h[0]
                st_ = max(free[e], rt)
                if best is None or (st_, gid) < (best[0], best[1]):
                    best = (st_, gid, e)
            st_, gid, e = best
            heapq.heappop(heaps[e])
            op = ops[gid]
            if op.dma:
                free[e] = st_ + 0.15
                fin[gid] = st_ + op.cost
            else:
                free[e] = st_ + op.cost
                fin[gid] = st_ + op.cost
            order.append(gid)
            for s_ in succ[gid]:
                so = ops[s_]
                lat = 0.05 if (so.eng == op.eng and not op.dma) else self.xlat
                ready_t[s_] = max(ready_t[s_], fin[gid] + lat)
                indeg[s_] -= 1
                if indeg[s_] == 0:
                    heapq.heappush(heaps[so.eng], (ready_t[s_], s_))
        return order

    def end_stage(self, resched=False):
        nc = self.nc
        ops = self.ops
        if resched and len(ops) > 2:
            order = self._resched(ops)
            remap = {g: k for k, g in enumerate(order)}
            ops = [ops[g] for g in order]
            for k, op in enumerate(ops):
                op.gid = k
                op.deps = {remap[d] for d in op.deps}
        for op in ops:
            if op.dma:
                k = self.ndma[op.eng]
                self.ndma[op.eng] = k + 1
                op.semi = k % NDMASEM
                op.use = k // NDMASEM + 1
        for op in ops:
            nd = set()
            for d in op.deps:
                p = ops[d]
                if (not p.dma) and (not op.dma) and p.eng == "pe" and op.eng == "pe":
                    continue
                nd.add(d)
                p.sig = True
            op.deps = nd
        for op in ops:
            if (not op.dma) and op.sig:
                self.cnt[op.eng] += 1
                op.cnt = self.cnt[op.eng]
        per = {e: [op for op in ops if op.eng == e] for e in ENGS}
        csem, dsem = self.csem, self.dsem
        self.ninstr += len(ops)
        with nc.Block() as block:
            def gen(ename):
                def body(eng):
                    seen_c = {}
                    seen_d = {}
                    for op in per[ename]:
                        need_c = {}
                        need_d = {}
                        for d in op.deps:
                            p = ops[d]
                            if p.dma:
                                key = (p.eng, p.semi)
                                need_d[key] = max(need_d.get(key, 0), 16 * p.use)
                            else:
                                need_c[p.eng] = max(need_c.get(p.eng, 0), p.cnt)
                        if op.dma and op.use > 1:
                            key = (op.eng, op.semi)
                            need_d[key] = max(need_d.get(key, 0), 16 * (op.use - 1))
                        for e, v in need_c.items():
                            if v > seen_c.get(e, 0):
                                eng.wait_ge(csem[e], v)
                                seen_c[e] = v
                        for key, v in need_d.items():
                            if v > seen_d.get(key, 0):
                                eng.wait_ge(dsem[key[0]][key[1]], v)
                                seen_d[key] = v
                        ins = op.fn(eng)
                        if op.dma:
                            ins.then_inc(dsem[op.eng][op.semi], 16)
                        elif op.sig:
                            ins.then_inc(csem[op.eng], 1)
                    last = {}
                    for op in per[ename]:
                        if op.dma:
                            last[op.semi] = op.use
                    for semi, use in last.items():
                        eng.wait_ge(dsem[ename][semi], 16 * use)
                return body

            block.tensor(gen("pe"))
            block.vector(gen("dve"))
            block.scalar(gen("act"))
            block.gpsimd(gen("pool"))
            block.sync(gen("sp"))
        for b in self.bufs:
            b.w = None
            b.rs = []
        self.bufs = []
        self.ops = []
        self.nstage += 1


class T:
    def __init__(self, t, name=""):
        self.t = t
        self.b = Buf(name)


def build_program(cfg):
    nlayers = cfg.get("nlayers", DEPTH)
    do_mix = cfg.get("mixers", True)
    nc = bass.Bass("TRN2", target_bir_lowering=False)

    def din(name, shape, dt=F32):
        return nc.dram_tensor(name, list(shape), dt, kind="ExternalInput").ap()

    def dout(name, shape, dt=F32):
        return nc.dram_tensor(name, list(shape), dt, kind="ExternalOutput").ap()

    dbg = cfg.get("debug", ())

    def dscr(name, shape, dt=F32):
        kind = "ExternalOutput" if name in dbg else "Internal"
        return nc.dram_tensor(name, list(shape), dt, kind=kind).ap()

    xT = din("xT", [128, 8, NTOK])
    condT = din("condT", [128, 8, 2])
    ada_w = din("ada_w", [DEPTH, 9, 128, 8, 1024])
    ada_b = din("ada_b", [128, DEPTH, 72, 2])
    norm_g = din("norm_g", [128, DEPTH, 3, 8, 2])
    w13 = din("w13", [DEPTH * 2, NFC, 128, 8, 256])
    w2 = din("w2", [DEPTH * 2, 8, 128, NFC, 128])
    yT = dout("yT", [128, 8, NTOK])
    xs = dscr("xs", [128, 8, NTOK])
    win = din("win", [2, 128, 8, 2432])
    wout = din("wout", [2, 128, 8, 1024])
    qkn = din("qkn", [128, 2, 2])
    sinkT = din("sinkT", [128, 2, 4])
    cosT_d = din("cosT", [128, 4096])
    sinT_d = din("sinT", [128, 4096])
    rotT_d = din("rotT", [128, 128])
    blk1_d = din("blk1", [128, 128])
    mprev_d = din("mprev", [128, 128])
    mnext_d = din("mnext", [128, 128])
    ckT = din("ckT", [2, 128, 2, 512])
    cvv = din("cvv", [2, 128, 4, 128])
    newk = dout("newk", [2, 2, 64, 1024])
    newv = dout("newv", [2, 1024, 128])
    qT_d = dscr("qT_d", [128, 4, NTOK])
    kT_d = dscr("kT_d", [128, 2, NTOK])
    u3T_d = dscr("u3T_d", [128, 12, NTOK])
    v_d = dscr("v_d", [NTOK, 128])
    ay_d = dscr("ay_d", [128, 8, NTOK], BF16)
    hcw = din("hcw", [128, 2, 12, 3])
    hcb = din("hcb", [128, 2, 12])
    hw1 = din("hw1", [2, 33, 64])
    hb1 = din("hb1", [64, 2, 2])
    hw2 = din("hw2", [2, 64, 64])
    hb2 = din("hb2", [64, 2, 2])
    hw3 = din("hw3", [2, 64, 2048])
    hbias = din("hbias", [128, 2, 2, 4])
    ident_d = din("ident", [128, 128])
    deltas_d = din("deltas", [128, 2048])
    HG = {}
    for L_ in (4096, 256):
        nb_ = L_ // 128
        TT_ = min(512, L_)
        HG[L_] = dict(
            F=din("dftF%d" % L_, [2, nb_, 128, nb_, 128], BF16),
            I=din("dftI%d" % L_, [2, L_ // TT_, 128, nb_, TT_], BF16),
            zf=din("zfeat%d" % L_, [33, L_]),
            tl=din("tlag%d" % L_, [128, nb_]))
    uc_d = dscr("uc_d", [128, 12, NTOK])
    z1T_d = dscr("z1T_d", [128, 4, NTOK])
    hsd_d = dscr("hsd_d", [2, 32, 128, 1024], BF16)
    H_d = dscr("H_d", [2, 2, 32, 128, 512])
    dwin = din("dwin", [2, 128, 8, 4128])
    dwout = din("dwout", [2, 128, 8, 1024])
    dcw = din("dcw", [128, 2, 24, 3])
    dprm = din("dprm", [128, 2, 32])
    dng = din("dng", [128, 2])
    dmask_d = din("dmask", [64, 5, 64])
    lvmask_d = din("lvmask", [64, 12, 64])
    sf0 = din("sf0", [2, 2, 128, 8, 128])
    nst = dout("nst", [2, 2, 4, 128, 8, 128])
    qkvT_d = dscr("qkvT_d", [128, 24, NTOK])
    qkvn_d = dscr("qkvn_d", [128, 24, NTOK])
    zT_d = dscr("zT_d", [128, 8, NTOK])
    ba_d = dscr("ba_d", [NTOK, 32])
    NCH = NTOK // 64
    u_d = dscr("u_d", [2, NCH, 64, 8, 128])
    wT_d = dscr("wT_d", [2, NCH, 128, 8, 64], BF16)
    QgT_d = dscr("QgT_d", [2, NCH, 128, 8, 64], BF16)
    AT_d = dscr("AT_d", [2, NCH, 64, 8, 64], BF16)
    Kd_d = dscr("Kd_d", [2, NCH, 64, 8, 128], BF16)
    egl_d = dscr("egl_d", [2, NCH, 128, 8])
    oT_d = dscr("oT_d", [2, 128, 8, NTOK])

    with contextlib.ExitStack() as top:
        S = Sched(nc, top)
        S.xlat = cfg.get("xlat", 3.0)

        uid = [0]

        def sb(st, name, shape, dt=F32):
            uid[0] += 1
            name = "%s_%d" % (name, uid[0])
            return T(st.enter_context(nc.sbuf_tensor(name, list(shape), dt)), name)

        PS = [T(top.enter_context(nc.psum_tensor("ps%d" % i, [128, 512], F32)), "ps%d" % i) for i in range(8)]
        for p_ in PS:
            p_.b.excl = True
        ones32 = sb(top, "ones32", [128, 128])
        epsT = sb(top, "epsT", [128, 1])
        scond = sb(top, "scond", [128, 8, 2], BF16)
        modT = sb(top, "modT", [128, 72, 2])
        gsT = sb(top, "gsT", [128, 3, 8, 2])
        hgT = sb(top, "hgT", [128, 3, 8, 2])
        adab = sb(top, "adab", [128, DEPTH, 72, 2])
        ng = sb(top, "ng", [128, DEPTH, 3, 8, 2])

        with contextlib.ExitStack() as st:
            cnd = sb(st, "cnd", [128, 8, 2])
            S.add("dve", lambda e: e.memset(ones32.t[:], 1.0), writes=[ones32.b])
            S.add("dve", lambda e: e.memset(epsT.t[:], EPS), writes=[epsT.b])
            S.add("sp", lambda e: e.dma_start(out=cnd.t[:], in_=condT), writes=[cnd.b], dma=True)
            S.add("sp", lambda e: e.dma_start(out=adab.t[:], in_=ada_b), writes=[adab.b], dma=True)
            S.add("sp", lambda e: e.dma_start(out=ng.t[:], in_=norm_g), writes=[ng.b], dma=True)
            S.add("act", lambda e: e.activation(out=scond.t[:], in_=cnd.t[:], func=AF.Silu),
                  reads=[cnd.b], writes=[scond.b])
            S.end_stage()

        def modulation_stage(l):
            with contextlib.ExitStack() as st:
                wa = [sb(st, "wa%d" % i, [128, 8, 1024], BF16) for i in range(2)]
                for blk in range(9):
                    w = wa[blk % 2]
                    S.add("pool", lambda e, w=w, blk=blk: e.dma_start(out=w.t[:], in_=ada_w[l, blk]),
                          writes=[w.b], dma=True)
                    ps = PS[blk % 2]
                    for cc in range(8):
                        for kc in range(8):
                            S.add("pe", lambda e, w=w, ps=ps, cc=cc, kc=kc: e.matmul(
                                ps.t[:, cc * 2:cc * 2 + 2], lhsT=w.t[:, kc, cc * 128:(cc + 1) * 128],
                                rhs=scond.t[:, kc, :], start=(kc == 0), stop=(kc == 7)),
                                reads=[w.b, scond.b], writes=[ps.b])
                    S.add("dve", lambda e, ps=ps, blk=blk: e.tensor_tensor(
                        out=modT.t[:, blk * 8:(blk + 1) * 8, :],
                        in0=ps.t[:, 0:16].rearrange("p (a b) -> p a b", b=2),
                        in1=adab.t[:, l, blk * 8:(blk + 1) * 8, :], op=ALU.add),
                        reads=[ps.b, adab.b], writes=[modT.b])
                for j in range(3):
                    S.add("dve", lambda e, j=j: e.scalar_tensor_tensor(
                        out=gsT.t[:, j], in0=modT.t[:, (3 * j + 1) * 8:(3 * j + 2) * 8, :], scalar=1.0,
                        in1=ng.t[:, l, j], op0=ALU.add, op1=ALU.mult),
                        reads=[modT.b, ng.b], writes=[gsT.b])
                    S.add("dve", lambda e, j=j: e.tensor_scalar(
                        out=hgT.t[:, j], in0=modT.t[:, (3 * j + 2) * 8:(3 * j + 3) * 8, :],
                        scalar1=(1.0 if j == 1 else 0.5), scalar2=None, op0=ALU.mult),
                        reads=[modT.b], writes=[hgT.b])
                S.end_stage(resched=cfg.get("rs_x", True))

        def norm_mod(st_tiles, xb, hb, j, which, ps_pair):
            sq, rstd, tmp = st_tiles
            for c in range(8):
                q = sq[c % 2]
                S.add("act", lambda e, q=q, c=c: e.activation(out=q.t[:], in_=xb.t[:, c, :], func=AF.Square),
                      reads=[xb.b], writes=[q.b])
                for s in range(TM // 512):
                    S.add("pe", lambda e, q=q, c=c, s=s: e.matmul(
                        ps_pair[s].t[:], lhsT=ones32.t[:], rhs=q.t[:, s * 512:(s + 1) * 512],
                        start=(c == 0), stop=(c == 7)), reads=[q.b, ones32.b], writes=[ps_pair[s].b])
            for s in range(TM // 512):
                S.add("act", lambda e, s=s: e.activation(
                    out=rstd.t[:, s * 512:(s + 1) * 512], in_=ps_pair[s].t[:], func=AF.Sqrt,
                    scale=1.0 / D, bias=epsT.t[:, 0:1]), reads=[ps_pair[s].b, epsT.b], writes=[rstd.b])
            S.add("dve", lambda e: e.reciprocal(out=rstd.t[:], in_=rstd.t[:]), reads=[rstd.b], writes=[rstd.b])
            for c in range(8):
                tp = tmp[c % 2]
                S.add("dve", lambda e, tp=tp, c=c: e.tensor_tensor(
                    out=tp.t[:], in0=xb.t[:, c, :], in1=rstd.t[:], op=ALU.mult),
                    reads=[xb.b, rstd.b], writes=[tp.b])
                S.add("act", lambda e, tp=tp, c=c: e.activation(
                    out=hb.t[:, c, :], in_=tp.t[:], func=AF.Identity,
                    scale=gsT.t[:, j, c, which:which + 1], bias=modT.t[:, 3 * j * 8 + c, which:which + 1]),
                    reads=[tp.b, gsT.b, modT.b], writes=[hb.b])

        def ffn_stage(l, hf, src, dst):
            j = 0 if hf == 0 else 2
            lh = l * 2 + hf
            with contextlib.ExitStack() as st:
                X = [sb(st, "x%d" % i, [128, 8, TM]) for i in range(2)]
                Hh = [sb(st, "h%d" % i, [128, 8, TM], BF16) for i in range(2)]
                sq = [sb(st, "sq%d" % i, [128, TM]) for i in range(2)]
                tmp = [sb(st, "tmp%d" % i, [128, TM]) for i in range(2)]
                rstd = sb(st, "rstd", [128, TM])
                actT = sb(st, "actT", [128, NFC, TM], BF16)
                sg = [sb(st, "sg%d" % i, [128, 512]) for i in range(2)]
                wp = [sb(st, "wp%d" % i, [128, 8, 256], BF16) for i in range(4)]
                w2t = [sb(st, "w2t%d" % i, [128, NFC, 128], BF16) for i in range(2)]
                dsrc = [Buf() for _ in range(NMT)]
                wctr = [0, 0]

                def load_x(mt):
                    xb = X[mt % 2]
                    S.add("sp", lambda e: e.dma_start(out=xb.t[:], in_=src[:, :, mt * TM:(mt + 1) * TM]),
                          reads=[dsrc[mt]], writes=[xb.b], dma=True)

                def stage_a(mt):
                    which = 0 if mt < 4 else 1
                    norm_mod((sq, rstd, tmp), X[mt % 2], Hh[mt % 2], j, which, (PS[4], PS[5]))

                load_x(0)
                stage_a(0)
                for mt in range(NMT):
                    which = 0 if mt < 4 else 1
                    xb = X[mt % 2]
                    hb = Hh[mt % 2]
                    if mt + 1 < NMT:
                        load_x(mt + 1)
                    for jp in range(NFC):
                        w = wp[wctr[0] % 4]
                        wctr[0] += 1
                        S.add("pool", lambda e, w=w, jp=jp: e.dma_start(out=w.t[:], in_=w13[lh, jp]),
                              writes=[w.b], dma=True)
                        for s in range(TM // 512):
                            k = jp * 2 + s
                            pg, pu = PS[k % 2], PS[2 + k % 2]
                            for half, pb in ((0, pg), (1, pu)):
                                for c in range(8):
                                    S.add("pe", lambda e, w=w, pb=pb, c=c, s=s, half=half, hb=hb: e.matmul(
                                        pb.t[:], lhsT=w.t[:, c, half * 128:(half + 1) * 128],
                                        rhs=hb.t[:, c, s * 512:(s + 1) * 512], start=(c == 0), stop=(c == 7)),
                                        reads=[w.b, hb.b], writes=[pb.b])
                            g = sg[k % 2]
                            S.add("act", lambda e, g=g, pg=pg: e.activation(out=g.t[:], in_=pg.t[:], func=AF.Silu),
                                  reads=[pg.b], writes=[g.b])
                            S.add("dve", lambda e, g=g, pu=pu, jp=jp, s=s: e.tensor_tensor(
                                out=actT.t[:, jp, s * 512:(s + 1) * 512], in0=g.t[:], in1=pu.t[:], op=ALU.mult),
                                reads=[g.b, pu.b], writes=[actT.b])
                        if jp == 11 and mt + 1 < NMT:
                            stage_a(mt + 1)
                    for m in range(8):
                        w = w2t[wctr[1] % 2]
                        wctr[1] += 1
                        S.add("pool", lambda e, w=w, m=m: e.dma_start(out=w.t[:], in_=w2[lh, m], max_dma_last_dim=4096),
                              writes=[w.b], dma=True)
                        for s in range(TM // 512):
                            pb = PS[6 + (m * 2 + s) % 2]
                            for f in range(NFC):
                                S.add("pe", lambda e, w=w, pb=pb, f=f, s=s: e.matmul(
                                    pb.t[:], lhsT=w.t[:, f, :], rhs=actT.t[:, f, s * 512:(s + 1) * 512],
                                    start=(f == 0), stop=(f == NFC - 1)), reads=[w.b, actT.b], writes=[pb.b])
                            S.add("dve", lambda e, pb=pb, m=m, s=s, xb=xb, which=which: e.scalar_tensor_tensor(
                                out=xb.t[:, m, s * 512:(s + 1) * 512], in0=pb.t[:],
                                scalar=hgT.t[:, j, m, which:which + 1], in1=xb.t[:, m, s * 512:(s + 1) * 512],
                                op0=ALU.mult, op1=ALU.add), reads=[pb.b, hgT.b, xb.b], writes=[xb.b])
                    S.add("sp", lambda e, xb=xb, mt=mt: e.dma_start(out=dst[:, :, mt * TM:(mt + 1) * TM], in_=xb.t[:]),
                          reads=[xb.b], writes=[dsrc[mt]], dma=True)
                S.end_stage(resched=cfg.get("rs_ffn", False))

        def ah_inproj(l, i):
            with contextlib.ExitStack() as st:
                X = [sb(st, "x", [128, 8, TM]) for _ in range(2)]
                Hh = [sb(st, "h", [128, 8, TM], BF16) for _ in range(2)]
                sq = [sb(st, "sq", [128, TM]) for _ in range(2)]
                tmp = [sb(st, "tmp", [128, TM]) for _ in range(2)]
                rstd = sb(st, "rstd", [128, TM])
                W = sb(st, "win", [128, 8, 2432], BF16)
                stg = [sb(st, "stg", [128, 512]) for _ in range(4)]
                vst = [sb(st, "vst", [128, 8, 128]) for _ in range(2)]
                for kc2 in range(4):
                    S.add("pool", lambda e, kc2=kc2: e.dma_start(out=W.t[:, 2 * kc2:2 * kc2 + 2, :], in_=win[i, :, 2 * kc2:2 * kc2 + 2, :],
                                                               max_dma_last_dim=4096), writes=[W.b], dma=True)

                def load_x(mt):
                    xb = X[mt % 2]
                    S.add("sp", lambda e: e.dma_start(out=xb.t[:], in_=xs[:, :, mt * TM:(mt + 1) * TM]),
                          writes=[xb.b], dma=True)

                def stage_a(mt):
                    which = 0 if mt < 4 else 1
                    norm_mod((sq, rstd, tmp), X[mt % 2], Hh[mt % 2], 1, which, (PS[4], PS[5]))

                load_x(0)
                stage_a(0)
                ctr = [0]
                for mt in range(NMT):
                    hb = Hh[mt % 2]
                    if mt + 1 < NMT:
                        load_x(mt + 1)
                    for oc in range(18):
                        if oc < 4:
                            dst, ch = qT_d, oc
                        elif oc < 6:
                            dst, ch = kT_d, oc - 4
                        else:
                            dst, ch = u3T_d, oc - 6
                        for s_ in range(TM // 512):
                            k = ctr[0]
                            ctr[0] += 1
                            pb = PS[k % 4]
                            sg_ = stg[k % 4]
                            for c in range(8):
                                S.add("pe", lambda e, pb=pb, c=c, s_=s_, oc=oc, hb=hb: e.matmul(
                                    pb.t[:], lhsT=W.t[:, c, oc * 128:(oc + 1) * 128],
                                    rhs=hb.t[:, c, s_ * 512:(s_ + 1) * 512], start=(c == 0), stop=(c == 7)),
                                    reads=[W.b, hb.b], writes=[pb.b])
                            if k % 2 == 0:
                                S.add("act", lambda e, pb=pb, sg_=sg_: e.activation(out=sg_.t[:], in_=pb.t[:], func=AF.Identity),
                                      reads=[pb.b], writes=[sg_.b])
                            else:
                                S.add("dve", lambda e, pb=pb, sg_=sg_: e.tensor_copy(out=sg_.t[:], in_=pb.t[:]),
                                      reads=[pb.b], writes=[sg_.b])
                            t0 = mt * TM + s_ * 512
                            S.add("sp", lambda e, dst=dst, ch=ch, t0=t0, sg_=sg_: e.dma_start(
                                out=dst[:, ch, t0:t0 + 512], in_=sg_.t[:]), reads=[sg_.b], dma=True)
                        if oc == 9 and mt + 1 < NMT:
                            stage_a(mt + 1)
                    vs_ = vst[mt % 2]
                    for tb in range(TM // 128):
                        pb = PS[6 + tb % 2]
                        for c in range(8):
                            S.add("pe", lambda e, pb=pb, c=c, tb=tb, hb=hb: e.matmul(
                                pb.t[:, 0:128], lhsT=hb.t[:, c, tb * 128:(tb + 1) * 128], rhs=W.t[:, c, 2304:2432],
                                start=(c == 0), stop=(c == 7)), reads=[W.b, hb.b], writes=[pb.b])
                        S.add("dve", lambda e, pb=pb, tb=tb, vs_=vs_: e.tensor_copy(out=vs_.t[:, tb, :], in_=pb.t[:, 0:128]),
                              reads=[pb.b], writes=[vs_.b])
                    S.add("sp", lambda e, mt=mt, vs_=vs_: e.dma_start(
                        out=v_d[mt * TM:(mt + 1) * TM, :].rearrange("(tb p) n -> p tb n", p=128), in_=vs_.t[:]),
                        reads=[vs_.b], dma=True)
                S.end_stage(resched=cfg.get("rs_x", True))

        def ah_attention(l, i):
            with contextlib.ExitStack() as st:
                cosT = sb(st, "cosT", [128, 4096])
                sinT = sb(st, "sinT", [128, 4096])
                rotT = sb(st, "rotT", [128, 128])
                blk1 = sb(st, "blk1", [128, 128])
                mprev = sb(st, "mprev", [128, 128], BF16)
                mnext = sb(st, "mnext", [128, 128], BF16)
                onesb = sb(st, "onesb", [128, 64], BF16)
                gn = sb(st, "gn", [128, 2])
                gq8 = sb(st, "gq8", [128, 1])
                esink = sb(st, "esink", [128, 4])
                KT = sb(st, "KT", [128, 2, 4096], BF16)
                VV = sb(st, "VV", [128, 32, 128], BF16)
                CK = sb(st, "CK", [128, 2, 512], BF16)
                CV = sb(st, "CV", [128, 4, 128], BF16)
                kin = [sb(st, "kin", [128, 2, 512]) for _ in range(2)]
                qin = [sb(st, "qin", [128, 4, 512]) for _ in range(2)]
                qp = [sb(st, "qp", [128, 4, 512], BF16) for _ in range(2)]
                sqq = [sb(st, "sqq", [128, 512]) for _ in range(2)]
                rs_ = [sb(st, "rs", [128, 512]) for _ in range(2)]
                kg_ = [sb(st, "kg", [128, 512]) for _ in range(2)]
                t1_ = [sb(st, "t1", [128, 512]) for _ in range(2)]
                t2_ = [sb(st, "t2", [128, 512]) for _ in range(2)]
                pt = [sb(st, "pt", [128, 512], BF16) for _ in range(3)]
                den = [sb(st, "den", [128, 512]) for _ in range(2)]
                aout = [sb(st, "aout", [128, 4, 512], BF16) for _ in range(2)]
                kno = [sb(st, "kno", [128, 2, 256]) for _ in range(2)]

                S.add("sp", lambda e: e.dma_start(out=cosT.t[:], in_=cosT_d), writes=[cosT.b], dma=True)
                S.add("sp", lambda e: e.dma_start(out=sinT.t[:], in_=sinT_d), writes=[sinT.b], dma=True)
                S.add("sp", lambda e: e.dma_start(out=rotT.t[:], in_=rotT_d), writes=[rotT.b], dma=True)
                S.add("sp", lambda e: e.dma_start(out=blk1.t[:], in_=blk1_d), writes=[blk1.b], dma=True)
                S.add("pool", lambda e: e.dma_start(out=mprev.t[:], in_=mprev_d), writes=[mprev.b], dma=True)
                S.add("pool", lambda e: e.dma_start(out=mnext.t[:], in_=mnext_d), writes=[mnext.b], dma=True)
                S.add("sp", lambda e: e.dma_start(out=gn.t[:], in_=qkn[:, i, :]), writes=[gn.b], dma=True)
                S.add("sp", lambda e: e.dma_start(out=esink.t[:], in_=sinkT[:, i, :]), writes=[esink.b], dma=True)
                S.add("pool", lambda e: e.dma_start(out=CK.t[:], in_=ckT[i]), writes=[CK.b], dma=True)
                S.add("pool", lambda e: e.dma_start(out=CV.t[:], in_=cvv[i]), writes=[CV.b], dma=True)
                S.add("dve", lambda e: e.memset(onesb.t[:], 1.0), writes=[onesb.b])
                S.add("dve", lambda e: e.tensor_scalar(out=gq8.t[:], in0=gn.t[:, 0:1], scalar1=0.125, scalar2=None, op0=ALU.mult),
                      reads=[gn.b], writes=[gq8.b])
                S.add("act", lambda e: e.activation(out=esink.t[:], in_=esink.t[:], func=AF.Exp), reads=[esink.b], writes=[esink.b])
                pctr = [0]

                def qk_prep(src, srcb, n, gain, gainb, rope_t0, out, outb, nout=None, noutb=None):
                    k = pctr[0]
                    pctr[0] += 1
                    sq_, r_, g_, a_, b_ = sqq[k % 2], rs_[k % 2], kg_[k % 2], t1_[k % 2], t2_[k % 2]
                    pb = PS[3]
                    S.add("act", lambda e: e.activation(out=sq_.t[:, 0:n], in_=src, func=AF.Square), reads=[srcb], writes=[sq_.b])
                    S.add("pe", lambda e: e.matmul(pb.t[:, 0:n], lhsT=blk1.t[:], rhs=sq_.t[:, 0:n], start=True, stop=True),
                          reads=[blk1.b, sq_.b], writes=[pb.b])
                    S.add("act", lambda e: e.activation(out=r_.t[:, 0:n], in_=pb.t[:, 0:n], func=AF.Sqrt, scale=1.0 / 64,
                                                        bias=epsT.t[:, 0:1]), reads=[pb.b, epsT.b], writes=[r_.b])
                    S.add("dve", lambda e: e.reciprocal(out=r_.t[:, 0:n], in_=r_.t[:, 0:n]), reads=[r_.b], writes=[r_.b])
                    if rope_t0 is None:
                        if nout is not None:
                            S.add("dve", lambda e: e.scalar_tensor_tensor(out=nout, in0=src, scalar=gain, in1=r_.t[:, 0:n],
                                                                          op0=ALU.mult, op1=ALU.mult),
                                  reads=[srcb, gainb, r_.b], writes=[noutb])
                        S.add("dve", lambda e: e.scalar_tensor_tensor(out=out, in0=src, scalar=gain, in1=r_.t[:, 0:n],
                                                                      op0=ALU.mult, op1=ALU.mult),
                              reads=[srcb, gainb, r_.b], writes=[outb])
                        return
                    S.add("dve", lambda e: e.scalar_tensor_tensor(out=g_.t[:, 0:n], in0=src, scalar=gain, in1=r_.t[:, 0:n],
                                                                  op0=ALU.mult, op1=ALU.mult),
                          reads=[srcb, gainb, r_.b], writes=[g_.b])
                    S.add("pe", lambda e: e.matmul(pb.t[:, 0:n], lhsT=rotT.t[:], rhs=g_.t[:, 0:n], start=True, stop=True),
                          reads=[rotT.b, g_.b], writes=[pb.b])
                    S.add("dve", lambda e: e.tensor_tensor(out=b_.t[:, 0:n], in0=pb.t[:, 0:n], in1=sinT.t[:, rope_t0:rope_t0 + n], op=ALU.mult),
                          reads=[pb.b, sinT.b], writes=[b_.b])
                    S.add("pool", lambda e: e.tensor_tensor(out=a_.t[:, 0:n], in0=g_.t[:, 0:n], in1=cosT.t[:, rope_t0:rope_t0 + n], op=ALU.mult),
                          reads=[g_.b, cosT.b], writes=[a_.b])
                    S.add("dve", lambda e: e.tensor_tensor(out=out, in0=a_.t[:, 0:n], in1=b_.t[:, 0:n], op=ALU.add),
                          reads=[a_.b, b_.b], writes=[outb])

                stc = [0]
                gctr = [0]

                def attend(qpt, nq, blocks, dst_t0):
                    gi = gctr[0]
                    gctr[0] += 1
                    ao = aout[gi % 2]
                    for c in range(4):
                        par = (gi * 4 + c) % 2
                        PA, PB = PS[4 + 2 * par], PS[5 + 2 * par]
                        for hh in range(2):
                            h = 2 * c + hh
                            kvh = h // 4
                            lo = hh * 64
                            nb = len(blocks)
                            for idx, (Kt, kcol, Vt, vblk, q0, q1, masks) in enumerate(blocks):
                                n = q1 - q0
                                k = stc[0]
                                stc[0] += 1
                                ST = PS[k % 3]
                                P_ = pt[k % 3]
                                S.add("pe", lambda e, ST=ST, Kt=Kt, kcol=kcol, q0=q0, q1=q1, n=n, lo=lo, kvh=kvh, c=c: e.matmul(
                                    ST.t[:, 0:n], lhsT=Kt.t[lo:lo + 64, kvh, kcol:kcol + 128], rhs=qpt.t[lo:lo + 64, c, q0:q1],
                                    start=True, stop=True), reads=[Kt.b, qpt.b], writes=[ST.b])
                                S.add("act", lambda e, ST=ST, P_=P_, n=n: e.activation(out=P_.t[:, 0:n], in_=ST.t[:, 0:n], func=AF.Exp),
                                      reads=[ST.b], writes=[P_.b])
                                for (moff, mt_) in masks:
                                    S.add("dve", lambda e, P_=P_, moff=moff, mt_=mt_: e.tensor_tensor(
                                        out=P_.t[:, moff:moff + 128], in0=P_.t[:, moff:moff + 128], in1=mt_.t[:], op=ALU.mult),
                                        reads=[P_.b, mt_.b], writes=[P_.b])
                                S.add("pe", lambda e, PA=PA, Vt=Vt, vblk=vblk, P_=P_, n=n, q0=q0, q1=q1, lo=lo, kvh=kvh, idx=idx, nb=nb: e.matmul(
                                    PA.t[lo:lo + 64, q0:q1], lhsT=Vt.t[:, vblk, kvh * 64:(kvh + 1) * 64], rhs=P_.t[:, 0:n],
                                    start=(idx == 0), stop=(idx == nb - 1)), reads=[Vt.b, P_.b], writes=[PA.b])
                                S.add("pe", lambda e, PB=PB, P_=P_, n=n, q0=q0, q1=q1, lo=lo, idx=idx, nb=nb: e.matmul(
                                    PB.t[lo:lo + 64, q0:q1], lhsT=onesb.t[:, 0:64], rhs=P_.t[:, 0:n],
                                    start=(idx == 0), stop=(idx == nb - 1)), reads=[onesb.b, P_.b], writes=[PB.b])
                        dn = den[(gi * 4 + c) % 2]
                        S.add("dve", lambda e, dn=dn, PB=PB, c=c: e.tensor_scalar(out=dn.t[:, 0:nq], in0=PB.t[:, 0:nq], scalar1=esink.t[:, c:c + 1],
                                                                             scalar2=None, op0=ALU.add), reads=[PB.b, esink.b], writes=[dn.b])
                        S.add("dve", lambda e, dn=dn: e.reciprocal(out=dn.t[:, 0:nq], in_=dn.t[:, 0:nq]), reads=[dn.b], writes=[dn.b])
                        S.add("dve", lambda e, dn=dn, PA=PA, c=c: e.tensor_tensor(out=ao.t[:, c, 0:nq], in0=PA.t[:, 0:nq], in1=dn.t[:, 0:nq], op=ALU.mult),
                              reads=[PA.b, dn.b], writes=[ao.b])
                    S.add("sp", lambda e: e.dma_start(out=ay_d[:, 0:4, dst_t0:dst_t0 + nq], in_=ao.t[:, :, 0:nq]), reads=[ao.b], dma=True)

                for tt in range(8):
                    ki = kin[tt % 2]
                    S.add("sp", lambda e, ki=ki, tt=tt: e.dma_start(out=ki.t[:], in_=kT_d[:, :, tt * 512:(tt + 1) * 512]), writes=[ki.b], dma=True)
                    S.add("pool", lambda e, tt=tt: e.dma_start(
                        out=VV.t[:, tt * 4:(tt + 1) * 4, :], in_=v_d[tt * 512:(tt + 1) * 512, :].rearrange("(tb p) n -> p tb n", p=128)),
                        writes=[VV.b], dma=True)
                    for ch in range(2):
                        qk_prep(ki.t[:, ch, :], ki.b, 512, gn.t[:, 1:2], gn.b, tt * 512, KT.t[:, ch, tt * 512:(tt + 1) * 512], KT.b)
                for g in range(8):
                    qi = qin[g % 2]
                    qq = qp[g % 2]
                    S.add("sp", lambda e, qi=qi, g=g: e.dma_start(out=qi.t[:], in_=qT_d[:, :, g * 512:(g + 1) * 512]), writes=[qi.b], dma=True)
                    for c in range(4):
                        qk_prep(qi.t[:, c, :], qi.b, 512, gq8.t[:, 0:1], gq8.b, g * 512, qq.t[:, c, :], qq.b)
                    blocks = []
                    for b_ in range(4):
                        blocks.append((CK, b_ * 128, CV, b_, 0, 512, []))
                    for kb in range(max(0, 4 * g - 1), min(32, 4 * g + 5)):
                        qb0 = max(kb - 1, 4 * g)
                        qb1 = min(kb + 1, 4 * g + 3)
                        masks = []
                        for qb in range(qb0, qb1 + 1):
                            if qb == kb + 1:
                                masks.append(((qb - qb0) * 128, mprev))
                            elif qb == kb - 1:
                                masks.append(((qb - qb0) * 128, mnext))
                        blocks.append((KT, kb * 128, VV, kb, (qb0 - 4 * g) * 128, (qb1 - 4 * g + 1) * 128, masks))
                    attend(qq, 512, blocks, g * 512)
                KTp = [sb(st, "KTp", [128, 2, 256], BF16) for _ in range(2)]
                VVp = [sb(st, "VVp", [128, 2, 128], BF16) for _ in range(2)]
                for pi in range(4):
                    t0 = 4096 + pi * 256
                    ki = kin[pi % 2]
                    kt, vv, kn_ = KTp[pi % 2], VVp[pi % 2], kno[pi % 2]
                    S.add("sp", lambda e, ki=ki, t0=t0: e.dma_start(out=ki.t[:, :, 0:256], in_=kT_d[:, :, t0:t0 + 256]), writes=[ki.b], dma=True)
                    S.add("pool", lambda e, vv=vv, t0=t0: e.dma_start(
                        out=vv.t[:], in_=v_d[t0:t0 + 256, :].rearrange("(tb p) n -> p tb n", p=128)), writes=[vv.b], dma=True)
                    for ch in range(2):
                        qk_prep(ki.t[:, ch, 0:256], ki.b, 256, gn.t[:, 1:2], gn.b, None, kt.t[:, ch, :], kt.b,
                                nout=kn_.t[:, ch, :], noutb=kn_.b)
                    S.add("sp", lambda e, kn_=kn_, pi=pi: e.dma_start(
                        out=newk[i, :, :, pi * 256:(pi + 1) * 256].rearrange("k d t -> d k t"), in_=kn_.t[0:64, :, :]),
                        reads=[kn_.b], dma=True)
                    qi = qin[pi % 2]
                    qq = qp[pi % 2]
                    S.add("sp", lambda e, qi=qi, t0=t0: e.dma_start(out=qi.t[:, :, 0:256], in_=qT_d[:, :, t0:t0 + 256]), writes=[qi.b], dma=True)
                    for c in range(4):
                        qk_prep(qi.t[:, c, 0:256], qi.b, 256, gq8.t[:, 0:1], gq8.b, None, qq.t[:, c, 0:256], qq.b)
                    blocks = [(kt, b_ * 128, vv, b_, 0, 256, []) for b_ in range(2)]
                    attend(qq, 256, blocks, t0)
                S.add("sp", lambda e: e.dma_start(out=newv[i], in_=v_d[4096:5120, :]), dma=True)
                S.end_stage(resched=True)

        def ah_outproj(l, i):
            with contextlib.ExitStack() as st:
                X = [sb(st, "x", [128, 8, TM]) for _ in range(2)]
                AY = [sb(st, "ay", [128, 8, TM], BF16) for _ in range(2)]
                W = sb(st, "wout", [128, 8, 1024], BF16)
                for kc2 in range(4):
                    S.add("pool", lambda e, kc2=kc2: e.dma_start(out=W.t[:, 2 * kc2:2 * kc2 + 2, :], in_=wout[i, :, 2 * kc2:2 * kc2 + 2, :],
                                                               max_dma_last_dim=4096), writes=[W.b], dma=True)

                def load(mt):
                    xb, ab = X[mt % 2], AY[mt % 2]
                    S.add("sp", lambda e: e.dma_start(out=xb.t[:], in_=xs[:, :, mt * TM:(mt + 1) * TM]), writes=[xb.b], dma=True)
                    S.add("sp", lambda e: e.dma_start(out=ab.t[:], in_=ay_d[:, :, mt * TM:(mt + 1) * TM]), writes=[ab.b], dma=True)

                load(0)
                for mt in range(NMT):
                    which = 0 if mt < 4 else 1
                    xb, ab = X[mt % 2], AY[mt % 2]
                    if mt + 1 < NMT:
                        load(mt + 1)
                    for m in range(8):
                        for s_ in range(TM // 512):
                            pb = PS[(m * 2 + s_) % 4]
                            for f in range(8):
                                S.add("pe", lambda e, pb=pb, f=f, m=m, s_=s_, ab=ab: e.matmul(
                                    pb.t[:], lhsT=W.t[:, f, m * 128:(m + 1) * 128], rhs=ab.t[:, f, s_ * 512:(s_ + 1) * 512],
                                    start=(f == 0), stop=(f == 7)), reads=[W.b, ab.b], writes=[pb.b])
                            S.add("dve", lambda e, pb=pb, m=m, s_=s_, xb=xb, which=which: e.scalar_tensor_tensor(
                                out=xb.t[:, m, s_ * 512:(s_ + 1) * 512], in0=pb.t[:],
                                scalar=hgT.t[:, 1, m, which:which + 1], in1=xb.t[:, m, s_ * 512:(s_ + 1) * 512],
                                op0=ALU.mult, op1=ALU.add), reads=[pb.b, hgT.b, xb.b], writes=[xb.b])
                    S.add("sp", lambda e, xb=xb, mt=mt: e.dma_start(out=xs[:, :, mt * TM:(mt + 1) * TM], in_=xb.t[:]),
                          reads=[xb.b], dma=True)
                S.end_stage(resched=cfg.get("rs_x", True))


        def hy_conv(l, i):
            with contextlib.ExitStack() as st:
                cw = sb(st, "cw", [128, 12, 3])
                cb = sb(st, "cb", [128, 12])
                U = [sb(st, "U", [128, 12, 514]) for _ in range(2)]
                O = [sb(st, "O", [128, 12, 512]) for _ in range(2)]
                S.add("sp", lambda e: e.dma_start(out=cw.t[:], in_=hcw[:, i]), writes=[cw.b], dma=True)
                S.add("sp", lambda e: e.dma_start(out=cb.t[:], in_=hcb[:, i]), writes=[cb.b], dma=True)
                tiles = [(tt * 512, 512, tt == 0, tt == 7) for tt in range(8)] + [(4096 + 256 * pi, 256, True, True) for pi in range(4)]
                for ti, (t0, n, first, lastt) in enumerate(tiles):
                    u, o = U[ti % 2], O[ti % 2]
                    a0 = t0 if first else t0 - 1
                    a1 = t0 + n if lastt else t0 + n + 1
                    c0 = 1 if first else 0
                    S.add("sp", lambda e, u=u, a0=a0, a1=a1, c0=c0: e.dma_start(out=u.t[:, :, c0:c0 + (a1 - a0)], in_=u3T_d[:, :, a0:a1]),
                          writes=[u.b], dma=True)
                    if first:
                        S.add("pool", lambda e, u=u: e.memset(u.t[:, :, 0:1], 0.0), writes=[u.b])
                    if lastt:
                        S.add("pool", lambda e, u=u, n=n: e.memset(u.t[:, :, n + 1:n + 2], 0.0), writes=[u.b])
                    for ch in range(12):
                        en = "dve"
                        S.add(en, lambda e, u=u, o=o, ch=ch, n=n: e.tensor_scalar(
                            out=o.t[:, ch, 0:n], in0=u.t[:, ch, 0:n], scalar1=cw.t[:, ch, 0:1], scalar2=cb.t[:, ch:ch + 1],
                            op0=ALU.mult, op1=ALU.add), reads=[u.b, cw.b, cb.b], writes=[o.b])
                        S.add(en, lambda e, u=u, o=o, ch=ch, n=n: e.scalar_tensor_tensor(
                            out=o.t[:, ch, 0:n], in0=u.t[:, ch, 1:n + 1], scalar=cw.t[:, ch, 1:2], in1=o.t[:, ch, 0:n],
                            op0=ALU.mult, op1=ALU.add), reads=[u.b, cw.b, o.b], writes=[o.b])
                        S.add(en, lambda e, u=u, o=o, ch=ch, n=n: e.scalar_tensor_tensor(
                            out=o.t[:, ch, 0:n], in0=u.t[:, ch, 2:n + 2], scalar=cw.t[:, ch, 2:3], in1=o.t[:, ch, 0:n],
                            op0=ALU.mult, op1=ALU.add), reads=[u.b, cw.b, o.b], writes=[o.b])
                    S.add("sp", lambda e, o=o, t0=t0, n=n: e.dma_start(out=uc_d[:, :, t0:t0 + n], in_=o.t[:, :, 0:n]), reads=[o.b], dma=True)
                S.end_stage(resched=True)

        MAGIC = 12582912.0

        def hy_filter_gen(i, L, rn):
            G = HG[L]
            nb = L // 128
            TT = min(512, L)
            with contextlib.ExitStack() as st:
                zf = sb(st, "zf", [33, L])
                w1 = sb(st, "w1", [33, 64])
                w2 = sb(st, "w2", [64, 64])
                w3 = sb(st, "w3", [64, 2048])
                p1 = sb(st, "p1", [64, 2])
                p2 = sb(st, "p2", [64, 2])
                h1T = sb(st, "h1T", [64, L])
                h2T = sb(st, "h2T", [64, L])
                delt = sb(st, "delt", [128, 2048])
                tl = sb(st, "tl", [128, nb])
                dec = [sb(st, "dec", [128, 2048]) for _ in range(2)]
                hh = [sb(st, "hh", [128, 2048]) for _ in range(2)]
                sqh = [sb(st, "sqh", [128, 2048]) for _ in range(2)]
                stg = [sb(st, "hstg", [128, 2, 1024], BF16) for _ in range(2)]
                uu = [sb(st, "uu", [64, 512]) for _ in range(2)]
                ta = [sb(st, "ta", [64, 512]) for _ in range(2)]
                nr = [sb(st, "nr", [64, 512]) for _ in range(2)]
                sst = sb(st, "sst", [128, 1024])
                for (tt_, src) in ((zf, G["zf"]), (w1, hw1[i]), (w2, hw2[i]), (w3, hw3[i]), (p1, hb1[:, i, :]), (p2, hb2[:, i, :]),
                                  (delt, deltas_d), (tl, G["tl"])):
                    S.add("sp", lambda e, tt_=tt_, src=src: e.dma_start(out=tt_.t[:], in_=src), writes=[tt_.b], dma=True)
                for pp in (p1, p2):
                    S.add("dve", lambda e, pp=pp: e.tensor_scalar(out=pp.t[:, 1:2], in0=pp.t[:, 1:2], scalar1=1.0 / (2 * math.pi), scalar2=None,
                                                                  op0=ALU.mult), reads=[pp.b], writes=[pp.b])
                k = 0
                for (wt, pp, srcT, dstT) in ((w1, p1, zf, h1T), (w2, p2, h1T, h2T)):
                    for tile_ in range(L // TT):
                        c0 = tile_ * TT
                        pb = PS[k % 2]
                        u_, a_, n_ = uu[k % 2], ta[k % 2], nr[k % 2]
                        k += 1
                        kk = wt.t.shape[0]
                        S.add("pe", lambda e, pb=pb, wt=wt, srcT=srcT, c0=c0, kk=kk: e.matmul(
                            pb.t[0:64, 0:TT], lhsT=wt.t[:, :], rhs=srcT.t[0:kk, c0:c0 + TT], start=True, stop=True),
                            reads=[wt.b, srcT.b], writes=[pb.b])
                        S.add("dve", lambda e, pb=pb, u_=u_, pp=pp: e.tensor_scalar(
                            out=u_.t[:, 0:TT], in0=pb.t[0:64, 0:TT], scalar1=pp.t[:, 0:1], scalar2=pp.t[:, 1:2], op0=ALU.add, op1=ALU.mult),
                            reads=[pb.b, pp.b], writes=[u_.b])
                        S.add("dve", lambda e, u_=u_, a_=a_: e.tensor_scalar(
                            out=a_.t[:, 0:TT], in0=u_.t[:, 0:TT], scalar1=MAGIC, scalar2=None, op0=ALU.add), reads=[u_.b], writes=[a_.b])
                        S.add("dve", lambda e, u_=u_, a_=a_, n_=n_: e.scalar_tensor_tensor(
                            out=n_.t[:, 0:TT], in0=a_.t[:, 0:TT], scalar=MAGIC, in1=u_.t[:, 0:TT], op0=ALU.subtract, op1=ALU.subtract),
                            reads=[a_.b, u_.b], writes=[n_.b])
                        S.add("dve", lambda e, n_=n_: e.tensor_scalar(
                            out=n_.t[:, 0:TT], in0=n_.t[:, 0:TT], scalar1=0.49999, scalar2=-0.49999, op0=ALU.min, op1=ALU.max),
                            reads=[n_.b], writes=[n_.b])
                        S.add("act", lambda e, n_=n_, dstT=dstT, c0=c0: e.activation(
                            out=dstT.t[:, c0:c0 + TT], in_=n_.t[:, 0:TT], func=AF.Sin, scale=-2.0 * math.pi), reads=[n_.b], writes=[dstT.b])
                for lb in range(nb):
                    d_, h_, q_, sg_ = dec[lb % 2], hh[lb % 2], sqh[lb % 2], stg[lb % 2]
                    S.add("act", lambda e, d_=d_, lb=lb: e.activation(out=d_.t[:], in_=delt.t[:], func=AF.Exp, scale=tl.t[:, lb:lb + 1]),
                          reads=[delt.b, tl.b], writes=[d_.b])
                    for ct in range(4):
                        pb = PS[ct]
                        S.add("pe", lambda e, pb=pb, lb=lb, ct=ct: e.matmul(
                            pb.t[:], lhsT=h2T.t[:, lb * 128:(lb + 1) * 128], rhs=w3.t[:, ct * 512:(ct + 1) * 512], start=True, stop=True),
                            reads=[h2T.b, w3.b], writes=[pb.b])
                        S.add("dve", lambda e, pb=pb, h_=h_, d_=d_, ct=ct: e.tensor_tensor(
                            out=h_.t[:, ct * 512:(ct + 1) * 512], in0=pb.t[:], in1=d_.t[:, ct * 512:(ct + 1) * 512], op=ALU.mult),
                            reads=[pb.b, d_.b], writes=[h_.b])
                    if lb == 0:
                        S.add("dve", lambda e, h_=h_: e.memset(h_.t[0:1, 1024:2048], 0.0), writes=[h_.b])
                    S.add("act", lambda e, h_=h_, q_=q_: e.activation(out=q_.t[:], in_=h_.t[:], func=AF.Square), reads=[h_.b], writes=[q_.b])
                    for ct in range(4):
                        S.add("pe", lambda e, q_=q_, ct=ct, lb=lb: e.matmul(
                            PS[4 + ct].t[:], lhsT=ones32.t[:], rhs=q_.t[:, ct * 512:(ct + 1) * 512], start=(lb == 0), stop=(lb == nb - 1)),
                            reads=[ones32.b, q_.b], writes=[PS[4 + ct].b])
                    S.add("pool", lambda e, h_=h_, sg_=sg_: e.tensor_tensor(out=sg_.t[:, 0, :], in0=h_.t[:, 0:1024], in1=h_.t[:, 1024:2048], op=ALU.add),
                          reads=[h_.b], writes=[sg_.b])
                    S.add("pool", lambda e, h_=h_, sg_=sg_: e.tensor_tensor(out=sg_.t[:, 1, :], in0=h_.t[:, 1024:2048], in1=h_.t[:, 0:1024], op=ALU.subtract),
                          reads=[h_.b], writes=[sg_.b])
                    S.add("sp", lambda e, sg_=sg_, lb=lb: e.dma_start(out=hsd_d[:, lb].rearrange("s p n -> p s n"), in_=sg_.t[:]), reads=[sg_.b], dma=True)
                for o in range(2):
                    S.add("dve", lambda e, o=o: e.tensor_copy(out=sst.t[:, o * 512:(o + 1) * 512], in_=PS[4 + o].t[:]), reads=[PS[4 + o].b], writes=[sst.b])
                    S.add("dve", lambda e, o=o: e.tensor_tensor(out=sst.t[:, o * 512:(o + 1) * 512], in0=sst.t[:, o * 512:(o + 1) * 512],
                                                                in1=PS[6 + o].t[:], op=ALU.add), reads=[sst.b, PS[6 + o].b], writes=[sst.b])
                S.add("act", lambda e: e.activation(out=rn.t[:], in_=sst.t[:], func=AF.Sqrt, bias=epsT.t[:, 0:1]), reads=[sst.b, epsT.b], writes=[rn.b])
                S.add("dve", lambda e: e.reciprocal(out=rn.t[:], in_=rn.t[:]), reads=[rn.b], writes=[rn.b])
                S.end_stage(resched=True)

        def hy_filter_dft(L, rn):
            G = HG[L]
            nb = L // 128
            with contextlib.ExitStack() as st:
                hs = sb(st, "hs", [128, nb, 1024], BF16)
                hd = sb(st, "hd", [128, nb, 1024], BF16)
                tC = [sb(st, "tC", [128, nb, 128], BF16) for _ in range(2)]
                tS = [sb(st, "tS", [128, nb, 128], BF16) for _ in range(2)]
                stg = [sb(st, "Hstg", [128, 512]) for _ in range(4)]
                step = max(1, nb // 4)
                for lb0 in range(0, nb, step):
                    S.add("sp", lambda e, lb0=lb0: e.dma_start(out=hs.t[:, lb0:lb0 + step, :], in_=hsd_d[0, lb0:lb0 + step].rearrange("l p n -> p l n")),
                          writes=[hs.b], dma=True)
                    S.add("sp", lambda e, lb0=lb0: e.dma_start(out=hd.t[:, lb0:lb0 + step, :], in_=hsd_d[1, lb0:lb0 + step].rearrange("l p n -> p l n")),
                          writes=[hd.b], dma=True)
                k = 0
                for kb in range(nb):
                    c_, s_ = tC[kb % 2], tS[kb % 2]
                    S.add("sp", lambda e, c_=c_, kb=kb: e.dma_start(out=c_.t[:], in_=G["F"][0, kb]), writes=[c_.b], dma=True)
                    S.add("pool", lambda e, s_=s_, kb=kb: e.dma_start(out=s_.t[:], in_=G["F"][1, kb]), writes=[s_.b], dma=True)
                    for o in range(2):
                        for ri, (tab, src) in enumerate(((c_, hs), (s_, hd))):
                            pb = PS[k % 8]
                            sg_ = stg[k % 4]
                            k += 1
                            for lb in range(nb):
                                S.add("pe", lambda e, pb=pb, tab=tab, src=src, lb=lb, o=o: e.matmul(
                                    pb.t[:], lhsT=tab.t[:, lb, :], rhs=src.t[:, lb, o * 512:(o + 1) * 512], start=(lb == 0), stop=(lb == nb - 1)),
                                    reads=[tab.b, src.b], writes=[pb.b])
                            S.add("dve", lambda e, pb=pb, sg_=sg_, o=o: e.tensor_tensor(out=sg_.t[:], in0=pb.t[:], in1=rn.t[:, o * 512:(o + 1) * 512], op=ALU.mult),
                                  reads=[pb.b, rn.b], writes=[sg_.b])
                            S.add("sp", lambda e, sg_=sg_, o=o, ri=ri, kb=kb: e.dma_start(out=H_d[o, ri, kb], in_=sg_.t[:]), reads=[sg_.b], dma=True)
                S.end_stage(resched=cfg.get("rs_x", True))

        def hy_order(l, i, L, offs, o, ident, hbs, Z, YR, YI):
            G = HG[L]
            nb = L // 128
            TT = min(512, L)
            ntt = L // TT
            nsub = TT // 128
            with contextlib.ExitStack() as st:
                zin = [sb(st, "zin", [128, 512]) for _ in range(3)]
                k = 0
                for si, t0 in enumerate(offs):
                    for tt in range(ntt):
                        for cc in range(4):
                            zi = zin[k % 3]
                            pb = PS[k % 4]
                            k += 1
                            src = uc_d[:, 8 + cc, t0 + tt * TT:t0 + (tt + 1) * TT] if o == 0 else z1T_d[:, cc, t0 + tt * TT:t0 + (tt + 1) * TT]
                            S.add("sp", lambda e, zi=zi, src=src: e.dma_start(out=zi.t[:, 0:TT], in_=src), writes=[zi.b], dma=True)
                            for j in range(nsub):
                                S.add("pe", lambda e, pb=pb, zi=zi, j=j: e.transpose(pb.t[:, j * 128:(j + 1) * 128], zi.t[:, j * 128:(j + 1) * 128], ident.t[:]),
                                      reads=[zi.b, ident.b], writes=[pb.b])
                            zt = Z[si]
                            if k % 2 == 0:
                                S.add("act", lambda e, pb=pb, zt=zt, tt=tt, cc=cc: e.activation(
                                    out=zt.t[:, tt * nsub:(tt + 1) * nsub, cc * 128:(cc + 1) * 128],
                                    in_=pb.t[:, 0:TT].rearrange("p (a b) -> p a b", b=128), func=AF.Identity), reads=[pb.b], writes=[zt.b])
                            else:
                                S.add("dve", lambda e, pb=pb, zt=zt, tt=tt, cc=cc: e.tensor_copy(
                                    out=zt.t[:, tt * nsub:(tt + 1) * nsub, cc * 128:(cc + 1) * 128],
                                    in_=pb.t[:, 0:TT].rearrange("p (a b) -> p a b", b=128)), reads=[pb.b], writes=[zt.b])
                S.end_stage(resched=True)
            with contextlib.ExitStack() as st:
                tC = [sb(st, "tC", [128, nb, 128], BF16) for _ in range(3)]
                tS = [sb(st, "tS", [128, nb, 128], BF16) for _ in range(3)]
                hr = [sb(st, "hr", [128, 512]) for _ in range(2)]
                hi = [sb(st, "hi", [128, 512]) for _ in range(2)]
                m_ = [[sb(st, "m", [128, 512]) for _ in range(2)] for _ in range(4)]
                k = 0
                for kb in range(nb):
                    c_, s_ = tC[kb % 3], tS[kb % 3]
                    hr_, hi_ = hr[kb % 2], hi[kb % 2]
                    S.add("sp", lambda e, c_=c_, kb=kb: e.dma_start(out=c_.t[:], in_=G["F"][0, kb]), writes=[c_.b], dma=True)
                    S.add("pool", lambda e, s_=s_, kb=kb: e.dma_start(out=s_.t[:], in_=G["F"][1, kb]), writes=[s_.b], dma=True)
                    S.add("sp", lambda e, hr_=hr_, kb=kb: e.dma_start(out=hr_.t[:], in_=H_d[o, 0, kb]), writes=[hr_.b], dma=True)
                    S.add("sp", lambda e, hi_=hi_, kb=kb: e.dma_start(out=hi_.t[:], in_=H_d[o, 1, kb]), writes=[hi_.b], dma=True)
                    for si in range(len(offs)):
                        zt, yr, yi = Z[si], YR[si], YI[si]
                        Pc, Ps = PS[(k % 4) * 2], PS[(k % 4) * 2 + 1]
                        mm = [m_[q][k % 2] for q in range(4)]
                        k += 1
                        for tb in range(nb):
                            S.add("pe", lambda e, Pc=Pc, c_=c_, zt=zt, tb=tb: e.matmul(Pc.t[:], lhsT=c_.t[:, tb, :], rhs=zt.t[:, tb, :],
                                                                                 start=(tb == 0), stop=(tb == nb - 1)), reads=[c_.b, zt.b], writes=[Pc.b])
                        for tb in range(nb):
                            S.add("pe", lambda e, Ps=Ps, s_=s_, zt=zt, tb=tb: e.matmul(Ps.t[:], lhsT=s_.t[:, tb, :], rhs=zt.t[:, tb, :],
                                                                                 start=(tb == 0), stop=(tb == nb - 1)), reads=[s_.b, zt.b], writes=[Ps.b])
                        for q, (hh_, pp_) in enumerate(((hr_, Pc), (hi_, Ps), (hr_, Ps), (hi_, Pc))):
                            S.add("dve", lambda e, q=q, hh_=hh_, pp_=pp_, mm=mm: e.tensor_tensor(out=mm[q].t[:], in0=pp_.t[:], in1=hh_.t[:], op=ALU.mult),
                                  reads=[hh_.b, pp_.b], writes=[mm[q].b])
                        S.add("pool", lambda e, mm=mm, yr=yr, kb=kb: e.tensor_tensor(out=yr.t[:, kb, :], in0=mm[0].t[:], in1=mm[1].t[:], op=ALU.add),
                              reads=[mm[0].b, mm[1].b], writes=[yr.b])
                        S.add("pool", lambda e, mm=mm, yi=yi, kb=kb: e.tensor_tensor(out=yi.t[:, kb, :], in0=mm[2].t[:], in1=mm[3].t[:], op=ALU.subtract),
                              reads=[mm[2].b, mm[3].b], writes=[yi.b])
                S.end_stage(resched=True)
            with contextlib.ExitStack() as st:
                KQ = min(8, nb)
                nkq = nb // KQ
                iC = [sb(st, "iC", [128, KQ, TT], BF16) for _ in range(3)]
                iS = [sb(st, "iS", [128, KQ, TT], BF16) for _ in range(3)]
                zt_ = [sb(st, "zt", [128, 512]) for _ in range(3)]
                gt_ = [sb(st, "gt", [128, 512]) for _ in range(3)]
                ab_ = [sb(st, "ab", [128, 512]) for _ in range(3)]
                of_ = [sb(st, "of", [128, 512]) for _ in range(3)]
                ob_ = [sb(st, "ob", [128, 512], BF16) for _ in range(3)]
                kt = 0
                ke = 0
                kk = 0
                for si, t0 in enumerate(offs):
                    yr, yi = YR[si], YI[si]
                    for tt in range(ntt):
                        banks = [PS[(kk % 2) * 4 + cc] for cc in range(4)]
                        kk += 1
                        for kq in range(nkq):
                            c_, s_ = iC[kt % 3], iS[kt % 3]
                            kt += 1
                            S.add("sp", lambda e, c_=c_, tt=tt, kq=kq: e.dma_start(out=c_.t[:], in_=G["I"][0, tt, :, kq * KQ:(kq + 1) * KQ, :]), writes=[c_.b], dma=True)
                            S.add("pool", lambda e, s_=s_, tt=tt, kq=kq: e.dma_start(out=s_.t[:], in_=G["I"][1, tt, :, kq * KQ:(kq + 1) * KQ, :]), writes=[s_.b], dma=True)
                            for cc in range(4):
                                for kbl in range(KQ):
                                    kb = kq * KQ + kbl
                                    S.add("pe", lambda e, bk=banks[cc], yr=yr, c_=c_, kb=kb, kbl=kbl, cc=cc: e.matmul(
                                        bk.t[:, 0:TT], lhsT=yr.t[:, kb, cc * 128:(cc + 1) * 128], rhs=c_.t[:, kbl, :], start=(kb == 0), stop=False),
                                        reads=[yr.b, c_.b], writes=[banks[cc].b])
                                    S.add("pe", lambda e, bk=banks[cc], yi=yi, s_=s_, kb=kb, kbl=kbl, cc=cc: e.matmul(
                                        bk.t[:, 0:TT], lhsT=yi.t[:, kb, cc * 128:(cc + 1) * 128], rhs=s_.t[:, kbl, :], start=False, stop=(kb == nb - 1)),
                                        reads=[yi.b, s_.b], writes=[banks[cc].b])
                        for cc in range(4):
                            z_, g_, a_, f_, b_ = zt_[ke % 3], gt_[ke % 3], ab_[ke % 3], of_[ke % 3], ob_[ke % 3]
                            ke += 1
                            tok = slice(t0 + tt * TT, t0 + (tt + 1) * TT)
                            zsrc = uc_d[:, 8 + cc, tok] if o == 0 else z1T_d[:, cc, tok]
                            S.add("sp", lambda e, z_=z_, zsrc=zsrc: e.dma_start(out=z_.t[:, 0:TT], in_=zsrc), writes=[z_.b], dma=True)
                            S.add("sp", lambda e, g_=g_, cc=cc, tok=tok: e.dma_start(out=g_.t[:, 0:TT], in_=uc_d[:, o * 4 + cc, tok]), writes=[g_.b], dma=True)
                            S.add("pool", lambda e, z_=z_, a_=a_, cc=cc: e.tensor_scalar(out=a_.t[:, 0:TT], in0=z_.t[:, 0:TT], scalar1=hbs.t[:, o, cc:cc + 1],
                                                                                    scalar2=None, op0=ALU.mult), reads=[z_.b, hbs.b], writes=[a_.b])
                            S.add("dve", lambda e, bk=banks[cc], a_=a_: e.scalar_tensor_tensor(out=a_.t[:, 0:TT], in0=bk.t[:, 0:TT], scalar=2.0 / (2 * L),
                                                                                            in1=a_.t[:, 0:TT], op0=ALU.mult, op1=ALU.add),
                                  reads=[banks[cc].b, a_.b], writes=[a_.b])
                            if o == 0:
                                S.add("pool", lambda e, a_=a_, g_=g_, f_=f_: e.tensor_tensor(out=f_.t[:, 0:TT], in0=a_.t[:, 0:TT], in1=g_.t[:, 0:TT], op=ALU.mult),
                                      reads=[a_.b, g_.b], writes=[f_.b])
                                S.add("sp", lambda e, f_=f_, cc=cc, tok=tok: e.dma_start(out=z1T_d[:, cc, tok], in_=f_.t[:, 0:TT]), reads=[f_.b], dma=True)
                            else:
                                S.add("pool", lambda e, a_=a_, g_=g_, b_=b_: e.tensor_tensor(out=b_.t[:, 0:TT], in0=a_.t[:, 0:TT], in1=g_.t[:, 0:TT], op=ALU.mult),
                                      reads=[a_.b, g_.b], writes=[b_.b])
                                S.add("sp", lambda e, b_=b_, cc=cc, tok=tok: e.dma_start(out=ay_d[:, 4 + cc, tok], in_=b_.t[:, 0:TT]), reads=[b_.b], dma=True)
                S.end_stage(resched=True)

        def ah_hyena(l, i):
            hy_conv(l, i)
            with contextlib.ExitStack() as st:
                ident = sb(st, "ident", [128, 128])
                hbs = sb(st, "hbs", [128, 2, 4])
                rn = sb(st, "rn", [128, 1024])
                S.add("sp", lambda e: e.dma_start(out=ident.t[:], in_=ident_d), writes=[ident.b], dma=True)
                S.add("sp", lambda e: e.dma_start(out=hbs.t[:], in_=hbias[:, i]), writes=[hbs.b], dma=True)
                S.end_stage()
                for (L, offs) in ((4096, [0]), (256, [4096 + 256 * pi for pi in range(4)])):
                    nb = L // 128
                    with contextlib.ExitStack() as st2:
                        hy_filter_gen(i, L, rn)
                        hy_filter_dft(L, rn)
                        Z = [sb(st2, "Z", [128, nb, 512], BF16) for _ in offs]
                        YR = [sb(st2, "YR", [128, nb, 512], BF16) for _ in offs]
                        YI = [sb(st2, "YI", [128, nb, 512], BF16) for _ in offs]
                        for o in range(2):
                            hy_order(l, i, L, offs, o, ident, hbs, Z, YR, YI)

        class Rot:
            def __init__(self, st, name, shape, dt, n):
                self.ts = [sb(st, name, shape, dt) for _ in range(n)]
                self.k = 0

            def next(self):
                t = self.ts[self.k % len(self.ts)]
                self.k += 1
                return t

        def dn_inproj(l, i):
            with contextlib.ExitStack() as st:
                X = [sb(st, "x", [128, 8, TM]) for _ in range(2)]
                Hh = [sb(st, "h", [128, 8, TM], BF16) for _ in range(2)]
                sq = [sb(st, "sq", [128, TM]) for _ in range(2)]
                tmp = [sb(st, "tmp", [128, TM]) for _ in range(2)]
                rstd = sb(st, "rstd", [128, TM])
                W = sb(st, "dwin", [128, 8, 4128], BF16)
                stg = [sb(st, "stg", [128, 512]) for _ in range(4)]
                vst = [sb(st, "vst", [128, 8, 32]) for _ in range(2)]
                for kc in range(8):
                    S.add("pool", lambda e, kc=kc: e.dma_start(out=W.t[:, kc, :], in_=dwin[i, :, kc, :], max_dma_last_dim=4096),
                          writes=[W.b], dma=True)

                def load_x(mt):
                    xb = X[mt % 2]
                    S.add("sp", lambda e: e.dma_start(out=xb.t[:], in_=xs[:, :, mt * TM:(mt + 1) * TM]), writes=[xb.b], dma=True)

                def stage_a(mt):
                    which = 0 if mt < 4 else 1
                    norm_mod((sq, rstd, tmp), X[mt % 2], Hh[mt % 2], 1, which, (PS[4], PS[5]))

                load_x(0)
                stage_a(0)
                ctr = [0]
                for mt in range(NMT):
                    hb = Hh[mt % 2]
                    if mt + 1 < NMT:
                        load_x(mt + 1)
                    for oc in range(32):
                        dst, ch = (qkvT_d, oc) if oc < 24 else (zT_d, oc - 24)
                        for s_ in range(TM // 512):
                            k = ctr[0]
                            ctr[0] += 1
                            pb = PS[k % 4]
                            sg_ = stg[k % 4]
                            for c in range(8):
                                S.add("pe", lambda e, pb=pb, c=c, s_=s_, oc=oc, hb=hb: e.matmul(
                                    pb.t[:], lhsT=W.t[:, c, oc * 128:(oc + 1) * 128],
                                    rhs=hb.t[:, c, s_ * 512:(s_ + 1) * 512], start=(c == 0), stop=(c == 7)),
                                    reads=[W.b, hb.b], writes=[pb.b])
                            if k % 2 == 0:
                                S.add("act", lambda e, pb=pb, sg_=sg_: e.activation(out=sg_.t[:], in_=pb.t[:], func=AF.Identity),
                                      reads=[pb.b], writes=[sg_.b])
                            else:
                                S.add("dve", lambda e, pb=pb, sg_=sg_: e.tensor_copy(out=sg_.t[:], in_=pb.t[:]),
                                      reads=[pb.b], writes=[sg_.b])
                            t0 = mt * TM + s_ * 512
                            S.add("sp", lambda e, dst=dst, ch=ch, t0=t0, sg_=sg_: e.dma_start(
                                out=dst[:, ch, t0:t0 + 512], in_=sg_.t[:]), reads=[sg_.b], dma=True)
                        if oc == 15 and mt + 1 < NMT:
                            stage_a(mt + 1)
                    vs_ = vst[mt % 2]
                    for tb in range(TM // 128):
                        pb = PS[6 + tb % 2]
                        for c in range(8):
                            S.add("pe", lambda e, pb=pb, c=c, tb=tb, hb=hb: e.matmul(
                                pb.t[:, 0:32], lhsT=hb.t[:, c, tb * 128:(tb + 1) * 128], rhs=W.t[:, c, 4096:4128],
                                start=(c == 0), stop=(c == 7)), reads=[W.b, hb.b], writes=[pb.b])
                        S.add("dve", lambda e, pb=pb, tb=tb, vs_=vs_: e.tensor_copy(out=vs_.t[:, tb, :], in_=pb.t[:, 0:32]),
                              reads=[pb.b], writes=[vs_.b])
                    S.add("sp", lambda e, mt=mt, vs_=vs_: e.dma_start(
                        out=ba_d[mt * TM:(mt + 1) * TM, :].rearrange("(tb p) n -> p tb n", p=128), in_=vs_.t[:]),
                        reads=[vs_.b], dma=True)
                S.end_stage(resched=cfg.get("rs_x", True))

        def dn_conv(l, i):
            with contextlib.ExitStack() as st:
                cw = sb(st, "dcw", [128, 24, 3])
                U = [sb(st, "U", [128, 12, 514]) for _ in range(2)]
                O = [sb(st, "O", [128, 12, 512]) for _ in range(2)]
                SQ = Rot(st, "dsq", [128, 512], F32, 8)
                RS = Rot(st, "drs", [128, 512], F32, 8)
                S.add("sp", lambda e: e.dma_start(out=cw.t[:], in_=dcw[:, i]), writes=[cw.b], dma=True)
                tiles = [(tt * 512, 512, tt == 0, tt == 7) for tt in range(8)] + [(4096 + 256 * pi, 256, True, True) for pi in range(4)]
                ti = 0
                for (t0, n, first, lastt) in tiles:
                    for half in range(2):
                        u, o = U[ti % 2], O[ti % 2]
                        ti += 1
                        a0 = t0 if first else t0 - 1
                        a1 = t0 + n if lastt else t0 + n + 1
                        c0 = 1 if first else 0
                        S.add("sp", lambda e, u=u, a0=a0, a1=a1, c0=c0, half=half: e.dma_start(
                            out=u.t[:, :, c0:c0 + (a1 - a0)], in_=qkvT_d[:, half * 12:(half + 1) * 12, a0:a1]), writes=[u.b], dma=True)
                        if first:
                            S.add("pool", lambda e, u=u: e.memset(u.t[:, :, 0:1], 0.0), writes=[u.b])
                        if lastt:
                            S.add("pool", lambda e, u=u, n=n: e.memset(u.t[:, :, n + 1:n + 2], 0.0), writes=[u.b])
                        for c12 in range(12):
                            ch = half * 12 + c12
                            S.add("dve", lambda e, u=u, o=o, ch=ch, c12=c12, n=n: e.tensor_scalar(
                                out=o.t[:, c12, 0:n], in0=u.t[:, c12, 0:n], scalar1=cw.t[:, ch, 0:1], scalar2=None, op0=ALU.mult),
                                reads=[u.b, cw.b], writes=[o.b])
                            for tap in (1, 2):
                                S.add("dve", lambda e, u=u, o=o, ch=ch, c12=c12, n=n, tap=tap: e.scalar_tensor_tensor(
                                    out=o.t[:, c12, 0:n], in0=u.t[:, c12, tap:n + tap], scalar=cw.t[:, ch, tap:tap + 1], in1=o.t[:, c12, 0:n],
                                    op0=ALU.mult, op1=ALU.add), reads=[u.b, cw.b, o.b], writes=[o.b])
                        S.add("act", lambda e, o=o, n=n: e.activation(out=o.t[:, :, 0:n], in_=o.t[:, :, 0:n], func=AF.Silu), reads=[o.b], writes=[o.b])
                        for c12 in range(12):
                            ch = half * 12 + c12
                            if ch >= 16:
                                continue
                            q_, r_ = SQ.next(), RS.next()
                            pb = PS[ch % 8]
                            S.add("act", lambda e, o=o, q_=q_, c12=c12, n=n: e.activation(out=q_.t[:, 0:n], in_=o.t[:, c12, 0:n], func=AF.Square),
                                  reads=[o.b], writes=[q_.b])
                            S.add("pe", lambda e, pb=pb, q_=q_, n=n: e.matmul(pb.t[:, 0:n], lhsT=ones32.t[:], rhs=q_.t[:, 0:n], start=True, stop=True),
                                  reads=[ones32.b, q_.b], writes=[pb.b])
                            S.add("act", lambda e, pb=pb, r_=r_, n=n: e.activation(out=r_.t[:, 0:n], in_=pb.t[:, 0:n], func=AF.Sqrt, bias=epsT.t[:, 0:1]),
                                  reads=[pb.b, epsT.b], writes=[r_.b])
                            S.add("dve", lambda e, r_=r_, n=n: e.reciprocal(out=r_.t[:, 0:n], in_=r_.t[:, 0:n]), reads=[r_.b], writes=[r_.b])
                            sc = (128.0 ** -0.5) if ch < 8 else 1.0
                            S.add("dve", lambda e, o=o, r_=r_, c12=c12, n=n, sc=sc: e.scalar_tensor_tensor(
                                out=o.t[:, c12, 0:n], in0=o.t[:, c12, 0:n], scalar=sc, in1=r_.t[:, 0:n], op0=ALU.mult, op1=ALU.mult),
                                reads=[o.b, r_.b], writes=[o.b])
                        S.add("sp", lambda e, o=o, t0=t0, n=n, half=half: e.dma_start(out=qkvn_d[:, half * 12:(half + 1) * 12, t0:t0 + n], in_=o.t[:, :, 0:n]),
                              reads=[o.b], dma=True)
                S.end_stage(resched=True)

        def dn_chunks(l, i):
            with contextlib.ExitStack() as st:
                msk = sb(st, "msk", [64, 5, 64])
                ident = sb(st, "ident", [128, 128])
                ones64 = sb(st, "ones64", [64, 128])
                prm = sb(st, "prm", [64, 32])
                ba = sb(st, "ba", [64, NCH, 32])
                gall = sb(st, "gall", [64, NCH, 16])
                ball = sb(st, "ball", [64, NCH, 16])
                KT_ = sb(st, "KTt", [128, 8, 256])
                QT_ = sb(st, "QTt", [128, 8, 256])
                VT_ = sb(st, "VTt", [128, 8, 256])
                Ktm = sb(st, "Ktm", [64, 8, 128])
                Vtm = sb(st, "Vtm", [64, 8, 128])
                r512L = [Rot(st, "r512", [128, 512], F32, 10) for _ in range(2)]
                ratr = Rot(st, "ratr", [64, 512], F32, 2)
                b512L = [Rot(st, "b512", [64, 512], BF16, 16) for _ in range(2)]
                batnL = [Rot(st, "batn", [64, 512], BF16, 4) for _ in range(2)]
                lvm_t = sb(st, "lvm", [64, 12, 64])
                S.add("sp", lambda e: e.dma_start(out=lvm_t.t[:], in_=lvmask_d), writes=[lvm_t.b], dma=True)
                b1kL = [Rot(st, "b1k", [64, 8, 128], BF16, 3) for _ in range(2)]
                bqL = [Rot(st, "bq", [128, 512], BF16, 2) for _ in range(2)]
                u1kL = [Rot(st, "u1k", [64, 8, 128], F32, 1) for _ in range(2)]
                smL = [Rot(st, "sm", [128, 8], F32, 10) for _ in range(2)]
                S.add("sp", lambda e: e.dma_start(out=msk.t[:], in_=dmask_d), writes=[msk.b], dma=True)
                S.add("sp", lambda e: e.dma_start(out=ident.t[:], in_=ident_d), writes=[ident.b], dma=True)
                S.add("sp", lambda e: e.dma_start(out=prm.t[:], in_=dprm[0:64, i, :]), writes=[prm.b], dma=True)
                S.add("sp", lambda e: e.dma_start(out=ba.t[:], in_=ba_d.rearrange("(c p) n -> p c n", p=64)), writes=[ba.b], dma=True)
                S.add("dve", lambda e: e.memset(ones64.t[:], 1.0), writes=[ones64.b])
                S.add("act", lambda e: e.activation(out=ball.t[:], in_=ba.t[:, :, 0:16], func=AF.Sigmoid), reads=[ba.b], writes=[ball.b])
                S.add("dve", lambda e: e.tensor_tensor(out=gall.t[:], in0=ba.t[:, :, 16:32],
                                                       in1=prm.t[:, 16:32].unsqueeze(1).broadcast_to([64, NCH, 16]), op=ALU.add),
                      reads=[ba.b, prm.b], writes=[gall.b])
                S.add("act", lambda e: e.activation(out=gall.t[:], in_=gall.t[:], func=AF.Exp), reads=[gall.b], writes=[gall.b])
                S.add("act", lambda e: e.activation(out=gall.t[:], in_=gall.t[:], func=AF.Ln, bias=1.0), reads=[gall.b], writes=[gall.b])
                S.add("act", lambda e: e.activation(out=prm.t[:, 0:16], in_=prm.t[:, 0:16], func=AF.Exp), reads=[prm.b], writes=[prm.b])
                S.add("dve", lambda e: e.scalar_tensor_tensor(out=gall.t[:], in0=gall.t[:], scalar=-1.0,
                                                              in1=prm.t[:, 0:16].unsqueeze(1).broadcast_to([64, NCH, 16]), op0=ALU.mult, op1=ALU.mult),
                      reads=[gall.b, prm.b], writes=[gall.b])
                Uf, Ub, Sf, Sb, I64 = (msk.t[:, k_, :] for k_ in range(5))

                def bc_h(m):
                    return m.unsqueeze(1).broadcast_to([64, 8, 64])

                def v3(t_, p=64):
                    return t_[0:p, :].rearrange("p (h i) -> p h i", i=64)

                def chunk_body(c):
                    cl = c % 4
                    if cl == 0:
                        t0 = c * 64
                        for (tile_, ch0) in ((QT_, 0), (KT_, 8), (VT_, 16)):
                            S.add("sp", lambda e, tile_=tile_, ch0=ch0, t0=t0: e.dma_start(out=tile_.t[:], in_=qkvn_d[:, ch0:ch0 + 8, t0:t0 + 256]),
                                  writes=[tile_.b], dma=True)
                    csl = slice(cl * 64, (cl + 1) * 64)
                    Kc, Qc, Vc = KT_.t[:, :, csl], QT_.t[:, :, csl], VT_.t[:, :, csl]
                    for (src, srcb, dst, bk0) in ((Kc, KT_.b, Ktm, 0), (Vc, VT_.b, Vtm, 4)):
                        for h in range(8):
                            pb = PS[bk0 + h // 4]
                            S.add("pe", lambda e, pb=pb, src=src, h=h: e.transpose(pb.t[0:64, (h % 4) * 128:(h % 4 + 1) * 128], src[:, h, :], ident.t[:]),
                                  reads=[srcb, ident.b], writes=[pb.b])
                        for hb_ in range(2):
                            S.add("act", lambda e, dst=dst, hb_=hb_, bk0=bk0: e.activation(
                                out=dst.t[:, hb_ * 4:(hb_ + 1) * 4, :], in_=PS[bk0 + hb_].t[0:64, :].rearrange("p (h d) -> p h d", d=128), func=AF.Identity),
                                reads=[PS[bk0 + hb_].b], writes=[dst.b])
                    for h in range(8):
                        S.add("pe", lambda e, h=h, Kc=Kc, Qc=Qc: e.matmul(PS[2].t[0:64, h * 64:(h + 1) * 64], lhsT=Kc[:, h, :], rhs=Qc[:, h, :], start=True, stop=True),
                              reads=[KT_.b, QT_.b], writes=[PS[2].b])
                    atraw = ratr.next()
                    S.add("dve", lambda e, atraw=atraw: e.tensor_copy(out=atraw.t[0:64, :], in_=PS[2].t[0:64, :]), reads=[PS[2].b], writes=[atraw.b])
                    def unit(d):
                        r512, b512, batn, b1k, bq, u1k, sm = r512L[d], b512L[d], batnL[d], b1kL[d], bqL[d], u1kL[d], smL[d]
                        B0, B1, B2, B3 = (PS[4 * d + k_] for k_ in range(4))
                        Ud, Sd, SdT = (Uf, Sf, Sb) if d == 0 else (Ub, Sb, Sf)
                        g_ = gall.t[:, c, d * 8:(d + 1) * 8]
                        b_ = ball.t[:, c, d * 8:(d + 1) * 8]
                        ug, ib = r512.next(), r512.next()
                        S.add("pool", lambda e, ug=ug, Ud=Ud, g_=g_: e.tensor_tensor(out=v3(ug.t), in0=bc_h(Ud), in1=g_.unsqueeze(2).broadcast_to([64, 8, 64]), op=ALU.mult),
                              reads=[msk.b, gall.b], writes=[ug.b])
                        S.add("pool", lambda e, ib=ib, b_=b_: e.tensor_tensor(out=v3(ib.t), in0=bc_h(I64), in1=b_.unsqueeze(2).broadcast_to([64, 8, 64]), op=ALU.mult),
                              reads=[msk.b, ball.b], writes=[ib.b])
                        S.add("pe", lambda e, Ud=Ud, g_=g_: e.matmul(B0.t[0:64, 0:8], lhsT=Ud, rhs=g_, start=True, stop=True),
                              reads=[msk.b, gall.b], writes=[B0.b])
                        S.add("pe", lambda e, g_=g_: e.matmul(B0.t[:, 8:16], lhsT=ones64.t[:], rhs=g_, start=True, stop=True),
                              reads=[ones64.b, gall.b], writes=[B0.b])
                        S.add("pe", lambda e, ug=ug: e.matmul(B1.t[:], lhsT=ones64.t[:], rhs=ug.t[0:64, :], start=True, stop=True),
                              reads=[ones64.b, ug.b], writes=[B1.b])
                        S.add("pe", lambda e, ib=ib: e.matmul(B2.t[:], lhsT=ones64.t[:], rhs=ib.t[0:64, :], start=True, stop=True),
                              reads=[ones64.b, ib.b], writes=[B2.b])
                        gcc, egl, egc, bg, kds = sm.next(), sm.next(), sm.next(), sm.next(), sm.next()
                        S.add("dve", lambda e, gcc=gcc: e.tensor_copy(out=gcc.t[0:64, :], in_=B0.t[0:64, 0:8]), reads=[B0.b], writes=[gcc.b])
                        S.add("act", lambda e, egl=egl: e.activation(out=egl.t[:], in_=B0.t[:, 8:16], func=AF.Exp), reads=[B0.b], writes=[egl.b])
                        S.add("sp", lambda e, egl=egl, d=d, c=c: e.dma_start(out=egl_d[d, c], in_=egl.t[:]), reads=[egl.b], dma=True)
                        S.add("act", lambda e, egc=egc, gcc=gcc: e.activation(out=egc.t[0:64, :], in_=gcc.t[0:64, :], func=AF.Exp), reads=[gcc.b], writes=[egc.b])
                        S.add("dve", lambda e, bg=bg, egc=egc, b_=b_: e.tensor_tensor(out=bg.t[0:64, :], in0=egc.t[0:64, :], in1=b_, op=ALU.mult),
                              reads=[egc.b, ball.b], writes=[bg.b])
                        S.add("dve", lambda e, kds=kds, gcc=gcc: e.tensor_tensor(out=kds.t[0:64, :], in0=B0.t[0:64, 8:16], in1=gcc.t[0:64, :], op=ALU.subtract),
                              reads=[B0.b, gcc.b], writes=[kds.b])
                        S.add("act", lambda e, kds=kds: e.activation(out=kds.t[0:64, :], in_=kds.t[0:64, :], func=AF.Exp), reads=[kds.b], writes=[kds.b])
                        Dt, E1, E2 = r512.next(), r512.next(), r512.next()
                        S.add("dve", lambda e, Dt=Dt, gcc=gcc: e.tensor_tensor(out=v3(Dt.t), in0=v3(B1.t), in1=gcc.t[0:64, :].unsqueeze(2).broadcast_to([64, 8, 64]),
                                                                        op=ALU.subtract), reads=[B1.b, gcc.b], writes=[Dt.b])
                        S.add("dve", lambda e, Dt=Dt, E1=E1: e.tensor_scalar(out=E1.t[0:64, :], in0=Dt.t[0:64, :], scalar1=0.0, scalar2=None, op0=ALU.min),
                              reads=[Dt.b], writes=[E1.b])
                        S.add("dve", lambda e, Dt=Dt, E2=E2: e.tensor_scalar(out=E2.t[0:64, :], in0=Dt.t[0:64, :], scalar1=-1.0, scalar2=0.0, op0=ALU.mult, op1=ALU.min),
                              reads=[Dt.b], writes=[E2.b])
                        S.add("act", lambda e, E1=E1: e.activation(out=E1.t[0:64, :], in_=E1.t[0:64, :], func=AF.Exp), reads=[E1.b], writes=[E1.b])
                        S.add("act", lambda e, E2=E2: e.activation(out=E2.t[0:64, :], in_=E2.t[0:64, :], func=AF.Exp), reads=[E2.b], writes=[E2.b])
                        decI, decS, decN = r512.next(), r512.next(), r512.next()
                        S.add("pool", lambda e, decI=decI, E1=E1, Ud=Ud: e.tensor_tensor(out=v3(decI.t), in0=v3(E1.t), in1=bc_h(Ud), op=ALU.mult),
                              reads=[E1.b, msk.b], writes=[decI.b])
                        S.add("pool", lambda e, decS=decS, E1=E1, Sd=Sd: e.tensor_tensor(out=v3(decS.t), in0=v3(E1.t), in1=bc_h(Sd), op=ALU.mult),
                              reads=[E1.b, msk.b], writes=[decS.b])
                        S.add("pool", lambda e, decN=decN, E2=E2, SdT=SdT: e.tensor_tensor(out=v3(decN.t), in0=v3(E2.t), in1=bc_h(SdT), op=ALU.mult),
                              reads=[E2.b, msk.b], writes=[decN.b])
                        eg = r512.next()
                        qg = bq.next()
                        S.add("act", lambda e, eg=eg: e.activation(out=eg.t[:], in_=B1.t[:], func=AF.Exp), reads=[B1.b], writes=[eg.b])
                        S.add("pool", lambda e, eg=eg, qg=qg, Qc=Qc: e.tensor_tensor(out=v3(qg.t, 128), in0=Qc, in1=v3(eg.t, 128), op=ALU.mult),
                              reads=[eg.b, QT_.b], writes=[qg.b])
                        S.add("sp", lambda e, qg=qg, d=d, c=c: e.dma_start(out=QgT_d[d, c], in_=v3(qg.t, 128)), reads=[qg.b], dma=True)
                        kbT = r512.next()
                        S.add("dve", lambda e, kbT=kbT, Kc=Kc: e.tensor_tensor(out=v3(kbT.t, 128), in0=Kc, in1=v3(B2.t, 128), op=ALU.mult),
                              reads=[B2.b, KT_.b], writes=[kbT.b])
                        for h in range(8):
                            S.add("pe", lambda e, h=h, Kc=Kc, kbT=kbT: e.matmul(B3.t[0:64, h * 64:(h + 1) * 64], lhsT=Kc[:, h, :], rhs=kbT.t[:, h * 64:(h + 1) * 64],
                                                                              start=True, stop=True), reads=[KT_.b, kbT.b], writes=[B3.b])
                        for h in range(8):
                            S.add("pe", lambda e, h=h, Kc=Kc, kbT=kbT: e.matmul(B0.t[0:64, h * 64:(h + 1) * 64], lhsT=kbT.t[:, h * 64:(h + 1) * 64], rhs=Kc[:, h, :],
                                                                              start=True, stop=True), reads=[KT_.b, kbT.b], writes=[B0.b])
                        AT, AN = batn.next(), batn.next()
                        S.add("dve", lambda e, AT=AT, decS=decS: e.scalar_tensor_tensor(out=AT.t[:], in0=B3.t[0:64, :], scalar=-1.0, in1=decS.t[0:64, :],
                                                                                    op0=ALU.mult, op1=ALU.mult), reads=[B3.b, decS.b], writes=[AT.b])
                        S.add("dve", lambda e, AN=AN, decN=decN: e.scalar_tensor_tensor(out=AN.t[:], in0=B0.t[0:64, :], scalar=-1.0, in1=decN.t[0:64, :],
                                                                                    op0=ALU.mult, op1=ALU.mult), reads=[B0.b, decN.b], writes=[AN.b])
                        def lvm(lv, tr):
                            k_ = 2 * lv + (tr if d == 0 else 1 - tr)
                            return bc_h(lvm_t.t[:, k_, :])
                        TN, TT = b512.next(), b512.next()
                        for (dst_, src_, tr) in ((TN, AN, 0), (TT, AT, 1)):
                            mk0 = lvm(0, tr)
                            S.add("pool", lambda e, dst_=dst_, src_=src_, mk0=mk0: e.tensor_tensor(out=v3(dst_.t), in0=v3(src_.t), in1=mk0, op=ALU.mult),
                                  reads=[src_.b, lvm_t.b], writes=[dst_.b])
                            S.add("pool", lambda e, dst_=dst_: e.tensor_tensor(out=v3(dst_.t), in0=v3(dst_.t), in1=bc_h(I64), op=ALU.add),
                                  reads=[dst_.b, msk.b], writes=[dst_.b])
                        for lv in range(1, 6):
                            LoN, LoT, M1, M2, TT2 = b512.next(), b512.next(), b512.next(), b512.next(), b512.next()
                            mkn, mkt = lvm(lv, 0), lvm(lv, 1)
                            S.add("pool", lambda e, LoN=LoN, AN=AN, mkn=mkn: e.tensor_tensor(out=v3(LoN.t), in0=v3(AN.t), in1=mkn, op=ALU.mult),
                                  reads=[AN.b, lvm_t.b], writes=[LoN.b])
                            S.add("pool", lambda e, LoT=LoT, AT=AT, mkt=mkt: e.tensor_tensor(out=v3(LoT.t), in0=v3(AT.t), in1=mkt, op=ALU.mult),
                                  reads=[AT.b, lvm_t.b], writes=[LoT.b])
                            for h in range(8):
                                hs_ = slice(h * 64, (h + 1) * 64)
                                S.add("pe", lambda e, hs_=hs_, LoN=LoN, TT=TT: e.matmul(B1.t[0:64, hs_], lhsT=LoN.t[:, hs_], rhs=TT.t[:, hs_], start=True, stop=True),
                                      reads=[LoN.b, TT.b], writes=[B1.b])
                            S.add("act", lambda e, M1=M1: e.activation(out=M1.t[:], in_=B1.t[0:64, :], func=AF.Identity), reads=[B1.b], writes=[M1.b])
                            if lv < 5:
                                for h in range(8):
                                    hs_ = slice(h * 64, (h + 1) * 64)
                                    S.add("pe", lambda e, hs_=hs_, LoT=LoT, TN=TN: e.matmul(B2.t[0:64, hs_], lhsT=LoT.t[:, hs_], rhs=TN.t[:, hs_], start=True, stop=True),
                                          reads=[LoT.b, TN.b], writes=[B2.b])
                                S.add("act", lambda e, M2=M2: e.activation(out=M2.t[:], in_=B2.t[0:64, :], func=AF.Identity), reads=[B2.b], writes=[M2.b])
                            for h in range(8):
                                hs_ = slice(h * 64, (h + 1) * 64)
                                S.add("pe", lambda e, hs_=hs_, TN=TN, M1=M1: e.matmul(B3.t[0:64, hs_], lhsT=TN.t[:, hs_], rhs=M1.t[:, hs_], start=True, stop=True),
                                      reads=[TN.b, M1.b], writes=[B3.b])
                            S.add("dve", lambda e, TT2=TT2, TT=TT: e.tensor_tensor(out=TT2.t[:], in0=B3.t[0:64, :], in1=TT.t[:], op=ALU.add),
                                  reads=[B3.b, TT.b], writes=[TT2.b])
                            if lv < 5:
                                TN2 = b512.next()
                                for h in range(8):
                                    hs_ = slice(h * 64, (h + 1) * 64)
                                    S.add("pe", lambda e, hs_=hs_, TT=TT, M2=M2: e.matmul(B0.t[0:64, hs_], lhsT=TT.t[:, hs_], rhs=M2.t[:, hs_], start=True, stop=True),
                                          reads=[TT.b, M2.b], writes=[B0.b])
                                S.add("dve", lambda e, TN2=TN2, TN=TN: e.tensor_tensor(out=TN2.t[:], in0=B0.t[0:64, :], in1=TN.t[:], op=ALU.add),
                                      reads=[B0.b, TN.b], writes=[TN2.b])
                                TN = TN2
                            TT = TT2
                        P = TT
                        ato = b512.next()
                        S.add("pool", lambda e, ato=ato, atraw=atraw, decI=decI: e.tensor_tensor(out=ato.t[:], in0=atraw.t[0:64, :], in1=decI.t[0:64, :], op=ALU.mult),
                              reads=[atraw.b, decI.b], writes=[ato.b])
                        S.add("sp", lambda e, ato=ato, d=d, c=c: e.dma_start(out=AT_d[d, c], in_=v3(ato.t)), reads=[ato.b], dma=True)
                        Vb, KBg, Kdd = b1k.next(), b1k.next(), b1k.next()
                        S.add("pool", lambda e, Vb=Vb, b_=b_: e.tensor_tensor(out=Vb.t[:], in0=Vtm.t[:], in1=b_.unsqueeze(2).broadcast_to([64, 8, 128]), op=ALU.mult),
                              reads=[Vtm.b, ball.b], writes=[Vb.b])
                        S.add("pool", lambda e, KBg=KBg, bg=bg: e.tensor_tensor(out=KBg.t[:], in0=Ktm.t[:], in1=bg.t[0:64, :].unsqueeze(2).broadcast_to([64, 8, 128]), op=ALU.mult),
                              reads=[Ktm.b, bg.b], writes=[KBg.b])
                        S.add("pool", lambda e, Kdd=Kdd, kds=kds: e.tensor_tensor(out=Kdd.t[:], in0=Ktm.t[:], in1=kds.t[0:64, :].unsqueeze(2).broadcast_to([64, 8, 128]), op=ALU.mult),
                              reads=[Ktm.b, kds.b], writes=[Kdd.b])
                        S.add("sp", lambda e, Kdd=Kdd, d=d, c=c: e.dma_start(out=Kd_d[d, c], in_=Kdd.t[:]), reads=[Kdd.b], dma=True)
                        for h in range(8):
                            pb = (B1, B2)[h // 4]
                            S.add("pe", lambda e, pb=pb, h=h, P=P, Vb=Vb: e.matmul(pb.t[0:64, (h % 4) * 128:(h % 4 + 1) * 128], lhsT=P.t[:, h * 64:(h + 1) * 64], rhs=Vb.t[:, h, :],
                                                                              start=True, stop=True), reads=[P.b, Vb.b], writes=[pb.b])
                        uo = u1k.next()
                        for hb_ in range(2):
                            S.add("act", lambda e, uo=uo, hb_=hb_: e.activation(out=uo.t[:, hb_ * 4:(hb_ + 1) * 4, :],
                                                                           in_=(B1, B2)[hb_].t[0:64, :].rearrange("p (h d) -> p h d", d=128), func=AF.Identity),
                                  reads=[(B1, B2)[hb_].b], writes=[uo.b])
                        S.add("sp", lambda e, uo=uo, d=d, c=c: e.dma_start(out=u_d[d, c], in_=uo.t[:]), reads=[uo.b], dma=True)
                        for h in range(8):
                            S.add("pe", lambda e, h=h, P=P, KBg=KBg: e.matmul(B3.t[:, h * 64:(h + 1) * 64], lhsT=KBg.t[:, h, :], rhs=P.t[:, h * 64:(h + 1) * 64],
                                                                            start=True, stop=True), reads=[P.b, KBg.b], writes=[B3.b])
                        wo = bq.next()
                        S.add("dve", lambda e, wo=wo: e.tensor_copy(out=wo.t[:], in_=B3.t[:]), reads=[B3.b], writes=[wo.b])
                        S.add("sp", lambda e, wo=wo, d=d, c=c: e.dma_start(out=wT_d[d, c], in_=v3(wo.t, 128)), reads=[wo.b], dma=True)
                    for d_ in range(2):
                        unit(d_)

                for c_ in range(NCH):
                    chunk_body(c_)
                S.end_stage(resched=True)

        def dn_scan(l, i):
            with contextlib.ExitStack() as st:
                uL = [Rot(st, "uL", [64, 8, 128], F32, 2) for _ in range(2)]
                wL = [Rot(st, "wL", [128, 8, 64], BF16, 2) for _ in range(2)]
                qL = [Rot(st, "qL", [128, 8, 64], BF16, 2) for _ in range(2)]
                aL = [Rot(st, "aL", [128, 8, 64], BF16, 2) for _ in range(2)]
                kL = [Rot(st, "kL", [64, 8, 128], BF16, 2) for _ in range(2)]
                eL = [Rot(st, "eL", [128, 8], F32, 2) for _ in range(2)]
                vn = [Rot(st, "vn", [128, 8, 128], BF16, 2) for _ in range(2)]
                for d in range(2):
                    for t_ in aL[d].ts + vn[d].ts:
                        S.add("pool", lambda e, t_=t_: e.memset(t_.t[:], 0.0), writes=[t_.b])
                oS = [Rot(st, "oS", [128, 8, 64], F32, 2) for _ in range(2)]
                seqs = [(0, 64, None)] + [(64 + 4 * pi, 4, pi) for pi in range(4)]
                for (c0, nch, pi) in seqs:
                    Sf = [sb(st, "S", [128, 8, 128]) for _ in range(2)]
                    Sbf = [sb(st, "Sbf", [128, 8, 128], BF16) for _ in range(2)]
                    for d in range(2):
                        if pi is None:
                            S.add("sp", lambda e, d=d, Sf=Sf: e.dma_start(out=Sf[d].t[:], in_=sf0[d, i]), writes=[Sf[d].b], dma=True)
                        else:
                            S.add("pool", lambda e, d=d, Sf=Sf: e.memset(Sf[d].t[:], 0.0), writes=[Sf[d].b])
                        S.add("act", lambda e, d=d, Sf=Sf, Sbf=Sbf: e.activation(out=Sbf[d].t[:], in_=Sf[d].t[:], func=AF.Identity), reads=[Sf[d].b], writes=[Sbf[d].b])
                    for step in range(nch):
                        for d in range(2):
                            c = c0 + step if d == 0 else c0 + nch - 1 - step
                            sF, sB = Sf[d], Sbf[d]
                            u_, w_, q_, a_, k_, e_ = uL[d].next(), wL[d].next(), qL[d].next(), aL[d].next(), kL[d].next(), eL[d].next()
                            q1 = "sp" if d == 0 else "act"
                            for (tile_, src) in ((u_, u_d[d, c]), (w_, wT_d[d, c]), (q_, QgT_d[d, c]), (a_, AT_d[d, c]), (k_, Kd_d[d, c]), (e_, egl_d[d, c])):
                                np_ = src.shape[0]
                                S.add("sp", lambda e, tile_=tile_, src=src, np_=np_: e.dma_start(out=tile_.t[0:np_], in_=src), writes=[tile_.b], dma=True)
                            pw = (PS[0], PS[1]) if d == 0 else (PS[4], PS[5])
                            po = PS[2] if d == 0 else PS[6]
                            for h in range(8):
                                pb = pw[h // 4]
                                S.add("pe", lambda e, pb=pb, h=h, w_=w_, sB=sB: e.matmul(pb.t[0:64, (h % 4) * 128:(h % 4 + 1) * 128], lhsT=w_.t[:, h, :], rhs=sB.t[:, h, :],
                                                                                    start=True, stop=True), reads=[w_.b, sB.b], writes=[pb.b])
                            v_ = vn[d].next()
                            for hb_ in range(2):
                                S.add("dve", lambda e, v_=v_, u_=u_, hb_=hb_, pw=pw: e.tensor_tensor(
                                    out=v_.t[0:64, hb_ * 4:(hb_ + 1) * 4, :], in0=u_.t[:, hb_ * 4:(hb_ + 1) * 4, :],
                                    in1=pw[hb_].t[0:64, :].rearrange("p (h d) -> p h d", d=128), op=ALU.subtract),
                                    reads=[u_.b, pw[hb_].b], writes=[v_.b])
                            for h in range(8):
                                S.add("pe", lambda e, po=po, h=h, q_=q_, sB=sB: e.matmul(po.t[:, h * 64:(h + 1) * 64], lhsT=sB.t[:, h, :], rhs=q_.t[:, h, :],
                                                                                    start=True, stop=False), reads=[q_.b, sB.b], writes=[po.b])
                                S.add("pe", lambda e, po=po, h=h, a_=a_, v_=v_: e.matmul(po.t[:, h * 64:(h + 1) * 64], lhsT=v_.t[:, h, :], rhs=a_.t[:, h, :],
                                                                                    start=False, stop=True), reads=[a_.b, v_.b], writes=[po.b])
                            o_ = oS[d].next()
                            S.add("act", lambda e, o_=o_, po=po: e.activation(out=o_.t[:], in_=po.t[:].rearrange("p (h i) -> p h i", i=64), func=AF.Identity),
                                  reads=[po.b], writes=[o_.b])
                            S.add("sp", lambda e, o_=o_, d=d, c=c: e.dma_start(out=oT_d[d, :, :, c * 64:(c + 1) * 64], in_=o_.t[:]), reads=[o_.b], dma=True)
                            for h in range(8):
                                pb = pw[h // 4]
                                S.add("pe", lambda e, pb=pb, h=h, k_=k_, v_=v_: e.matmul(pb.t[:, (h % 4) * 128:(h % 4 + 1) * 128], lhsT=k_.t[:, h, :], rhs=v_.t[0:64, h, :],
                                                                                    start=True, stop=True), reads=[k_.b, v_.b], writes=[pb.b])
                            S.add("pool", lambda e, sF=sF, e_=e_: e.tensor_tensor(out=sF.t[:], in0=sF.t[:], in1=e_.t[:].unsqueeze(2).broadcast_to([128, 8, 128]), op=ALU.mult),
                                  reads=[sF.b, e_.b], writes=[sF.b])
                            for hb_ in range(2):
                                S.add("dve", lambda e, sF=sF, hb_=hb_, pw=pw: e.tensor_tensor(
                                    out=sF.t[:, hb_ * 4:(hb_ + 1) * 4, :], in0=sF.t[:, hb_ * 4:(hb_ + 1) * 4, :],
                                    in1=pw[hb_].t[:].rearrange("p (h d) -> p h d", d=128), op=ALU.add), reads=[sF.b, pw[hb_].b], writes=[sF.b])
                            S.add("act", lambda e, sF=sF, sB=sB: e.activation(out=sB.t[:], in_=sF.t[:], func=AF.Identity), reads=[sF.b], writes=[sB.b])
                    if pi is not None:
                        for d in range(2):
                            S.add("sp", lambda e, d=d, pi=pi, Sf=Sf: e.dma_start(out=nst[d, i, pi], in_=Sf[d].t[:]), reads=[Sf[d].b], dma=True)
                S.end_stage(resched=True)

        def dn_final(l, i):
            with contextlib.ExitStack() as st:
                W = sb(st, "dwout", [128, 8, 1024], BF16)
                gn2 = sb(st, "dng", [128, 2])
                OF = Rot(st, "OF", [128, 8, 512], F32, 2)
                OB = Rot(st, "OB", [128, 8, 512], F32, 2)
                ZZ = Rot(st, "ZZ", [128, 8, 512], F32, 2)
                XX = Rot(st, "XX", [128, 8, 512], F32, 2)
                OG = Rot(st, "OG", [128, 8, 512], BF16, 2)
                SQ = Rot(st, "fsq", [128, 512], F32, 4)
                RS = Rot(st, "frs", [128, 512], F32, 4)
                for kc2 in range(4):
                    S.add("pool", lambda e, kc2=kc2: e.dma_start(out=W.t[:, 2 * kc2:2 * kc2 + 2, :], in_=dwout[i, :, 2 * kc2:2 * kc2 + 2, :],
                                                               max_dma_last_dim=4096), writes=[W.b], dma=True)
                S.add("sp", lambda e: e.dma_start(out=gn2.t[:], in_=dng), writes=[gn2.b], dma=True)
                for tt in range(NTOK // 512):
                    which = 0 if tt < 8 else 1
                    tok = slice(tt * 512, (tt + 1) * 512)
                    of_, ob_, zz, xx, og = OF.next(), OB.next(), ZZ.next(), XX.next(), OG.next()
                    S.add("sp", lambda e, of_=of_, tok=tok: e.dma_start(out=of_.t[:], in_=oT_d[0, :, :, tok]), writes=[of_.b], dma=True)
                    S.add("sp", lambda e, ob_=ob_, tok=tok: e.dma_start(out=ob_.t[:], in_=oT_d[1, :, :, tok]), writes=[ob_.b], dma=True)
                    S.add("sp", lambda e, zz=zz, tok=tok: e.dma_start(out=zz.t[:], in_=zT_d[:, :, tok]), writes=[zz.b], dma=True)
                    S.add("sp", lambda e, xx=xx, tok=tok: e.dma_start(out=xx.t[:], in_=xs[:, :, tok]), writes=[xx.b], dma=True)
                    S.add("pool", lambda e, of_=of_, ob_=ob_: e.tensor_tensor(out=of_.t[:], in0=of_.t[:], in1=ob_.t[:], op=ALU.add),
                          reads=[of_.b, ob_.b], writes=[of_.b])
                    S.add("act", lambda e, zz=zz: e.activation(out=zz.t[:], in_=zz.t[:], func=AF.Silu), reads=[zz.b], writes=[zz.b])
                    for h in range(8):
                        q_, r_ = SQ.next(), RS.next()
                        pb = PS[h % 4]
                        S.add("act", lambda e, q_=q_, of_=of_, h=h: e.activation(out=q_.t[:], in_=of_.t[:, h, :], func=AF.Square), reads=[of_.b], writes=[q_.b])
                        S.add("pe", lambda e, pb=pb, q_=q_: e.matmul(pb.t[:], lhsT=ones32.t[:], rhs=q_.t[:], start=True, stop=True),
                              reads=[ones32.b, q_.b], writes=[pb.b])
                        S.add("act", lambda e, pb=pb, r_=r_: e.activation(out=r_.t[:], in_=pb.t[:], func=AF.Sqrt, scale=1.0 / 128, bias=epsT.t[:, 0:1]),
                              reads=[pb.b, epsT.b], writes=[r_.b])
                        S.add("dve", lambda e, r_=r_: e.reciprocal(out=r_.t[:], in_=r_.t[:]), reads=[r_.b], writes=[r_.b])
                        S.add("dve", lambda e, r_=r_, of_=of_, h=h: e.scalar_tensor_tensor(out=r_.t[:], in0=of_.t[:, h, :], scalar=gn2.t[:, i:i + 1], in1=r_.t[:],
                                                                                      op0=ALU.mult, op1=ALU.mult), reads=[r_.b, of_.b, gn2.b], writes=[r_.b])
                        S.add("pool", lambda e, r_=r_, zz=zz, og=og, h=h: e.tensor_tensor(out=og.t[:, h, :], in0=r_.t[:], in1=zz.t[:, h, :], op=ALU.mult),
                              reads=[r_.b, zz.b], writes=[og.b])
                    for m in range(8):
                        pb = PS[4 + m % 4]
                        for f in range(8):
                            S.add("pe", lambda e, pb=pb, f=f, m=m, og=og: e.matmul(pb.t[:], lhsT=W.t[:, f, m * 128:(m + 1) * 128], rhs=og.t[:, f, :],
                                                                              start=(f == 0), stop=(f == 7)), reads=[W.b, og.b], writes=[pb.b])
                        S.add("dve", lambda e, pb=pb, m=m, xx=xx, which=which: e.scalar_tensor_tensor(
                            out=xx.t[:, m, :], in0=pb.t[:], scalar=hgT.t[:, 1, m, which:which + 1], in1=xx.t[:, m, :], op0=ALU.mult, op1=ALU.add),
                            reads=[pb.b, hgT.b, xx.b], writes=[xx.b])
                    S.add("sp", lambda e, xx=xx, tok=tok: e.dma_start(out=xs[:, :, tok], in_=xx.t[:]), reads=[xx.b], dma=True)
                S.end_stage(resched=True)

        def dn_mixer(l, i):
            nst_ = cfg.get("dn_stages", 5)
            for k_, fn in enumerate((dn_inproj, dn_conv, dn_chunks, dn_scan, dn_final)):
                if k_ < nst_:
                    fn(l, i)

        def ah_hyena_zero(l, i):
            with contextlib.ExitStack() as st:
                z = sb(st, "zz", [128, 4, 1024], BF16)
                S.add("dve", lambda e: e.memset(z.t[:], 0.0), writes=[z.b])
                for mt in range(NMT):
                    S.add("sp", lambda e, mt=mt: e.dma_start(out=ay_d[:, 4:8, mt * TM:(mt + 1) * TM], in_=z.t[:]), reads=[z.b], dma=True)
                S.end_stage()

        cur = xT
        layer_list = cfg.get("layers", None)
        layer_list = list(layer_list) if layer_list is not None else list(range(nlayers))
        do_ffn = cfg.get("ffn", True)

        def copy_stage(src, dst):
            for mt in range(NMT):
                S.add("sp", lambda e, mt=mt: e.dma_start(out=dst[:, :, mt * TM:(mt + 1) * TM], in_=src[:, :, mt * TM:(mt + 1) * TM]), dma=True)
            S.end_stage()

        for li, l in enumerate(layer_list):
            modulation_stage(l)
            if do_ffn:
                ffn_stage(l, 0, cur, xs)
            else:
                copy_stage(cur, xs)
            cur = xs
            if do_mix and l % 2 == 0:
                ah_inproj(l, l // 2)
                ah_attention(l, l // 2)
                if cfg.get("hyena", 1):
                    ah_hyena(l, l // 2)
                else:
                    ah_hyena_zero(l, l // 2)
                ah_outproj(l, l // 2)
            if do_mix and l % 2 == 1:
                dn_mixer(l, l // 2)
            last = (li == len(layer_list) - 1)
            if do_ffn:
                ffn_stage(l, 1, cur, yT if last else xs)
            elif last:
                copy_stage(xs, yT)
        print("stages", S.nstage, "ops", S.ninstr)
    return nc


def host_layout(inputs, core):
    f = lambda a: np.ascontiguousarray(a, dtype=np.float32)
    xs_ = np.asarray(inputs["x_sample"][core])
    xp_ = np.asarray(inputs["x_prompt"][4 * core:4 * core + 4]).reshape(1024, D)
    x = np.concatenate([xs_, xp_], axis=0)
    m = {}
    m["xT"] = f(x.T.reshape(8, 128, NTOK).transpose(1, 0, 2))
    cond = np.stack([np.asarray(inputs["c"][core]), np.asarray(inputs["c_ctx"])], axis=1)
    m["condT"] = f(cond.reshape(8, 128, 2).transpose(1, 0, 2))
    ck = np.asarray(inputs["cache_k"][core])
    ckT = ck.transpose(0, 3, 2, 1)
    m["ckT"] = f(np.concatenate([ckT, ckT], axis=1))
    cv = np.asarray(inputs["cache_v"][core]).reshape(2, 4, 128, 128)
    m["cvv"] = f(cv.transpose(0, 2, 1, 3))
    st_ = np.stack([np.asarray(inputs["state_fwd"][core]), np.asarray(inputs["state_bwd"][core])], axis=0)
    m["sf0"] = f(st_.transpose(0, 1, 3, 2, 4))
    return m


def shared_layout(inputs):
    f = lambda a: np.ascontiguousarray(a, dtype=np.float32)
    m = {}
    aw = np.asarray(inputs["ada_w"])
    m["ada_w"] = f(aw.reshape(DEPTH, 8, 128, 9, 1024).transpose(0, 3, 2, 1, 4))
    ab = np.asarray(inputs["ada_b"]).reshape(DEPTH, 72, 128).transpose(2, 0, 1)
    m["ada_b"] = f(np.repeat(ab[..., None], 2, axis=-1))
    g = np.asarray(inputs["norm_g"]).reshape(DEPTH, 3, 8, 128).transpose(3, 0, 1, 2)
    m["norm_g"] = f(np.repeat(g[..., None], 2, axis=-1))
    w13 = np.asarray(inputs["ffn_w13"]).reshape(DEPTH * 2, 8, 128, 2, NFC, 128)
    m["w13"] = f(w13.transpose(0, 4, 2, 1, 3, 5).reshape(DEPTH * 2, NFC, 128, 8, 256))
    w2 = np.asarray(inputs["ffn_w2"]).reshape(DEPTH * 2, NFC, 128, 8, 128)
    m["w2"] = f(w2.transpose(0, 3, 2, 1, 4))
    wi = np.asarray(inputs["mx_w_in"])
    wcat = np.concatenate([wi[:, :, 0:512], wi[:, :, 512:576], wi[:, :, 512:576], wi[:, :, 576:640], wi[:, :, 576:640],
                           wi[:, :, 768:2304], wi[:, :, 640:768]], axis=2)
    m["win"] = f(wcat.reshape(2, 8, 128, 2432).transpose(0, 2, 1, 3))
    wo = np.asarray(inputs["mx_w_out"])
    m["wout"] = f(wo.reshape(2, 8, 128, 1024).transpose(0, 2, 1, 3))
    qn = np.asarray(inputs["q_norm"])
    kn = np.asarray(inputs["k_norm"])
    pidx = np.arange(128)
    m["qkn"] = f(np.stack([qn[:, pidx % 64].T, kn[:, pidx % 64].T], axis=-1))
    sk = np.asarray(inputs["attn_sink"])
    hidx = 2 * np.arange(4)[None, :] + (pidx // 64)[:, None]
    m["sinkT"] = f(sk[:, hidx].transpose(1, 0, 2))
    cw = np.asarray(inputs["hy_conv_w"]).reshape(2, 3, 12, 128)
    m["hcw"] = f(cw.transpose(3, 0, 2, 1))
    cb = np.asarray(inputs["hy_conv_b"]).reshape(2, 12, 128)
    m["hcb"] = f(cb.transpose(2, 0, 1))
    m["hw1"] = f(inputs["hy_w1"])
    m["hb1"] = f(np.stack([np.asarray(inputs["hy_b1"]).T, np.asarray(inputs["hy_freq1"]).T], axis=-1))
    m["hw2"] = f(inputs["hy_w2"])
    m["hb2"] = f(np.stack([np.asarray(inputs["hy_b2"]).T, np.asarray(inputs["hy_freq2"]).T], axis=-1))
    m["hw3"] = f(inputs["hy_w3"])
    hbz = np.asarray(inputs["hy_bias"]).reshape(2, 2, 4, 128)
    m["hbias"] = f(hbz.transpose(3, 0, 1, 2))
    dwi = np.asarray(inputs["dn_w_in"])
    m["dwin"] = f(dwi.reshape(2, 8, 128, 4128).transpose(0, 2, 1, 3))
    dwo = np.asarray(inputs["dn_w_out"])
    m["dwout"] = f(dwo.reshape(2, 8, 128, 1024).transpose(0, 2, 1, 3))
    dc = np.asarray(inputs["dn_conv_w"]).reshape(2, 3, 24, 128)
    m["dcw"] = f(dc.transpose(3, 0, 2, 1))
    prm = np.concatenate([np.asarray(inputs["dn_a_log"]).reshape(2, 16), np.asarray(inputs["dn_dt_bias"]).reshape(2, 16)], axis=1)
    m["dprm"] = f(np.broadcast_to(prm[None], (128, 2, 32)))
    m["dng"] = f(np.asarray(inputs["dn_norm_g"]).T)
    m.update(const_tables())
    return m


_CONST = {}


def const_tables():
    if _CONST:
        return _CONST
    f = lambda a: np.ascontiguousarray(a, dtype=np.float32)
    pidx = np.arange(128)
    a = (pidx % 64) % 32
    inv = (np.float32(10000.0) ** (-np.arange(0, 32, 2, dtype=np.float32) / np.float32(32))).astype(np.float32)
    t = np.arange(4096)
    r = (t // 64).astype(np.float32)
    col = (t % 64).astype(np.float32)
    ang = np.where((a < 16)[:, None], r[None, :] * inv[a % 16][:, None], col[None, :] * inv[a % 16][:, None]).astype(np.float32)
    _CONST["cosT"] = f(np.cos(ang))
    _CONST["sinT"] = f(np.sin(ang))
    rot = np.zeros((128, 128), np.float32)
    for d_out in range(128):
        dd = d_out % 64
        base = d_out - dd
        if dd < 32:
            rot[base + dd + 32, d_out] = -1.0
        else:
            rot[base + dd - 32, d_out] = 1.0
    _CONST["rotT"] = rot
    blk = np.zeros((128, 128), np.float32)
    blk[:64, :64] = 1.0
    blk[64:, 64:] = 1.0
    _CONST["blk1"] = blk
    ko = np.arange(128)[:, None]
    qo = np.arange(128)[None, :]
    _CONST["mprev"] = f(ko >= qo)
    _CONST["mnext"] = f(ko <= qo)
    _CONST["ident"] = np.eye(128, dtype=np.float32)
    pp = np.arange(64)[:, None]
    ff = np.arange(64)[None, :]
    _CONST["dmask"] = f(np.stack([pp <= ff, pp >= ff, pp < ff, pp > ff, pp == ff], axis=1))
    lvm = []
    for lv in range(6):
        b_ = 2 ** lv
        mn = (pp // (2 * b_) == ff // (2 * b_)) & (pp % (2 * b_) >= b_) & (ff % (2 * b_) < b_)
        lvm.append(mn)
        lvm.append(mn.T)
    _CONST["lvmask"] = f(np.stack(lvm, axis=1))
    HY_MIN = math.log(1e-2) / 1.5
    HY_MAX = math.log(1e-2) / 0.3
    dl = np.abs(np.linspace(HY_MIN, HY_MAX, 2048, dtype=np.float32)).astype(np.float32)
    _CONST["deltas"] = f(np.broadcast_to(dl[None, :], (128, 2048)))
    for L in (4096, 256):
        N = 2 * L
        nb = L // 128
        TT = min(512, L)
        k = np.arange(L, dtype=np.int64)
        t = np.arange(L, dtype=np.int64)
        mm = ((2 * k[:, None] + 1) * t[None, :]) % (2 * N)
        ang = mm.astype(np.float64) * (np.pi / N)
        Ckt = np.cos(ang).astype(np.float32)
        Skt = np.sin(ang).astype(np.float32)
        del mm, ang
        F = np.empty((2, nb, 128, nb, 128), ml_dtypes.bfloat16)
        I = np.empty((2, L // TT, 128, nb, TT), ml_dtypes.bfloat16)
        for ci, tab in enumerate((Ckt, Skt)):
            t4 = tab.reshape(nb, 128, nb, 128)
            F[ci] = t4.transpose(0, 3, 2, 1).astype(ml_dtypes.bfloat16)
            t5 = tab.reshape(nb, 128, L // TT, TT)
            I[ci] = t5.transpose(2, 1, 0, 3).astype(ml_dtypes.bfloat16)
        _CONST["dftF%d" % L] = F
        _CONST["dftI%d" % L] = I
        tt_ = np.linspace(0.0, 1.0, L, dtype=np.float32)
        w = (np.float32(2 * math.pi) * np.arange(L, dtype=np.float32) / np.float32(L)).astype(np.float32)
        fr = np.linspace(1e-4, 15.0, 16, dtype=np.float32)
        fw = (fr[None, :] * w[:, None]).astype(np.float32)
        z = np.concatenate([tt_[:, None], np.cos(fw), -np.sin(fw)], axis=-1).astype(np.float32)
        _CONST["zfeat%d" % L] = f(z.T)
        _CONST["tlag%d" % L] = f(-tt_.reshape(nb, 128).T)
    return _CONST


_CACHE = {}


def run(inputs, cfg, ncores=8, cores=None):
    key = tuple(sorted(cfg.items()))
    if key not in _CACHE:
        _CACHE[key] = build_program(cfg)
    nc = _CACHE[key]
    shared = shared_layout(inputs)
    in_maps = []
    for core in (cores if cores is not None else range(ncores)):
        m = dict(shared)
        m.update(host_layout(inputs, core))
        in_maps.append(m)
    res = run_bass_kernel_spmd(nc, in_maps, core_ids=list(range(ncores)))
    return res


def assemble(res, ncores=8):
    yp = np.zeros((32, 256, D), np.float32)
    ys = np.zeros((8, 4096, D), np.float32)
    nk = np.zeros((32, 2, 256, 2, 64), np.float32)
    nv = np.zeros((32, 2, 256, 2, 64), np.float32)
    nsf = np.zeros((32, 2, 8, 128, 128), np.float32)
    nsb = np.zeros((32, 2, 8, 128, 128), np.float32)
    for core in range(ncores):
        r = res.results[core]
        y = r["yT"].transpose(1, 0, 2).reshape(D, NTOK).T
        ys[core] = y[:4096]
        yp[4 * core:4 * core + 4] = y[4096:].reshape(4, 256, D)
        k = r["newk"].reshape(2, 2, 64, 4, 256)
        nk[4 * core:4 * core + 4] = k.transpose(3, 0, 4, 1, 2)
        v = r["newv"].reshape(2, 4, 256, 2, 64)
        nv[4 * core:4 * core + 4] = v.transpose(1, 0, 2, 3, 4)
        s_ = r["nst"]
        nsf[4 * core:4 * core + 4] = s_[0].transpose(1, 0, 3, 2, 4)
        nsb[4 * core:4 * core + 4] = s_[1].transpose(1, 0, 3, 2, 4)
    return yp, ys, nk, nv, nsf, nsb


def kernel(**inputs):
    res = run(inputs, {})
    return assemble(res)
```
